# Optimizing a Trainium2 kernel written in Bass

```python
import math
import jax, jax.numpy as jnp
from jax import lax
import numpy as np

D_MODEL = 2048
BATCH = 4
SEQ = 8192
DEPTH = 1
DEC_BATCH = 1
DEC_SEQ = 8192
PAST_LEN = 128

MIX_WIDTH = D_MODEL
ATT_WIDTH = MIX_WIDTH // 2
SSM_WIDTH = MIX_WIDTH - ATT_WIDTH
V_HEAD_DIM = 128
QK_NOPE_DIM = 128
QK_ROPE_DIM = 64
QK_HEAD_DIM = QK_NOPE_DIM + QK_ROPE_DIM
N_HEADS = ATT_WIDTH // V_HEAD_DIM
KV_LORA_RANK = 512
ROPE_BASE = 10000.0
Q_BLOCK = 128
SSM_GROUP = 16
N_SSM_GROUPS = SSM_WIDTH // SSM_GROUP
SSM_STATE = 64
DT_MIN = 1e-3
DT_MAX = 1e-1
D_FF = -(-8 * D_MODEL // (3 * 256)) * 256
Q_COLS = N_HEADS * QK_HEAD_DIM
IN_COLS = Q_COLS + KV_LORA_RANK + QK_ROPE_DIM + SSM_WIDTH
NORM_EPS = 1e-6

kernel_name = "hybrid_mla_s5_encoder_layer"


def rms_norm(x, g):
    xf = x.astype(jnp.float32)
    y = xf * lax.rsqrt(jnp.mean(xf * xf, axis=-1, keepdims=True) + NORM_EPS)
    return (y * g.astype(jnp.float32)).astype(x.dtype)


def rope_tables(length):
    pos = jnp.arange(length, dtype=jnp.float32)
    inv_freq = ROPE_BASE ** (-jnp.arange(0, QK_ROPE_DIM, 2, dtype=jnp.float32) / QK_ROPE_DIM)
    ang = pos[:, None] * inv_freq[None, :]
    return jnp.cos(ang), jnp.sin(ang)


def apply_rope(x, cos, sin):
    xf = x.astype(jnp.float32)
    x1, x2 = xf[..., : QK_ROPE_DIM // 2], xf[..., QK_ROPE_DIM // 2:]
    out = jnp.concatenate([x1 * cos - x2 * sin, x2 * cos + x1 * sin], axis=-1)
    return out.astype(x.dtype)


def dense_attention(q, k, v):
    b, h, length, dk = q.shape
    nb = length // Q_BLOCK
    scale = 1.0 / math.sqrt(dk)
    qb = q.reshape(b, h, nb, Q_BLOCK, dk).transpose(2, 0, 1, 3, 4)

    def one_block(q_blk):
        s = jnp.einsum("bhqd,bhkd->bhqk", q_blk, k, preferred_element_type=jnp.float32) * scale
        p = jax.nn.softmax(s, axis=-1)
        return jnp.einsum("bhqk,bhkd->bhqd", p.astype(v.dtype), v)

    o = lax.map(one_block, qb)
    return o.transpose(1, 0, 3, 2, 4).reshape(b, length, h * v.shape[-1])


def mla_mixer(q, c_kv, k_rope, kv_norm_g, w_ukv, q_norm_g, k_norm_g):
    b, length, _ = q.shape
    q = q.reshape(b, length, N_HEADS, QK_HEAD_DIM).transpose(0, 2, 1, 3)
    kv = rms_norm(c_kv, kv_norm_g) @ w_ukv
    kv = kv.reshape(b, length, N_HEADS, QK_NOPE_DIM + V_HEAD_DIM).transpose(0, 2, 1, 3)
    k_nope, v = kv[..., :QK_NOPE_DIM], kv[..., QK_NOPE_DIM:]
    k_r = jnp.broadcast_to(k_rope[:, None], (b, N_HEADS, length, QK_ROPE_DIM))
    k = jnp.concatenate([k_nope, k_r], axis=-1)
    q = rms_norm(q, q_norm_g)
    k = rms_norm(k, k_norm_g)
    cos, sin = rope_tables(length)
    q = jnp.concatenate([q[..., :QK_NOPE_DIM], apply_rope(q[..., QK_NOPE_DIM:], cos, sin)], axis=-1)
    k = jnp.concatenate([k[..., :QK_NOPE_DIM], apply_rope(k[..., QK_NOPE_DIM:], cos, sin)], axis=-1)
    return dense_attention(q, k, v)


def _scan_op(e1, e2):
    a1, b1 = e1
    a2, b2 = e2
    return a1 * a2, a2 * b1 + b2


def s5_direction(u, lam_re, lam_im, log_dt, b_re, b_im, c_re, c_im, reverse):
    lam = lax.complex(lam_re.astype(jnp.float32), lam_im.astype(jnp.float32))
    dt = jnp.exp(log_dt.astype(jnp.float32))[:, None]
    lam_bar = jnp.exp(lam * dt)
    b_mat = lax.complex(b_re.astype(jnp.float32), b_im.astype(jnp.float32))
    b_bar = ((lam_bar - 1.0) / lam)[..., None] * b_mat
    bu = jnp.einsum("gpc,lgc->lgp", b_bar, u.astype(jnp.complex64))
    a = jnp.broadcast_to(lam_bar, bu.shape)
    _, states = lax.associative_scan(_scan_op, (a, bu), axis=0, reverse=reverse)
    c_mat = lax.complex(c_re.astype(jnp.float32), c_im.astype(jnp.float32))
    return jnp.einsum("gcp,lgp->lgc", c_mat, states).real


def s5_mixer(u, lam_re, lam_im, log_dt, b_re, b_im, c_re, c_im, d_skip, w_glu, b_glu):
    b, length, _ = u.shape
    uf = u.astype(jnp.float32).reshape(b, length, N_SSM_GROUPS, SSM_GROUP)

    def one_sequence(us):
        fwd = s5_direction(us, lam_re[0], lam_im[0], log_dt[0], b_re, b_im,
                           c_re[0], c_im[0], reverse=False)
        bwd = s5_direction(us, lam_re[1], lam_im[1], log_dt[1], b_re, b_im,
                           c_re[1], c_im[1], reverse=True)
        return fwd + bwd

    y = lax.map(one_sequence, uf).reshape(b, length, SSM_WIDTH)
    y = y + d_skip.astype(jnp.float32) * uf.reshape(b, length, SSM_WIDTH)
    y = jax.nn.gelu(y).astype(u.dtype)
    return y * jax.nn.sigmoid(y @ w_glu + b_glu)


def encoder_layer(x, c, w_ada, b_ada, norm_mix_g, w_in, kv_norm_g, w_ukv, q_norm_g, k_norm_g,
                  lam_re, lam_im, log_dt, b_re, b_im, c_re, c_im, d_skip, w_glu, b_glu,
                  att_out_g, ssm_out_g, w_o, norm_ffn_g, w1, w3, w2):
    mod = (jax.nn.silu(c) @ w_ada + b_ada)[:, None, :]
    shift1, scale1, gate1, shift2, scale2, gate2 = jnp.split(mod, 6, axis=-1)
    h = rms_norm(x, norm_mix_g) * (1.0 + scale1) + shift1
    proj = h @ w_in
    q, c_kv, k_rope, u = jnp.split(
        proj, [Q_COLS, Q_COLS + KV_LORA_RANK, Q_COLS + KV_LORA_RANK + QK_ROPE_DIM], axis=-1)
    att = mla_mixer(q, c_kv, k_rope, kv_norm_g, w_ukv, q_norm_g, k_norm_g)
    ssm = s5_mixer(u, lam_re, lam_im, log_dt, b_re, b_im, c_re, c_im, d_skip, w_glu, b_glu)
    mixed = jnp.concatenate([rms_norm(att, att_out_g), rms_norm(ssm, ssm_out_g)], axis=-1)
    x = x + gate1 * (mixed @ w_o)
    h = rms_norm(x, norm_ffn_g) * (1.0 + scale2) + shift2
    ffn = (jax.nn.silu(h @ w1) * (h @ w3)) @ w2
    return x + gate2 * ffn


def setup_inputs(seed: int = 0) -> dict:
    key = jax.random.key(seed)
    ks = jax.random.split(key, 32)
    f32 = jnp.float32

    def nrm(k, shape, scale):
        return jax.random.normal(k, shape, f32) * scale

    def gain(k, n):
        return 1.0 + nrm(k, (DEPTH, n), 0.1)

    G, P = N_SSM_GROUPS, SSM_STATE
    n_idx = jnp.arange(P, dtype=f32)
    return {
        "x_prompt": nrm(ks[0], (BATCH, SEQ, D_MODEL), 1.0),
        "x_sample": nrm(ks[1], (DEC_BATCH, DEC_SEQ, D_MODEL), 1.0),
        "c_prompt": nrm(ks[2], (BATCH, D_MODEL), 1.0),
        "c_sample": nrm(ks[3], (DEC_BATCH, D_MODEL), 1.0),
        "w_ada": nrm(ks[4], (DEPTH, D_MODEL, 6 * D_MODEL), 0.5 * D_MODEL ** -0.5),
        "b_ada": nrm(ks[5], (DEPTH, 6 * D_MODEL), 0.01),
        "norm_mix_g": gain(ks[6], D_MODEL),
        "w_in": nrm(ks[7], (DEPTH, D_MODEL, IN_COLS), D_MODEL ** -0.5),
        "kv_norm_g": gain(ks[8], KV_LORA_RANK),
        "w_ukv": nrm(ks[9], (DEPTH, KV_LORA_RANK, N_HEADS * (QK_NOPE_DIM + V_HEAD_DIM)),
                     KV_LORA_RANK ** -0.5),
        "q_norm_g": gain(ks[10], QK_HEAD_DIM),
        "k_norm_g": gain(ks[11], QK_HEAD_DIM),
        "lam_re": -0.5 * jnp.exp(nrm(ks[12], (DEPTH, 2, G, P), 0.05)),
        "lam_im": jnp.pi * n_idx + nrm(ks[13], (DEPTH, 2, G, P), 0.01),
        "log_dt": jax.random.uniform(ks[14], (DEPTH, 2, G), f32,
                                     math.log(DT_MIN), math.log(DT_MAX)),
        "b_re": nrm(ks[15], (DEPTH, G, P, SSM_GROUP), (2 * SSM_GROUP) ** -0.5),
        "b_im": nrm(ks[16], (DEPTH, G, P, SSM_GROUP), (2 * SSM_GROUP) ** -0.5),
        "c_re": nrm(ks[17], (DEPTH, 2, G, SSM_GROUP, P), (2 * P) ** -0.5),
        "c_im": nrm(ks[18], (DEPTH, 2, G, SSM_GROUP, P), (2 * P) ** -0.5),
        "d_skip": nrm(ks[19], (DEPTH, SSM_WIDTH), 1.0),
        "w_glu": nrm(ks[20], (DEPTH, SSM_WIDTH, SSM_WIDTH), SSM_WIDTH ** -0.5),
        "b_glu": nrm(ks[21], (DEPTH, SSM_WIDTH), 0.01),
        "att_out_g": gain(ks[22], ATT_WIDTH),
        "ssm_out_g": gain(ks[23], SSM_WIDTH),
        "w_o": nrm(ks[24], (DEPTH, MIX_WIDTH, D_MODEL), MIX_WIDTH ** -0.5),
        "norm_ffn_g": gain(ks[25], D_MODEL),
        "w1": nrm(ks[26], (DEPTH, D_MODEL, D_FF), D_MODEL ** -0.5),
        "w3": nrm(ks[27], (DEPTH, D_MODEL, D_FF), D_MODEL ** -0.5),
        "w2": nrm(ks[28], (DEPTH, D_FF, D_MODEL), D_FF ** -0.5),
    }


def reference(x_prompt, x_sample, c_prompt, c_sample, w_ada, b_ada, norm_mix_g, w_in,
              kv_norm_g, w_ukv, q_norm_g, k_norm_g, lam_re, lam_im, log_dt, b_re, b_im,
              c_re, c_im, d_skip, w_glu, b_glu, att_out_g, ssm_out_g, w_o, norm_ffn_g,
              w1, w3, w2):
    y_prompt, y_sample = x_prompt, x_sample
    for l in range(DEPTH):
        layer = dict(w_ada=w_ada[l], b_ada=b_ada[l], norm_mix_g=norm_mix_g[l], w_in=w_in[l],
                     kv_norm_g=kv_norm_g[l], w_ukv=w_ukv[l], q_norm_g=q_norm_g[l],
                     k_norm_g=k_norm_g[l], lam_re=lam_re[l], lam_im=lam_im[l],
                     log_dt=log_dt[l], b_re=b_re[l], b_im=b_im[l], c_re=c_re[l], c_im=c_im[l],
                     d_skip=d_skip[l], w_glu=w_glu[l], b_glu=b_glu[l],
                     att_out_g=att_out_g[l], ssm_out_g=ssm_out_g[l], w_o=w_o[l],
                     norm_ffn_g=norm_ffn_g[l], w1=w1[l], w3=w3[l], w2=w2[l])
        y_prompt = encoder_layer(y_prompt, c_prompt, **layer)
        y_sample = encoder_layer(y_sample, c_sample, **layer)
    return (y_prompt, y_sample)
```

```python
import contextlib
import numpy as np
import concourse.bass as bass
import concourse.mybir as mybir
from concourse.bass_utils import run_bass_kernel_spmd

F32 = mybir.dt.float32
BF16 = mybir.dt.bfloat16
AF = mybir.ActivationFunctionType
ALU = mybir.AluOpType
AX = mybir.AxisListType

D = 2048
NH = 8
DFF = 5632
INC = 3136
EPS = 1e-6
MEMF = 52000


class Prog:
    def __init__(self, nc):
        self.nc = nc
        self.ops = {e: [] for e in ("pe", "act", "dve", "pool", "sp")}
        self.count = {}
        self.waited = {e: {} for e in self.ops}
        self.res = {}
        self.semkeys = []
        self.chslot = {}

    def _tok(self, semkey, inc):
        if semkey not in self.count:
            self.count[semkey] = 0
            self.semkeys.append(semkey)
        self.count[semkey] += inc
        return (semkey, self.count[semkey])

    def op(self, eng, fn, reads=(), writes=(), ch=None):
        deps = {}

        def add(toks, raw):
            for k, v in toks.items():
                if ch is None and k == eng and (eng == "pe" or not raw):
                    continue
                if deps.get(k, 0) < v:
                    deps[k] = v
        for r in reads:
            st = self.res.get(r)
            if st is not None:
                add(st[0], True)
        for w in writes:
            st = self.res.get(w)
            if st is not None:
                add(st[0], True)
                add(st[1], False)
        if ch is not None:
            if ch not in self.chslot:
                self.chslot[ch] = len(self.chslot)
            chkey = "dma:%d" % self.chslot[ch]
            if self.count.get(chkey, 0) > 0:
                deps[chkey] = self.count[chkey]
        waits = []
        wd = self.waited[eng]
        for k, v in deps.items():
            if wd.get(k, 0) < v:
                wd[k] = v
                waits.append((k, v))
        if ch is None:
            tok = self._tok(eng, 1)
            inc = 1
        else:
            tok = self._tok(chkey, 16)
            inc = 16
        self.ops[eng].append((waits, fn, tok[0], inc))
        for r in reads:
            st = self.res.setdefault(r, [{}, {}])
            if st[1].get(tok[0], 0) < tok[1]:
                st[1][tok[0]] = tok[1]
        for w in writes:
            self.res[w] = [{tok[0]: tok[1]}, {}]
        return tok

    def barrier(self):
        for eng in self.ops:
            waits = []
            wd = self.waited[eng]
            for k, v in self.count.items():
                if wd.get(k, 0) < v:
                    wd[k] = v
                    waits.append((k, v))
            if waits:
                self.ops[eng].append((waits, None, None, 0))
        self.res = {}
        self.chslot = {}

    def emit(self):
        nc = self.nc
        with contextlib.ExitStack() as es:
            sems = {}
            for k in self.semkeys:
                sems[k] = es.enter_context(nc.semaphore("s_" + k.replace(":", "_")))
            block = es.enter_context(nc.Block())

            def run(engname):
                def body(e):
                    for waits, fn, semkey, inc in self.ops[engname]:
                        for k, v in waits:
                            e.wait_ge(sems[k], v)
                        if fn is not None:
                            ins = fn(e)
                            ins.then_inc(sems[semkey], inc)
                return body
            block.tensor(run("pe"))
            block.scalar(run("act"))
            block.vector(run("dve"))
            block.gpsimd(run("pool"))
            block.sync(run("sp"))


def build(L, dbg=(), upto=99):
    NT = L // 128
    nc = bass.Bass("TRN2", target_bir_lowering=False)
    P = Prog(nc)

    def din(name, shape):
        return nc.dram_tensor(name, list(shape), F32, kind="ExternalInput").ap()

    def dscr(name, shape, dt):
        kind = "ExternalOutput" if name in dbg else "Internal"
        return nc.dram_tensor(name, list(shape), dt, kind=kind).ap()

    x = din("x", [L, D]); cvec = din("c", [D])
    w_ada = din("w_ada", [D, 6 * D]); b_ada = din("b_ada", [6 * D])
    norm_mix_g = din("norm_mix_g", [D]); w_in = din("w_in", [D, INC])
    kv_norm_g = din("kv_norm_g", [512]); w_ukv = din("w_ukv", [512, 2048])
    q_norm_g = din("q_norm_g", [192]); k_norm_g = din("k_norm_g", [192])
    lam_re = din("lam_re", [2, 64, 64]); lam_im = din("lam_im", [2, 64, 64]); log_dt = din("log_dt", [2, 64])
    b_re = din("b_re", [64, 64, 16]); b_im = din("b_im", [64, 64, 16])
    c_re = din("c_re", [2, 64, 16, 64]); c_im = din("c_im", [2, 64, 16, 64])
    d_skip = din("d_skip", [1024]); w_glu = din("w_glu", [1024, 1024]); b_glu = din("b_glu", [1024])
    att_out_g = din("att_out_g", [1024]); ssm_out_g = din("ssm_out_g", [1024])
    w_o = din("w_o", [D, D]); norm_ffn_g = din("norm_ffn_g", [D])
    w1 = din("w1", [D, DFF]); w3 = din("w3", [D, DFF]); w2 = din("w2", [DFF, D])
    ident_d = din("ident", [128, 128]); ropec = din("ropec", [L, 32]); ropes = din("ropes", [L, 32])
    maskf_d = din("maskf", [128, 128]); maskb_d = din("maskb", [128, 128])
    sel_d = din("sel", [128, 64, 128]); selt_d = din("selt", [128, 64, 128])
    y_out = nc.dram_tensor("y", [L, D], F32, kind="ExternalOutput").ap()

    MOD_d = dscr("MOD_d", [6 * D], F32)
    QT_d = dscr("QT_d", [NH, 128, L], BF16); QR_d = dscr("QR_d", [NH, 64, L], BF16)
    KT_d = dscr("KT_d", [NH, 128, L], BF16); KR_d = dscr("KR_d", [NH, 64, L], BF16)
    V_d = dscr("V_d", [NH, L, 128], BF16)
    UT_d = dscr("UT_d", [1024, L], BF16)
    W1s = dscr("W1s", [22, 128, 16 * 256], BF16)
    W3s = dscr("W3s", [22, 128, 16 * 256], BF16)
    W2s = dscr("W2s", [2, 11, 128, 4 * 1024], BF16)
    ATTT_d = dscr("ATTT_d", [1024, L], BF16)

    es = contextlib.ExitStack()
    mem = es.enter_context(nc.sbuf_tensor("mem", [128, MEMF], F32))
    ps = [es.enter_context(nc.psum_tensor(f"ps{i}", [128, 512], F32)) for i in range(8)]

    class Alloc:
        def __init__(self, base=0):
            self.off = base
            self.base = base

        def reset(self):
            self.off = self.base

        def t(self, shape, dt):
            n = int(np.prod(shape[1:]))
            nb = n * (2 if dt == BF16 else 4)
            nb4 = (nb + 3) // 4
            assert self.off + nb4 <= MEMF, ("SBUF arena overflow", self.off, nb4)
            v = mem[0:shape[0], self.off:self.off + nb4]
            if dt != F32:
                v = v.bitcast(dt)
                if nb4 * 2 != n:
                    v = v[:, 0:n]
            if len(shape) == 3:
                v = v.rearrange("p (a b) -> p a b", b=shape[2])
            elif len(shape) == 4:
                v = v.rearrange("p (a b c) -> p a b c", b=shape[2], c=shape[3])
            self.off += nb4
            return v

    def MM(out, lhsT, rhs, st, sp, r, w):
        P.op("pe", lambda e: e.matmul(out, lhsT, rhs, start=st, stop=sp), reads=r, writes=w)

    def ACT(out, in_, func, r, w, bias=None, scale=None, accum=None):
        kw = {}
        if bias is not None:
            kw["bias"] = bias
        if scale is not None:
            kw["scale"] = scale
        if accum is not None:
            kw["accum_out"] = accum
        P.op("act", lambda e: e.activation(out, in_, func, **kw), reads=r, writes=w)

    def TT(eng, out, a, b, op, r, w):
        P.op(eng, lambda e: e.tensor_tensor(out, a, b, op), reads=r, writes=w)

    def TS(eng, out, a, s1, s2, op0, op1, r, w):
        if s2 is None:
            P.op(eng, lambda e: e.tensor_scalar(out, a, s1, None, op0), reads=r, writes=w)
        else:
            P.op(eng, lambda e: e.tensor_scalar(out, a, s1, s2, op0, op1), reads=r, writes=w)

    def STT(eng, out, a, s, b, op0, op1, r, w):
        P.op(eng, lambda e: e.scalar_tensor_tensor(out, a, s, b, op0, op1), reads=r, writes=w)

    def CP(eng, out, in_, r, w):
        if eng == "act":
            P.op(eng, lambda e: e.copy(out, in_), reads=r, writes=w)
        else:
            P.op(eng, lambda e: e.tensor_copy(out, in_), reads=r, writes=w)

    def RSUM(eng, out, in_, r, w):
        P.op(eng, lambda e: e.reduce_sum(out, in_, AX.X), reads=r, writes=w)

    def MSET(eng, out, val, w):
        P.op(eng, lambda e: e.memset(out, val), writes=w)

    def DMA(q, out, in_, ch, r, w, slow=False):
        if slow:
            P.op(q, lambda e: e.dma_start(out=out, in_=in_, allow_slow_non_contiguous=True), reads=r, writes=w, ch=ch)
        else:
            P.op(q, lambda e: e.dma_start(out=out, in_=in_), reads=r, writes=w, ch=ch)

    def RSTD(dst, src, n, name, srcname):
        TS("dve", dst, src, 1.0 / n, EPS, ALU.mult, ALU.add, r=[srcname], w=[name])
        P.op("act", lambda e: e.sqrt(dst, dst), reads=[name], writes=[name])
        P.op("dve", lambda e: e.reciprocal(dst, dst), reads=[name], writes=[name])

    def bc(ap, shape, axis):
        return ap.unsqueeze(axis).to_broadcast(shape)

    PA = Alloc(0)
    IDF = PA.t([128, 128], F32)
    IDB = PA.t([128, 128], BF16)
    GS = PA.t([128, 64], F32)
    A8T = PA.t([128, 2, 2, 64], F32)
    SSS = PA.t([128, 64], F32)
    SSA = PA.t([128, 64], F32)
    pers_end = PA.off
    A = Alloc(pers_end)

    DMA("sp", IDF, ident_d, "c0", [], ["IDF"])
    DMA("pool", IDB, ident_d, "c1", [], ["IDB"])

    SC = A.t([128, 16], F32)
    SCB = A.t([128, 16, 128], F32)
    WA = [A.t([128, 16, 512], F32) for _ in range(2)]
    BAb = [A.t([128, 512], F32) for _ in range(2)]
    ONES = A.t([1, 128], F32)
    MODB = A.t([128, 6 * D], F32)
    TMP0 = A.t([128, 96, 128], F32)
    COLS = A.t([128, 96], F32)
    NG = A.t([128, 32], F32)

    DMA("sp", SC, cvec.rearrange("(k p) -> p k", p=128), "c2", [], ["SC"], slow=True)
    DMA("sp", NG[:, 0:16], norm_mix_g.rearrange("(k p) -> p k", p=128), "c3", [], ["NG0"], slow=True)
    DMA("sp", NG[:, 16:32], norm_ffn_g.rearrange("(k p) -> p k", p=128), "c4", [], ["NG1"], slow=True)
    ACT(SC, SC, AF.Silu, ["SC"], ["SC"])
    CP("dve", SCB, bc(SC, [128, 16, 128], 2), ["SC"], ["SCB"])
    MSET("pool", ONES, 1.0, ["ONES"])
    w_ada_v = w_ada.rearrange("(k p) n -> p k n", p=128)
    b_ada_v = b_ada.rearrange("(o n) -> o n", o=1)
    for nb in range(24):
        b = nb % 2
        DMA("sp", WA[b], w_ada_v[:, :, nb * 512:(nb + 1) * 512], f"WA{b}", [], [f"WA{b}"])
        DMA("sp", BAb[b], b_ada[nb * 512:(nb + 1) * 512].partition_broadcast(128), f"BA{b}", [], [f"BA{b}"])
        pt = ps[b]
        for k in range(16):
            MM(pt[:, :], SCB[:, k, :], WA[b][:, k, :], k == 0, k == 15, ["SCB", f"WA{b}"], [f"ps{b}"])
        TT("dve", MODB[:, nb * 512:(nb + 1) * 512], pt[:, :], BAb[b], ALU.add, [f"ps{b}", f"BA{b}"], [f"MODB{nb}"])
    allmod = [f"MODB{nb}" for nb in range(24)]
    DMA("sp", MOD_d.rearrange("(o n) -> o n", o=1), MODB[0:1, :], "c5", allmod, ["MOD_d"])
    MODB3 = MODB.rearrange("p (a b) -> p a b", b=128)
    TT("dve", TMP0, MODB3, bc(IDF, [128, 96, 128], 1), ALU.mult, allmod + ["IDF"], ["TMP0"])
    RSUM("dve", COLS, TMP0, ["TMP0"], ["COLS"])
    STT("dve", GS[:, 0:16], COLS[:, 16:32], 1.0, NG[:, 0:16], ALU.add, ALU.mult, ["COLS", "NG0"], ["GS"])
    CP("dve", GS[:, 16:32], COLS[:, 0:16], ["COLS"], ["GS"])
    STT("dve", GS[:, 32:48], COLS[:, 64:80], 1.0, NG[:, 16:32], ALU.add, ALU.mult, ["COLS", "NG1"], ["GS"])
    CP("dve", GS[:, 48:64], COLS[:, 48:64], ["COLS"], ["GS"])
    P.barrier()
    if upto < 1:
        P.emit(); es.close(); return nc

    A.reset()
    WIN = A.t([128, 16, INC], BF16)
    WUKV = A.t([128, 4, 2048], BF16)
    XTd = [A.t([128, D], F32) for _ in range(2)]
    XBd = [A.t([128, D], BF16) for _ in range(2)]
    STX = A.t([128, 4], F32)
    JNKX = A.t([128, D], BF16)
    HT = A.t([128, 16, 128], BF16)
    QF = A.t([128, 1536], F32)
    CKV = A.t([128, 576], F32)
    UF = A.t([128, 1024], BF16)
    KVF = A.t([128, 2048], F32)
    SCR = A.t([128, 2048], F32)
    QN = A.t([128, 8, 192], BF16)
    KN = A.t([128, 8, 192], BF16)
    QTs = A.t([128, 8, 128], BF16)
    QRs = A.t([64, 8, 128], BF16)
    KTs = A.t([128, 8, 128], BF16)
    KRs = A.t([64, 8, 128], BF16)
    UTs = A.t([128, 8, 128], BF16)
    VS = A.t([128, 8, 128], BF16)
    CKN = A.t([128, 512], BF16)
    CKT = A.t([128, 4, 128], BF16)
    RC = A.t([128, 32], F32)
    RS = A.t([128, 32], F32)
    GQ = A.t([128, 192], F32)
    GK = A.t([128, 192], F32)
    KVG = A.t([128, 4], F32)
    ST = A.t([128, 32], F32)
    RT = A.t([128, 8, 32], F32)
    RT2 = A.t([128, 8, 32], F32)
    RT3 = A.t([128, 8, 32], F32)
    RT4 = A.t([128, 8, 32], F32)
    KRG = A.t([128, 64], F32)
    KRR = A.t([128, 64], F32)

    for k in range(16):
        DMA("pool", WIN[:, k, :], w_in[k * 128:(k + 1) * 128, :], f"WIN{k}", [], [f"WIN{k}"])
    for k in range(4):
        DMA("pool", WUKV[:, k, :], w_ukv[k * 128:(k + 1) * 128, :], f"WUKV{k}", [], [f"WUKV{k}"])
    DMA("sp", GQ, q_norm_g.partition_broadcast(128), "c6", [], ["GQ"])
    DMA("sp", GK, k_norm_g.partition_broadcast(128), "c7", [], ["GK"])
    DMA("sp", KVG, kv_norm_g.rearrange("(k p) -> p k", p=128), "c8", [], ["KVG"], slow=True)

    blocks = [(0, 512), (512, 512), (1024, 512), (1536, 512), (2048, 64), (2112, 512), (2624, 512)]
    import os
    P1STOP = int(os.environ.get("P1STOP", "-1"))

    class _Stop(Exception):
        pass

    def CK(n):
        if P1STOP == n:
            raise _Stop()
    def front_load(t_):
        pb_ = t_ % 2
        rr = t_ * 128
        DMA("sp", XTd[pb_], x[rr:rr + 128, :], f"XT{pb_}", [], [f"XT{pb_}"])

    def front_a(t_):
        pb_ = t_ % 2
        MSET("pool", STX[:, pb_ * 2:pb_ * 2 + 2], 0.0, [f"x_ss{pb_}", f"xr{pb_}"])
        ACT(JNKX, XTd[pb_], AF.Square, [f"XT{pb_}"], ["JNKX", f"x_ss{pb_}"], accum=STX[:, pb_ * 2:pb_ * 2 + 1])
        RSTD(STX[:, pb_ * 2 + 1:pb_ * 2 + 2], STX[:, pb_ * 2:pb_ * 2 + 1], D, f"xr{pb_}", f"x_ss{pb_}")
        ACT(XBd[pb_], XTd[pb_], AF.Copy, [f"XT{pb_}", f"xr{pb_}"], [f"XB{pb_}"], scale=STX[:, pb_ * 2 + 1:pb_ * 2 + 2])

    def p1_head(t):
        r0 = t * 128
        XB = XBd[t % 2]
        DMA("sp", RC, ropec[r0:r0 + 128, :], "RC", [], ["RC"])
        DMA("sp", RS, ropes[r0:r0 + 128, :], "RS", [], ["RS"])
        MSET("pool", ST, 0.0, ["c_ss", "qr_ss", "k_ssn", "kr_ss", "kr_ss2", "cr", "qr", "kr"])


    def p1_xT(t):
        r0 = t * 128
        XB = XBd[t % 2]
        allht = [f"HT{k}" for k in range(16)]
        for k in range(16):
            MM(ps[k // 4][:, (k % 4) * 128:(k % 4 + 1) * 128], XB[:, k * 128:(k + 1) * 128], IDB, True, True,
               [f"XB{t % 2}", "IDB"], [f"ps{k // 4}"])
        for k in range(16):
            src = ps[k // 4][:, (k % 4) * 128:(k % 4 + 1) * 128]
            if False:
                ACT(HT[:, k, :], src, AF.Identity, [f"ps{k // 4}", "GS"], [f"HT{k}"],
                    bias=GS[:, 16 + k:17 + k], scale=GS[:, k:k + 1])
            else:
                TS("dve", HT[:, k, :], src, GS[:, k:k + 1], GS[:, 16 + k:17 + k], ALU.mult, ALU.add,
                   [f"ps{k // 4}", "GS"], [f"HT{k}"])

    def p1_proj(t):
        allht = [f"HT{k}" for k in range(16)]
        for bi, (c0, w) in enumerate(blocks):
            pi = 4 + bi % 4
            pt = ps[pi]
            for k in range(16):
                MM(pt[:, 0:w], HT[:, k, :], WIN[:, k, c0:c0 + w], k == 0, k == 15, allht + [f"WIN{k}"], [f"ps{pi}"])
            if bi < 3:
                CP("act", QF[:, c0:c0 + w], pt[:, 0:w], [f"ps{pi}"], [f"QF{bi}"])
            elif bi == 3:
                CP("dve", CKV[:, 0:512], pt[:, 0:w], [f"ps{pi}"], ["CKVa"])
            elif bi == 4:
                CP("dve", CKV[:, 512:576], pt[:, 0:w], [f"ps{pi}"], ["CKVb"])
            else:
                CP("act", UF[:, c0 - 2112:c0 - 2112 + w], pt[:, 0:w], [f"ps{pi}"], [f"UF{bi}"])


    def p1_ckv(t):
        r0 = t * 128
        allkv = [f"KVF{nb}" for nb in range(4)]
        ACT(JNKX[:, 0:512], CKV[:, 0:512], AF.Square, ["CKVa"], ["JNKX", "c_ss"], accum=ST[:, 2:3])
        RSTD(ST[:, 3:4], ST[:, 2:3], 512, "cr", "c_ss")
        ACT(CKN, CKV[:, 0:512], AF.Copy, ["CKVa", "cr"], ["CKN"], scale=ST[:, 3:4])
        for k in range(4):
            MM(ps[0][:, k * 128:(k + 1) * 128], CKN[:, k * 128:(k + 1) * 128], IDB, True, True, ["CKN", "IDB"], ["ps0"])
        for k in range(4):
            TS("dve", CKT[:, k, :], ps[0][:, k * 128:(k + 1) * 128], KVG[:, k:k + 1], None, ALU.mult, None,
               ["ps0", "KVG"], ["CKT"])
        for nb in range(4):
            pi = 4 + nb
            for k in range(4):
                MM(ps[pi][:, :], CKT[:, k, :], WUKV[:, k, nb * 512:(nb + 1) * 512], k == 0, k == 3,
                   ["CKT", f"WUKV{k}"], [f"ps{pi}"])
            CP("act", KVF[:, nb * 512:(nb + 1) * 512], ps[pi][:, :], [f"ps{pi}"], [f"KVF{nb}"])
        allkv = [f"KVF{nb}" for nb in range(4)]


    def p1_qchain(t):
        r0 = t * 128
        allq = ["QF0", "QF1", "QF2"]
        allq = ["QF0", "QF1", "QF2"]
        QF3 = QF.rearrange("p (h d) -> p h d", d=192)
        SCR3 = SCR[:, 0:1536].rearrange("p (h d) -> p h d", d=192)
        TT("pool", SCR[:, 0:1536], QF, QF, ALU.mult, allq, ["SCR"])
        RSUM("dve", ST[:, 8:16], SCR3, ["SCR"], ["qr_ss"])
        RSTD(ST[:, 8:16], ST[:, 8:16], 192, "qr", "qr_ss")
        TT("pool", QF3, QF3, bc(ST[:, 8:16], [128, 8, 192], 2), ALU.mult, allq + ["qr"], ["QFn"])
        TT("pool", QF3, QF3, bc(GQ, [128, 8, 192], 1), ALU.mult, ["QFn", "GQ"], ["QFn"])
        cosb = bc(RC, [128, 8, 32], 1)
        sinb = bc(RS, [128, 8, 32], 1)
        TT("pool", RT, QF3[:, :, 128:160], cosb, ALU.mult, ["QFn", "RC"], ["RT"])
        TT("pool", RT2, QF3[:, :, 160:192], sinb, ALU.mult, ["QFn", "RS"], ["RT2"])
        TT("pool", RT3, QF3[:, :, 160:192], cosb, ALU.mult, ["QFn", "RC"], ["RT3"])
        TT("pool", RT4, QF3[:, :, 128:160], sinb, ALU.mult, ["QFn", "RS"], ["RT4"])
        TT("dve", QN[:, :, 128:160], RT, RT2, ALU.subtract, ["RT", "RT2"], ["QNa"])
        TT("dve", QN[:, :, 160:192], RT3, RT4, ALU.add, ["RT3", "RT4"], ["QNb"])
        CP("pool", QN[:, :, 0:128], QF3[:, :, 0:128], ["QFn"], ["QNc"])
        allqn = ["QNa", "QNb", "QNc"]


    def p1_qT(t):
        r0 = t * 128
        allqn = ["QNa", "QNb", "QNc"]
        for h in range(8):
            MM(ps[h // 4][:, (h % 4) * 128:(h % 4 + 1) * 128], QN[:, h, 0:128], IDB, True, True,
               allqn + ["IDB"], [f"ps{h // 4}"])
            MM(ps[2 + h // 4][0:64, (h % 4) * 128:(h % 4 + 1) * 128], QN[:, h, 128:192], IDB, True, True,
               allqn + ["IDB"], [f"ps{2 + h // 4}"])
        for hb in range(2):
            CP("dve", QTs[:, hb * 4:(hb + 1) * 4, :], ps[hb][:, :].rearrange("p (a b) -> p a b", b=128),
               [f"ps{hb}"], [f"QTs{hb}"])
            CP("act", QRs[:, hb * 4:(hb + 1) * 4, :], ps[2 + hb][0:64, :].rearrange("p (a b) -> p a b", b=128),
               [f"ps{2 + hb}"], [f"QRs{hb}"])
        DMA("sp", QT_d[:, :, r0:r0 + 128].rearrange("h d t -> d h t"), QTs, "sQT", ["QTs0", "QTs1"], [])
        DMA("sp", QR_d[:, :, r0:r0 + 128].rearrange("h d t -> d h t"), QRs, "sQR", ["QRs0", "QRs1"], [])


    def p1_kchain(t):
        r0 = t * 128
        allkv = [f"KVF{nb}" for nb in range(4)]
        allq = ["QF0", "QF1", "QF2"]
        KV3 = KVF.rearrange("p (h d) -> p h d", d=256)
        SCRk = SCR[:, 0:1024].rearrange("p (h d) -> p h d", d=128)
        TT("pool", SCRk, KV3[:, :, 0:128], KV3[:, :, 0:128], ALU.mult, allkv, ["SCR"])
        RSUM("dve", ST[:, 16:24], SCRk, ["SCR"], ["k_ssn"])
        TT("pool", KRG, CKV[:, 512:576], CKV[:, 512:576], ALU.mult, ["CKVb"], ["KRG"])
        RSUM("dve", ST[:, 4:5], KRG, ["KRG"], ["kr_ss"])
        TS("dve", ST[:, 16:24], ST[:, 16:24], ST[:, 4:5], None, ALU.add, None, ["k_ssn", "kr_ss"], ["kr_ss2"])
        TS("dve", ST[:, 16:24], ST[:, 16:24], 1.0 / 192, EPS, ALU.mult, ALU.add, ["kr_ss2"], ["kr"])
        P.op("act", lambda e: e.sqrt(ST[:, 16:24], ST[:, 16:24]), reads=["kr"], writes=["kr"])
        P.op("dve", lambda e: e.reciprocal(ST[:, 16:24], ST[:, 16:24]), reads=["kr"], writes=["kr"])
        TT("pool", SCRk, KV3[:, :, 0:128], bc(ST[:, 16:24], [128, 8, 128], 2), ALU.mult, allkv + ["kr"], ["SCR"])
        TT("pool", KN[:, :, 0:128], SCRk, bc(GK[:, 0:128], [128, 8, 128], 1), ALU.mult, ["SCR", "GK"], ["KNc"])
        TT("pool", KRG, CKV[:, 512:576], GK[:, 128:192], ALU.mult, ["CKVb", "GK", "kr_ss"], ["KRG"])
        TT("pool", RT[:, 0, :], KRG[:, 0:32], RC, ALU.mult, ["KRG", "RC", "QNa"], ["RT"])
        TT("pool", RT2[:, 0, :], KRG[:, 32:64], RS, ALU.mult, ["KRG", "RS", "QNa"], ["RT2"])
        TT("pool", RT3[:, 0, :], KRG[:, 32:64], RC, ALU.mult, ["KRG", "RC", "QNb"], ["RT3"])
        TT("pool", RT4[:, 0, :], KRG[:, 0:32], RS, ALU.mult, ["KRG", "RS", "QNb"], ["RT4"])
        TT("dve", KRR[:, 0:32], RT[:, 0, :], RT2[:, 0, :], ALU.subtract, ["RT", "RT2"], ["KRRa"])
        TT("dve", KRR[:, 32:64], RT3[:, 0, :], RT4[:, 0, :], ALU.add, ["RT3", "RT4"], ["KRRb"])
        TT("pool", KN[:, :, 128:192], bc(KRR, [128, 8, 64], 1), bc(ST[:, 16:24], [128, 8, 64], 2), ALU.mult,
           ["KRRa", "KRRb", "kr"], ["KNr"])
        CP("dve", VS, KV3[:, :, 128:256], allkv, ["VS"])
        DMA("sp", V_d[:, r0:r0 + 128, :].rearrange("h t d -> t h d"), VS, "sV", ["VS"], [])


    def p1_kT(t):
        r0 = t * 128
        allkn = ["KNc", "KNr"]
        for h in range(8):
            MM(ps[h // 4][:, (h % 4) * 128:(h % 4 + 1) * 128], KN[:, h, 0:128], IDB, True, True,
               allkn + ["IDB"], [f"ps{h // 4}"])
            MM(ps[2 + h // 4][0:64, (h % 4) * 128:(h % 4 + 1) * 128], KN[:, h, 128:192], IDB, True, True,
               allkn + ["IDB"], [f"ps{2 + h // 4}"])
        for hb in range(2):
            CP("dve", KTs[:, hb * 4:(hb + 1) * 4, :], ps[hb][:, :].rearrange("p (a b) -> p a b", b=128),
               [f"ps{hb}"], [f"KTs{hb}"])
            CP("act", KRs[:, hb * 4:(hb + 1) * 4, :], ps[2 + hb][0:64, :].rearrange("p (a b) -> p a b", b=128),
               [f"ps{2 + hb}"], [f"KRs{hb}"])
        DMA("sp", KT_d[:, :, r0:r0 + 128].rearrange("h d t -> d h t"), KTs, "sKT", ["KTs0", "KTs1"], [])
        DMA("sp", KR_d[:, :, r0:r0 + 128].rearrange("h d t -> d h t"), KRs, "sKR", ["KRs0", "KRs1"], [])


    def p1_u(t):
        r0 = t * 128
        for k in range(8):
            MM(ps[4 + k // 4][:, (k % 4) * 128:(k % 4 + 1) * 128], UF[:, k * 128:(k + 1) * 128], IDB, True, True,
               ["UF5", "UF6", "IDB"], [f"ps{4 + k // 4}"])
        for hb in range(2):
            CP("act" if hb == 0 else "dve", UTs[:, hb * 4:(hb + 1) * 4, :],
               ps[4 + hb][:, :].rearrange("p (a b) -> p a b", b=128), [f"ps{4 + hb}"], [f"UTs{hb}"])
        DMA("sp", UT_d.rearrange("(b f) t -> f b t", f=128)[:, :, r0:r0 + 128], UTs, "sUT", ["UTs0", "UTs1"], [])

    front_load(0)
    front_a(0)
    p1_xT(0)
    for t in range(NT):
        p1_head(t)
        if t + 1 < NT:
            front_load(t + 1)
        p1_proj(t)
        if t + 1 < NT:
            front_a(t + 1)
        if t > 0:
            p1_qT(t - 1)
            p1_kT(t - 1)
        if t + 1 < NT:
            p1_xT(t + 1)
        p1_ckv(t)
        p1_qchain(t)
        p1_u(t)
        p1_kchain(t)
    p1_qT(NT - 1)
    p1_kT(NT - 1)
    P.barrier()
    if upto < 2:
        P.emit(); es.close(); return nc

    if upto < 3:
        P.emit(); es.close(); return nc

    NJ = L // 8
    NJC = L // 1024
    WINC_d = dscr("WINC_d", [128, 256 * 128], BF16)
    WOUT_d = dscr("WOUT_d", [128, 256 * 128], BF16)
    M_d = dscr("M_d", [128, 64 * 128], BF16)
    INC_d = [dscr(f"INC{d}_d", [128, NJC, 128 * 64], F32) for d in range(2)]
    S_d = [dscr(f"S{d}_d", [128, NJC, 64, 128], BF16) for d in range(2)]
    YT_d = dscr("YT_d", [1024, L], BF16)
    SSMT_d = dscr("SSMT_d", [1024, L], BF16)

    A.reset()
    LR = A.t([128, 64], F32); LI = A.t([128, 64], F32); DT = A.t([128, 64], F32)
    XX = A.t([128, 64], F32); TH = A.t([128, 64], F32)
    CC = A.t([128, 64], F32); SS = A.t([128, 64], F32)
    T1 = A.t([128, 64], F32); T2 = A.t([128, 64], F32)
    HPI = A.t([128, 1], F32)
    UR = A.t([128, 9, 64], F32); UI = A.t([128, 9, 64], F32)
    ER = A.t([128, 9, 64], F32); EI = A.t([128, 9, 64], F32)
    EIR = A.t([128, 9, 64], F32); EII = A.t([128, 9, 64], F32)
    MG = A.t([128, 64], F32); IMG = A.t([128, 64], F32)
    QRE = A.t([128, 64], F32); QIM = A.t([128, 64], F32)
    BRE = A.t([128, 32, 16], F32); BIM = A.t([128, 32, 16], F32)
    BBR = A.t([128, 2, 32, 16], F32); BBI = A.t([128, 2, 32, 16], F32)
    CRE = A.t([128, 2, 32, 16], F32); CIM = A.t([128, 2, 32, 16], F32)
    CN2 = [A.t([128, 128], F32) for _ in range(2)]
    TA = A.t([128, 32, 16], F32); TB = A.t([128, 32, 16], F32)
    TA2 = A.t([128, 16, 16], F32); TB2 = A.t([128, 16, 16], F32)
    XR = A.t([128, 16, 8, 16], F32); XI = A.t([128, 16, 8, 16], F32)
    WR = A.t([128, 16, 8, 16], F32); WI = A.t([128, 16, 8, 16], F32)
    WTR = A.t([128, 16, 8, 16], F32); WTI = A.t([128, 16, 8, 16], F32)
    TD = A.t([128, 16, 8, 16], F32)
    MACC = A.t([128, 64, 128], F32)
    MB = A.t([128, 64, 128], BF16)
    MKF = A.t([128, 128], F32); MKB = A.t([128, 128], F32)
    IH = [A.t([128, 128], F32) for _ in range(2)]
    DCOL = A.t([128, 64], F32)
    WINs = [A.t([128, 4, 128], BF16) for _ in range(2)]
    WOUs = [A.t([128, 4, 128], BF16) for _ in range(2)]
    TMPM = A.t([128, 128], F32)

    def rawap(ap, off, dims):
        return bass.AP(ap.tensor, off, dims)
    for q4 in range(4):
        DMA("sp", LR[:, q4 * 16:(q4 + 1) * 16], rawap(lam_re, (q4 // 2) * 4096 + (q4 % 2) * 16 * 128, [[1, 128], [128, 16]]),
            f"g{q4}", [], ["LR"], slow=True)
        DMA("sp", LI[:, q4 * 16:(q4 + 1) * 16], rawap(lam_im, (q4 // 2) * 4096 + (q4 % 2) * 16 * 128, [[1, 128], [128, 16]]),
            f"g{4 + q4}", [], ["LI"], slow=True)
    for gpar in range(2):
        DMA("sp", DT[gpar * 64:(gpar + 1) * 64, :].rearrange("p (d g) -> p d g", d=2),
            rawap(log_dt, gpar, [[0, 64], [64, 2], [2, 32]]), f"g{8 + gpar}", [], ["DT"], slow=True)
    for q4 in range(4):
        DMA("sp", BRE[:, q4 * 8:(q4 + 1) * 8, :], rawap(b_re, q4 * 8 * 2048, [[16, 128], [2048, 8], [1, 16]]),
            f"g{10 + q4}", [], ["BRE"], slow=True)
        DMA("sp", BIM[:, q4 * 8:(q4 + 1) * 8, :], rawap(b_im, q4 * 8 * 2048, [[16, 128], [2048, 8], [1, 16]]),
            f"g{14 + q4}", [], ["BIM"], slow=True)
    for s in range(8):
        DMA("sp", DCOL[s * 16:(s + 1) * 16, :], rawap(d_skip, 0, [[1, 16], [16, 64]]), f"g{18 + s}", [], ["DCOL"], slow=True)
    DMA("sp", MKF, maskf_d, "g26", [], ["MKF"])
    DMA("sp", MKB, maskb_d, "g27", [], ["MKB"])
    MSET("pool", HPI, float(np.pi / 2), ["HPI"])
    MSET("pool", WOUs[0], 0.0, ["WOUs0"])
    MSET("pool", WOUs[1], 0.0, ["WOUs1"])
    for hh in range(2):
        CP("pool", IH[hh], IDF, ["IDF"], [f"IH{hh}"])
        MSET("pool", IH[hh][(1 - hh) * 64:(2 - hh) * 64, :], 0.0, [f"IH{hh}"])
    it = 0
    for ri, csrc, cdst in ((0, c_re, CRE), (1, c_im, CIM)):
        for d in range(2):
            cv = csrc[d].rearrange("g c p -> (g c) p")
            for gb in range(8):
                b = it % 2
                it += 1
                DMA("sp", CN2[b][:, 0:64], cv[gb * 128:(gb + 1) * 128, :], f"CNa{b}", [], [f"CN2a{b}"])
                DMA("sp", CN2[b][:, 64:128], cv[gb * 128:(gb + 1) * 128, :], f"CNb{b}", [], [f"CN2b{b}"])
                MM(ps[b][:, 0:128], CN2[b], IDF, True, True, [f"CN2a{b}", f"CN2b{b}", "IDF"], [f"ps{b}"])
                pv = ps[b][:, 0:128].rearrange("p (g c) -> p g c", c=16)
                CP("dve", cdst[0:64, d, gb * 4:(gb + 1) * 4, :], pv[0:64, 0:8:2, :], [f"ps{b}"], [f"C{ri}"])
                CP("act", cdst[64:128, d, gb * 4:(gb + 1) * 4, :], pv[64:128, 1:8:2, :], [f"ps{b}"], [f"C{ri}"])

    def E(out, a, b_, op, r, w, eng="dve"):
        TT(eng, out, a, b_, op, r, w)
    ACT(DT, DT, AF.Exp, ["DT"], ["DT"])
    E(XX, LR, DT, ALU.mult, ["LR", "DT"], ["XX"])
    E(TH, LI, DT, ALU.mult, ["LI", "DT"], ["TH"])
    ACT(SS, TH, AF.Sin, ["TH"], ["SS"], scale=1.0 / 16)
    ACT(CC, TH, AF.Sin, ["TH", "HPI"], ["CC"], scale=1.0 / 16, bias=HPI[:, 0:1])
    for i in range(4):
        E(T1, CC, CC, ALU.mult, ["CC"], ["T1"])
        E(T2, SS, SS, ALU.mult, ["SS"], ["T2"])
        STT("dve", SS, CC, 2.0, SS, ALU.mult, ALU.mult, ["CC", "SS"], ["SS"])
        E(CC, T1, T2, ALU.subtract, ["T1", "T2"], ["CC"])
    CP("dve", UR[:, 1, :], CC, ["CC"], ["U"])
    CP("dve", UI[:, 1, :], SS, ["SS"], ["U"])
    for k in range(2, 9):
        E(T1, UR[:, k - 1, :], CC, ALU.mult, ["U", "CC"], ["T1"])
        E(T2, UI[:, k - 1, :], SS, ALU.mult, ["U", "SS"], ["T2"])
        E(UR[:, k, :], T1, T2, ALU.subtract, ["T1", "T2"], ["U"])
        E(T1, UR[:, k - 1, :], SS, ALU.mult, ["U", "SS"], ["T1"])
        E(T2, UI[:, k - 1, :], CC, ALU.mult, ["U", "CC"], ["T2"])
        E(UI[:, k, :], T1, T2, ALU.add, ["T1", "T2"], ["U"])
    for k in range(1, 9):
        ACT(MG, XX, AF.Exp, ["XX"], ["MG"], scale=float(k))
        ACT(IMG, XX, AF.Exp, ["XX"], ["IMG"], scale=float(-k))
        E(ER[:, k, :], MG, UR[:, k, :], ALU.mult, ["MG", "U"], ["ET"])
        E(EI[:, k, :], MG, UI[:, k, :], ALU.mult, ["MG", "U"], ["ET"])
        E(EIR[:, k, :], IMG, UR[:, k, :], ALU.mult, ["IMG", "U"], ["ET"])
        STT("dve", EII[:, k, :], UI[:, k, :], -1.0, IMG, ALU.mult, ALU.mult, ["IMG", "U"], ["ET"])
    for d in range(2):
        dsl = slice(d * 32, (d + 1) * 32)
        CP("dve", A8T[:, d, 0, 0:32], ER[:, 8, dsl], ["ET"], ["A8T"])
        CP("dve", A8T[:, d, 0, 32:64], ER[:, 8, dsl], ["ET"], ["A8T"])
        TS("dve", A8T[:, d, 1, 0:32], EI[:, 8, dsl], -1.0, None, ALU.mult, None, ["ET"], ["A8T"])
        CP("dve", A8T[:, d, 1, 32:64], EI[:, 8, dsl], ["ET"], ["A8T"])
    TS("dve", T1, ER[:, 1, :], -1.0, None, ALU.add, None, ["ET"], ["T1"])
    E(QRE, T1, LR, ALU.mult, ["T1", "LR"], ["QRE"])
    E(T2, EI[:, 1, :], LI, ALU.mult, ["ET", "LI"], ["T2"])
    E(QRE, QRE, T2, ALU.add, ["QRE", "T2"], ["QRE"])
    E(QIM, EI[:, 1, :], LR, ALU.mult, ["ET", "LR"], ["QIM"])
    E(T2, T1, LI, ALU.mult, ["T1", "LI"], ["T2"])
    E(QIM, QIM, T2, ALU.subtract, ["QIM", "T2"], ["QIM"])
    E(T1, LR, LR, ALU.mult, ["LR"], ["T1"])
    E(T2, LI, LI, ALU.mult, ["LI"], ["T2"])
    E(T1, T1, T2, ALU.add, ["T1", "T2"], ["T1"])
    P.op("dve", lambda e: e.reciprocal(T1, T1), reads=["T1"], writes=["T1"])
    E(QRE, QRE, T1, ALU.mult, ["QRE", "T1"], ["QRE"])
    E(QIM, QIM, T1, ALU.mult, ["QIM", "T1"], ["QIM"])
    for d in range(2):
        dsl = slice(d * 32, (d + 1) * 32)
        qr = bc(QRE[:, dsl], [128, 32, 16], 2)
        qi = bc(QIM[:, dsl], [128, 32, 16], 2)
        E(TA, qr, BRE, ALU.mult, ["QRE", "BRE"], ["TA"])
        E(TB, qi, BIM, ALU.mult, ["QIM", "BIM"], ["TB"])
        E(BBR[:, d], TA, TB, ALU.subtract, ["TA", "TB"], ["BBR"])
        E(TA, qr, BIM, ALU.mult, ["QRE", "BIM"], ["TA"])
        E(TB, qi, BRE, ALU.mult, ["QIM", "BRE"], ["TB"])
        E(BBI[:, d], TA, TB, ALU.add, ["TA", "TB"], ["BBI"])

    for d in range(2):
      for gph in range(2):
        dsl = slice(d * 32 + gph * 16, d * 32 + (gph + 1) * 16)
        gsl = slice(gph * 16, (gph + 1) * 16)
        TAh = TA[:, 0:16, :]
        TBh = TB[:, 0:16, :]
        for s in range(8):
            k = s + 1 if d == 0 else 8 - s
            er = bc(EIR[:, k, dsl], [128, 16, 16], 2)
            ei = bc(EII[:, k, dsl], [128, 16, 16], 2)
            E(TAh, er, BBR[:, d, gsl], ALU.mult, ["ET", "BBR"], ["TA"])
            E(TBh, ei, BBI[:, d, gsl], ALU.mult, ["ET", "BBI"], ["TB"])
            E(XR[:, :, s, :], TAh, TBh, ALU.subtract, ["TA", "TB"], ["XR"])
            E(TAh, er, BBI[:, d, gsl], ALU.mult, ["ET", "BBI"], ["TA"])
            E(TBh, ei, BBR[:, d, gsl], ALU.mult, ["ET", "BBR"], ["TB"])
            E(XI[:, :, s, :], TAh, TBh, ALU.add, ["TA", "TB"], ["XI"])
        for t in range(8):
            k = t + 1 if d == 0 else 8 - t
            er = bc(ER[:, k, dsl], [128, 16, 16], 2)
            ei = bc(EI[:, k, dsl], [128, 16, 16], 2)
            E(TAh, CRE[:, d, gsl], er, ALU.mult, ["ET", "C0"], ["TA"])
            E(TBh, CIM[:, d, gsl], ei, ALU.mult, ["ET", "C1"], ["TB"])
            E(WR[:, :, t, :], TAh, TBh, ALU.subtract, ["TA", "TB"], ["WR"])
            E(TAh, CRE[:, d, gsl], ei, ALU.mult, ["ET", "C0"], ["TA"])
            E(TBh, CIM[:, d, gsl], er, ALU.mult, ["ET", "C1"], ["TB"])
            STT("dve", WI[:, :, t, :], TAh, -1.0, TBh, ALU.mult, ALU.subtract, ["TA", "TB"], ["WI"])
        X3 = lambda tt_: tt_.rearrange("p g s c -> p g (s c)")
        e8r = bc(ER[:, 8, dsl], [128, 16, 128], 2)
        e8i = bc(EI[:, 8, dsl], [128, 16, 128], 2)
        E(X3(WTR), e8r, X3(XR), ALU.mult, ["ET", "XR"], ["WTR"])
        E(X3(TD), e8i, X3(XI), ALU.mult, ["ET", "XI"], ["TD"])
        E(X3(WTR), X3(WTR), X3(TD), ALU.subtract, ["WTR", "TD"], ["WTR"])
        E(X3(WTI), e8r, X3(XI), ALU.mult, ["ET", "XI"], ["WTI"])
        E(X3(TD), e8i, X3(XR), ALU.mult, ["ET", "XR"], ["TD"])
        E(X3(WTI), X3(WTI), X3(TD), ALU.add, ["WTI", "TD"], ["WTI"])
        for gpl in range(16):
            gp = gph * 16 + gpl
            sb_ = gp % 2
            for gpar in range(2):
                g = 2 * gp + gpar
                hs = slice(gpar * 64, (gpar + 1) * 64)
                pm = ps[gpar]
                MM(pm[:, 0:128], X3(XR)[hs, gpl, :], X3(WR)[hs, gpl, :], True, False, ["XR", "WR"], [f"ps{gpar}"])
                MM(pm[:, 0:128], X3(XI)[hs, gpl, :], X3(WI)[hs, gpl, :], False, True, ["XI", "WI"], [f"ps{gpar}"])
                if d == 0:
                    TT("dve", MACC[:, g, :], pm[:, 0:128], MKF, ALU.mult, [f"ps{gpar}", "MKF"], [f"MACC{g}"])
                else:
                    TT("dve", TMPM, pm[:, 0:128], MKB, ALU.mult, [f"ps{gpar}", "MKB"], ["TMPM"])
                    TT("dve", MACC[:, g, :], MACC[:, g, :], TMPM, ALU.add, [f"MACC{g}", "TMPM"], [f"MACC{g}"])
                for ri in range(2):
                    src = X3(WTR if ri == 0 else WTI)
                    pw = ps[2 + gpar * 2 + ri]
                    MM(pw[:, 0:128], src[:, gpl, :], IH[gpar], True, True, ["WTR", "WTI", f"IH{gpar}"],
                       [f"ps{2 + gpar * 2 + ri}"])
                    CP("act", WINs[sb_][:, gpar * 2 + ri, :], pw[:, 0:128], [f"ps{2 + gpar * 2 + ri}"],
                       [f"WINs{sb_}"])
                    CP("pool", WOUs[sb_][hs, gpar * 2 + ri, :], X3(WR if ri == 0 else WI)[hs, gpl, :],
                       ["WR", "WI"], [f"WOUs{sb_}"])
            base = (d * 128 + 4 * gp) * 128
            DMA("sp", WINC_d[:, base:base + 512], WINs[sb_].rearrange("p a b -> p (a b)"), f"sWIN{sb_}",
                [f"WINs{sb_}"], ["WINC_d"])
            DMA("sp", WOUT_d[:, base:base + 512], WOUs[sb_].rearrange("p a b -> p (a b)"), f"sWOU{sb_}",
                [f"WOUs{sb_}"], ["WOUT_d"])
    for g in range(64):
        STT("dve", MACC[:, g, :], IDF, DCOL[:, g:g + 1], MACC[:, g, :], ALU.mult, ALU.add,
            [f"MACC{g}", "IDF", "DCOL"], [f"MACC{g}"])
    CP("act", MB, MACC, [f"MACC{g}" for g in range(64)], ["MB"])
    DMA("sp", M_d, MB.rearrange("p a b -> p (a b)"), "sMB", ["MB"], ["M_d"])
    P.barrier()
    if upto < 4:
        P.emit(); es.close(); return nc

    A.reset()
    WINC = A.t([128, 256, 128], BF16)
    SEL = A.t([128, 64, 128], BF16)
    UTc = A.t([128, 8, 1024], BF16)
    UGall = A.t([128, 64, 128], BF16)
    INCc = [A.t([128, 128, 64], F32) for _ in range(2)]
    for i4 in range(4):
        DMA("sp", WINC[:, i4 * 64:(i4 + 1) * 64, :].rearrange("p a b -> p (a b)"),
            WINC_d[:, i4 * 64 * 128:(i4 + 1) * 64 * 128], f"lWINC{i4}", [], [f"WINC{i4}"])
    allwinc = [f"WINC{i4}" for i4 in range(4)]
    DMA("pool", SEL, sel_d, "lSEL", [], ["SEL"])
    UT_v = UT_d.rearrange("(b f) t -> f b t", f=128)
    for jc in range(NJC):
        t0 = jc * 1024
        DMA("sp", UTc, UT_v[:, :, t0:t0 + 1024], "lUTc", [], ["UTc"])
        for g4 in range(16):
            pb = g4 % 2
            for gi in range(4):
                g = g4 * 4 + gi
                blk, gl = g // 8, g % 8
                for s in range(8):
                    MM(ps[pb][:, gi * 128:(gi + 1) * 128], SEL[:, s * 8 + gl, :], UTc[:, blk, s:1024:8],
                       s == 0, s == 7, ["SEL", "UTc"], [f"ps{pb}"])
            CP("act" if pb == 0 else "dve", UGall[:, g4 * 4:(g4 + 1) * 4, :],
               ps[pb][:, :].rearrange("p (a b) -> p a b", b=128), [f"ps{pb}"], [f"UG{g4}"])
        allug = [f"UG{g4}" for g4 in range(16)]
        n = 0
        for d in range(2):
            for gp in range(32):
                pb = 2 + n % 4
                n += 1
                for ri in range(2):
                    for gpar in range(2):
                        g = 2 * gp + gpar
                        MM(ps[pb][:, ri * 128:(ri + 1) * 128], WINC[:, (d * 64 + g) * 2 + ri, :], UGall[:, g, :],
                           gpar == 0, gpar == 1, allwinc + allug, [f"ps{pb}"])
                dst = INCc[d].rearrange("p j (r g) -> p r j g", r=2)[:, :, :, gp]
                CP("act" if n % 2 == 0 else "dve", dst, ps[pb][:, 0:256].rearrange("p (r j) -> p r j", r=2),
                   [f"ps{pb}"], [f"INCc{d}"])
        for d in range(2):
            DMA("sp", INC_d[d][:, jc, :], INCc[d].rearrange("p a b -> p (a b)"), f"sINC{d}", [f"INCc{d}"], [f"INC_d{d}"])
    P.barrier()
    if upto < 5:
        P.emit(); es.close(); return nc

    A.reset()
    KTh = [A.t([128, L], BF16) for _ in range(1)]
    KRh = [A.t([128, L], BF16) for _ in range(1)]
    Vh = [A.t([128, NT, 128], BF16) for _ in range(1)]
    QTb = [A.t([128, 512], BF16) for _ in range(2)]
    QRb = [A.t([128, 512], BF16) for _ in range(2)]
    PT = [A.t([128, 512], BF16) for _ in range(6)]
    PS2 = [A.t([128, 512], BF16) for _ in range(3)]
    RSb = A.t([128, 512], F32)
    ATn = A.t([128, 512], F32)
    SQa = A.t([128, 512], BF16)
    ATo = [A.t([128, 512], BF16) for _ in range(2)]
    ONESB = A.t([128, 128], BF16)
    ONEC2 = A.t([128, 2], BF16)
    AOGc = A.t([128, 8], F32)
    HJ = 64
    INs = [A.t([128, HJ, 64], F32) for _ in range(2)]
    SO = [A.t([128, HJ + 1, 64], F32) for _ in range(2)]
    SB16 = [A.t([128, 64, 128], BF16) for _ in range(2)]
    PQ = [A.t([128, 2, 64], F32) for _ in range(2)]
    MSET("pool", ONESB, 1.0, ["ONESB"])
    MSET("pool", ONEC2, 1.0, ["ONEC2"])
    MSET("pool", KRh[0][64:128, :], 0.0, ["KRh0"])
    for b_ in range(2):
        MSET("pool", QRb[b_][64:128, :], 0.0, [f"QRb{b_}"])
    DMA("sp", AOGc, att_out_g.rearrange("(k p) -> p k", p=128), "lAOGc", [], ["AOGc"], slow=True)

    def scan_gen(d, eng):
        if d == 0:
            MSET(eng, SO[0][:, 0, :], 0.0, ["SO0"])
        else:
            MSET(eng, SO[1][:, HJ, :], 0.0, ["SO1"])
        AA = A8T[:, d, 0, :]
        AIMS = A8T[:, d, 1, :]
        nh = 128 // HJ
        for step in range(NJC * nh):
            hc = step if d == 0 else NJC * nh - 1 - step
            jc, hh = hc // nh, hc % nh
            DMA("pool", INs[d].rearrange("p a b -> p (a b)"), INC_d[d][:, jc, hh * HJ * 64:(hh + 1) * HJ * 64],
                f"lIN{d}", [f"INC_d{d}"], [f"INs{d}"])
            for ii in range(HJ):
                i = ii if d == 0 else HJ - 1 - ii
                src = SO[d][:, i, :] if d == 0 else SO[d][:, i + 1, :]
                dst = SO[d][:, i + 1, :] if d == 0 else SO[d][:, i, :]
                swp = src.rearrange("p (r g) -> p r g", r=2)[:, ::-1, :]
                TT(eng, PQ[d][:, 0, :], AA, src, ALU.mult, [f"SO{d}", "A8T"], [f"P{d}"])
                TT(eng, PQ[d][:, 1, :].rearrange("p (r g) -> p r g", r=2), AIMS.rearrange("p (r g) -> p r g", r=2), swp,
                   ALU.mult, [f"SO{d}", "A8T"], [f"Q{d}"])
                TT(eng, PQ[d][:, 0, :], PQ[d][:, 0, :], PQ[d][:, 1, :], ALU.add, [f"P{d}", f"Q{d}"], [f"P{d}"])
                TT(eng, dst, PQ[d][:, 0, :], INs[d][:, i, :], ALU.add, [f"P{d}", f"INs{d}"], [f"SO{d}"])
                yield
            lo = 0 if d == 0 else 1
            CP(eng, SB16[d][:, :, hh * HJ:(hh + 1) * HJ], SO[d][:, lo:lo + HJ, :].rearrange("p j c -> p c j"),
               [f"SO{d}"], [f"SB16{d}"])
            last_half = (hh == nh - 1) if d == 0 else (hh == 0)
            if last_half:
                DMA("pool", S_d[d][:, jc], SB16[d], f"sS{d}", [f"SB16{d}"], [f"S_d{d}"])
            if d == 0:
                CP(eng, SO[d][:, 0, :], SO[d][:, HJ, :], [f"SO{d}"], [f"SO{d}"])
            else:
                CP(eng, SO[d][:, HJ, :], SO[d][:, 0, :], [f"SO{d}"], [f"SO{d}"])
        while True:
            yield

    def att_tail(h, qb, ib, po, pz):
        P.op("dve", (lambda pz_: lambda e: e.reciprocal(RSb, ps[pz_][:, :]))(pz), reads=[f"ps{pz}"], writes=["RSb"])
        TT("dve", ATn, ps[po][:, :], RSb, ALU.mult, [f"ps{po}", "RSb"], ["ATn"])
        TT("pool", SQa, ATn, ATn, ALU.mult, ["ATn"], ["SQa"])
        for qs in range(4):
            MM(ps[6][:, qs:qs + 1], SQa[:, qs * 128:(qs + 1) * 128], ONEC2[:, 0:1], True, True, ["SQa", "ONEC2"], ["ps6"])
        if h == 0:
            CP("dve", SSA[:, qb * 4:(qb + 1) * 4], ps[6][:, 0:4], ["ps6"], [f"SSA{qb}"])
        else:
            TT("dve", SSA[:, qb * 4:(qb + 1) * 4], SSA[:, qb * 4:(qb + 1) * 4], ps[6][:, 0:4], ALU.add,
               ["ps6", f"SSA{qb}"], [f"SSA{qb}"])
        TS("dve", ATo[ib], ATn, AOGc[:, h:h + 1], None, ALU.mult, None, ["ATn", "AOGc"], [f"ATo{ib}"])
        DMA("sp", ATTT_d[h * 128:(h + 1) * 128, qb * 512:(qb + 1) * 512], ATo[ib], f"sATo{ib}", [f"ATo{ib}"], [])

    def cast_gen():
        n = 0
        for wsrc, wdst in ((w1, W1s), (w3, W3s)):
            wv = wsrc.rearrange("(k p) n -> p k n", p=128)
            for pr in range(22):
                DMA("pool", wdst[pr].rearrange("p (k c) -> p k c", k=16), wv[:, :, pr * 256:(pr + 1) * 256], f"cast{n % 4}", [], ["Wsd"])
                n += 1
                yield
        w2v = w2.rearrange("(g j p) n -> g p j n", j=4, p=128)
        for hh in range(2):
            for g4 in range(11):
                DMA("pool", W2s[hh, g4].rearrange("p (j c) -> p j c", j=4), w2v[g4][:, :, hh * 1024:(hh + 1) * 1024],
                    f"cast{n % 4}", [], ["Wsd"])
                n += 1
                yield
        while True:
            yield

    castg = cast_gen()
    pending_tail = None
    scans = [scan_gen(0, "dve"), scan_gen(1, "pool")]
    steps_per_it = -(-(NJ) // (NH * (L // 512)))
    sc = 1.0 / float(np.sqrt(192.0))
    NQB = L // 512
    it = 0
    for h in range(NH):
        hb = 0
        DMA("sp", KTh[hb], KT_d[h], f"KTh{hb}", [], [f"KTh{hb}"])
        DMA("sp", KRh[hb][0:64, :], KR_d[h], f"KRh{hb}", [], [f"KRh{hb}"])
        DMA("sp", Vh[hb], V_d[h].rearrange("(n p) d -> p n d", p=128), f"Vh{hb}", [], [f"Vh{hb}"])
        for qb in range(NQB):
            ib = it % 2
            it += 1
            po = 2 + ib
            pz = 4 + ib
            DMA("sp", QTb[ib], QT_d[h][:, qb * 512:(qb + 1) * 512], f"QTb{ib}", [], [f"QTb{ib}"])
            DMA("sp", QRb[ib][0:64, :], QR_d[h][:, qb * 512:(qb + 1) * 512], f"QRb{ib}", [], [f"QRb{ib}"])

            SBK = [0, 1, 7]

            def S(kt):
                sb_ = SBK[kt % 3]
                MM(ps[sb_][:, :], KTh[hb][:, kt * 128:(kt + 1) * 128], QTb[ib][:, :], True, False,
                   [f"KTh{hb}", f"QTb{ib}"], [f"ps{sb_}"])
                MM(ps[sb_][:, :], KRh[hb][:, kt * 128:(kt + 1) * 128], QRb[ib][:, :], False, True,
                   [f"KRh{hb}", f"QRb{ib}"], [f"ps{sb_}"])
            S(0)
            if NT > 1:
                S(1)
            for kt in range(NT):
                sb_ = SBK[kt % 3]
                pb3 = kt % 6
                if kt + 2 < NT:
                    S(kt + 2)
                if kt == min(3, NT - 1) and pending_tail is not None:
                    att_tail(*pending_tail)
                    pending_tail = None
                ACT(PT[pb3], ps[sb_][:, :], AF.Exp, [f"ps{sb_}"], [f"PT{pb3}"], scale=sc)
                MM(ps[po][:, :], Vh[hb][:, kt, :], PT[pb3], kt == 0, kt == NT - 1, [f"PT{pb3}", f"Vh{hb}"], [f"ps{po}"])
                if kt % 2 == 1:
                    pr_ = kt // 2
                    pp = pr_ % 3
                    eng = "dve" if pr_ % 2 == 0 else "pool"
                    TT(eng, PS2[pp], PT[(kt - 1) % 6], PT[pb3], ALU.add, [f"PT{(kt - 1) % 6}", f"PT{pb3}"], [f"PS2{pp}"])
                    if pr_ >= 1:
                        pq = (pr_ - 1) % 3
                        MM(ps[pz][:, :], ONESB, PS2[pq], pr_ == 1, False, [f"PS2{pq}", "ONESB"], [f"ps{pz}"])
            pq = (NT // 2 - 1) % 3
            MM(ps[pz][:, :], ONESB, PS2[pq], NT // 2 == 1, True, [f"PS2{pq}", "ONESB"], [f"ps{pz}"])
            pending_tail = (h, qb, ib, po, pz)
            for _ in range(steps_per_it):
                next(scans[0])
                next(scans[1])
            next(castg)
    att_tail(*pending_tail)
    for _ in range(70):
        next(castg)
    for _ in range(4 * HJ):
        next(scans[0])
        next(scans[1])
    P.barrier()
    A.reset()
    JG = min(4, NJC)
    NW = 128 * JG
    NTK = 1024 * JG
    SELT = A.t([128, 64, 128], BF16)
    SEL = A.t([128, 64, 128], BF16)
    WOb = A.t([128, 2, 16, 128], BF16)
    Mb = A.t([128, 8, 128], BF16)
    UTb = [A.t([128, NTK], BF16) for _ in range(2)]
    Sb = [[A.t([128, 2, 4, NW], BF16) for _ in range(2)] for _ in range(2)]
    UG8 = A.t([128, 8, NW], BF16)
    YG = A.t([128, 8, NW], BF16)
    YTb = A.t([128, NTK], F32)
    GT1 = A.t([128, NTK], F32)
    GT2 = A.t([128, NTK], F32)
    GOUT = [A.t([128, NTK], BF16) for _ in range(2)]
    DMA("pool", SEL, sel_d, "lSEL", [], ["SEL"])
    DMA("pool", SELT, selt_d, "lSELT", [], ["SELT"])
    GC = float(2.0 * np.sqrt(2.0 / np.pi))

    def load53(it_):
        blk_, jg_ = it_ // (NJC // JG), it_ % (NJC // JG)
        ib_ = it_ % 2
        tt0 = jg_ * NTK
        DMA("sp", UTb[ib_], UT_d[blk_ * 128:(blk_ + 1) * 128, tt0:tt0 + NTK], f"lUTb{ib_}", [], [f"UTb{ib_}"])
        for d in range(2):
            for ri in range(2):
                for gpl in range(4):
                    DMA("sp", Sb[ib_][d][:, ri, gpl, :].rearrange("p (c j) -> p c j", c=JG),
                        S_d[d][:, jg_ * JG:(jg_ + 1) * JG, ri * 32 + blk_ * 4 + gpl, :],
                        f"lSb{ib_}{d}{ri}{gpl}", [], [f"Sb{ib_}{d}{ri}{gpl}"])
    it = 0
    ne = 0
    for blk in range(8):
        for d in range(2):
            base = ((d * 64 + blk * 8) * 2) * 128
            DMA("sp", WOb[:, d].rearrange("p a b -> p (a b)"), WOUT_d[:, base:base + 16 * 128], f"lWOb{d}", [], [f"WOb{d}"])
        DMA("sp", Mb.rearrange("p a b -> p (a b)"), M_d[:, blk * 8 * 128:(blk + 1) * 8 * 128], "lMb", [], ["Mb"])
        for jg in range(NJC // JG):
            ib = it % 2
            t0 = jg * NTK
            if it == 0:
                load53(0)
            if it + 1 < 8 * (NJC // JG):
                load53(it + 1)
            it += 1
            for gl in range(8):
                pb = gl % 4
                for s in range(8):
                    MM(ps[pb][:, 0:NW], SEL[:, s * 8 + gl, :], UTb[ib][:, s:NTK:8], s == 0, s == 7,
                       ["SEL", f"UTb{ib}"], [f"ps{pb}"])
                ne += 1
                CP("act" if ne % 2 == 0 else "dve", UG8[:, gl, :], ps[pb][:, 0:NW], [f"ps{pb}"], [f"UG8{gl}"])
            for gl in range(8):
                pb = 4 + gl % 4
                o = ps[pb][:, 0:NW]
                MM(o, Mb[:, gl, :], UG8[:, gl, :], True, False, ["Mb", f"UG8{gl}"], [f"ps{pb}"])
                for d in range(2):
                    for ri in range(2):
                        MM(o, WOb[:, d, gl * 2 + ri, :], Sb[ib][d][:, ri, gl // 2, :], False, (d == 1 and ri == 1),
                           [f"WOb{d}", f"Sb{ib}{d}{ri}{gl // 2}"], [f"ps{pb}"])
                ne += 1
                CP("act" if ne % 2 == 0 else "dve", YG[:, gl, :], o, [f"ps{pb}"], [f"YG{gl}"])
            allyg = [f"YG{gl}" for gl in range(8)]
            YTv = YTb.rearrange("p (j t) -> p t j", t=8)
            for t in range(8):
                pb = t % 4
                for gl in range(8):
                    MM(ps[pb][:, 0:NW], SELT[:, t * 8 + gl, :], YG[:, gl, :], gl == 0, gl == 7, ["SELT"] + allyg, [f"ps{pb}"])
                ne += 1
                CP("act" if ne % 2 == 0 else "dve", YTv[:, t, :], ps[pb][:, 0:NW], [f"ps{pb}"], [f"YTb{t}"])
            ally = [f"YTb{t}" for t in range(8)]
            TT("pool", GT1, YTb, YTb, ALU.mult, ally, ["GT1"])
            TS("pool", GT1, GT1, 0.044715, 1.0, ALU.mult, ALU.add, ["GT1"], ["GT1"])
            TT("pool", GT1, GT1, YTb, ALU.mult, ["GT1"] + ally, ["GT1"])
            ACT(GT2, GT1, AF.Sigmoid, ["GT1"], ["GT2"], scale=GC)
            TT("dve", GOUT[ib], GT2, YTb, ALU.mult, ["GT2"] + ally, [f"GOUT{ib}"])
            DMA("sp", YT_d[blk * 128:(blk + 1) * 128, t0:t0 + NTK], GOUT[ib], f"sGOUT{ib}", [f"GOUT{ib}"], ["YT_d"])
    P.barrier()
    if upto < 7:
        P.emit(); es.close(); return nc

    A.reset()
    WG = A.t([128, 8, 1024], BF16)
    BG = A.t([128, 8], F32)
    SOG = A.t([128, 8], F32)
    ONEC = A.t([128, 2], BF16)
    YTc = [A.t([128, 8, 512], BF16) for _ in range(2)]
    SG = A.t([128, 512], F32)
    SSMf = A.t([128, 512], F32)
    SQb = A.t([128, 8, 512], BF16)
    SSMo = [A.t([128, 8, 512], BF16) for _ in range(2)]
    for k in range(8):
        DMA("pool", WG[:, k, :], w_glu[k * 128:(k + 1) * 128, :], f"lWG{k}", [], [f"WG{k}"])
    allwg = [f"WG{k}" for k in range(8)]
    DMA("sp", BG, b_glu.rearrange("(k p) -> p k", p=128), "g0", [], ["BG"], slow=True)
    DMA("sp", SOG, ssm_out_g.rearrange("(k p) -> p k", p=128), "g1", [], ["SOG"], slow=True)
    MSET("pool", ONEC, 1.0, ["ONEC"])
    YT_v = YT_d.rearrange("(b f) t -> f b t", f=128)
    SSMT_v = SSMT_d.rearrange("(b f) t -> f b t", f=128)
    DMA("sp", YTc[0], YT_v[:, :, 0:512], "lYTc0", ["YT_d"], ["YTc0"])
    for tc in range(L // 512):
        ib = tc % 2
        t0 = tc * 512
        if tc + 1 < L // 512:
            DMA("sp", YTc[1 - ib], YT_v[:, :, t0 + 512:t0 + 1024], f"lYTc{1 - ib}", ["YT_d"], [f"YTc{1 - ib}"])
        for oc in range(8):
            pb = oc % 2
            for k in range(8):
                MM(ps[pb][:, :], WG[:, k, oc * 128:(oc + 1) * 128], YTc[ib][:, k, :], k == 0, k == 7,
                   allwg + [f"YTc{ib}"], [f"ps{pb}"])
            ACT(SG, ps[pb][:, :], AF.Sigmoid, [f"ps{pb}", "BG"], ["SG"], bias=BG[:, oc:oc + 1])
            TT("dve", SSMf, SG, YTc[ib][:, oc, :], ALU.mult, ["SG", f"YTc{ib}"], ["SSMf"])
            TT("pool", SQb[:, oc, :], SSMf, SSMf, ALU.mult, ["SSMf"], [f"SQb{oc}"])
            TS("dve", SSMo[ib][:, oc, :], SSMf, SOG[:, oc:oc + 1], None, ALU.mult, None, ["SSMf", "SOG"], [f"SSMo{ib}_{oc}"])
        for tt_ in range(4):
            for oc in range(8):
                MM(ps[2][:, tt_:tt_ + 1], SQb[:, oc, tt_ * 128:(tt_ + 1) * 128], ONEC[:, 0:1], oc == 0, oc == 7,
                   [f"SQb{o2}" for o2 in range(8)] + ["ONEC"], ["ps2"])
        CP("dve", SSS[:, tc * 4:(tc + 1) * 4], ps[2][:, 0:4], ["ps2"], ["SSS"])
        DMA("sp", SSMT_v[:, :, t0:t0 + 512], SSMo[ib], f"sSSMo{ib}", [f"SSMo{ib}_{oc}" for oc in range(8)], ["SSMT_d"])
    P.barrier()

    if upto < 8:
        P.emit(); es.close(); return nc

    X1_d = dscr("X1_d", [L, D], F32)
    A.reset()
    WO = A.t([128, 16, 2048], BF16)
    G1B = A.t([128, 2048], F32)
    AOG = A.t([128, 8], F32)
    XT2 = [A.t([128, 2048], F32) for _ in range(2)]
    MIXT = [A.t([128, 16, 128], BF16) for _ in range(2)]
    X1t = [A.t([128, 2048], F32) for _ in range(2)]
    TE1 = [A.t([128, 512], F32) for _ in range(2)]
    TE2 = [A.t([128, 512], F32) for _ in range(2)]
    ST4 = A.t([128, 8], F32)
    WST = [A.t([128, 16, 256], BF16) for _ in range(2)]
    DMA("sp", G1B, MOD_d[2 * D:3 * D].partition_broadcast(128), "lG1B", ["MOD_d"], ["G1B"])
    for k in range(16):
        DMA("pool", WO[:, k, :], w_o[k * 128:(k + 1) * 128, :], f"lWO{k}", [], [f"WO{k}"])
        TT("dve", WO[:, k, :], WO[:, k, :], G1B, ALU.mult, [f"WO{k}", "G1B"], [f"WO{k}"])
    SSMT_v2 = SSMT_d.rearrange("(b f) t -> f b t", f=128)
    ATTT_v2 = ATTT_d.rearrange("(b f) t -> f b t", f=128)
    ne = 0
    def load4a(t_):
        rr = t_ * 128
        tb_ = t_ % 2
        DMA("sp", XT2[tb_], x[rr:rr + 128, :], f"lXT2{tb_}", [], [f"XT2{tb_}"])
        DMA("sp", MIXT[tb_][:, 8:16, :], SSMT_v2[:, :, rr:rr + 128], f"lMIXs{tb_}", [], [f"MIXs{tb_}"])
        DMA("sp", MIXT[tb_][:, 0:8, :], ATTT_v2[:, :, rr:rr + 128], f"lMIXa{tb_}", [], [f"MIXa{tb_}"])
    load4a(0)
    for t in range(NT):
        r0 = t * 128
        tb = t % 2
        if t + 1 < NT:
            load4a(t + 1)
        RSTD(ST4[:, tb * 4 + 1:tb * 4 + 2], SSA[:, t:t + 1], 1024, f"ar{tb}", "SSA")
        RSTD(ST4[:, tb * 4 + 2:tb * 4 + 3], SSS[:, t:t + 1], 1024, f"sr{tb}", "SSS")
        for nb in range(4):
            pa = ps[2 * nb]
            pss = ps[2 * nb + 1]
            na, ns = f"ps{2 * nb}", f"ps{2 * nb + 1}"
            cs = slice(nb * 512, (nb + 1) * 512)
            eb = ne % 2
            ne += 1
            for k in range(8):
                MM(pa[:, :], MIXT[tb][:, k, :], WO[:, k, cs], k == 0, k == 7, [f"MIXa{tb}", f"WO{k}"], [na])
            for k in range(8, 16):
                MM(pss[:, :], MIXT[tb][:, k, :], WO[:, k, cs], k == 8, k == 15, [f"MIXs{tb}", f"WO{k}"], [ns])
            ACT(TE1[eb], pa[:, :], AF.Copy, [na, f"ar{tb}"], [f"TE1{eb}"], scale=ST4[:, tb * 4 + 1:tb * 4 + 2])
            STT("dve", TE2[eb], pss[:, :], ST4[:, tb * 4 + 2:tb * 4 + 3], TE1[eb], ALU.mult, ALU.add, [ns, f"sr{tb}", f"TE1{eb}"], [f"TE2{eb}"])
            TT("pool", X1t[tb][:, cs], TE2[eb], XT2[tb][:, cs], ALU.add, [f"TE2{eb}", f"XT2{tb}"], [f"X1t{tb}_{nb}"])
        DMA("pool", X1_d[r0:r0 + 128, :], X1t[tb], f"sX1{tb}", [f"X1t{tb}_{nb}" for nb in range(4)], ["X1_d"])
    P.barrier()
    if upto < 9:
        P.emit(); es.close(); return nc

    A.reset()
    X1s = A.t([128, 4, 2048], F32)
    XB2 = A.t([128, 2048], BF16)
    JNK2 = A.t([128, 2048], BF16)
    H2T = A.t([128, 16, 512], BF16)
    W1t = [A.t([128, 16, 256], BF16) for _ in range(2)]
    W3t = [A.t([128, 16, 256], BF16) for _ in range(2)]
    GT = A.t([128, 44, 512], BF16)
    W2t = [A.t([128, 4, 1024], BF16) for _ in range(2)]
    G2B = A.t([128, 2048], F32)
    OUTs = [A.t([128, 512], F32) for _ in range(2)]
    SA = [A.t([128, 512], F32) for _ in range(2)]
    TE3 = A.t([128, 512], F32)
    ST5 = A.t([128, 8], F32)
    DMA("sp", G2B, MOD_d[5 * D:6 * D].partition_broadcast(128), "lG2B", [], ["G2B"])
    nst = 0
    nw = 0
    no = 0
    for st_ in range(L // 512):
        r0 = st_ * 512
        DMA("sp", X1s, X1_d[r0:r0 + 512, :].rearrange("(a p) n -> p a n", p=128), "lX1s", [], ["X1s"])
        for a in range(4):
            MSET("pool", ST5, 0.0, ["f_ss", "fr"])
            ACT(JNK2, X1s[:, a, :], AF.Square, ["X1s"], ["JNK2", "f_ss"], accum=ST5[:, 0:1])
            RSTD(ST5[:, 1:2], ST5[:, 0:1], D, "fr", "f_ss")
            ACT(XB2, X1s[:, a, :], AF.Copy, ["X1s", "fr"], ["XB2"], scale=ST5[:, 1:2])
            for k in range(16):
                MM(ps[4 + k // 4][:, (k % 4) * 128:(k % 4 + 1) * 128], XB2[:, k * 128:(k + 1) * 128], IDB, True, True,
                   ["XB2", "IDB"], [f"ps{4 + k // 4}"])
            for k in range(16):
                TS("dve", H2T[:, k, a * 128:(a + 1) * 128], ps[4 + k // 4][:, (k % 4) * 128:(k % 4 + 1) * 128],
                   GS[:, 32 + k:33 + k], GS[:, 48 + k:49 + k], ALU.mult, ALU.add, [f"ps{4 + k // 4}", "GS"], [f"H2T{k}"])
        for pr in range(22):
            wb = nw % 2
            nw += 1
            DMA("sp", W1t[wb].rearrange("p a b -> p (a b)"), W1s[pr], f"lW1t{wb}", [], [f"W1t{wb}"])
            DMA("sp", W3t[wb].rearrange("p a b -> p (a b)"), W3s[pr], f"lW3t{wb}", [], [f"W3t{wb}"])
            for c2 in range(2):
                ffc = pr * 2 + c2
                sb_ = ffc % 2
                pa, pb_ = ps[sb_ * 2], ps[sb_ * 2 + 1]
                na, nb_ = f"ps{sb_ * 2}", f"ps{sb_ * 2 + 1}"
                for k in range(16):
                    MM(pa[:, :], W1t[wb][:, k, c2 * 128:(c2 + 1) * 128], H2T[:, k, :], k == 0, k == 15,
                       [f"W1t{wb}", f"H2T{k}"], [na])
                for k in range(16):
                    MM(pb_[:, :], W3t[wb][:, k, c2 * 128:(c2 + 1) * 128], H2T[:, k, :], k == 0, k == 15,
                       [f"W3t{wb}", f"H2T{k}"], [nb_])
                ACT(SA[sb_], pa[:, :], AF.Silu, [na], [f"SA{sb_}"])
                TT("dve", GT[:, ffc, :], SA[sb_], pb_[:, :], ALU.mult, [f"SA{sb_}", nb_], [f"GT{ffc}"])
        for h in range(2):
            for g4 in range(11):
                wb = nw % 2
                nw += 1
                DMA("sp", W2t[wb].rearrange("p a b -> p (a b)"), W2s[h, g4], f"lW2t{wb}", [], [f"W2t{wb}"])
                for j in range(4):
                    ffc = g4 * 4 + j
                    for a in range(4):
                        for nb in range(2):
                            MM(ps[a * 2 + nb][:, :], GT[:, ffc, a * 128:(a + 1) * 128], W2t[wb][:, j, nb * 512:(nb + 1) * 512],
                               ffc == 0, ffc == 43, [f"GT{ffc}", f"W2t{wb}"], [f"ps{a * 2 + nb}"])
            for a in range(4):
                for nb in range(2):
                    ob = no % 2
                    no += 1
                    cs = slice(h * 1024 + nb * 512, h * 1024 + (nb + 1) * 512)
                    TT("dve", TE3, ps[a * 2 + nb][:, :], G2B[:, cs], ALU.mult, [f"ps{a * 2 + nb}", "G2B"], ["TE3"])
                    TT("pool", OUTs[ob], TE3, X1s[:, a, cs], ALU.add, ["TE3", "X1s"], [f"OUTs{ob}"])
                    DMA("pool", y_out[r0 + a * 128:r0 + (a + 1) * 128, cs], OUTs[ob], f"sOUT{ob}", [f"OUTs{ob}"], [])
    P.barrier()

    P.emit()
    es.close()
    return nc


def rope_tables_host(L):
    pos = np.arange(L, dtype=np.float32)
    inv_freq = (np.float32(10000.0) ** (-np.arange(0, 64, 2, dtype=np.float32) / np.float32(64))).astype(np.float32)
    ang = (pos[:, None] * inv_freq[None, :]).astype(np.float32)
    return np.cos(ang).astype(np.float32), np.sin(ang).astype(np.float32)


_S5C = {}


def s5_constants():
    if _S5C:
        return _S5C
    sel = np.zeros((128, 64, 128), np.float32)
    selt = np.zeros((128, 64, 128), np.float32)
    for s in range(8):
        for gl in range(8):
            for c in range(16):
                sel[gl * 16 + c, s * 8 + gl, s * 16 + c] = 1.0
                selt[s * 16 + c, s * 8 + gl, gl * 16 + c] = 1.0
    si = np.arange(128)[:, None] // 16
    ti = np.arange(128)[None, :] // 16
    _S5C.update(sel=sel, selt=selt, maskf=(si <= ti).astype(np.float32), maskb=(si >= ti).astype(np.float32))
    return _S5C


def make_core_inputs(inp, seq_x, seq_c, L):
    cos, sin = rope_tables_host(L)
    m = {"x": np.ascontiguousarray(seq_x[:L]), "c": np.ascontiguousarray(seq_c),
         "ident": np.eye(128, dtype=np.float32), "ropec": cos, "ropes": sin}
    m.update(s5_constants())
    for k, v in inp.items():
        if k in ("x_prompt", "x_sample", "c_prompt", "c_sample"):
            continue
        m[k] = np.ascontiguousarray(v[0])
    return m


def kernel(**inputs):
    L = 8192
    nc = build(L)
    xs = [inputs["x_prompt"][i] for i in range(4)] + [inputs["x_sample"][0]]
    cs = [inputs["c_prompt"][i] for i in range(4)] + [inputs["c_sample"][0]]
    in_maps = []
    for core in range(8):
        i = core if core < 5 else core - 5
        in_maps.append(make_core_inputs(inputs, xs[i], cs[i], L))
    res = run_bass_kernel_spmd(nc, in_maps, core_ids=list(range(8)))
    ys = [np.asarray(res.results[i]["y"], dtype=np.float32) for i in range(5)]
    y_prompt = np.stack(ys[:4], axis=0)
    y_sample = ys[4][None]
    return (y_prompt, y_sample)
```

```python
import contextlib
import numpy as np
import concourse.bass as bass
import concourse.mybir as mybir
from concourse.bass_utils import run_bass_kernel_spmd

F32 = mybir.dt.float32
BF16 = mybir.dt.bfloat16
AF = mybir.ActivationFunctionType
ALU = mybir.AluOpType
AX = mybir.AxisListType

D = 2048
NH = 8
DFF = 5632
INC = 3136
EPS = 1e-6
MEMF = 52000


class Prog:
    def __init__(self, nc):
        self.nc = nc
        self.ops = {e: [] for e in ("pe", "act", "dve", "pool", "sp")}
        self.count = {}
        self.waited = {e: {} for e in self.ops}
        self.res = {}
        self.semkeys = []
        self.chslot = {}

    def _tok(self, semkey, inc):
        if semkey not in self.count:
            self.count[semkey] = 0
            self.semkeys.append(semkey)
        self.count[semkey] += inc
        return (semkey, self.count[semkey])

    def op(self, eng, fn, reads=(), writes=(), ch=None):
        deps = {}

        def add(toks, raw):
            for k, v in toks.items():
                if ch is None and k == eng and (eng == "pe" or not raw):
                    continue
                if deps.get(k, 0) < v:
                    deps[k] = v
        for r in reads:
            st = self.res.get(r)
            if st is not None:
                add(st[0], True)
        for w in writes:
            st = self.res.get(w)
            if st is not None:
                add(st[0], True)
                add(st[1], False)
        if ch is not None:
            if ch not in self.chslot:
                self.chslot[ch] = len(self.chslot)
            chkey = "dma:%d" % self.chslot[ch]
            if self.count.get(chkey, 0) > 0:
                deps[chkey] = self.count[chkey]
        waits = []
        wd = self.waited[eng]
        for k, v in deps.items():
            if wd.get(k, 0) < v:
                wd[k] = v
                waits.append((k, v))
        if ch is None:
            tok = self._tok(eng, 1)
            inc = 1
        else:
            tok = self._tok(chkey, 16)
            inc = 16
        self.ops[eng].append((waits, fn, tok[0], inc))
        for r in reads:
            st = self.res.setdefault(r, [{}, {}])
            if st[1].get(tok[0], 0) < tok[1]:
                st[1][tok[0]] = tok[1]
        for w in writes:
            self.res[w] = [{tok[0]: tok[1]}, {}]
        return tok

    def barrier(self):
        for eng in self.ops:
            waits = []
            wd = self.waited[eng]
            for k, v in self.count.items():
                if wd.get(k, 0) < v:
                    wd[k] = v
                    waits.append((k, v))
            if waits:
                self.ops[eng].append((waits, None, None, 0))
        self.res = {}
        self.chslot = {}

    def emit(self):
        nc = self.nc
        with contextlib.ExitStack() as es:
            sems = {}
            for k in self.semkeys:
                sems[k] = es.enter_context(nc.semaphore("s_" + k.replace(":", "_")))
            block = es.enter_context(nc.Block())

            def run(engname):
                def body(e):
                    for waits, fn, semkey, inc in self.ops[engname]:
                        for k, v in waits:
                            e.wait_ge(sems[k], v)
                        if fn is not None:
                            ins = fn(e)
                            ins.then_inc(sems[semkey], inc)
                return body
            block.tensor(run("pe"))
            block.scalar(run("act"))
            block.vector(run("dve"))
            block.gpsimd(run("pool"))
            block.sync(run("sp"))


def build(L, dbg=(), upto=99):
    NT = L // 128
    nc = bass.Bass("TRN2", target_bir_lowering=False)
    P = Prog(nc)

    def din(name, shape):
        return nc.dram_tensor(name, list(shape), F32, kind="ExternalInput").ap()

    def dscr(name, shape, dt):
        kind = "ExternalOutput" if name in dbg else "Internal"
        return nc.dram_tensor(name, list(shape), dt, kind=kind).ap()

    x = din("x", [L, D]); cvec = din("c", [D])
    w_ada = din("w_ada", [D, 6 * D]); b_ada = din("b_ada", [6 * D])
    norm_mix_g = din("norm_mix_g", [D]); w_in = din("w_in", [D, INC])
    kv_norm_g = din("kv_norm_g", [512]); w_ukv = din("w_ukv", [512, 2048])
    q_norm_g = din("q_norm_g", [192]); k_norm_g = din("k_norm_g", [192])
    lam_re = din("lam_re", [2, 64, 64]); lam_im = din("lam_im", [2, 64, 64]); log_dt = din("log_dt", [2, 64])
    b_re = din("b_re", [64, 64, 16]); b_im = din("b_im", [64, 64, 16])
    c_re = din("c_re", [2, 64, 16, 64]); c_im = din("c_im", [2, 64, 16, 64])
    d_skip = din("d_skip", [1024]); w_glu = din("w_glu", [1024, 1024]); b_glu = din("b_glu", [1024])
    att_out_g = din("att_out_g", [1024]); ssm_out_g = din("ssm_out_g", [1024])
    w_o = din("w_o", [D, D]); norm_ffn_g = din("norm_ffn_g", [D])
    w1 = din("w1", [D, DFF]); w3 = din("w3", [D, DFF]); w2 = din("w2", [DFF, D])
    ident_d = din("ident", [128, 128]); ropec = din("ropec", [L, 32]); ropes = din("ropes", [L, 32])
    maskf_d = din("maskf", [128, 128]); maskb_d = din("maskb", [128, 128])
    sel_d = din("sel", [128, 64, 128]); selt_d = din("selt", [128, 64, 128])
    y_out = nc.dram_tensor("y", [L, D], F32, kind="ExternalOutput").ap()

    MOD_d = dscr("MOD_d", [6 * D], F32)
    QT_d = dscr("QT_d", [NH, 128, L], BF16); QR_d = dscr("QR_d", [NH, 64, L], BF16)
    KT_d = dscr("KT_d", [NH, 128, L], BF16); KR_d = dscr("KR_d", [NH, 64, L], BF16)
    V_d = dscr("V_d", [NH, L, 128], BF16)
    UT_d = dscr("UT_d", [1024, L], BF16)
    W1s = dscr("W1s", [22, 128, 16 * 256], BF16)
    W3s = dscr("W3s", [22, 128, 16 * 256], BF16)
    W2s = dscr("W2s", [2, 11, 128, 4 * 1024], BF16)
    ATTT_d = dscr("ATTT_d", [1024, L], BF16)

    es = contextlib.ExitStack()
    mem = es.enter_context(nc.sbuf_tensor("mem", [128, MEMF], F32))
    ps = [es.enter_context(nc.psum_tensor(f"ps{i}", [128, 512], F32)) for i in range(8)]

    class Alloc:
        def __init__(self, base=0):
            self.off = base
            self.base = base

        def reset(self):
            self.off = self.base

        def t(self, shape, dt):
            n = int(np.prod(shape[1:]))
            nb = n * (2 if dt == BF16 else 4)
            nb4 = (nb + 3) // 4
            assert self.off + nb4 <= MEMF, ("SBUF arena overflow", self.off, nb4)
            v = mem[0:shape[0], self.off:self.off + nb4]
            if dt != F32:
                v = v.bitcast(dt)
                if nb4 * 2 != n:
                    v = v[:, 0:n]
            if len(shape) == 3:
                v = v.rearrange("p (a b) -> p a b", b=shape[2])
            elif len(shape) == 4:
                v = v.rearrange("p (a b c) -> p a b c", b=shape[2], c=shape[3])
            self.off += nb4
            return v

    def MM(out, lhsT, rhs, st, sp, r, w):
        P.op("pe", lambda e: e.matmul(out, lhsT, rhs, start=st, stop=sp), reads=r, writes=w)

    def ACT(out, in_, func, r, w, bias=None, scale=None, accum=None):
        kw = {}
        if bias is not None:
            kw["bias"] = bias
        if scale is not None:
            kw["scale"] = scale
        if accum is not None:
            kw["accum_out"] = accum
        P.op("act", lambda e: e.activation(out, in_, func, **kw), reads=r, writes=w)

    def TT(eng, out, a, b, op, r, w):
        P.op(eng, lambda e: e.tensor_tensor(out, a, b, op), reads=r, writes=w)

    def TS(eng, out, a, s1, s2, op0, op1, r, w):
        if s2 is None:
            P.op(eng, lambda e: e.tensor_scalar(out, a, s1, None, op0), reads=r, writes=w)
        else:
            P.op(eng, lambda e: e.tensor_scalar(out, a, s1, s2, op0, op1), reads=r, writes=w)

    def STT(eng, out, a, s, b, op0, op1, r, w):
        P.op(eng, lambda e: e.scalar_tensor_tensor(out, a, s, b, op0, op1), reads=r, writes=w)

    def CP(eng, out, in_, r, w):
        if eng == "act":
            P.op(eng, lambda e: e.copy(out, in_), reads=r, writes=w)
        else:
            P.op(eng, lambda e: e.tensor_copy(out, in_), reads=r, writes=w)

    def RSUM(eng, out, in_, r, w):
        P.op(eng, lambda e: e.reduce_sum(out, in_, AX.X), reads=r, writes=w)

    def MSET(eng, out, val, w):
        P.op(eng, lambda e: e.memset(out, val), writes=w)

    def DMA(q, out, in_, ch, r, w, slow=False):
        if slow:
            P.op(q, lambda e: e.dma_start(out=out, in_=in_, allow_slow_non_contiguous=True), reads=r, writes=w, ch=ch)
        else:
            P.op(q, lambda e: e.dma_start(out=out, in_=in_), reads=r, writes=w, ch=ch)

    def RSTD(dst, src, n, name, srcname):
        TS("dve", dst, src, 1.0 / n, EPS, ALU.mult, ALU.add, r=[srcname], w=[name])
        P.op("act", lambda e: e.sqrt(dst, dst), reads=[name], writes=[name])
        P.op("dve", lambda e: e.reciprocal(dst, dst), reads=[name], writes=[name])

    def bc(ap, shape, axis):
        return ap.unsqueeze(axis).to_broadcast(shape)

    PA = Alloc(0)
    IDF = PA.t([128, 128], F32)
    IDB = PA.t([128, 128], BF16)
    GS = PA.t([128, 64], F32)
    A8T = PA.t([128, 2, 2, 64], F32)
    SSS = PA.t([128, 64], F32)
    SSA = PA.t([128, 64], F32)
    pers_end = PA.off
    A = Alloc(pers_end)

    DMA("sp", IDF, ident_d, "c0", [], ["IDF"])
    DMA("pool", IDB, ident_d, "c1", [], ["IDB"])

    SC = A.t([128, 16], F32)
    SCB = A.t([128, 16, 128], F32)
    WA = [A.t([128, 16, 512], F32) for _ in range(2)]
    BAb = [A.t([128, 512], F32) for _ in range(2)]
    ONES = A.t([1, 128], F32)
    MODB = A.t([128, 6 * D], F32)
    TMP0 = A.t([128, 96, 128], F32)
    COLS = A.t([128, 96], F32)
    NG = A.t([128, 32], F32)

    DMA("sp", SC, cvec.rearrange("(k p) -> p k", p=128), "c2", [], ["SC"], slow=True)
    DMA("sp", NG[:, 0:16], norm_mix_g.rearrange("(k p) -> p k", p=128), "c3", [], ["NG0"], slow=True)
    DMA("sp", NG[:, 16:32], norm_ffn_g.rearrange("(k p) -> p k", p=128), "c4", [], ["NG1"], slow=True)
    ACT(SC, SC, AF.Silu, ["SC"], ["SC"])
    CP("dve", SCB, bc(SC, [128, 16, 128], 2), ["SC"], ["SCB"])
    MSET("pool", ONES, 1.0, ["ONES"])
    w_ada_v = w_ada.rearrange("(k p) n -> p k n", p=128)
    b_ada_v = b_ada.rearrange("(o n) -> o n", o=1)
    for nb in range(24):
        b = nb % 2
        DMA("sp", WA[b], w_ada_v[:, :, nb * 512:(nb + 1) * 512], f"WA{b}", [], [f"WA{b}"])
        DMA("sp", BAb[b], b_ada[nb * 512:(nb + 1) * 512].partition_broadcast(128), f"BA{b}", [], [f"BA{b}"])
        pt = ps[b]
        for k in range(16):
            MM(pt[:, :], SCB[:, k, :], WA[b][:, k, :], k == 0, k == 15, ["SCB", f"WA{b}"], [f"ps{b}"])
        TT("dve", MODB[:, nb * 512:(nb + 1) * 512], pt[:, :], BAb[b], ALU.add, [f"ps{b}", f"BA{b}"], [f"MODB{nb}"])
    allmod = [f"MODB{nb}" for nb in range(24)]
    DMA("sp", MOD_d.rearrange("(o n) -> o n", o=1), MODB[0:1, :], "c5", allmod, ["MOD_d"])
    MODB3 = MODB.rearrange("p (a b) -> p a b", b=128)
    TT("dve", TMP0, MODB3, bc(IDF, [128, 96, 128], 1), ALU.mult, allmod + ["IDF"], ["TMP0"])
    RSUM("dve", COLS, TMP0, ["TMP0"], ["COLS"])
    STT("dve", GS[:, 0:16], COLS[:, 16:32], 1.0, NG[:, 0:16], ALU.add, ALU.mult, ["COLS", "NG0"], ["GS"])
    CP("dve", GS[:, 16:32], COLS[:, 0:16], ["COLS"], ["GS"])
    STT("dve", GS[:, 32:48], COLS[:, 64:80], 1.0, NG[:, 16:32], ALU.add, ALU.mult, ["COLS", "NG1"], ["GS"])
    CP("dve", GS[:, 48:64], COLS[:, 48:64], ["COLS"], ["GS"])
    P.barrier()
    if upto < 1:
        P.emit(); es.close(); return nc

    A.reset()
    WIN = A.t([128, 16, INC], BF16)
    WUKV = A.t([128, 4, 2048], BF16)
    XTd = [A.t([128, D], F32) for _ in range(2)]
    XBd = [A.t([128, D], BF16) for _ in range(2)]
    STX = A.t([128, 4], F32)
    JNKX = A.t([128, D], BF16)
    HT = A.t([128, 16, 128], BF16)
    QF = A.t([128, 1536], F32)
    CKV = A.t([128, 576], F32)
    UF = A.t([128, 1024], BF16)
    KVF = A.t([128, 2048], F32)
    SCR = A.t([128, 2048], F32)
    QN = A.t([128, 8, 192], BF16)
    KN = A.t([128, 8, 192], BF16)
    QTs = A.t([128, 8, 128], BF16)
    QRs = A.t([64, 8, 128], BF16)
    KTs = A.t([128, 8, 128], BF16)
    KRs = A.t([64, 8, 128], BF16)
    UTs = A.t([128, 8, 128], BF16)
    VS = A.t([128, 8, 128], BF16)
    CKN = A.t([128, 512], BF16)
    CKT = A.t([128, 4, 128], BF16)
    RC = A.t([128, 32], F32)
    RS = A.t([128, 32], F32)
    GQ = A.t([128, 192], F32)
    GK = A.t([128, 192], F32)
    KVG = A.t([128, 4], F32)
    ST = A.t([128, 32], F32)
    RT = A.t([128, 8, 32], F32)
    RT2 = A.t([128, 8, 32], F32)
    RT3 = A.t([128, 8, 32], F32)
    RT4 = A.t([128, 8, 32], F32)
    KRG = A.t([128, 64], F32)
    KRR = A.t([128, 64], F32)

    for k in range(16):
        DMA("pool", WIN[:, k, :], w_in[k * 128:(k + 1) * 128, :], f"WIN{k}", [], [f"WIN{k}"])
    for k in range(4):
        DMA("pool", WUKV[:, k, :], w_ukv[k * 128:(k + 1) * 128, :], f"WUKV{k}", [], [f"WUKV{k}"])
    DMA("sp", GQ, q_norm_g.partition_broadcast(128), "c6", [], ["GQ"])
    DMA("sp", GK, k_norm_g.partition_broadcast(128), "c7", [], ["GK"])
    DMA("sp", KVG, kv_norm_g.rearrange("(k p) -> p k", p=128), "c8", [], ["KVG"], slow=True)

    blocks = [(0, 512), (512, 512), (1024, 512), (1536, 512), (2048, 64), (2112, 512), (2624, 512)]
    import os
    P1STOP = int(os.environ.get("P1STOP", "-1"))

    class _Stop(Exception):
        pass

    def CK(n):
        if P1STOP == n:
            raise _Stop()
    def front_load(t_):
        pb_ = t_ % 2
        rr = t_ * 128
        DMA("sp", XTd[pb_], x[rr:rr + 128, :], f"XT{pb_}", [], [f"XT{pb_}"])

    def front_a(t_):
        pb_ = t_ % 2
        MSET("pool", STX[:, pb_ * 2:pb_ * 2 + 2], 0.0, [f"x_ss{pb_}", f"xr{pb_}"])
        ACT(JNKX, XTd[pb_], AF.Square, [f"XT{pb_}"], ["JNKX", f"x_ss{pb_}"], accum=STX[:, pb_ * 2:pb_ * 2 + 1])
        RSTD(STX[:, pb_ * 2 + 1:pb_ * 2 + 2], STX[:, pb_ * 2:pb_ * 2 + 1], D, f"xr{pb_}", f"x_ss{pb_}")
        ACT(XBd[pb_], XTd[pb_], AF.Copy, [f"XT{pb_}", f"xr{pb_}"], [f"XB{pb_}"], scale=STX[:, pb_ * 2 + 1:pb_ * 2 + 2])

    def p1_head(t):
        r0 = t * 128
        XB = XBd[t % 2]
        DMA("sp", RC, ropec[r0:r0 + 128, :], "RC", [], ["RC"])
        DMA("sp", RS, ropes[r0:r0 + 128, :], "RS", [], ["RS"])
        MSET("pool", ST, 0.0, ["c_ss", "qr_ss", "k_ssn", "kr_ss", "kr_ss2", "cr", "qr", "kr"])


    def p1_xT(t):
        r0 = t * 128
        XB = XBd[t % 2]
        allht = [f"HT{k}" for k in range(16)]
        for k in range(16):
            MM(ps[k // 4][:, (k % 4) * 128:(k % 4 + 1) * 128], XB[:, k * 128:(k + 1) * 128], IDB, True, True,
               [f"XB{t % 2}", "IDB"], [f"ps{k // 4}"])
        for k in range(16):
            src = ps[k // 4][:, (k % 4) * 128:(k % 4 + 1) * 128]
            if False:
                ACT(HT[:, k, :], src, AF.Identity, [f"ps{k // 4}", "GS"], [f"HT{k}"],
                    bias=GS[:, 16 + k:17 + k], scale=GS[:, k:k + 1])
            else:
                TS("dve", HT[:, k, :], src, GS[:, k:k + 1], GS[:, 16 + k:17 + k], ALU.mult, ALU.add,
                   [f"ps{k // 4}", "GS"], [f"HT{k}"])

    def p1_proj(t):
        allht = [f"HT{k}" for k in range(16)]
        for bi, (c0, w) in enumerate(blocks):
            pi = 4 + bi % 4
            pt = ps[pi]
            for k in range(16):
                MM(pt[:, 0:w], HT[:, k, :], WIN[:, k, c0:c0 + w], k == 0, k == 15, allht + [f"WIN{k}"], [f"ps{pi}"])
            if bi < 3:
                CP("act", QF[:, c0:c0 + w], pt[:, 0:w], [f"ps{pi}"], [f"QF{bi}"])
            elif bi == 3:
                CP("dve", CKV[:, 0:512], pt[:, 0:w], [f"ps{pi}"], ["CKVa"])
            elif bi == 4:
                CP("dve", CKV[:, 512:576], pt[:, 0:w], [f"ps{pi}"], ["CKVb"])
            else:
                CP("act", UF[:, c0 - 2112:c0 - 2112 + w], pt[:, 0:w], [f"ps{pi}"], [f"UF{bi}"])


    def p1_ckv(t):
        r0 = t * 128
        allkv = [f"KVF{nb}" for nb in range(4)]
        ACT(JNKX[:, 0:512], CKV[:, 0:512], AF.Square, ["CKVa"], ["JNKX", "c_ss"], accum=ST[:, 2:3])
        RSTD(ST[:, 3:4], ST[:, 2:3], 512, "cr", "c_ss")
        ACT(CKN, CKV[:, 0:512], AF.Copy, ["CKVa", "cr"], ["CKN"], scale=ST[:, 3:4])
        for k in range(4):
            MM(ps[0][:, k * 128:(k + 1) * 128], CKN[:, k * 128:(k + 1) * 128], IDB, True, True, ["CKN", "IDB"], ["ps0"])
        for k in range(4):
            TS("dve", CKT[:, k, :], ps[0][:, k * 128:(k + 1) * 128], KVG[:, k:k + 1], None, ALU.mult, None,
               ["ps0", "KVG"], ["CKT"])
        for nb in range(4):
            pi = 4 + nb
            for k in range(4):
                MM(ps[pi][:, :], CKT[:, k, :], WUKV[:, k, nb * 512:(nb + 1) * 512], k == 0, k == 3,
                   ["CKT", f"WUKV{k}"], [f"ps{pi}"])
            CP("act", KVF[:, nb * 512:(nb + 1) * 512], ps[pi][:, :], [f"ps{pi}"], [f"KVF{nb}"])
        allkv = [f"KVF{nb}" for nb in range(4)]


    def p1_qchain(t):
        r0 = t * 128
        allq = ["QF0", "QF1", "QF2"]
        allq = ["QF0", "QF1", "QF2"]
        QF3 = QF.rearrange("p (h d) -> p h d", d=192)
        SCR3 = SCR[:, 0:1536].rearrange("p (h d) -> p h d", d=192)
        TT("pool", SCR[:, 0:1536], QF, QF, ALU.mult, allq, ["SCR"])
        RSUM("dve", ST[:, 8:16], SCR3, ["SCR"], ["qr_ss"])
        RSTD(ST[:, 8:16], ST[:, 8:16], 192, "qr", "qr_ss")
        TT("pool", QF3, QF3, bc(ST[:, 8:16], [128, 8, 192], 2), ALU.mult, allq + ["qr"], ["QFn"])
        TT("pool", QF3, QF3, bc(GQ, [128, 8, 192], 1), ALU.mult, ["QFn", "GQ"], ["QFn"])
        cosb = bc(RC, [128, 8, 32], 1)
        sinb = bc(RS, [128, 8, 32], 1)
        TT("pool", RT, QF3[:, :, 128:160], cosb, ALU.mult, ["QFn", "RC"], ["RT"])
        TT("pool", RT2, QF3[:, :, 160:192], sinb, ALU.mult, ["QFn", "RS"], ["RT2"])
        TT("pool", RT3, QF3[:, :, 160:192], cosb, ALU.mult, ["QFn", "RC"], ["RT3"])
        TT("pool", RT4, QF3[:, :, 128:160], sinb, ALU.mult, ["QFn", "RS"], ["RT4"])
        TT("dve", QN[:, :, 128:160], RT, RT2, ALU.subtract, ["RT", "RT2"], ["QNa"])
        TT("dve", QN[:, :, 160:192], RT3, RT4, ALU.add, ["RT3", "RT4"], ["QNb"])
        CP("pool", QN[:, :, 0:128], QF3[:, :, 0:128], ["QFn"], ["QNc"])
        allqn = ["QNa", "QNb", "QNc"]


    def p1_qT(t):
        r0 = t * 128
        allqn = ["QNa", "QNb", "QNc"]
        for h in range(8):
            MM(ps[h // 4][:, (h % 4) * 128:(h % 4 + 1) * 128], QN[:, h, 0:128], IDB, True, True,
               allqn + ["IDB"], [f"ps{h // 4}"])
            MM(ps[2 + h // 4][0:64, (h % 4) * 128:(h % 4 + 1) * 128], QN[:, h, 128:192], IDB, True, True,
               allqn + ["IDB"], [f"ps{2 + h // 4}"])
        for hb in range(2):
            CP("dve", QTs[:, hb * 4:(hb + 1) * 4, :], ps[hb][:, :].rearrange("p (a b) -> p a b", b=128),
               [f"ps{hb}"], [f"QTs{hb}"])
            CP("act", QRs[:, hb * 4:(hb + 1) * 4, :], ps[2 + hb][0:64, :].rearrange("p (a b) -> p a b", b=128),
               [f"ps{2 + hb}"], [f"QRs{hb}"])
        DMA("sp", QT_d[:, :, r0:r0 + 128].rearrange("h d t -> d h t"), QTs, "sQT", ["QTs0", "QTs1"], [])
        DMA("sp", QR_d[:, :, r0:r0 + 128].rearrange("h d t -> d h t"), QRs, "sQR", ["QRs0", "QRs1"], [])


    def p1_kchain(t):
        r0 = t * 128
        allkv = [f"KVF{nb}" for nb in range(4)]
        allq = ["QF0", "QF1", "QF2"]
        KV3 = KVF.rearrange("p (h d) -> p h d", d=256)
        SCRk = SCR[:, 0:1024].rearrange("p (h d) -> p h d", d=128)
        TT("pool", SCRk, KV3[:, :, 0:128], KV3[:, :, 0:128], ALU.mult, allkv, ["SCR"])
        RSUM("dve", ST[:, 16:24], SCRk, ["SCR"], ["k_ssn"])
        TT("pool", KRG, CKV[:, 512:576], CKV[:, 512:576], ALU.mult, ["CKVb"], ["KRG"])
        RSUM("dve", ST[:, 4:5], KRG, ["KRG"], ["kr_ss"])
        TS("dve", ST[:, 16:24], ST[:, 16:24], ST[:, 4:5], None, ALU.add, None, ["k_ssn", "kr_ss"], ["kr_ss2"])
        TS("dve", ST[:, 16:24], ST[:, 16:24], 1.0 / 192, EPS, ALU.mult, ALU.add, ["kr_ss2"], ["kr"])
        P.op("act", lambda e: e.sqrt(ST[:, 16:24], ST[:, 16:24]), reads=["kr"], writes=["kr"])
        P.op("dve", lambda e: e.reciprocal(ST[:, 16:24], ST[:, 16:24]), reads=["kr"], writes=["kr"])
        TT("pool", SCRk, KV3[:, :, 0:128], bc(ST[:, 16:24], [128, 8, 128], 2), ALU.mult, allkv + ["kr"], ["SCR"])
        TT("pool", KN[:, :, 0:128], SCRk, bc(GK[:, 0:128], [128, 8, 128], 1), ALU.mult, ["SCR", "GK"], ["KNc"])
        TT("pool", KRG, CKV[:, 512:576], GK[:, 128:192], ALU.mult, ["CKVb", "GK", "kr_ss"], ["KRG"])
        TT("pool", RT[:, 0, :], KRG[:, 0:32], RC, ALU.mult, ["KRG", "RC", "QNa"], ["RT"])
        TT("pool", RT2[:, 0, :], KRG[:, 32:64], RS, ALU.mult, ["KRG", "RS", "QNa"], ["RT2"])
        TT("pool", RT3[:, 0, :], KRG[:, 32:64], RC, ALU.mult, ["KRG", "RC", "QNb"], ["RT3"])
        TT("pool", RT4[:, 0, :], KRG[:, 0:32], RS, ALU.mult, ["KRG", "RS", "QNb"], ["RT4"])
        TT("dve", KRR[:, 0:32], RT[:, 0, :], RT2[:, 0, :], ALU.subtract, ["RT", "RT2"], ["KRRa"])
        TT("dve", KRR[:, 32:64], RT3[:, 0, :], RT4[:, 0, :], ALU.add, ["RT3", "RT4"], ["KRRb"])
        TT("pool", KN[:, :, 128:192], bc(KRR, [128, 8, 64], 1), bc(ST[:, 16:24], [128, 8, 64], 2), ALU.mult,
           ["KRRa", "KRRb", "kr"], ["KNr"])
        CP("dve", VS, KV3[:, :, 128:256], allkv, ["VS"])
        DMA("sp", V_d[:, r0:r0 + 128, :].rearrange("h t d -> t h d"), VS, "sV", ["VS"], [])


    def p1_kT(t):
        r0 = t * 128
        allkn = ["KNc", "KNr"]
        for h in range(8):
            MM(ps[h // 4][:, (h % 4) * 128:(h % 4 + 1) * 128], KN[:, h, 0:128], IDB, True, True,
               allkn + ["IDB"], [f"ps{h // 4}"])
            MM(ps[2 + h // 4][0:64, (h % 4) * 128:(h % 4 + 1) * 128], KN[:, h, 128:192], IDB, True, True,
               allkn + ["IDB"], [f"ps{2 + h // 4}"])
        for hb in range(2):
            CP("dve", KTs[:, hb * 4:(hb + 1) * 4, :], ps[hb][:, :].rearrange("p (a b) -> p a b", b=128),
               [f"ps{hb}"], [f"KTs{hb}"])
            CP("act", KRs[:, hb * 4:(hb + 1) * 4, :], ps[2 + hb][0:64, :].rearrange("p (a b) -> p a b", b=128),
               [f"ps{2 + hb}"], [f"KRs{hb}"])
        DMA("sp", KT_d[:, :, r0:r0 + 128].rearrange("h d t -> d h t"), KTs, "sKT", ["KTs0", "KTs1"], [])
        DMA("sp", KR_d[:, :, r0:r0 + 128].rearrange("h d t -> d h t"), KRs, "sKR", ["KRs0", "KRs1"], [])


    def p1_u(t):
        r0 = t * 128
        for k in range(8):
            MM(ps[4 + k // 4][:, (k % 4) * 128:(k % 4 + 1) * 128], UF[:, k * 128:(k + 1) * 128], IDB, True, True,
               ["UF5", "UF6", "IDB"], [f"ps{4 + k // 4}"])
        for hb in range(2):
            CP("act" if hb == 0 else "dve", UTs[:, hb * 4:(hb + 1) * 4, :],
               ps[4 + hb][:, :].rearrange("p (a b) -> p a b", b=128), [f"ps{4 + hb}"], [f"UTs{hb}"])
        DMA("sp", UT_d.rearrange("(b f) t -> f b t", f=128)[:, :, r0:r0 + 128], UTs, "sUT", ["UTs0", "UTs1"], [])

    front_load(0)
    front_a(0)
    p1_xT(0)
    for t in range(NT):
        p1_head(t)
        if t + 1 < NT:
            front_load(t + 1)
        p1_proj(t)
        if t + 1 < NT:
            front_a(t + 1)
        if t > 0:
            p1_qT(t - 1)
            p1_kT(t - 1)
        if t + 1 < NT:
            p1_xT(t + 1)
        p1_ckv(t)
        p1_qchain(t)
        p1_u(t)
        p1_kchain(t)
    p1_qT(NT - 1)
    p1_kT(NT - 1)
    P.barrier()
    if upto < 2:
        P.emit(); es.close(); return nc

    if upto < 3:
        P.emit(); es.close(); return nc

    NJ = L // 8
    NJC = L // 1024
    WINC_d = dscr("WINC_d", [128, 256 * 128], BF16)
    WOUT_d = dscr("WOUT_d", [128, 256 * 128], BF16)
    M_d = dscr("M_d", [128, 64 * 128], BF16)
    INC_d = [dscr(f"INC{d}_d", [128, NJC, 128 * 64], F32) for d in range(2)]
    S_d = [dscr(f"S{d}_d", [128, NJC, 64, 128], BF16) for d in range(2)]
    YT_d = dscr("YT_d", [1024, L], BF16)
    SSMT_d = dscr("SSMT_d", [1024, L], BF16)

    A.reset()
    LR = A.t([128, 64], F32); LI = A.t([128, 64], F32); DT = A.t([128, 64], F32)
    XX = A.t([128, 64], F32); TH = A.t([128, 64], F32)
    CC = A.t([128, 64], F32); SS = A.t([128, 64], F32)
    T1 = A.t([128, 64], F32); T2 = A.t([128, 64], F32)
    HPI = A.t([128, 1], F32)
    UR = A.t([128, 9, 64], F32); UI = A.t([128, 9, 64], F32)
    ER = A.t([128, 9, 64], F32); EI = A.t([128, 9, 64], F32)
    EIR = A.t([128, 9, 64], F32); EII = A.t([128, 9, 64], F32)
    MG = A.t([128, 64], F32); IMG = A.t([128, 64], F32)
    QRE = A.t([128, 64], F32); QIM = A.t([128, 64], F32)
    BRE = A.t([128, 32, 16], F32); BIM = A.t([128, 32, 16], F32)
    BBR = A.t([128, 2, 32, 16], F32); BBI = A.t([128, 2, 32, 16], F32)
    CRE = A.t([128, 2, 32, 16], F32); CIM = A.t([128, 2, 32, 16], F32)
    CN2 = [A.t([128, 128], F32) for _ in range(2)]
    TA = A.t([128, 32, 16], F32); TB = A.t([128, 32, 16], F32)
    TA2 = A.t([128, 16, 16], F32); TB2 = A.t([128, 16, 16], F32)
    XR = A.t([128, 16, 8, 16], F32); XI = A.t([128, 16, 8, 16], F32)
    WR = A.t([128, 16, 8, 16], F32); WI = A.t([128, 16, 8, 16], F32)
    WTR = A.t([128, 16, 8, 16], F32); WTI = A.t([128, 16, 8, 16], F32)
    TD = A.t([128, 16, 8, 16], F32)
    MACC = A.t([128, 64, 128], F32)
    MB = A.t([128, 64, 128], BF16)
    MKF = A.t([128, 128], F32); MKB = A.t([128, 128], F32)
    IH = [A.t([128, 128], F32) for _ in range(2)]
    DCOL = A.t([128, 64], F32)
    WINs = [A.t([128, 4, 128], BF16) for _ in range(2)]
    WOUs = [A.t([128, 4, 128], BF16) for _ in range(2)]
    TMPM = A.t([128, 128], F32)

    def rawap(ap, off, dims):
        return bass.AP(ap.tensor, off, dims)
    for q4 in range(4):
        DMA("sp", LR[:, q4 * 16:(q4 + 1) * 16], rawap(lam_re, (q4 // 2) * 4096 + (q4 % 2) * 16 * 128, [[1, 128], [128, 16]]),
            f"g{q4}", [], ["LR"], slow=True)
        DMA("sp", LI[:, q4 * 16:(q4 + 1) * 16], rawap(lam_im, (q4 // 2) * 4096 + (q4 % 2) * 16 * 128, [[1, 128], [128, 16]]),
            f"g{4 + q4}", [], ["LI"], slow=True)
    for gpar in range(2):
        DMA("sp", DT[gpar * 64:(gpar + 1) * 64, :].rearrange("p (d g) -> p d g", d=2),
            rawap(log_dt, gpar, [[0, 64], [64, 2], [2, 32]]), f"g{8 + gpar}", [], ["DT"], slow=True)
    for q4 in range(4):
        DMA("sp", BRE[:, q4 * 8:(q4 + 1) * 8, :], rawap(b_re, q4 * 8 * 2048, [[16, 128], [2048, 8], [1, 16]]),
            f"g{10 + q4}", [], ["BRE"], slow=True)
        DMA("sp", BIM[:, q4 * 8:(q4 + 1) * 8, :], rawap(b_im, q4 * 8 * 2048, [[16, 128], [2048, 8], [1, 16]]),
            f"g{14 + q4}", [], ["BIM"], slow=True)
    for s in range(8):
        DMA("sp", DCOL[s * 16:(s + 1) * 16, :], rawap(d_skip, 0, [[1, 16], [16, 64]]), f"g{18 + s}", [], ["DCOL"], slow=True)
    DMA("sp", MKF, maskf_d, "g26", [], ["MKF"])
    DMA("sp", MKB, maskb_d, "g27", [], ["MKB"])
    MSET("pool", HPI, float(np.pi / 2), ["HPI"])
    MSET("pool", WOUs[0], 0.0, ["WOUs0"])
    MSET("pool", WOUs[1], 0.0, ["WOUs1"])
    for hh in range(2):
        CP("pool", IH[hh], IDF, ["IDF"], [f"IH{hh}"])
        MSET("pool", IH[hh][(1 - hh) * 64:(2 - hh) * 64, :], 0.0, [f"IH{hh}"])
    it = 0
    for ri, csrc, cdst in ((0, c_re, CRE), (1, c_im, CIM)):
        for d in range(2):
            cv = csrc[d].rearrange("g c p -> (g c) p")
            for gb in range(8):
                b = it % 2
                it += 1
                DMA("sp", CN2[b][:, 0:64], cv[gb * 128:(gb + 1) * 128, :], f"CNa{b}", [], [f"CN2a{b}"])
                DMA("sp", CN2[b][:, 64:128], cv[gb * 128:(gb + 1) * 128, :], f"CNb{b}", [], [f"CN2b{b}"])
                MM(ps[b][:, 0:128], CN2[b], IDF, True, True, [f"CN2a{b}", f"CN2b{b}", "IDF"], [f"ps{b}"])
                pv = ps[b][:, 0:128].rearrange("p (g c) -> p g c", c=16)
                CP("dve", cdst[0:64, d, gb * 4:(gb + 1) * 4, :], pv[0:64, 0:8:2, :], [f"ps{b}"], [f"C{ri}"])
                CP("act", cdst[64:128, d, gb * 4:(gb + 1) * 4, :], pv[64:128, 1:8:2, :], [f"ps{b}"], [f"C{ri}"])

    def E(out, a, b_, op, r, w, eng="dve"):
        TT(eng, out, a, b_, op, r, w)
    ACT(DT, DT, AF.Exp, ["DT"], ["DT"])
    E(XX, LR, DT, ALU.mult, ["LR", "DT"], ["XX"])
    E(TH, LI, DT, ALU.mult, ["LI", "DT"], ["TH"])
    ACT(SS, TH, AF.Sin, ["TH"], ["SS"], scale=1.0 / 16)
    ACT(CC, TH, AF.Sin, ["TH", "HPI"], ["CC"], scale=1.0 / 16, bias=HPI[:, 0:1])
    for i in range(4):
        E(T1, CC, CC, ALU.mult, ["CC"], ["T1"])
        E(T2, SS, SS, ALU.mult, ["SS"], ["T2"])
        STT("dve", SS, CC, 2.0, SS, ALU.mult, ALU.mult, ["CC", "SS"], ["SS"])
        E(CC, T1, T2, ALU.subtract, ["T1", "T2"], ["CC"])
    CP("dve", UR[:, 1, :], CC, ["CC"], ["U"])
    CP("dve", UI[:, 1, :], SS, ["SS"], ["U"])
    for k in range(2, 9):
        E(T1, UR[:, k - 1, :], CC, ALU.mult, ["U", "CC"], ["T1"])
        E(T2, UI[:, k - 1, :], SS, ALU.mult, ["U", "SS"], ["T2"])
        E(UR[:, k, :], T1, T2, ALU.subtract, ["T1", "T2"], ["U"])
        E(T1, UR[:, k - 1, :], SS, ALU.mult, ["U", "SS"], ["T1"])
        E(T2, UI[:, k - 1, :], CC, ALU.mult, ["U", "CC"], ["T2"])
        E(UI[:, k, :], T1, T2, ALU.add, ["T1", "T2"], ["U"])
    for k in range(1, 9):
        ACT(MG, XX, AF.Exp, ["XX"], ["MG"], scale=float(k))
        ACT(IMG, XX, AF.Exp, ["XX"], ["IMG"], scale=float(-k))
        E(ER[:, k, :], MG, UR[:, k, :], ALU.mult, ["MG", "U"], ["ET"])
        E(EI[:, k, :], MG, UI[:, k, :], ALU.mult, ["MG", "U"], ["ET"])
        E(EIR[:, k, :], IMG, UR[:, k, :], ALU.mult, ["IMG", "U"], ["ET"])
        STT("dve", EII[:, k, :], UI[:, k, :], -1.0, IMG, ALU.mult, ALU.mult, ["IMG", "U"], ["ET"])
    for d in range(2):
        dsl = slice(d * 32, (d + 1) * 32)
        CP("dve", A8T[:, d, 0, 0:32], ER[:, 8, dsl], ["ET"], ["A8T"])
        CP("dve", A8T[:, d, 0, 32:64], ER[:, 8, dsl], ["ET"], ["A8T"])
        TS("dve", A8T[:, d, 1, 0:32], EI[:, 8, dsl], -1.0, None, ALU.mult, None, ["ET"], ["A8T"])
        CP("dve", A8T[:, d, 1, 32:64], EI[:, 8, dsl], ["ET"], ["A8T"])
    TS("dve", T1, ER[:, 1, :], -1.0, None, ALU.add, None, ["ET"], ["T1"])
    E(QRE, T1, LR, ALU.mult, ["T1", "LR"], ["QRE"])
    E(T2, EI[:, 1, :], LI, ALU.mult, ["ET", "LI"], ["T2"])
    E(QRE, QRE, T2, ALU.add, ["QRE", "T2"], ["QRE"])
    E(QIM, EI[:, 1, :], LR, ALU.mult, ["ET", "LR"], ["QIM"])
    E(T2, T1, LI, ALU.mult, ["T1", "LI"], ["T2"])
    E(QIM, QIM, T2, ALU.subtract, ["QIM", "T2"], ["QIM"])
    E(T1, LR, LR, ALU.mult, ["LR"], ["T1"])
    E(T2, LI, LI, ALU.mult, ["LI"], ["T2"])
    E(T1, T1, T2, ALU.add, ["T1", "T2"], ["T1"])
    P.op("dve", lambda e: e.reciprocal(T1, T1), reads=["T1"], writes=["T1"])
    E(QRE, QRE, T1, ALU.mult, ["QRE", "T1"], ["QRE"])
    E(QIM, QIM, T1, ALU.mult, ["QIM", "T1"], ["QIM"])
    for d in range(2):
        dsl = slice(d * 32, (d + 1) * 32)
        qr = bc(QRE[:, dsl], [128, 32, 16], 2)
        qi = bc(QIM[:, dsl], [128, 32, 16], 2)
        E(TA, qr, BRE, ALU.mult, ["QRE", "BRE"], ["TA"])
        E(TB, qi, BIM, ALU.mult, ["QIM", "BIM"], ["TB"])
        E(BBR[:, d], TA, TB, ALU.subtract, ["TA", "TB"], ["BBR"])
        E(TA, qr, BIM, ALU.mult, ["QRE", "BIM"], ["TA"])
        E(TB, qi, BRE, ALU.mult, ["QIM", "BRE"], ["TB"])
        E(BBI[:, d], TA, TB, ALU.add, ["TA", "TB"], ["BBI"])

    for d in range(2):
      for gph in range(2):
        dsl = slice(d * 32 + gph * 16, d * 32 + (gph + 1) * 16)
        gsl = slice(gph * 16, (gph + 1) * 16)
        TAh = TA[:, 0:16, :]
        TBh = TB[:, 0:16, :]
        for s in range(8):
            k = s + 1 if d == 0 else 8 - s
            er = bc(EIR[:, k, dsl], [128, 16, 16], 2)
            ei = bc(EII[:, k, dsl], [128, 16, 16], 2)
            E(TAh, er, BBR[:, d, gsl], ALU.mult, ["ET", "BBR"], ["TA"])
            E(TBh, ei, BBI[:, d, gsl], ALU.mult, ["ET", "BBI"], ["TB"])
            E(XR[:, :, s, :], TAh, TBh, ALU.subtract, ["TA", "TB"], ["XR"])
            E(TAh, er, BBI[:, d, gsl], ALU.mult, ["ET", "BBI"], ["TA"])
            E(TBh, ei, BBR[:, d, gsl], ALU.mult, ["ET", "BBR"], ["TB"])
            E(XI[:, :, s, :], TAh, TBh, ALU.add, ["TA", "TB"], ["XI"])
        for t in range(8):
            k = t + 1 if d == 0 else 8 - t
            er = bc(ER[:, k, dsl], [128, 16, 16], 2)
            ei = bc(EI[:, k, dsl], [128, 16, 16], 2)
            E(TAh, CRE[:, d, gsl], er, ALU.mult, ["ET", "C0"], ["TA"])
            E(TBh, CIM[:, d, gsl], ei, ALU.mult, ["ET", "C1"], ["TB"])
            E(WR[:, :, t, :], TAh, TBh, ALU.subtract, ["TA", "TB"], ["WR"])
            E(TAh, CRE[:, d, gsl], ei, ALU.mult, ["ET", "C0"], ["TA"])
            E(TBh, CIM[:, d, gsl], er, ALU.mult, ["ET", "C1"], ["TB"])
            STT("dve", WI[:, :, t, :], TAh, -1.0, TBh, ALU.mult, ALU.subtract, ["TA", "TB"], ["WI"])
        X3 = lambda tt_: tt_.rearrange("p g s c -> p g (s c)")
        e8r = bc(ER[:, 8, dsl], [128, 16, 128], 2)
        e8i = bc(EI[:, 8, dsl], [128, 16, 128], 2)
        E(X3(WTR), e8r, X3(XR), ALU.mult, ["ET", "XR"], ["WTR"])
        E(X3(TD), e8i, X3(XI), ALU.mult, ["ET", "XI"], ["TD"])
        E(X3(WTR), X3(WTR), X3(TD), ALU.subtract, ["WTR", "TD"], ["WTR"])
        E(X3(WTI), e8r, X3(XI), ALU.mult, ["ET", "XI"], ["WTI"])
        E(X3(TD), e8i, X3(XR), ALU.mult, ["ET", "XR"], ["TD"])
        E(X3(WTI), X3(WTI), X3(TD), ALU.add, ["WTI", "TD"], ["WTI"])
        for gpl in range(16):
            gp = gph * 16 + gpl
            sb_ = gp % 2
            for gpar in range(2):
                g = 2 * gp + gpar
                hs = slice(gpar * 64, (gpar + 1) * 64)
                pm = ps[gpar]
                MM(pm[:, 0:128], X3(XR)[hs, gpl, :], X3(WR)[hs, gpl, :], True, False, ["XR", "WR"], [f"ps{gpar}"])
                MM(pm[:, 0:128], X3(XI)[hs, gpl, :], X3(WI)[hs, gpl, :], False, True, ["XI", "WI"], [f"ps{gpar}"])
                if d == 0:
                    TT("dve", MACC[:, g, :], pm[:, 0:128], MKF, ALU.mult, [f"ps{gpar}", "MKF"], [f"MACC{g}"])
                else:
                    TT("dve", TMPM, pm[:, 0:128], MKB, ALU.mult, [f"ps{gpar}", "MKB"], ["TMPM"])
                    TT("dve", MACC[:, g, :], MACC[:, g, :], TMPM, ALU.add, [f"MACC{g}", "TMPM"], [f"MACC{g}"])
                for ri in range(2):
                    src = X3(WTR if ri == 0 else WTI)
                    pw = ps[2 + gpar * 2 + ri]
                    MM(pw[:, 0:128], src[:, gpl, :], IH[gpar], True, True, ["WTR", "WTI", f"IH{gpar}"],
                       [f"ps{2 + gpar * 2 + ri}"])
                    CP("act", WINs[sb_][:, gpar * 2 + ri, :], pw[:, 0:128], [f"ps{2 + gpar * 2 + ri}"],
                       [f"WINs{sb_}"])
                    CP("pool", WOUs[sb_][hs, gpar * 2 + ri, :], X3(WR if ri == 0 else WI)[hs, gpl, :],
                       ["WR", "WI"], [f"WOUs{sb_}"])
            base = (d * 128 + 4 * gp) * 128
            DMA("sp", WINC_d[:, base:base + 512], WINs[sb_].rearrange("p a b -> p (a b)"), f"sWIN{sb_}",
                [f"WINs{sb_}"], ["WINC_d"])
            DMA("sp", WOUT_d[:, base:base + 512], WOUs[sb_].rearrange("p a b -> p (a b)"), f"sWOU{sb_}",
                [f"WOUs{sb_}"], ["WOUT_d"])
    for g in range(64):
        STT("dve", MACC[:, g, :], IDF, DCOL[:, g:g + 1], MACC[:, g, :], ALU.mult, ALU.add,
            [f"MACC{g}", "IDF", "DCOL"], [f"MACC{g}"])
    CP("act", MB, MACC, [f"MACC{g}" for g in range(64)], ["MB"])
    DMA("sp", M_d, MB.rearrange("p a b -> p (a b)"), "sMB", ["MB"], ["M_d"])
    P.barrier()
    if upto < 4:
        P.emit(); es.close(); return nc

    A.reset()
    WINC = A.t([128, 256, 128], BF16)
    SEL = A.t([128, 64, 128], BF16)
    UTc = A.t([128, 8, 1024], BF16)
    UGall = A.t([128, 64, 128], BF16)
    INCc = [A.t([128, 128, 64], F32) for _ in range(2)]
    for i4 in range(4):
        DMA("sp", WINC[:, i4 * 64:(i4 + 1) * 64, :].rearrange("p a b -> p (a b)"),
            WINC_d[:, i4 * 64 * 128:(i4 + 1) * 64 * 128], f"lWINC{i4}", [], [f"WINC{i4}"])
    allwinc = [f"WINC{i4}" for i4 in range(4)]
    DMA("pool", SEL, sel_d, "lSEL", [], ["SEL"])
    UT_v = UT_d.rearrange("(b f) t -> f b t", f=128)
    for jc in range(NJC):
        t0 = jc * 1024
        DMA("sp", UTc, UT_v[:, :, t0:t0 + 1024], "lUTc", [], ["UTc"])
        for g4 in range(16):
            pb = g4 % 2
            for gi in range(4):
                g = g4 * 4 + gi
                blk, gl = g // 8, g % 8
                for s in range(8):
                    MM(ps[pb][:, gi * 128:(gi + 1) * 128], SEL[:, s * 8 + gl, :], UTc[:, blk, s:1024:8],
                       s == 0, s == 7, ["SEL", "UTc"], [f"ps{pb}"])
            CP("act" if pb == 0 else "dve", UGall[:, g4 * 4:(g4 + 1) * 4, :],
               ps[pb][:, :].rearrange("p (a b) -> p a b", b=128), [f"ps{pb}"], [f"UG{g4}"])
        allug = [f"UG{g4}" for g4 in range(16)]
        n = 0
        for d in range(2):
            for gp in range(32):
                pb = 2 + n % 4
                n += 1
                for ri in range(2):
                    for gpar in range(2):
                        g = 2 * gp + gpar
                        MM(ps[pb][:, ri * 128:(ri + 1) * 128], WINC[:, (d * 64 + g) * 2 + ri, :], UGall[:, g, :],
                           gpar == 0, gpar == 1, allwinc + allug, [f"ps{pb}"])
                dst = INCc[d].rearrange("p j (r g) -> p r j g", r=2)[:, :, :, gp]
                CP("act" if n % 2 == 0 else "dve", dst, ps[pb][:, 0:256].rearrange("p (r j) -> p r j", r=2),
                   [f"ps{pb}"], [f"INCc{d}"])
        for d in range(2):
            DMA("sp", INC_d[d][:, jc, :], INCc[d].rearrange("p a b -> p (a b)"), f"sINC{d}", [f"INCc{d}"], [f"INC_d{d}"])
    P.barrier()
    if upto < 5:
        P.emit(); es.close(); return nc

    A.reset()
    KTh = [A.t([128, L], BF16) for _ in range(1)]
    KRh = [A.t([128, L], BF16) for _ in range(1)]
    Vh = [A.t([128, NT, 128], BF16) for _ in range(1)]
    QTb = [A.t([128, 512], BF16) for _ in range(2)]
    QRb = [A.t([128, 512], BF16) for _ in range(2)]
    PT = [A.t([128, 512], BF16) for _ in range(4)]
    RSb = A.t([128, 512], F32)
    ATn = A.t([128, 512], F32)
    SQa = A.t([128, 512], BF16)
    ATo = [A.t([128, 512], BF16) for _ in range(2)]
    ONESB = A.t([128, 128], BF16)
    ONEC2 = A.t([128, 2], BF16)
    AOGc = A.t([128, 8], F32)
    HJ = 64
    INs = [A.t([128, HJ, 64], F32) for _ in range(2)]
    SO = [A.t([128, HJ + 1, 64], F32) for _ in range(2)]
    SB16 = [A.t([128, 64, 128], BF16) for _ in range(2)]
    PQ = [A.t([128, 2, 64], F32) for _ in range(2)]
    G2Bt = A.t([128, 2048], F32)
    WSTG = [A.t([128, 4, 1024], BF16) for _ in range(2)]
    DMA("sp", G2Bt, MOD_d[5 * D:6 * D].partition_broadcast(128), "lG2Bt", [], ["G2Bt"])
    MSET("pool", ONESB, 1.0, ["ONESB"])
    MSET("pool", ONEC2, 1.0, ["ONEC2"])
    MSET("pool", KRh[0][64:128, :], 0.0, ["KRh0"])
    for b_ in range(2):
        MSET("pool", QRb[b_][64:128, :], 0.0, [f"QRb{b_}"])
    DMA("sp", AOGc, att_out_g.rearrange("(k p) -> p k", p=128), "lAOGc", [], ["AOGc"], slow=True)

    def scan_gen(d, eng):
        if d == 0:
            MSET(eng, SO[0][:, 0, :], 0.0, ["SO0"])
        else:
            MSET(eng, SO[1][:, HJ, :], 0.0, ["SO1"])
        AA = A8T[:, d, 0, :]
        AIMS = A8T[:, d, 1, :]
        nh = 128 // HJ
        for step in range(NJC * nh):
            hc = step if d == 0 else NJC * nh - 1 - step
            jc, hh = hc // nh, hc % nh
            DMA("pool", INs[d].rearrange("p a b -> p (a b)"), INC_d[d][:, jc, hh * HJ * 64:(hh + 1) * HJ * 64],
                f"lIN{d}", [f"INC_d{d}"], [f"INs{d}"])
            for ii in range(HJ):
                i = ii if d == 0 else HJ - 1 - ii
                src = SO[d][:, i, :] if d == 0 else SO[d][:, i + 1, :]
                dst = SO[d][:, i + 1, :] if d == 0 else SO[d][:, i, :]
                swp = src.rearrange("p (r g) -> p r g", r=2)[:, ::-1, :]
                TT(eng, PQ[d][:, 0, :], AA, src, ALU.mult, [f"SO{d}", "A8T"], [f"P{d}"])
                TT(eng, PQ[d][:, 1, :].rearrange("p (r g) -> p r g", r=2), AIMS.rearrange("p (r g) -> p r g", r=2), swp,
                   ALU.mult, [f"SO{d}", "A8T"], [f"Q{d}"])
                TT(eng, PQ[d][:, 0, :], PQ[d][:, 0, :], PQ[d][:, 1, :], ALU.add, [f"P{d}", f"Q{d}"], [f"P{d}"])
                TT(eng, dst, PQ[d][:, 0, :], INs[d][:, i, :], ALU.add, [f"P{d}", f"INs{d}"], [f"SO{d}"])
                yield
            lo = 0 if d == 0 else 1
            CP(eng, SB16[d][:, :, hh * HJ:(hh + 1) * HJ], SO[d][:, lo:lo + HJ, :].rearrange("p j c -> p c j"),
               [f"SO{d}"], [f"SB16{d}"])
            last_half = (hh == nh - 1) if d == 0 else (hh == 0)
            if last_half:
                DMA("pool", S_d[d][:, jc], SB16[d], f"sS{d}", [f"SB16{d}"], [f"S_d{d}"])
            if d == 0:
                CP(eng, SO[d][:, 0, :], SO[d][:, HJ, :], [f"SO{d}"], [f"SO{d}"])
            else:
                CP(eng, SO[d][:, HJ, :], SO[d][:, 0, :], [f"SO{d}"], [f"SO{d}"])
        while True:
            yield

    def att_tail(h, qb, ib, po, pz):
        P.op("dve", (lambda pz_: lambda e: e.reciprocal(RSb, ps[pz_][:, :]))(pz), reads=[f"ps{pz}"], writes=["RSb"])
        TT("dve", ATn, ps[po][:, :], RSb, ALU.mult, [f"ps{po}", "RSb"], ["ATn"])
        TT("pool", SQa, ATn, ATn, ALU.mult, ["ATn"], ["SQa"])
        for qs in range(4):
            MM(ps[6][:, qs:qs + 1], SQa[:, qs * 128:(qs + 1) * 128], ONEC2[:, 0:1], True, True, ["SQa", "ONEC2"], ["ps6"])
        if h == 0:
            CP("dve", SSA[:, qb * 4:(qb + 1) * 4], ps[6][:, 0:4], ["ps6"], [f"SSA{qb}"])
        else:
            TT("dve", SSA[:, qb * 4:(qb + 1) * 4], SSA[:, qb * 4:(qb + 1) * 4], ps[6][:, 0:4], ALU.add,
               ["ps6", f"SSA{qb}"], [f"SSA{qb}"])
        TS("dve", ATo[ib], ATn, AOGc[:, h:h + 1], None, ALU.mult, None, ["ATn", "AOGc"], [f"ATo{ib}"])
        DMA("sp", ATTT_d[h * 128:(h + 1) * 128, qb * 512:(qb + 1) * 512], ATo[ib], f"sATo{ib}", [f"ATo{ib}"], [])

    def cast_gen():
        n = 0
        for wsrc, wdst in ((w1, W1s), (w3, W3s)):
            wv = wsrc.rearrange("(k p) n -> p k n", p=128)
            for pr in range(22):
                DMA("pool", wdst[pr].rearrange("p (k c) -> p k c", k=16), wv[:, :, pr * 256:(pr + 1) * 256], f"cast{n % 4}", [], ["Wsd"])
                n += 1
                yield
        w2v = w2.rearrange("(g j p) n -> g p j n", j=4, p=128)
        for hh in range(2):
            for g4 in range(11):
                b_ = n % 2
                n += 1
                DMA("pool", WSTG[b_], w2v[g4][:, :, hh * 1024:(hh + 1) * 1024], f"castl{b_}", [], [f"WSTG{b_}"])
                yield
                TT("dve", WSTG[b_], WSTG[b_], bc(G2Bt[:, hh * 1024:(hh + 1) * 1024], [128, 4, 1024], 1), ALU.mult,
                   [f"WSTG{b_}", "G2Bt"], [f"WSTG{b_}"])
                DMA("pool", W2s[hh, g4].rearrange("p (j c) -> p j c", j=4), WSTG[b_], f"casts{b_}", [f"WSTG{b_}"], ["Wsd"])
                yield
        cast_done.append(1)
        while True:
            yield

    cast_done = []
    castg = cast_gen()
    pending_tail = None
    scans = [scan_gen(0, "dve"), scan_gen(1, "pool")]
    steps_per_it = -(-(NJ) // (NH * (L // 512)))
    sc = 1.0 / float(np.sqrt(192.0))
    NQB = L // 512
    it = 0
    for h in range(NH):
        hb = 0
        DMA("sp", KTh[hb], KT_d[h], f"KTh{hb}", [], [f"KTh{hb}"])
        DMA("sp", KRh[hb][0:64, :], KR_d[h], f"KRh{hb}", [], [f"KRh{hb}"])
        DMA("sp", Vh[hb], V_d[h].rearrange("(n p) d -> p n d", p=128), f"Vh{hb}", [], [f"Vh{hb}"])
        for qb in range(NQB):
            ib = it % 2
            it += 1
            po = 2 + ib
            pz = 4 + ib
            DMA("sp", QTb[ib], QT_d[h][:, qb * 512:(qb + 1) * 512], f"QTb{ib}", [], [f"QTb{ib}"])
            DMA("sp", QRb[ib][0:64, :], QR_d[h][:, qb * 512:(qb + 1) * 512], f"QRb{ib}", [], [f"QRb{ib}"])

            SBK = [0, 1, 7]

            def S(kt):
                sb_ = SBK[kt % 3]
                MM(ps[sb_][:, :], KTh[hb][:, kt * 128:(kt + 1) * 128], QTb[ib][:, :], True, False,
                   [f"KTh{hb}", f"QTb{ib}"], [f"ps{sb_}"])
                MM(ps[sb_][:, :], KRh[hb][:, kt * 128:(kt + 1) * 128], QRb[ib][:, :], False, True,
                   [f"KRh{hb}", f"QRb{ib}"], [f"ps{sb_}"])
            S(0)
            if NT > 1:
                S(1)
            for kt in range(NT):
                sb_ = SBK[kt % 3]
                pb3 = kt % 4
                if kt + 2 < NT:
                    S(kt + 2)
                if kt == min(3, NT - 1) and pending_tail is not None:
                    att_tail(*pending_tail)
                    pending_tail = None
                ACT(PT[pb3], ps[sb_][:, :], AF.Exp, [f"ps{sb_}"], [f"PT{pb3}"], scale=sc)
                MM(ps[po][:, :], Vh[hb][:, kt, :], PT[pb3], kt == 0, kt == NT - 1, [f"PT{pb3}", f"Vh{hb}"], [f"ps{po}"])
                MM(ps[pz][:, :], ONESB, PT[pb3], kt == 0, kt == NT - 1, [f"PT{pb3}", "ONESB"], [f"ps{pz}"])
            pending_tail = (h, qb, ib, po, pz)
            for _ in range(steps_per_it):
                next(scans[0])
                next(scans[1])
            next(castg)
    att_tail(*pending_tail)
    while not cast_done:
        next(castg)
    for _ in range(4 * HJ):
        next(scans[0])
        next(scans[1])
    P.barrier()
    A.reset()
    JG = min(4, NJC)
    NW = 128 * JG
    NTK = 1024 * JG
    SELT = A.t([128, 64, 128], BF16)
    SEL = A.t([128, 64, 128], BF16)
    WOb = A.t([128, 2, 16, 128], BF16)
    Mb = A.t([128, 8, 128], BF16)
    UTb = [A.t([128, NTK], BF16) for _ in range(2)]
    Sb = [[A.t([128, 2, 4, NW], BF16) for _ in range(2)] for _ in range(2)]
    UG8 = A.t([128, 8, NW], BF16)
    YG = A.t([128, 8, NW], BF16)
    YTb = A.t([128, NTK], F32)
    GT1 = A.t([128, NTK], F32)
    GT2 = A.t([128, NTK], F32)
    GOUT = [A.t([128, NTK], BF16) for _ in range(2)]
    DMA("pool", SEL, sel_d, "lSEL", [], ["SEL"])
    DMA("pool", SELT, selt_d, "lSELT", [], ["SELT"])
    GC = float(2.0 * np.sqrt(2.0 / np.pi))

    def load53(it_):
        blk_, jg_ = it_ // (NJC // JG), it_ % (NJC // JG)
        ib_ = it_ % 2
        tt0 = jg_ * NTK
        DMA("sp", UTb[ib_], UT_d[blk_ * 128:(blk_ + 1) * 128, tt0:tt0 + NTK], f"lUTb{ib_}", [], [f"UTb{ib_}"])
        for d in range(2):
            for ri in range(2):
                for gpl in range(4):
                    DMA("sp", Sb[ib_][d][:, ri, gpl, :].rearrange("p (c j) -> p c j", c=JG),
                        S_d[d][:, jg_ * JG:(jg_ + 1) * JG, ri * 32 + blk_ * 4 + gpl, :],
                        f"lSb{ib_}{d}{ri}{gpl}", [], [f"Sb{ib_}{d}{ri}{gpl}"])
    it = 0
    ne = 0
    for blk in range(8):
        for d in range(2):
            base = ((d * 64 + blk * 8) * 2) * 128
            DMA("sp", WOb[:, d].rearrange("p a b -> p (a b)"), WOUT_d[:, base:base + 16 * 128], f"lWOb{d}", [], [f"WOb{d}"])
        DMA("sp", Mb.rearrange("p a b -> p (a b)"), M_d[:, blk * 8 * 128:(blk + 1) * 8 * 128], "lMb", [], ["Mb"])
        for jg in range(NJC // JG):
            ib = it % 2
            t0 = jg * NTK
            if it == 0:
                load53(0)
            if it + 1 < 8 * (NJC // JG):
                load53(it + 1)
            it += 1
            for gl in range(8):
                pb = gl % 4
                for s in range(8):
                    MM(ps[pb][:, 0:NW], SEL[:, s * 8 + gl, :], UTb[ib][:, s:NTK:8], s == 0, s == 7,
                       ["SEL", f"UTb{ib}"], [f"ps{pb}"])
                ne += 1
                CP("act" if ne % 2 == 0 else "dve", UG8[:, gl, :], ps[pb][:, 0:NW], [f"ps{pb}"], [f"UG8{gl}"])
            for gl in range(8):
                pb = 4 + gl % 4
                o = ps[pb][:, 0:NW]
                MM(o, Mb[:, gl, :], UG8[:, gl, :], True, False, ["Mb", f"UG8{gl}"], [f"ps{pb}"])
                for d in range(2):
                    for ri in range(2):
                        MM(o, WOb[:, d, gl * 2 + ri, :], Sb[ib][d][:, ri, gl // 2, :], False, (d == 1 and ri == 1),
                           [f"WOb{d}", f"Sb{ib}{d}{ri}{gl // 2}"], [f"ps{pb}"])
                ne += 1
                CP("act" if ne % 2 == 0 else "dve", YG[:, gl, :], o, [f"ps{pb}"], [f"YG{gl}"])
            allyg = [f"YG{gl}" for gl in range(8)]
            YTv = YTb.rearrange("p (j t) -> p t j", t=8)
            for t in range(8):
                pb = t % 4
                for gl in range(8):
                    MM(ps[pb][:, 0:NW], SELT[:, t * 8 + gl, :], YG[:, gl, :], gl == 0, gl == 7, ["SELT"] + allyg, [f"ps{pb}"])
                ne += 1
                CP("act" if ne % 2 == 0 else "dve", YTv[:, t, :], ps[pb][:, 0:NW], [f"ps{pb}"], [f"YTb{t}"])
            ally = [f"YTb{t}" for t in range(8)]
            TT("pool", GT1, YTb, YTb, ALU.mult, ally, ["GT1"])
            TS("pool", GT1, GT1, 0.044715, 1.0, ALU.mult, ALU.add, ["GT1"], ["GT1"])
            TT("pool", GT1, GT1, YTb, ALU.mult, ["GT1"] + ally, ["GT1"])
            ACT(GT2, GT1, AF.Sigmoid, ["GT1"], ["GT2"], scale=GC)
            TT("dve", GOUT[ib], GT2, YTb, ALU.mult, ["GT2"] + ally, [f"GOUT{ib}"])
            DMA("sp", YT_d[blk * 128:(blk + 1) * 128, t0:t0 + NTK], GOUT[ib], f"sGOUT{ib}", [f"GOUT{ib}"], ["YT_d"])
    P.barrier()
    if upto < 7:
        P.emit(); es.close(); return nc

    A.reset()
    WG = A.t([128, 8, 1024], BF16)
    BG = A.t([128, 8], F32)
    SOG = A.t([128, 8], F32)
    ONEC = A.t([128, 2], BF16)
    YTc = [A.t([128, 8, 512], BF16) for _ in range(2)]
    SG = A.t([128, 512], F32)
    SSMf = A.t([128, 512], F32)
    SQb = A.t([128, 8, 512], BF16)
    SSMo = [A.t([128, 8, 512], BF16) for _ in range(2)]
    for k in range(8):
        DMA("pool", WG[:, k, :], w_glu[k * 128:(k + 1) * 128, :], f"lWG{k}", [], [f"WG{k}"])
    allwg = [f"WG{k}" for k in range(8)]
    DMA("sp", BG, b_glu.rearrange("(k p) -> p k", p=128), "g0", [], ["BG"], slow=True)
    DMA("sp", SOG, ssm_out_g.rearrange("(k p) -> p k", p=128), "g1", [], ["SOG"], slow=True)
    MSET("pool", ONEC, 1.0, ["ONEC"])
    YT_v = YT_d.rearrange("(b f) t -> f b t", f=128)
    SSMT_v = SSMT_d.rearrange("(b f) t -> f b t", f=128)
    DMA("sp", YTc[0], YT_v[:, :, 0:512], "lYTc0", ["YT_d"], ["YTc0"])
    for tc in range(L // 512):
        ib = tc % 2
        t0 = tc * 512
        if tc + 1 < L // 512:
            DMA("sp", YTc[1 - ib], YT_v[:, :, t0 + 512:t0 + 1024], f"lYTc{1 - ib}", ["YT_d"], [f"YTc{1 - ib}"])
        for oc in range(8):
            pb = oc % 2
            for k in range(8):
                MM(ps[pb][:, :], WG[:, k, oc * 128:(oc + 1) * 128], YTc[ib][:, k, :], k == 0, k == 7,
                   allwg + [f"YTc{ib}"], [f"ps{pb}"])
            ACT(SG, ps[pb][:, :], AF.Sigmoid, [f"ps{pb}", "BG"], ["SG"], bias=BG[:, oc:oc + 1])
            TT("dve", SSMf, SG, YTc[ib][:, oc, :], ALU.mult, ["SG", f"YTc{ib}"], ["SSMf"])
            TT("pool", SQb[:, oc, :], SSMf, SSMf, ALU.mult, ["SSMf"], [f"SQb{oc}"])
            TS("dve", SSMo[ib][:, oc, :], SSMf, SOG[:, oc:oc + 1], None, ALU.mult, None, ["SSMf", "SOG"], [f"SSMo{ib}_{oc}"])
        for tt_ in range(4):
            for oc in range(8):
                MM(ps[2][:, tt_:tt_ + 1], SQb[:, oc, tt_ * 128:(tt_ + 1) * 128], ONEC[:, 0:1], oc == 0, oc == 7,
                   [f"SQb{o2}" for o2 in range(8)] + ["ONEC"], ["ps2"])
        CP("dve", SSS[:, tc * 4:(tc + 1) * 4], ps[2][:, 0:4], ["ps2"], ["SSS"])
        DMA("sp", SSMT_v[:, :, t0:t0 + 512], SSMo[ib], f"sSSMo{ib}", [f"SSMo{ib}_{oc}" for oc in range(8)], ["SSMT_d"])
    P.barrier()

    if upto < 8:
        P.emit(); es.close(); return nc

    X1_d = dscr("X1_d", [L, D], F32)
    A.reset()
    WO = A.t([128, 16, 2048], BF16)
    G1B = A.t([128, 2048], F32)
    AOG = A.t([128, 8], F32)
    XT2 = [A.t([128, 2048], F32) for _ in range(2)]
    MIXT = [A.t([128, 16, 128], BF16) for _ in range(2)]
    X1t = [A.t([128, 2048], F32) for _ in range(2)]
    TE1 = [A.t([128, 512], F32) for _ in range(2)]
    TE2 = [A.t([128, 512], F32) for _ in range(2)]
    ST4 = A.t([128, 8], F32)
    WST = [A.t([128, 16, 256], BF16) for _ in range(2)]
    DMA("sp", G1B, MOD_d[2 * D:3 * D].partition_broadcast(128), "lG1B", ["MOD_d"], ["G1B"])
    for k in range(16):
        DMA("pool", WO[:, k, :], w_o[k * 128:(k + 1) * 128, :], f"lWO{k}", [], [f"WO{k}"])
        TT("dve", WO[:, k, :], WO[:, k, :], G1B, ALU.mult, [f"WO{k}", "G1B"], [f"WO{k}"])
    SSMT_v2 = SSMT_d.rearrange("(b f) t -> f b t", f=128)
    ATTT_v2 = ATTT_d.rearrange("(b f) t -> f b t", f=128)
    ne = 0
    def load4a(t_):
        rr = t_ * 128
        tb_ = t_ % 2
        DMA("sp", XT2[tb_], x[rr:rr + 128, :], f"lXT2{tb_}", [], [f"XT2{tb_}"])
        DMA("sp", MIXT[tb_][:, 8:16, :], SSMT_v2[:, :, rr:rr + 128], f"lMIXs{tb_}", [], [f"MIXs{tb_}"])
        DMA("sp", MIXT[tb_][:, 0:8, :], ATTT_v2[:, :, rr:rr + 128], f"lMIXa{tb_}", [], [f"MIXa{tb_}"])
    load4a(0)
    for t in range(NT):
        r0 = t * 128
        tb = t % 2
        if t + 1 < NT:
            load4a(t + 1)
        RSTD(ST4[:, tb * 4 + 1:tb * 4 + 2], SSA[:, t:t + 1], 1024, f"ar{tb}", "SSA")
        RSTD(ST4[:, tb * 4 + 2:tb * 4 + 3], SSS[:, t:t + 1], 1024, f"sr{tb}", "SSS")
        for nb in range(4):
            pa = ps[2 * nb]
            pss = ps[2 * nb + 1]
            na, ns = f"ps{2 * nb}", f"ps{2 * nb + 1}"
            cs = slice(nb * 512, (nb + 1) * 512)
            eb = ne % 2
            ne += 1
            for k in range(8):
                MM(pa[:, :], MIXT[tb][:, k, :], WO[:, k, cs], k == 0, k == 7, [f"MIXa{tb}", f"WO{k}"], [na])
            for k in range(8, 16):
                MM(pss[:, :], MIXT[tb][:, k, :], WO[:, k, cs], k == 8, k == 15, [f"MIXs{tb}", f"WO{k}"], [ns])
            ACT(TE1[eb], pa[:, :], AF.Copy, [na, f"ar{tb}"], [f"TE1{eb}"], scale=ST4[:, tb * 4 + 1:tb * 4 + 2])
            STT("dve", TE2[eb], pss[:, :], ST4[:, tb * 4 + 2:tb * 4 + 3], TE1[eb], ALU.mult, ALU.add, [ns, f"sr{tb}", f"TE1{eb}"], [f"TE2{eb}"])
            TT("pool", X1t[tb][:, cs], TE2[eb], XT2[tb][:, cs], ALU.add, [f"TE2{eb}", f"XT2{tb}"], [f"X1t{tb}_{nb}"])
        DMA("pool", X1_d[r0:r0 + 128, :], X1t[tb], f"sX1{tb}", [f"X1t{tb}_{nb}" for nb in range(4)], ["X1_d"])
    P.barrier()
    if upto < 9:
        P.emit(); es.close(); return nc

    A.reset()
    X1s = [A.t([128, 4, 2048], F32) for _ in range(2)]
    XB2 = [A.t([128, 2048], BF16) for _ in range(2)]
    JNK2 = A.t([128, 2048], BF16)
    H2T = A.t([128, 16, 512], BF16)
    W1t = [A.t([128, 16, 256], BF16) for _ in range(2)]
    W3t = [A.t([128, 16, 256], BF16) for _ in range(2)]
    GT = A.t([128, 44, 512], BF16)
    W2t = [A.t([128, 4, 1024], BF16) for _ in range(2)]
    OUTs = [A.t([128, 512], F32) for _ in range(4)]
    SA = [A.t([128, 512], F32) for _ in range(2)]
    ST5 = A.t([128, 8], F32)
    nw = 0
    no = 0
    NST = L // 512
    DMA("sp", X1s[0], X1_d[0:512, :].rearrange("(a p) n -> p a n", p=128), "lX1s0", [], ["X1s0"])
    for st_ in range(NST):
        r0 = st_ * 512
        xb = st_ % 2
        if st_ + 1 < NST:
            DMA("sp", X1s[1 - xb], X1_d[r0 + 512:r0 + 1024, :].rearrange("(a p) n -> p a n", p=128), f"lX1s{1 - xb}", [],
                [f"X1s{1 - xb}"])
        MSET("pool", ST5, 0.0, [f"f_ss{a}" for a in range(4)] + [f"fr{a}" for a in range(4)])
        for a in range(4):
            ab = a % 2
            ACT(JNK2, X1s[xb][:, a, :], AF.Square, [f"X1s{xb}"], ["JNK2", f"f_ss{a}"], accum=ST5[:, 2 * a:2 * a + 1])
            RSTD(ST5[:, 2 * a + 1:2 * a + 2], ST5[:, 2 * a:2 * a + 1], D, f"fr{a}", f"f_ss{a}")
            ACT(XB2[ab], X1s[xb][:, a, :], AF.Copy, [f"X1s{xb}", f"fr{a}"], [f"XB2{ab}"], scale=ST5[:, 2 * a + 1:2 * a + 2])
            for k in range(16):
                pi = 4 * ab + k // 4
                MM(ps[pi][:, (k % 4) * 128:(k % 4 + 1) * 128], XB2[ab][:, k * 128:(k + 1) * 128], IDB, True, True,
                   [f"XB2{ab}", "IDB"], [f"ps{pi}"])
            for k in range(16):
                pi = 4 * ab + k // 4
                TS("dve", H2T[:, k, a * 128:(a + 1) * 128], ps[pi][:, (k % 4) * 128:(k % 4 + 1) * 128],
                   GS[:, 32 + k:33 + k], GS[:, 48 + k:49 + k], ALU.mult, ALU.add, [f"ps{pi}", "GS"], [f"H2T{k}"])
        for pr in range(22):
            wb = nw % 2
            nw += 1
            DMA("sp", W1t[wb].rearrange("p a b -> p (a b)"), W1s[pr], f"lW1t{wb}", [], [f"W1t{wb}"])
            DMA("sp", W3t[wb].rearrange("p a b -> p (a b)"), W3s[pr], f"lW3t{wb}", [], [f"W3t{wb}"])
            for c2 in range(2):
                ffc = pr * 2 + c2
                sb_ = ffc % 2
                pa, pb_ = ps[sb_ * 2], ps[sb_ * 2 + 1]
                na, nb_ = f"ps{sb_ * 2}", f"ps{sb_ * 2 + 1}"
                for k in range(16):
                    MM(pa[:, :], W1t[wb][:, k, c2 * 128:(c2 + 1) * 128], H2T[:, k, :], k == 0, k == 15,
                       [f"W1t{wb}", f"H2T{k}"], [na])
                for k in range(16):
                    MM(pb_[:, :], W3t[wb][:, k, c2 * 128:(c2 + 1) * 128], H2T[:, k, :], k == 0, k == 15,
                       [f"W3t{wb}", f"H2T{k}"], [nb_])
                ACT(SA[sb_], pa[:, :], AF.Silu, [na], [f"SA{sb_}"])
                TT("dve", GT[:, ffc, :], SA[sb_], pb_[:, :], ALU.mult, [f"SA{sb_}", nb_], [f"GT{ffc}"])
        for h in range(2):
            for g4 in range(11):
                wb = nw % 2
                nw += 1
                DMA("sp", W2t[wb].rearrange("p a b -> p (a b)"), W2s[h, g4], f"lW2t{wb}", [], [f"W2t{wb}"])
                for j in range(4):
                    ffc = g4 * 4 + j
                    for a in range(4):
                        for nb in range(2):
                            MM(ps[a * 2 + nb][:, :], GT[:, ffc, a * 128:(a + 1) * 128], W2t[wb][:, j, nb * 512:(nb + 1) * 512],
                               ffc == 0, ffc == 43, [f"GT{ffc}", f"W2t{wb}"], [f"ps{a * 2 + nb}"])
            for a in range(4):
                for nb in range(2):
                    ob = no % 4
                    no += 1
                    cs = slice(h * 1024 + nb * 512, h * 1024 + (nb + 1) * 512)
                    TT("dve", OUTs[ob], ps[a * 2 + nb][:, :], X1s[xb][:, a, cs], ALU.add, [f"ps{a * 2 + nb}", f"X1s{xb}"],
                       [f"OUTs{ob}"])
                    DMA("pool", y_out[r0 + a * 128:r0 + (a + 1) * 128, cs], OUTs[ob], f"sOUT{ob}", [f"OUTs{ob}"], [])
    P.barrier()

    P.emit()
    es.close()
    return nc


def rope_tables_host(L):
    pos = np.arange(L, dtype=np.float32)
    inv_freq = (np.float32(10000.0) ** (-np.arange(0, 64, 2, dtype=np.float32) / np.float32(64))).astype(np.float32)
    ang = (pos[:, None] * inv_freq[None, :]).astype(np.float32)
    return np.cos(ang).astype(np.float32), np.sin(ang).astype(np.float32)


_S5C = {}


def s5_constants():
    if _S5C:
        return _S5C
    sel = np.zeros((128, 64, 128), np.float32)
    selt = np.zeros((128, 64, 128), np.float32)
    for s in range(8):
        for gl in range(8):
            for c in range(16):
                sel[gl * 16 + c, s * 8 + gl, s * 16 + c] = 1.0
                selt[s * 16 + c, s * 8 + gl, gl * 16 + c] = 1.0
    si = np.arange(128)[:, None] // 16
    ti = np.arange(128)[None, :] // 16
    _S5C.update(sel=sel, selt=selt, maskf=(si <= ti).astype(np.float32), maskb=(si >= ti).astype(np.float32))
    return _S5C


def make_core_inputs(inp, seq_x, seq_c, L):
    cos, sin = rope_tables_host(L)
    m = {"x": np.ascontiguousarray(seq_x[:L]), "c": np.ascontiguousarray(seq_c),
         "ident": np.eye(128, dtype=np.float32), "ropec": cos, "ropes": sin}
    m.update(s5_constants())
    for k, v in inp.items():
        if k in ("x_prompt", "x_sample", "c_prompt", "c_sample"):
            continue
        m[k] = np.ascontiguousarray(v[0])
    return m


def kernel(**inputs):
    L = 8192
    nc = build(L)
    xs = [inputs["x_prompt"][i] for i in range(4)] + [inputs["x_sample"][0]]
    cs = [inputs["c_prompt"][i] for i in range(4)] + [inputs["c_sample"][0]]
    in_maps = []
    for core in range(8):
        i = core if core < 5 else core - 5
        in_maps.append(make_core_inputs(inputs, xs[i], cs[i], L))
    res = run_bass_kernel_spmd(nc, in_maps, core_ids=list(range(8)))
    ys = [np.asarray(res.results[i]["y"], dtype=np.float32) for i in range(5)]
    y_prompt = np.stack(ys[:4], axis=0)
    y_sample = ys[4][None]
    return (y_prompt, y_sample)
```

```python
import contextlib
import numpy as np
import concourse.bass as bass
import concourse.mybir as mybir
from concourse.bass_utils import run_bass_kernel_spmd

F32 = mybir.dt.float32
BF16 = mybir.dt.bfloat16
AF = mybir.ActivationFunctionType
ALU = mybir.AluOpType
AX = mybir.AxisListType

D = 2048
NH = 8
DFF = 5632
INC = 3136
EPS = 1e-6
MEMF = 52000


class Prog:
    def __init__(self, nc):
        self.nc = nc
        self.ops = {e: [] for e in ("pe", "act", "dve", "pool", "sp")}
        self.count = {}
        self.waited = {e: {} for e in self.ops}
        self.res = {}
        self.semkeys = []
        self.chslot = {}

    def _tok(self, semkey, inc):
        if semkey not in self.count:
            self.count[semkey] = 0
            self.semkeys.append(semkey)
        self.count[semkey] += inc
        return (semkey, self.count[semkey])

    def op(self, eng, fn, reads=(), writes=(), ch=None):
        deps = {}

        def add(toks, raw):
            for k, v in toks.items():
                if ch is None and k == eng and (eng == "pe" or not raw):
                    continue
                if deps.get(k, 0) < v:
                    deps[k] = v
        for r in reads:
            st = self.res.get(r)
            if st is not None:
                add(st[0], True)
        for w in writes:
            st = self.res.get(w)
            if st is not None:
                add(st[0], True)
                add(st[1], False)
        if ch is not None:
            if ch not in self.chslot:
                self.chslot[ch] = len(self.chslot)
            chkey = "dma:%d" % self.chslot[ch]
            if self.count.get(chkey, 0) > 0:
                deps[chkey] = self.count[chkey]
        waits = []
        wd = self.waited[eng]
        for k, v in deps.items():
            if wd.get(k, 0) < v:
                wd[k] = v
                waits.append((k, v))
        if ch is None:
            tok = self._tok(eng, 1)
            inc = 1
        else:
            tok = self._tok(chkey, 16)
            inc = 16
        self.ops[eng].append((waits, fn, tok[0], inc))
        for r in reads:
            st = self.res.setdefault(r, [{}, {}])
            if st[1].get(tok[0], 0) < tok[1]:
                st[1][tok[0]] = tok[1]
        for w in writes:
            self.res[w] = [{tok[0]: tok[1]}, {}]
        return tok

    def barrier(self):
        for eng in self.ops:
            waits = []
            wd = self.waited[eng]
            for k, v in self.count.items():
                if wd.get(k, 0) < v:
                    wd[k] = v
                    waits.append((k, v))
            if waits:
                self.ops[eng].append((waits, None, None, 0))
        self.res = {}
        self.chslot = {}

    def emit(self):
        nc = self.nc
        with contextlib.ExitStack() as es:
            sems = {}
            for k in self.semkeys:
                sems[k] = es.enter_context(nc.semaphore("s_" + k.replace(":", "_")))
            block = es.enter_context(nc.Block())

            def run(engname):
                def body(e):
                    for waits, fn, semkey, inc in self.ops[engname]:
                        for k, v in waits:
                            e.wait_ge(sems[k], v)
                        if fn is not None:
                            ins = fn(e)
                            ins.then_inc(sems[semkey], inc)
                return body
            block.tensor(run("pe"))
            block.scalar(run("act"))
            block.vector(run("dve"))
            block.gpsimd(run("pool"))
            block.sync(run("sp"))


def build(L, dbg=(), upto=99):
    NT = L // 128
    nc = bass.Bass("TRN2", target_bir_lowering=False)
    P = Prog(nc)

    def din(name, shape):
        return nc.dram_tensor(name, list(shape), F32, kind="ExternalInput").ap()

    def dscr(name, shape, dt):
        kind = "ExternalOutput" if name in dbg else "Internal"
        return nc.dram_tensor(name, list(shape), dt, kind=kind).ap()

    x = din("x", [L, D]); cvec = din("c", [D])
    w_ada = din("w_ada", [D, 6 * D]); b_ada = din("b_ada", [6 * D])
    norm_mix_g = din("norm_mix_g", [D]); w_in = din("w_in", [D, INC])
    kv_norm_g = din("kv_norm_g", [512]); w_ukv = din("w_ukv", [512, 2048])
    q_norm_g = din("q_norm_g", [192]); k_norm_g = din("k_norm_g", [192])
    lam_re = din("lam_re", [2, 64, 64]); lam_im = din("lam_im", [2, 64, 64]); log_dt = din("log_dt", [2, 64])
    b_re = din("b_re", [64, 64, 16]); b_im = din("b_im", [64, 64, 16])
    c_re = din("c_re", [2, 64, 16, 64]); c_im = din("c_im", [2, 64, 16, 64])
    d_skip = din("d_skip", [1024]); w_glu = din("w_glu", [1024, 1024]); b_glu = din("b_glu", [1024])
    att_out_g = din("att_out_g", [1024]); ssm_out_g = din("ssm_out_g", [1024])
    w_o = din("w_o", [D, D]); norm_ffn_g = din("norm_ffn_g", [D])
    w1 = din("w1", [D, DFF]); w3 = din("w3", [D, DFF]); w2 = din("w2", [DFF, D])
    ident_d = din("ident", [128, 128]); ropec = din("ropec", [L, 32]); ropes = din("ropes", [L, 32])
    maskf_d = din("maskf", [128, 128]); maskb_d = din("maskb", [128, 128])
    sel_d = din("sel", [128, 64, 128]); selt_d = din("selt", [128, 64, 128])
    y_out = nc.dram_tensor("y", [L, D], F32, kind="ExternalOutput").ap()

    MOD_d = dscr("MOD_d", [6 * D], F32)
    QT_d = dscr("QT_d", [NH, 128, L], BF16); QR_d = dscr("QR_d", [NH, 64, L], BF16)
    KT_d = dscr("KT_d", [NH, 128, L], BF16); KR_d = dscr("KR_d", [NH, 64, L], BF16)
    V_d = dscr("V_d", [NH, L, 128], BF16)
    UT_d = dscr("UT_d", [1024, L], BF16)
    W1s = dscr("W1s", [22, 128, 16 * 256], BF16)
    W3s = dscr("W3s", [22, 128, 16 * 256], BF16)
    W2s = dscr("W2s", [2, 11, 128, 4 * 1024], BF16)
    ATTT_d = dscr("ATTT_d", [1024, L], BF16)

    es = contextlib.ExitStack()
    mem = es.enter_context(nc.sbuf_tensor("mem", [128, MEMF], F32))
    ps = [es.enter_context(nc.psum_tensor(f"ps{i}", [128, 512], F32)) for i in range(8)]

    class Alloc:
        def __init__(self, base=0):
            self.off = base
            self.base = base

        def reset(self):
            self.off = self.base

        def t(self, shape, dt):
            n = int(np.prod(shape[1:]))
            nb = n * (2 if dt == BF16 else 4)
            nb4 = (nb + 3) // 4
            assert self.off + nb4 <= MEMF, ("SBUF arena overflow", self.off, nb4)
            v = mem[0:shape[0], self.off:self.off + nb4]
            if dt != F32:
                v = v.bitcast(dt)
                if nb4 * 2 != n:
                    v = v[:, 0:n]
            if len(shape) == 3:
                v = v.rearrange("p (a b) -> p a b", b=shape[2])
            elif len(shape) == 4:
                v = v.rearrange("p (a b c) -> p a b c", b=shape[2], c=shape[3])
            self.off += nb4
            return v

    def MM(out, lhsT, rhs, st, sp, r, w):
        P.op("pe", lambda e: e.matmul(out, lhsT, rhs, start=st, stop=sp), reads=r, writes=w)

    def ACT(out, in_, func, r, w, bias=None, scale=None, accum=None):
        kw = {}
        if bias is not None:
            kw["bias"] = bias
        if scale is not None:
            kw["scale"] = scale
        if accum is not None:
            kw["accum_out"] = accum
        P.op("act", lambda e: e.activation(out, in_, func, **kw), reads=r, writes=w)

    def TT(eng, out, a, b, op, r, w):
        P.op(eng, lambda e: e.tensor_tensor(out, a, b, op), reads=r, writes=w)

    def TS(eng, out, a, s1, s2, op0, op1, r, w):
        if s2 is None:
            P.op(eng, lambda e: e.tensor_scalar(out, a, s1, None, op0), reads=r, writes=w)
        else:
            P.op(eng, lambda e: e.tensor_scalar(out, a, s1, s2, op0, op1), reads=r, writes=w)

    def STT(eng, out, a, s, b, op0, op1, r, w):
        P.op(eng, lambda e: e.scalar_tensor_tensor(out, a, s, b, op0, op1), reads=r, writes=w)

    def CP(eng, out, in_, r, w):
        if eng == "act":
            P.op(eng, lambda e: e.copy(out, in_), reads=r, writes=w)
        else:
            P.op(eng, lambda e: e.tensor_copy(out, in_), reads=r, writes=w)

    def RSUM(eng, out, in_, r, w):
        P.op(eng, lambda e: e.reduce_sum(out, in_, AX.X), reads=r, writes=w)

    def MSET(eng, out, val, w):
        P.op(eng, lambda e: e.memset(out, val), writes=w)

    def DMA(q, out, in_, ch, r, w, slow=False):
        if slow:
            P.op(q, lambda e: e.dma_start(out=out, in_=in_, allow_slow_non_contiguous=True), reads=r, writes=w, ch=ch)
        else:
            P.op(q, lambda e: e.dma_start(out=out, in_=in_), reads=r, writes=w, ch=ch)

    def RSTD(dst, src, n, name, srcname):
        TS("dve", dst, src, 1.0 / n, EPS, ALU.mult, ALU.add, r=[srcname], w=[name])
        P.op("act", lambda e: e.sqrt(dst, dst), reads=[name], writes=[name])
        P.op("dve", lambda e: e.reciprocal(dst, dst), reads=[name], writes=[name])

    def bc(ap, shape, axis):
        return ap.unsqueeze(axis).to_broadcast(shape)

    PA = Alloc(0)
    IDF = PA.t([128, 128], F32)
    IDB = PA.t([128, 128], BF16)
    GS = PA.t([128, 64], F32)
    A8T = PA.t([128, 2, 2, 64], F32)
    SSS = PA.t([128, 64], F32)
    SSA = PA.t([128, 64], F32)
    pers_end = PA.off
    A = Alloc(pers_end)

    DMA("sp", IDF, ident_d, "c0", [], ["IDF"])
    DMA("pool", IDB, ident_d, "c1", [], ["IDB"])

    SC = A.t([128, 16], F32)
    SCB = A.t([128, 16, 128], F32)
    WA = [A.t([128, 16, 512], F32) for _ in range(2)]
    BAb = [A.t([128, 512], F32) for _ in range(2)]
    ONES = A.t([1, 128], F32)
    MODB = A.t([128, 6 * D], F32)
    TMP0 = A.t([128, 96, 128], F32)
    COLS = A.t([128, 96], F32)
    NG = A.t([128, 32], F32)

    DMA("sp", SC, cvec.rearrange("(k p) -> p k", p=128), "c2", [], ["SC"], slow=True)
    DMA("sp", NG[:, 0:16], norm_mix_g.rearrange("(k p) -> p k", p=128), "c3", [], ["NG0"], slow=True)
    DMA("sp", NG[:, 16:32], norm_ffn_g.rearrange("(k p) -> p k", p=128), "c4", [], ["NG1"], slow=True)
    ACT(SC, SC, AF.Silu, ["SC"], ["SC"])
    CP("dve", SCB, bc(SC, [128, 16, 128], 2), ["SC"], ["SCB"])
    MSET("pool", ONES, 1.0, ["ONES"])
    w_ada_v = w_ada.rearrange("(k p) n -> p k n", p=128)
    b_ada_v = b_ada.rearrange("(o n) -> o n", o=1)
    for nb in range(24):
        b = nb % 2
        DMA("sp", WA[b], w_ada_v[:, :, nb * 512:(nb + 1) * 512], f"WA{b}", [], [f"WA{b}"])
        DMA("sp", BAb[b], b_ada[nb * 512:(nb + 1) * 512].partition_broadcast(128), f"BA{b}", [], [f"BA{b}"])
        pt = ps[b]
        for k in range(16):
            MM(pt[:, :], SCB[:, k, :], WA[b][:, k, :], k == 0, k == 15, ["SCB", f"WA{b}"], [f"ps{b}"])
        TT("dve", MODB[:, nb * 512:(nb + 1) * 512], pt[:, :], BAb[b], ALU.add, [f"ps{b}", f"BA{b}"], [f"MODB{nb}"])
    allmod = [f"MODB{nb}" for nb in range(24)]
    DMA("sp", MOD_d.rearrange("(o n) -> o n", o=1), MODB[0:1, :], "c5", allmod, ["MOD_d"])
    MODB3 = MODB.rearrange("p (a b) -> p a b", b=128)
    TT("dve", TMP0, MODB3, bc(IDF, [128, 96, 128], 1), ALU.mult, allmod + ["IDF"], ["TMP0"])
    RSUM("dve", COLS, TMP0, ["TMP0"], ["COLS"])
    STT("dve", GS[:, 0:16], COLS[:, 16:32], 1.0, NG[:, 0:16], ALU.add, ALU.mult, ["COLS", "NG0"], ["GS"])
    CP("dve", GS[:, 16:32], COLS[:, 0:16], ["COLS"], ["GS"])
    STT("dve", GS[:, 32:48], COLS[:, 64:80], 1.0, NG[:, 16:32], ALU.add, ALU.mult, ["COLS", "NG1"], ["GS"])
    CP("dve", GS[:, 48:64], COLS[:, 48:64], ["COLS"], ["GS"])
    P.barrier()
    if upto < 1:
        P.emit(); es.close(); return nc

    A.reset()
    WIN = A.t([128, 16, INC], BF16)
    WUKV = A.t([128, 4, 2048], BF16)
    XTd = [A.t([128, D], F32) for _ in range(2)]
    XBd = [A.t([128, D], BF16) for _ in range(2)]
    STX = A.t([128, 4], F32)
    JNKX = A.t([128, D], BF16)
    HT = A.t([128, 16, 128], BF16)
    QF = A.t([128, 1536], F32)
    CKV = A.t([128, 576], F32)
    UF = A.t([128, 1024], BF16)
    KVF = A.t([128, 2048], F32)
    SCR = A.t([128, 2048], F32)
    QN = A.t([128, 8, 192], BF16)
    KN = A.t([128, 8, 192], BF16)
    QTs = A.t([128, 8, 128], BF16)
    QRs = A.t([64, 8, 128], BF16)
    KTs = A.t([128, 8, 128], BF16)
    KRs = A.t([64, 8, 128], BF16)
    UTs = A.t([128, 8, 128], BF16)
    VS = A.t([128, 8, 128], BF16)
    CKN = A.t([128, 512], BF16)
    CKT = A.t([128, 4, 128], BF16)
    RC = A.t([128, 32], F32)
    RS = A.t([128, 32], F32)
    GQ = A.t([128, 192], F32)
    GK = A.t([128, 192], F32)
    KVG = A.t([128, 4], F32)
    ST = A.t([128, 32], F32)
    RT = A.t([128, 8, 32], F32)
    RT2 = A.t([128, 8, 32], F32)
    RT3 = A.t([128, 8, 32], F32)
    RT4 = A.t([128, 8, 32], F32)
    KRG = A.t([128, 64], F32)
    KRR = A.t([128, 64], F32)

    for k in range(16):
        DMA("pool", WIN[:, k, :], w_in[k * 128:(k + 1) * 128, :], f"WIN{k}", [], [f"WIN{k}"])
    for k in range(4):
        DMA("pool", WUKV[:, k, :], w_ukv[k * 128:(k + 1) * 128, :], f"WUKV{k}", [], [f"WUKV{k}"])
    DMA("sp", GQ, q_norm_g.partition_broadcast(128), "c6", [], ["GQ"])
    DMA("sp", GK, k_norm_g.partition_broadcast(128), "c7", [], ["GK"])
    DMA("sp", KVG, kv_norm_g.rearrange("(k p) -> p k", p=128), "c8", [], ["KVG"], slow=True)

    blocks = [(0, 512), (512, 512), (1024, 512), (1536, 512), (2048, 64), (2112, 512), (2624, 512)]
    import os
    P1STOP = int(os.environ.get("P1STOP", "-1"))

    class _Stop(Exception):
        pass

    def CK(n):
        if P1STOP == n:
            raise _Stop()
    def front_load(t_):
        pb_ = t_ % 2
        rr = t_ * 128
        DMA("sp", XTd[pb_], x[rr:rr + 128, :], f"XT{pb_}", [], [f"XT{pb_}"])

    def front_a(t_):
        pb_ = t_ % 2
        MSET("pool", STX[:, pb_ * 2:pb_ * 2 + 2], 0.0, [f"x_ss{pb_}", f"xr{pb_}"])
        ACT(JNKX, XTd[pb_], AF.Square, [f"XT{pb_}"], ["JNKX", f"x_ss{pb_}"], accum=STX[:, pb_ * 2:pb_ * 2 + 1])
        RSTD(STX[:, pb_ * 2 + 1:pb_ * 2 + 2], STX[:, pb_ * 2:pb_ * 2 + 1], D, f"xr{pb_}", f"x_ss{pb_}")
        ACT(XBd[pb_], XTd[pb_], AF.Copy, [f"XT{pb_}", f"xr{pb_}"], [f"XB{pb_}"], scale=STX[:, pb_ * 2 + 1:pb_ * 2 + 2])

    def p1_head(t):
        r0 = t * 128
        XB = XBd[t % 2]
        DMA("sp", RC, ropec[r0:r0 + 128, :], "RC", [], ["RC"])
        DMA("sp", RS, ropes[r0:r0 + 128, :], "RS", [], ["RS"])
        MSET("pool", ST, 0.0, ["c_ss", "qr_ss", "k_ssn", "kr_ss", "kr_ss2", "cr", "qr", "kr"])


    def p1_xT(t):
        r0 = t * 128
        XB = XBd[t % 2]
        allht = [f"HT{k}" for k in range(16)]
        for k in range(16):
            MM(ps[k // 4][:, (k % 4) * 128:(k % 4 + 1) * 128], XB[:, k * 128:(k + 1) * 128], IDB, True, True,
               [f"XB{t % 2}", "IDB"], [f"ps{k // 4}"])
        for k in range(16):
            src = ps[k // 4][:, (k % 4) * 128:(k % 4 + 1) * 128]
            if False:
                ACT(HT[:, k, :], src, AF.Identity, [f"ps{k // 4}", "GS"], [f"HT{k}"],
                    bias=GS[:, 16 + k:17 + k], scale=GS[:, k:k + 1])
            else:
                TS("dve", HT[:, k, :], src, GS[:, k:k + 1], GS[:, 16 + k:17 + k], ALU.mult, ALU.add,
                   [f"ps{k // 4}", "GS"], [f"HT{k}"])

    def p1_proj(t):
        allht = [f"HT{k}" for k in range(16)]
        for bi, (c0, w) in enumerate(blocks):
            pi = 4 + bi % 4
            pt = ps[pi]
            for k in range(16):
                MM(pt[:, 0:w], HT[:, k, :], WIN[:, k, c0:c0 + w], k == 0, k == 15, allht + [f"WIN{k}"], [f"ps{pi}"])
            if bi < 3:
                CP("act", QF[:, c0:c0 + w], pt[:, 0:w], [f"ps{pi}"], [f"QF{bi}"])
            elif bi == 3:
                CP("dve", CKV[:, 0:512], pt[:, 0:w], [f"ps{pi}"], ["CKVa"])
            elif bi == 4:
                CP("dve", CKV[:, 512:576], pt[:, 0:w], [f"ps{pi}"], ["CKVb"])
            else:
                CP("act", UF[:, c0 - 2112:c0 - 2112 + w], pt[:, 0:w], [f"ps{pi}"], [f"UF{bi}"])


    def p1_ckv(t):
        r0 = t * 128
        allkv = [f"KVF{nb}" for nb in range(4)]
        ACT(JNKX[:, 0:512], CKV[:, 0:512], AF.Square, ["CKVa"], ["JNKX", "c_ss"], accum=ST[:, 2:3])
        RSTD(ST[:, 3:4], ST[:, 2:3], 512, "cr", "c_ss")
        ACT(CKN, CKV[:, 0:512], AF.Copy, ["CKVa", "cr"], ["CKN"], scale=ST[:, 3:4])
        for k in range(4):
            MM(ps[0][:, k * 128:(k + 1) * 128], CKN[:, k * 128:(k + 1) * 128], IDB, True, True, ["CKN", "IDB"], ["ps0"])
        for k in range(4):
            TS("dve", CKT[:, k, :], ps[0][:, k * 128:(k + 1) * 128], KVG[:, k:k + 1], None, ALU.mult, None,
               ["ps0", "KVG"], ["CKT"])
        for nb in range(4):
            pi = 4 + nb
            for k in range(4):
                MM(ps[pi][:, :], CKT[:, k, :], WUKV[:, k, nb * 512:(nb + 1) * 512], k == 0, k == 3,
                   ["CKT", f"WUKV{k}"], [f"ps{pi}"])
            CP("act", KVF[:, nb * 512:(nb + 1) * 512], ps[pi][:, :], [f"ps{pi}"], [f"KVF{nb}"])
        allkv = [f"KVF{nb}" for nb in range(4)]


    def p1_qchain(t):
        r0 = t * 128
        allq = ["QF0", "QF1", "QF2"]
        allq = ["QF0", "QF1", "QF2"]
        QF3 = QF.rearrange("p (h d) -> p h d", d=192)
        SCR3 = SCR[:, 0:1536].rearrange("p (h d) -> p h d", d=192)
        for h in range(8):
            ACT(JNKX[:, 0:192], QF3[:, h, :], AF.Square, allq, ["JNKX", "qr_ss"], accum=ST[:, 8 + h:9 + h])
        RSTD(ST[:, 8:16], ST[:, 8:16], 192, "qr", "qr_ss")
        for h in range(8):
            STT("dve", QF3[:, h, :], QF3[:, h, :], ST[:, 8 + h:9 + h], GQ, ALU.mult, ALU.mult, allq + ["qr", "GQ"], ["QFn"])
        cosb = bc(RC, [128, 8, 32], 1)
        sinb = bc(RS, [128, 8, 32], 1)
        TT("pool", RT, QF3[:, :, 128:160], cosb, ALU.mult, ["QFn", "RC"], ["RT"])
        TT("pool", RT2, QF3[:, :, 160:192], sinb, ALU.mult, ["QFn", "RS"], ["RT2"])
        TT("pool", RT3, QF3[:, :, 160:192], cosb, ALU.mult, ["QFn", "RC"], ["RT3"])
        TT("pool", RT4, QF3[:, :, 128:160], sinb, ALU.mult, ["QFn", "RS"], ["RT4"])
        TT("dve", QN[:, :, 128:160], RT, RT2, ALU.subtract, ["RT", "RT2"], ["QNa"])
        TT("dve", QN[:, :, 160:192], RT3, RT4, ALU.add, ["RT3", "RT4"], ["QNb"])
        CP("act", QN[:, :, 0:128], QF3[:, :, 0:128], ["QFn"], ["QNc"])
        allqn = ["QNa", "QNb", "QNc"]


    def p1_qT(t):
        r0 = t * 128
        allqn = ["QNa", "QNb", "QNc"]
        for h in range(8):
            MM(ps[h // 4][:, (h % 4) * 128:(h % 4 + 1) * 128], QN[:, h, 0:128], IDB, True, True,
               allqn + ["IDB"], [f"ps{h // 4}"])
            MM(ps[2 + h // 4][0:64, (h % 4) * 128:(h % 4 + 1) * 128], QN[:, h, 128:192], IDB, True, True,
               allqn + ["IDB"], [f"ps{2 + h // 4}"])
        for hb in range(2):
            CP("dve", QTs[:, hb * 4:(hb + 1) * 4, :], ps[hb][:, :].rearrange("p (a b) -> p a b", b=128),
               [f"ps{hb}"], [f"QTs{hb}"])
            CP("act", QRs[:, hb * 4:(hb + 1) * 4, :], ps[2 + hb][0:64, :].rearrange("p (a b) -> p a b", b=128),
               [f"ps{2 + hb}"], [f"QRs{hb}"])
        DMA("sp", QT_d[:, :, r0:r0 + 128].rearrange("h d t -> d h t"), QTs, "sQT", ["QTs0", "QTs1"], [])
        DMA("sp", QR_d[:, :, r0:r0 + 128].rearrange("h d t -> d h t"), QRs, "sQR", ["QRs0", "QRs1"], [])


    def p1_kchain(t):
        r0 = t * 128
        allkv = [f"KVF{nb}" for nb in range(4)]
        allq = ["QF0", "QF1", "QF2"]
        KV3 = KVF.rearrange("p (h d) -> p h d", d=256)
        SCRk = SCR[:, 0:1024].rearrange("p (h d) -> p h d", d=128)
        for h in range(8):
            ACT(JNKX[:, 0:128], KV3[:, h, 0:128], AF.Square, allkv, ["JNKX", "k_ssn"], accum=ST[:, 16 + h:17 + h])
        ACT(JNKX[:, 0:64], CKV[:, 512:576], AF.Square, ["CKVb"], ["JNKX", "kr_ss"], accum=ST[:, 4:5])
        TS("dve", ST[:, 16:24], ST[:, 16:24], ST[:, 4:5], None, ALU.add, None, ["k_ssn", "kr_ss"], ["kr_ss2"])
        TS("dve", ST[:, 16:24], ST[:, 16:24], 1.0 / 192, EPS, ALU.mult, ALU.add, ["kr_ss2"], ["kr"])
        P.op("act", lambda e: e.sqrt(ST[:, 16:24], ST[:, 16:24]), reads=["kr"], writes=["kr"])
        P.op("dve", lambda e: e.reciprocal(ST[:, 16:24], ST[:, 16:24]), reads=["kr"], writes=["kr"])
        for h in range(8):
            STT("dve", KN[:, h, 0:128], KV3[:, h, 0:128], ST[:, 16 + h:17 + h], GK[:, 0:128], ALU.mult, ALU.mult,
                allkv + ["kr", "GK"], ["KNc"])
        TT("pool", KRG, CKV[:, 512:576], GK[:, 128:192], ALU.mult, ["CKVb", "GK", "kr_ss"], ["KRG"])
        TT("pool", RT[:, 0, :], KRG[:, 0:32], RC, ALU.mult, ["KRG", "RC", "QNa"], ["RT"])
        TT("pool", RT2[:, 0, :], KRG[:, 32:64], RS, ALU.mult, ["KRG", "RS", "QNa"], ["RT2"])
        TT("pool", RT3[:, 0, :], KRG[:, 32:64], RC, ALU.mult, ["KRG", "RC", "QNb"], ["RT3"])
        TT("pool", RT4[:, 0, :], KRG[:, 0:32], RS, ALU.mult, ["KRG", "RS", "QNb"], ["RT4"])
        TT("dve", KRR[:, 0:32], RT[:, 0, :], RT2[:, 0, :], ALU.subtract, ["RT", "RT2"], ["KRRa"])
        TT("dve", KRR[:, 32:64], RT3[:, 0, :], RT4[:, 0, :], ALU.add, ["RT3", "RT4"], ["KRRb"])
        TT("pool", KN[:, :, 128:192], bc(KRR, [128, 8, 64], 1), bc(ST[:, 16:24], [128, 8, 64], 2), ALU.mult,
           ["KRRa", "KRRb", "kr"], ["KNr"])
        CP("dve", VS, KV3[:, :, 128:256], allkv, ["VS"])
        DMA("sp", V_d[:, r0:r0 + 128, :].rearrange("h t d -> t h d"), VS, "sV", ["VS"], [])


    def p1_kT(t):
        r0 = t * 128
        allkn = ["KNc", "KNr"]
        for h in range(8):
            MM(ps[h // 4][:, (h % 4) * 128:(h % 4 + 1) * 128], KN[:, h, 0:128], IDB, True, True,
               allkn + ["IDB"], [f"ps{h // 4}"])
            MM(ps[2 + h // 4][0:64, (h % 4) * 128:(h % 4 + 1) * 128], KN[:, h, 128:192], IDB, True, True,
               allkn + ["IDB"], [f"ps{2 + h // 4}"])
        for hb in range(2):
            CP("dve", KTs[:, hb * 4:(hb + 1) * 4, :], ps[hb][:, :].rearrange("p (a b) -> p a b", b=128),
               [f"ps{hb}"], [f"KTs{hb}"])
            CP("act", KRs[:, hb * 4:(hb + 1) * 4, :], ps[2 + hb][0:64, :].rearrange("p (a b) -> p a b", b=128),
               [f"ps{2 + hb}"], [f"KRs{hb}"])
        DMA("sp", KT_d[:, :, r0:r0 + 128].rearrange("h d t -> d h t"), KTs, "sKT", ["KTs0", "KTs1"], [])
        DMA("sp", KR_d[:, :, r0:r0 + 128].rearrange("h d t -> d h t"), KRs, "sKR", ["KRs0", "KRs1"], [])


    def p1_u(t):
        r0 = t * 128
        for k in range(8):
            MM(ps[4 + k // 4][:, (k % 4) * 128:(k % 4 + 1) * 128], UF[:, k * 128:(k + 1) * 128], IDB, True, True,
               ["UF5", "UF6", "IDB"], [f"ps{4 + k // 4}"])
        for hb in range(2):
            CP("act" if hb == 0 else "dve", UTs[:, hb * 4:(hb + 1) * 4, :],
               ps[4 + hb][:, :].rearrange("p (a b) -> p a b", b=128), [f"ps{4 + hb}"], [f"UTs{hb}"])
        DMA("sp", UT_d.rearrange("(b f) t -> f b t", f=128)[:, :, r0:r0 + 128], UTs, "sUT", ["UTs0", "UTs1"], [])

    front_load(0)
    front_a(0)
    p1_xT(0)
    for t in range(NT):
        p1_head(t)
        if t + 1 < NT:
            front_load(t + 1)
        p1_proj(t)
        if t > 0:
            p1_qT(t - 1)
            p1_kT(t - 1)
        p1_qchain(t)
        if t + 1 < NT:
            front_a(t + 1)
            p1_xT(t + 1)
        p1_ckv(t)
        p1_u(t)
        p1_kchain(t)
    p1_qT(NT - 1)
    p1_kT(NT - 1)
    P.barrier()
    if upto < 2:
        P.emit(); es.close(); return nc

    if upto < 3:
        P.emit(); es.close(); return nc

    NJ = L // 8
    NJC = L // 1024
    WINC_d = dscr("WINC_d", [128, 256 * 128], BF16)
    WOUT_d = dscr("WOUT_d", [128, 256 * 128], BF16)
    M_d = dscr("M_d", [128, 64 * 128], BF16)
    INC_d = [dscr(f"INC{d}_d", [128, NJC, 128 * 64], F32) for d in range(2)]
    S_d = [dscr(f"S{d}_d", [128, NJC, 64, 128], BF16) for d in range(2)]
    YT_d = dscr("YT_d", [1024, L], BF16)
    SSMT_d = dscr("SSMT_d", [1024, L], BF16)

    A.reset()
    LR = A.t([128, 64], F32); LI = A.t([128, 64], F32); DT = A.t([128, 64], F32)
    XX = A.t([128, 64], F32); TH = A.t([128, 64], F32)
    CC = A.t([128, 64], F32); SS = A.t([128, 64], F32)
    T1 = A.t([128, 64], F32); T2 = A.t([128, 64], F32)
    HPI = A.t([128, 1], F32)
    UR = A.t([128, 9, 64], F32); UI = A.t([128, 9, 64], F32)
    ER = A.t([128, 9, 64], F32); EI = A.t([128, 9, 64], F32)
    EIR = A.t([128, 9, 64], F32); EII = A.t([128, 9, 64], F32)
    MG = A.t([128, 64], F32); IMG = A.t([128, 64], F32)
    QRE = A.t([128, 64], F32); QIM = A.t([128, 64], F32)
    BRE = A.t([128, 32, 16], F32); BIM = A.t([128, 32, 16], F32)
    BBR = A.t([128, 2, 32, 16], F32); BBI = A.t([128, 2, 32, 16], F32)
    CRE = A.t([128, 2, 32, 16], F32); CIM = A.t([128, 2, 32, 16], F32)
    CN2 = [A.t([128, 128], F32) for _ in range(2)]
    TA = A.t([128, 32, 16], F32); TB = A.t([128, 32, 16], F32)
    TA2 = A.t([128, 16, 16], F32); TB2 = A.t([128, 16, 16], F32)
    XR = A.t([128, 16, 8, 16], F32); XI = A.t([128, 16, 8, 16], F32)
    WR = A.t([128, 16, 8, 16], F32); WI = A.t([128, 16, 8, 16], F32)
    WTR = A.t([128, 16, 8, 16], F32); WTI = A.t([128, 16, 8, 16], F32)
    TD = A.t([128, 16, 8, 16], F32)
    MACC = A.t([128, 64, 128], F32)
    MB = A.t([128, 64, 128], BF16)
    MKF = A.t([128, 128], F32); MKB = A.t([128, 128], F32)
    IH = [A.t([128, 128], F32) for _ in range(2)]
    DCOL = A.t([128, 64], F32)
    WINs = [A.t([128, 4, 128], BF16) for _ in range(2)]
    WOUs = [A.t([128, 4, 128], BF16) for _ in range(2)]
    TMPM = A.t([128, 128], F32)

    def rawap(ap, off, dims):
        return bass.AP(ap.tensor, off, dims)
    for q4 in range(4):
        DMA("sp", LR[:, q4 * 16:(q4 + 1) * 16], rawap(lam_re, (q4 // 2) * 4096 + (q4 % 2) * 16 * 128, [[1, 128], [128, 16]]),
            f"g{q4}", [], ["LR"], slow=True)
        DMA("sp", LI[:, q4 * 16:(q4 + 1) * 16], rawap(lam_im, (q4 // 2) * 4096 + (q4 % 2) * 16 * 128, [[1, 128], [128, 16]]),
            f"g{4 + q4}", [], ["LI"], slow=True)
    for gpar in range(2):
        DMA("sp", DT[gpar * 64:(gpar + 1) * 64, :].rearrange("p (d g) -> p d g", d=2),
            rawap(log_dt, gpar, [[0, 64], [64, 2], [2, 32]]), f"g{8 + gpar}", [], ["DT"], slow=True)
    for q4 in range(4):
        DMA("sp", BRE[:, q4 * 8:(q4 + 1) * 8, :], rawap(b_re, q4 * 8 * 2048, [[16, 128], [2048, 8], [1, 16]]),
            f"g{10 + q4}", [], ["BRE"], slow=True)
        DMA("sp", BIM[:, q4 * 8:(q4 + 1) * 8, :], rawap(b_im, q4 * 8 * 2048, [[16, 128], [2048, 8], [1, 16]]),
            f"g{14 + q4}", [], ["BIM"], slow=True)
    for s in range(8):
        DMA("sp", DCOL[s * 16:(s + 1) * 16, :], rawap(d_skip, 0, [[1, 16], [16, 64]]), f"g{18 + s}", [], ["DCOL"], slow=True)
    DMA("sp", MKF, maskf_d, "g26", [], ["MKF"])
    DMA("sp", MKB, maskb_d, "g27", [], ["MKB"])
    MSET("pool", HPI, float(np.pi / 2), ["HPI"])
    MSET("pool", WOUs[0], 0.0, ["WOUs0"])
    MSET("pool", WOUs[1], 0.0, ["WOUs1"])
    for hh in range(2):
        CP("pool", IH[hh], IDF, ["IDF"], [f"IH{hh}"])
        MSET("pool", IH[hh][(1 - hh) * 64:(2 - hh) * 64, :], 0.0, [f"IH{hh}"])
    it = 0
    for ri, csrc, cdst in ((0, c_re, CRE), (1, c_im, CIM)):
        for d in range(2):
            cv = csrc[d].rearrange("g c p -> (g c) p")
            for gb in range(8):
                b = it % 2
                it += 1
                DMA("sp", CN2[b][:, 0:64], cv[gb * 128:(gb + 1) * 128, :], f"CNa{b}", [], [f"CN2a{b}"])
                DMA("sp", CN2[b][:, 64:128], cv[gb * 128:(gb + 1) * 128, :], f"CNb{b}", [], [f"CN2b{b}"])
                MM(ps[b][:, 0:128], CN2[b], IDF, True, True, [f"CN2a{b}", f"CN2b{b}", "IDF"], [f"ps{b}"])
                pv = ps[b][:, 0:128].rearrange("p (g c) -> p g c", c=16)
                CP("dve", cdst[0:64, d, gb * 4:(gb + 1) * 4, :], pv[0:64, 0:8:2, :], [f"ps{b}"], [f"C{ri}"])
                CP("act", cdst[64:128, d, gb * 4:(gb + 1) * 4, :], pv[64:128, 1:8:2, :], [f"ps{b}"], [f"C{ri}"])

    def E(out, a, b_, op, r, w, eng="dve"):
        TT(eng, out, a, b_, op, r, w)
    ACT(DT, DT, AF.Exp, ["DT"], ["DT"])
    E(XX, LR, DT, ALU.mult, ["LR", "DT"], ["XX"])
    E(TH, LI, DT, ALU.mult, ["LI", "DT"], ["TH"])
    ACT(SS, TH, AF.Sin, ["TH"], ["SS"], scale=1.0 / 16)
    ACT(CC, TH, AF.Sin, ["TH", "HPI"], ["CC"], scale=1.0 / 16, bias=HPI[:, 0:1])
    for i in range(4):
        E(T1, CC, CC, ALU.mult, ["CC"], ["T1"])
        E(T2, SS, SS, ALU.mult, ["SS"], ["T2"])
        STT("dve", SS, CC, 2.0, SS, ALU.mult, ALU.mult, ["CC", "SS"], ["SS"])
        E(CC, T1, T2, ALU.subtract, ["T1", "T2"], ["CC"])
    CP("dve", UR[:, 1, :], CC, ["CC"], ["U"])
    CP("dve", UI[:, 1, :], SS, ["SS"], ["U"])
    for k in range(2, 9):
        E(T1, UR[:, k - 1, :], CC, ALU.mult, ["U", "CC"], ["T1"])
        E(T2, UI[:, k - 1, :], SS, ALU.mult, ["U", "SS"], ["T2"])
        E(UR[:, k, :], T1, T2, ALU.subtract, ["T1", "T2"], ["U"])
        E(T1, UR[:, k - 1, :], SS, ALU.mult, ["U", "SS"], ["T1"])
        E(T2, UI[:, k - 1, :], CC, ALU.mult, ["U", "CC"], ["T2"])
        E(UI[:, k, :], T1, T2, ALU.add, ["T1", "T2"], ["U"])
    for k in range(1, 9):
        ACT(MG, XX, AF.Exp, ["XX"], ["MG"], scale=float(k))
        ACT(IMG, XX, AF.Exp, ["XX"], ["IMG"], scale=float(-k))
        E(ER[:, k, :], MG, UR[:, k, :], ALU.mult, ["MG", "U"], ["ET"])
        E(EI[:, k, :], MG, UI[:, k, :], ALU.mult, ["MG", "U"], ["ET"])
        E(EIR[:, k, :], IMG, UR[:, k, :], ALU.mult, ["IMG", "U"], ["ET"])
        STT("dve", EII[:, k, :], UI[:, k, :], -1.0, IMG, ALU.mult, ALU.mult, ["IMG", "U"], ["ET"])
    for d in range(2):
        dsl = slice(d * 32, (d + 1) * 32)
        CP("dve", A8T[:, d, 0, 0:32], ER[:, 8, dsl], ["ET"], ["A8T"])
        CP("dve", A8T[:, d, 0, 32:64], ER[:, 8, dsl], ["ET"], ["A8T"])
        TS("dve", A8T[:, d, 1, 0:32], EI[:, 8, dsl], -1.0, None, ALU.mult, None, ["ET"], ["A8T"])
        CP("dve", A8T[:, d, 1, 32:64], EI[:, 8, dsl], ["ET"], ["A8T"])
    TS("dve", T1, ER[:, 1, :], -1.0, None, ALU.add, None, ["ET"], ["T1"])
    E(QRE, T1, LR, ALU.mult, ["T1", "LR"], ["QRE"])
    E(T2, EI[:, 1, :], LI, ALU.mult, ["ET", "LI"], ["T2"])
    E(QRE, QRE, T2, ALU.add, ["QRE", "T2"], ["QRE"])
    E(QIM, EI[:, 1, :], LR, ALU.mult, ["ET", "LR"], ["QIM"])
    E(T2, T1, LI, ALU.mult, ["T1", "LI"], ["T2"])
    E(QIM, QIM, T2, ALU.subtract, ["QIM", "T2"], ["QIM"])
    E(T1, LR, LR, ALU.mult, ["LR"], ["T1"])
    E(T2, LI, LI, ALU.mult, ["LI"], ["T2"])
    E(T1, T1, T2, ALU.add, ["T1", "T2"], ["T1"])
    P.op("dve", lambda e: e.reciprocal(T1, T1), reads=["T1"], writes=["T1"])
    E(QRE, QRE, T1, ALU.mult, ["QRE", "T1"], ["QRE"])
    E(QIM, QIM, T1, ALU.mult, ["QIM", "T1"], ["QIM"])
    for d in range(2):
        dsl = slice(d * 32, (d + 1) * 32)
        qr = bc(QRE[:, dsl], [128, 32, 16], 2)
        qi = bc(QIM[:, dsl], [128, 32, 16], 2)
        E(TA, qr, BRE, ALU.mult, ["QRE", "BRE"], ["TA"])
        E(TB, qi, BIM, ALU.mult, ["QIM", "BIM"], ["TB"])
        E(BBR[:, d], TA, TB, ALU.subtract, ["TA", "TB"], ["BBR"])
        E(TA, qr, BIM, ALU.mult, ["QRE", "BIM"], ["TA"])
        E(TB, qi, BRE, ALU.mult, ["QIM", "BRE"], ["TB"])
        E(BBI[:, d], TA, TB, ALU.add, ["TA", "TB"], ["BBI"])

    for d in range(2):
      for gph in range(2):
        dsl = slice(d * 32 + gph * 16, d * 32 + (gph + 1) * 16)
        gsl = slice(gph * 16, (gph + 1) * 16)
        TAh = TA[:, 0:16, :]
        TBh = TB[:, 0:16, :]
        for s in range(8):
            k = s + 1 if d == 0 else 8 - s
            er = bc(EIR[:, k, dsl], [128, 16, 16], 2)
            ei = bc(EII[:, k, dsl], [128, 16, 16], 2)
            E(TAh, er, BBR[:, d, gsl], ALU.mult, ["ET", "BBR"], ["TA"])
            E(TBh, ei, BBI[:, d, gsl], ALU.mult, ["ET", "BBI"], ["TB"])
            E(XR[:, :, s, :], TAh, TBh, ALU.subtract, ["TA", "TB"], ["XR"])
            E(TAh, er, BBI[:, d, gsl], ALU.mult, ["ET", "BBI"], ["TA"])
            E(TBh, ei, BBR[:, d, gsl], ALU.mult, ["ET", "BBR"], ["TB"])
            E(XI[:, :, s, :], TAh, TBh, ALU.add, ["TA", "TB"], ["XI"])
        for t in range(8):
            k = t + 1 if d == 0 else 8 - t
            er = bc(ER[:, k, dsl], [128, 16, 16], 2)
            ei = bc(EI[:, k, dsl], [128, 16, 16], 2)
            E(TAh, CRE[:, d, gsl], er, ALU.mult, ["ET", "C0"], ["TA"])
            E(TBh, CIM[:, d, gsl], ei, ALU.mult, ["ET", "C1"], ["TB"])
            E(WR[:, :, t, :], TAh, TBh, ALU.subtract, ["TA", "TB"], ["WR"])
            E(TAh, CRE[:, d, gsl], ei, ALU.mult, ["ET", "C0"], ["TA"])
            E(TBh, CIM[:, d, gsl], er, ALU.mult, ["ET", "C1"], ["TB"])
            STT("dve", WI[:, :, t, :], TAh, -1.0, TBh, ALU.mult, ALU.subtract, ["TA", "TB"], ["WI"])
        X3 = lambda tt_: tt_.rearrange("p g s c -> p g (s c)")
        e8r = bc(ER[:, 8, dsl], [128, 16, 128], 2)
        e8i = bc(EI[:, 8, dsl], [128, 16, 128], 2)
        E(X3(WTR), e8r, X3(XR), ALU.mult, ["ET", "XR"], ["WTR"])
        E(X3(TD), e8i, X3(XI), ALU.mult, ["ET", "XI"], ["TD"])
        E(X3(WTR), X3(WTR), X3(TD), ALU.subtract, ["WTR", "TD"], ["WTR"])
        E(X3(WTI), e8r, X3(XI), ALU.mult, ["ET", "XI"], ["WTI"])
        E(X3(TD), e8i, X3(XR), ALU.mult, ["ET", "XR"], ["TD"])
        E(X3(WTI), X3(WTI), X3(TD), ALU.add, ["WTI", "TD"], ["WTI"])
        for gpl in range(16):
            gp = gph * 16 + gpl
            sb_ = gp % 2
            for gpar in range(2):
                g = 2 * gp + gpar
                hs = slice(gpar * 64, (gpar + 1) * 64)
                pm = ps[gpar]
                MM(pm[:, 0:128], X3(XR)[hs, gpl, :], X3(WR)[hs, gpl, :], True, False, ["XR", "WR"], [f"ps{gpar}"])
                MM(pm[:, 0:128], X3(XI)[hs, gpl, :], X3(WI)[hs, gpl, :], False, True, ["XI", "WI"], [f"ps{gpar}"])
                if d == 0:
                    TT("dve", MACC[:, g, :], pm[:, 0:128], MKF, ALU.mult, [f"ps{gpar}", "MKF"], [f"MACC{g}"])
                else:
                    TT("dve", TMPM, pm[:, 0:128], MKB, ALU.mult, [f"ps{gpar}", "MKB"], ["TMPM"])
                    TT("dve", MACC[:, g, :], MACC[:, g, :], TMPM, ALU.add, [f"MACC{g}", "TMPM"], [f"MACC{g}"])
                for ri in range(2):
                    src = X3(WTR if ri == 0 else WTI)
                    pw = ps[2 + gpar * 2 + ri]
                    MM(pw[:, 0:128], src[:, gpl, :], IH[gpar], True, True, ["WTR", "WTI", f"IH{gpar}"],
                       [f"ps{2 + gpar * 2 + ri}"])
                    CP("act", WINs[sb_][:, gpar * 2 + ri, :], pw[:, 0:128], [f"ps{2 + gpar * 2 + ri}"],
                       [f"WINs{sb_}"])
                    CP("pool", WOUs[sb_][hs, gpar * 2 + ri, :], X3(WR if ri == 0 else WI)[hs, gpl, :],
                       ["WR", "WI"], [f"WOUs{sb_}"])
            base = (d * 128 + 4 * gp) * 128
            DMA("sp", WINC_d[:, base:base + 512], WINs[sb_].rearrange("p a b -> p (a b)"), f"sWIN{sb_}",
                [f"WINs{sb_}"], ["WINC_d"])
            DMA("sp", WOUT_d[:, base:base + 512], WOUs[sb_].rearrange("p a b -> p (a b)"), f"sWOU{sb_}",
                [f"WOUs{sb_}"], ["WOUT_d"])
    for g in range(64):
        STT("dve", MACC[:, g, :], IDF, DCOL[:, g:g + 1], MACC[:, g, :], ALU.mult, ALU.add,
            [f"MACC{g}", "IDF", "DCOL"], [f"MACC{g}"])
    CP("act", MB, MACC, [f"MACC{g}" for g in range(64)], ["MB"])
    DMA("sp", M_d, MB.rearrange("p a b -> p (a b)"), "sMB", ["MB"], ["M_d"])
    P.barrier()
    if upto < 4:
        P.emit(); es.close(); return nc

    A.reset()
    WINC = A.t([128, 256, 128], BF16)
    SEL = A.t([128, 64, 128], BF16)
    UTc = A.t([128, 8, 1024], BF16)
    UGall = A.t([128, 64, 128], BF16)
    INCc = [A.t([128, 128, 64], F32) for _ in range(2)]
    for i4 in range(4):
        DMA("sp", WINC[:, i4 * 64:(i4 + 1) * 64, :].rearrange("p a b -> p (a b)"),
            WINC_d[:, i4 * 64 * 128:(i4 + 1) * 64 * 128], f"lWINC{i4}", [], [f"WINC{i4}"])
    allwinc = [f"WINC{i4}" for i4 in range(4)]
    DMA("pool", SEL, sel_d, "lSEL", [], ["SEL"])
    UT_v = UT_d.rearrange("(b f) t -> f b t", f=128)
    for jc in range(NJC):
        t0 = jc * 1024
        DMA("sp", UTc, UT_v[:, :, t0:t0 + 1024], "lUTc", [], ["UTc"])
        for g4 in range(16):
            pb = g4 % 2
            for gi in range(4):
                g = g4 * 4 + gi
                blk, gl = g // 8, g % 8
                for s in range(8):
                    MM(ps[pb][:, gi * 128:(gi + 1) * 128], SEL[:, s * 8 + gl, :], UTc[:, blk, s:1024:8],
                       s == 0, s == 7, ["SEL", "UTc"], [f"ps{pb}"])
            CP("act" if pb == 0 else "dve", UGall[:, g4 * 4:(g4 + 1) * 4, :],
               ps[pb][:, :].rearrange("p (a b) -> p a b", b=128), [f"ps{pb}"], [f"UG{g4}"])
        allug = [f"UG{g4}" for g4 in range(16)]
        n = 0
        for d in range(2):
            for gp in range(32):
                pb = 2 + n % 4
                n += 1
                for ri in range(2):
                    for gpar in range(2):
                        g = 2 * gp + gpar
                        MM(ps[pb][:, ri * 128:(ri + 1) * 128], WINC[:, (d * 64 + g) * 2 + ri, :], UGall[:, g, :],
                           gpar == 0, gpar == 1, allwinc + allug, [f"ps{pb}"])
                dst = INCc[d].rearrange("p j (r g) -> p r j g", r=2)[:, :, :, gp]
                CP("act" if n % 2 == 0 else "dve", dst, ps[pb][:, 0:256].rearrange("p (r j) -> p r j", r=2),
                   [f"ps{pb}"], [f"INCc{d}"])
        for d in range(2):
            DMA("sp", INC_d[d][:, jc, :], INCc[d].rearrange("p a b -> p (a b)"), f"sINC{d}", [f"INCc{d}"], [f"INC_d{d}"])
    P.barrier()
    if upto < 5:
        P.emit(); es.close(); return nc

    A.reset()
    KTh = [A.t([128, L], BF16) for _ in range(1)]
    KRh = [A.t([128, L], BF16) for _ in range(1)]
    Vh = [A.t([128, NT, 128], BF16) for _ in range(1)]
    QTb = [A.t([128, 512], BF16) for _ in range(2)]
    QRb = [A.t([128, 512], BF16) for _ in range(2)]
    PT = [A.t([128, 512], BF16) for _ in range(4)]
    RSb = A.t([128, 512], F32)
    ATn = A.t([128, 512], F32)
    SQa = A.t([128, 512], BF16)
    ATo = [A.t([128, 512], BF16) for _ in range(2)]
    ONESB = A.t([128, 128], BF16)
    ONEC2 = A.t([128, 2], BF16)
    AOGc = A.t([128, 8], F32)
    HJ = 64
    INs = [A.t([128, HJ, 64], F32) for _ in range(2)]
    SO = [A.t([128, HJ + 1, 64], F32) for _ in range(2)]
    SB16 = [A.t([128, 64, 128], BF16) for _ in range(2)]
    PQ = [A.t([128, 2, 64], F32) for _ in range(2)]
    G2Bt = A.t([128, 2048], F32)
    WSTG = [A.t([128, 4, 1024], BF16) for _ in range(2)]
    DMA("sp", G2Bt, MOD_d[5 * D:6 * D].partition_broadcast(128), "lG2Bt", [], ["G2Bt"])
    MSET("pool", ONESB, 1.0, ["ONESB"])
    MSET("pool", ONEC2, 1.0, ["ONEC2"])
    MSET("pool", KRh[0][64:128, :], 0.0, ["KRh0"])
    for b_ in range(2):
        MSET("pool", QRb[b_][64:128, :], 0.0, [f"QRb{b_}"])
    DMA("sp", AOGc, att_out_g.rearrange("(k p) -> p k", p=128), "lAOGc", [], ["AOGc"], slow=True)

    def scan_gen(d, eng):
        if d == 0:
            MSET(eng, SO[0][:, 0, :], 0.0, ["SO0"])
        else:
            MSET(eng, SO[1][:, HJ, :], 0.0, ["SO1"])
        AA = A8T[:, d, 0, :]
        AIMS = A8T[:, d, 1, :]
        nh = 128 // HJ
        for step in range(NJC * nh):
            hc = step if d == 0 else NJC * nh - 1 - step
            jc, hh = hc // nh, hc % nh
            DMA("pool", INs[d].rearrange("p a b -> p (a b)"), INC_d[d][:, jc, hh * HJ * 64:(hh + 1) * HJ * 64],
                f"lIN{d}", [f"INC_d{d}"], [f"INs{d}"])
            for ii in range(HJ):
                i = ii if d == 0 else HJ - 1 - ii
                src = SO[d][:, i, :] if d == 0 else SO[d][:, i + 1, :]
                dst = SO[d][:, i + 1, :] if d == 0 else SO[d][:, i, :]
                swp = src.rearrange("p (r g) -> p r g", r=2)[:, ::-1, :]
                TT(eng, PQ[d][:, 0, :], AA, src, ALU.mult, [f"SO{d}", "A8T"], [f"P{d}"])
                TT(eng, PQ[d][:, 1, :].rearrange("p (r g) -> p r g", r=2), AIMS.rearrange("p (r g) -> p r g", r=2), swp,
                   ALU.mult, [f"SO{d}", "A8T"], [f"Q{d}"])
                TT(eng, PQ[d][:, 0, :], PQ[d][:, 0, :], PQ[d][:, 1, :], ALU.add, [f"P{d}", f"Q{d}"], [f"P{d}"])
                TT(eng, dst, PQ[d][:, 0, :], INs[d][:, i, :], ALU.add, [f"P{d}", f"INs{d}"], [f"SO{d}"])
                yield
            lo = 0 if d == 0 else 1
            CP(eng, SB16[d][:, :, hh * HJ:(hh + 1) * HJ], SO[d][:, lo:lo + HJ, :].rearrange("p j c -> p c j"),
               [f"SO{d}"], [f"SB16{d}"])
            last_half = (hh == nh - 1) if d == 0 else (hh == 0)
            if last_half:
                DMA("pool", S_d[d][:, jc], SB16[d], f"sS{d}", [f"SB16{d}"], [f"S_d{d}"])
            if d == 0:
                CP(eng, SO[d][:, 0, :], SO[d][:, HJ, :], [f"SO{d}"], [f"SO{d}"])
            else:
                CP(eng, SO[d][:, HJ, :], SO[d][:, 0, :], [f"SO{d}"], [f"SO{d}"])
        while True:
            yield

    def att_tail(h, qb, ib, po, pz):
        P.op("dve", (lambda pz_: lambda e: e.reciprocal(RSb, ps[pz_][:, :]))(pz), reads=[f"ps{pz}"], writes=["RSb"])
        TT("dve", ATn, ps[po][:, :], RSb, ALU.mult, [f"ps{po}", "RSb"], ["ATn"])
        TT("pool", SQa, ATn, ATn, ALU.mult, ["ATn"], ["SQa"])
        for qs in range(4):
            MM(ps[6][:, qs:qs + 1], SQa[:, qs * 128:(qs + 1) * 128], ONEC2[:, 0:1], True, True, ["SQa", "ONEC2"], ["ps6"])
        if h == 0:
            CP("dve", SSA[:, qb * 4:(qb + 1) * 4], ps[6][:, 0:4], ["ps6"], [f"SSA{qb}"])
        else:
            TT("dve", SSA[:, qb * 4:(qb + 1) * 4], SSA[:, qb * 4:(qb + 1) * 4], ps[6][:, 0:4], ALU.add,
               ["ps6", f"SSA{qb}"], [f"SSA{qb}"])
        TS("dve", ATo[ib], ATn, AOGc[:, h:h + 1], None, ALU.mult, None, ["ATn", "AOGc"], [f"ATo{ib}"])
        DMA("sp", ATTT_d[h * 128:(h + 1) * 128, qb * 512:(qb + 1) * 512], ATo[ib], f"sATo{ib}", [f"ATo{ib}"], [])

    def cast_gen():
        n = 0
        for wsrc, wdst in ((w1, W1s), (w3, W3s)):
            wv = wsrc.rearrange("(k p) n -> p k n", p=128)
            for pr in range(22):
                DMA("pool", wdst[pr].rearrange("p (k c) -> p k c", k=16), wv[:, :, pr * 256:(pr + 1) * 256], f"cast{n % 4}", [], ["Wsd"])
                n += 1
                yield
        w2v = w2.rearrange("(g j p) n -> g p j n", j=4, p=128)
        for hh in range(2):
            for g4 in range(11):
                b_ = n % 2
                n += 1
                DMA("pool", WSTG[b_], w2v[g4][:, :, hh * 1024:(hh + 1) * 1024], f"castl{b_}", [], [f"WSTG{b_}"])
                yield
                TT("dve", WSTG[b_], WSTG[b_], bc(G2Bt[:, hh * 1024:(hh + 1) * 1024], [128, 4, 1024], 1), ALU.mult,
                   [f"WSTG{b_}", "G2Bt"], [f"WSTG{b_}"])
                DMA("pool", W2s[hh, g4].rearrange("p (j c) -> p j c", j=4), WSTG[b_], f"casts{b_}", [f"WSTG{b_}"], ["Wsd"])
                yield
        cast_done.append(1)
        while True:
            yield

    cast_done = []
    castg = cast_gen()
    pending_tail = None
    scans = [scan_gen(0, "dve"), scan_gen(1, "pool")]
    steps_per_it = -(-(NJ) // (NH * (L // 512)))
    sc = 1.0 / float(np.sqrt(192.0))
    NQB = L // 512
    it = 0
    for h in range(NH):
        hb = 0
        DMA("sp", KTh[hb], KT_d[h], f"KTh{hb}", [], [f"KTh{hb}"])
        DMA("sp", KRh[hb][0:64, :], KR_d[h], f"KRh{hb}", [], [f"KRh{hb}"])
        DMA("sp", Vh[hb], V_d[h].rearrange("(n p) d -> p n d", p=128), f"Vh{hb}", [], [f"Vh{hb}"])
        for qb in range(NQB):
            ib = it % 2
            it += 1
            po = 2 + ib
            pz = 4 + ib
            DMA("sp", QTb[ib], QT_d[h][:, qb * 512:(qb + 1) * 512], f"QTb{ib}", [], [f"QTb{ib}"])
            DMA("sp", QRb[ib][0:64, :], QR_d[h][:, qb * 512:(qb + 1) * 512], f"QRb{ib}", [], [f"QRb{ib}"])

            SBK = [0, 1, 7]

            def S(kt):
                sb_ = SBK[kt % 3]
                MM(ps[sb_][:, :], KTh[hb][:, kt * 128:(kt + 1) * 128], QTb[ib][:, :], True, False,
                   [f"KTh{hb}", f"QTb{ib}"], [f"ps{sb_}"])
                MM(ps[sb_][:, :], KRh[hb][:, kt * 128:(kt + 1) * 128], QRb[ib][:, :], False, True,
                   [f"KRh{hb}", f"QRb{ib}"], [f"ps{sb_}"])
            S(0)
            if NT > 1:
                S(1)
            for kt in range(NT):
                sb_ = SBK[kt % 3]
                pb3 = kt % 4
                if kt + 2 < NT:
                    S(kt + 2)
                if kt == min(3, NT - 1) and pending_tail is not None:
                    att_tail(*pending_tail)
                    pending_tail = None
                ACT(PT[pb3], ps[sb_][:, :], AF.Exp, [f"ps{sb_}"], [f"PT{pb3}"], scale=sc)
                MM(ps[po][:, :], Vh[hb][:, kt, :], PT[pb3], kt == 0, kt == NT - 1, [f"PT{pb3}", f"Vh{hb}"], [f"ps{po}"])
                MM(ps[pz][:, :], ONESB, PT[pb3], kt == 0, kt == NT - 1, [f"PT{pb3}", "ONESB"], [f"ps{pz}"])
            pending_tail = (h, qb, ib, po, pz)
            for _ in range(steps_per_it):
                next(scans[0])
                next(scans[1])
            next(castg)
    att_tail(*pending_tail)
    while not cast_done:
        next(castg)
    for _ in range(4 * HJ):
        next(scans[0])
        next(scans[1])
    P.barrier()
    A.reset()
    JG = min(4, NJC)
    NW = 128 * JG
    NTK = 1024 * JG
    SELT = A.t([128, 64, 128], BF16)
    SEL = A.t([128, 64, 128], BF16)
    WOb = A.t([128, 2, 16, 128], BF16)
    Mb = A.t([128, 8, 128], BF16)
    UTb = [A.t([128, NTK], BF16) for _ in range(2)]
    Sb = [[A.t([128, 2, 4, NW], BF16) for _ in range(2)] for _ in range(2)]
    UG8 = A.t([128, 8, NW], BF16)
    YG = A.t([128, 8, NW], BF16)
    YTb = A.t([128, NTK], F32)
    GT1 = A.t([128, NTK], F32)
    GT2 = A.t([128, NTK], F32)
    GOUT = [A.t([128, NTK], BF16) for _ in range(2)]
    DMA("pool", SEL, sel_d, "lSEL", [], ["SEL"])
    DMA("pool", SELT, selt_d, "lSELT", [], ["SELT"])
    GC = float(2.0 * np.sqrt(2.0 / np.pi))

    def load53(it_):
        blk_, jg_ = it_ // (NJC // JG), it_ % (NJC // JG)
        ib_ = it_ % 2
        tt0 = jg_ * NTK
        DMA("sp", UTb[ib_], UT_d[blk_ * 128:(blk_ + 1) * 128, tt0:tt0 + NTK], f"lUTb{ib_}", [], [f"UTb{ib_}"])
        for d in range(2):
            for ri in range(2):
                for gpl in range(4):
                    DMA("sp", Sb[ib_][d][:, ri, gpl, :].rearrange("p (c j) -> p c j", c=JG),
                        S_d[d][:, jg_ * JG:(jg_ + 1) * JG, ri * 32 + blk_ * 4 + gpl, :],
                        f"lSb{ib_}{d}{ri}{gpl}", [], [f"Sb{ib_}{d}{ri}{gpl}"])
    it = 0
    ne = 0
    for blk in range(8):
        for d in range(2):
            base = ((d * 64 + blk * 8) * 2) * 128
            DMA("sp", WOb[:, d].rearrange("p a b -> p (a b)"), WOUT_d[:, base:base + 16 * 128], f"lWOb{d}", [], [f"WOb{d}"])
        DMA("sp", Mb.rearrange("p a b -> p (a b)"), M_d[:, blk * 8 * 128:(blk + 1) * 8 * 128], "lMb", [], ["Mb"])
        for jg in range(NJC // JG):
            ib = it % 2
            t0 = jg * NTK
            if it == 0:
                load53(0)
            if it + 1 < 8 * (NJC // JG):
                load53(it + 1)
            it += 1
            for gl in range(8):
                pb = gl % 4
                for s in range(8):
                    MM(ps[pb][:, 0:NW], SEL[:, s * 8 + gl, :], UTb[ib][:, s:NTK:8], s == 0, s == 7,
                       ["SEL", f"UTb{ib}"], [f"ps{pb}"])
                ne += 1
                CP("act" if ne % 2 == 0 else "dve", UG8[:, gl, :], ps[pb][:, 0:NW], [f"ps{pb}"], [f"UG8{gl}"])
            for gl in range(8):
                pb = 4 + gl % 4
                o = ps[pb][:, 0:NW]
                MM(o, Mb[:, gl, :], UG8[:, gl, :], True, False, ["Mb", f"UG8{gl}"], [f"ps{pb}"])
                for d in range(2):
                    for ri in range(2):
                        MM(o, WOb[:, d, gl * 2 + ri, :], Sb[ib][d][:, ri, gl // 2, :], False, (d == 1 and ri == 1),
                           [f"WOb{d}", f"Sb{ib}{d}{ri}{gl // 2}"], [f"ps{pb}"])
                ne += 1
                CP("act" if ne % 2 == 0 else "dve", YG[:, gl, :], o, [f"ps{pb}"], [f"YG{gl}"])
            allyg = [f"YG{gl}" for gl in range(8)]
            YTv = YTb.rearrange("p (j t) -> p t j", t=8)
            for t in range(8):
                pb = t % 4
                for gl in range(8):
                    MM(ps[pb][:, 0:NW], SELT[:, t * 8 + gl, :], YG[:, gl, :], gl == 0, gl == 7, ["SELT"] + allyg, [f"ps{pb}"])
                ne += 1
                CP("act" if ne % 2 == 0 else "dve", YTv[:, t, :], ps[pb][:, 0:NW], [f"ps{pb}"], [f"YTb{t}"])
            ally = [f"YTb{t}" for t in range(8)]
            TT("pool", GT1, YTb, YTb, ALU.mult, ally, ["GT1"])
            TS("pool", GT1, GT1, 0.044715, 1.0, ALU.mult, ALU.add, ["GT1"], ["GT1"])
            TT("pool", GT1, GT1, YTb, ALU.mult, ["GT1"] + ally, ["GT1"])
            ACT(GT2, GT1, AF.Sigmoid, ["GT1"], ["GT2"], scale=GC)
            TT("dve", GOUT[ib], GT2, YTb, ALU.mult, ["GT2"] + ally, [f"GOUT{ib}"])
            DMA("sp", YT_d[blk * 128:(blk + 1) * 128, t0:t0 + NTK], GOUT[ib], f"sGOUT{ib}", [f"GOUT{ib}"], ["YT_d"])
    P.barrier()
    if upto < 7:
        P.emit(); es.close(); return nc

    A.reset()
    WG = A.t([128, 8, 1024], BF16)
    BG = A.t([128, 8], F32)
    SOG = A.t([128, 8], F32)
    ONEC = A.t([128, 2], BF16)
    YTc = [A.t([128, 8, 512], BF16) for _ in range(2)]
    SG = A.t([128, 512], F32)
    SSMf = A.t([128, 512], F32)
    SQb = A.t([128, 8, 512], BF16)
    SSMo = [A.t([128, 8, 512], BF16) for _ in range(2)]
    for k in range(8):
        DMA("pool", WG[:, k, :], w_glu[k * 128:(k + 1) * 128, :], f"lWG{k}", [], [f"WG{k}"])
    allwg = [f"WG{k}" for k in range(8)]
    DMA("sp", BG, b_glu.rearrange("(k p) -> p k", p=128), "g0", [], ["BG"], slow=True)
    DMA("sp", SOG, ssm_out_g.rearrange("(k p) -> p k", p=128), "g1", [], ["SOG"], slow=True)
    MSET("pool", ONEC, 1.0, ["ONEC"])
    YT_v = YT_d.rearrange("(b f) t -> f b t", f=128)
    SSMT_v = SSMT_d.rearrange("(b f) t -> f b t", f=128)
    DMA("sp", YTc[0], YT_v[:, :, 0:512], "lYTc0", ["YT_d"], ["YTc0"])
    for tc in range(L // 512):
        ib = tc % 2
        t0 = tc * 512
        if tc + 1 < L // 512:
            DMA("sp", YTc[1 - ib], YT_v[:, :, t0 + 512:t0 + 1024], f"lYTc{1 - ib}", ["YT_d"], [f"YTc{1 - ib}"])
        for oc in range(8):
            pb = oc % 2
            for k in range(8):
                MM(ps[pb][:, :], WG[:, k, oc * 128:(oc + 1) * 128], YTc[ib][:, k, :], k == 0, k == 7,
                   allwg + [f"YTc{ib}"], [f"ps{pb}"])
            ACT(SG, ps[pb][:, :], AF.Sigmoid, [f"ps{pb}", "BG"], ["SG"], bias=BG[:, oc:oc + 1])
            TT("dve", SSMf, SG, YTc[ib][:, oc, :], ALU.mult, ["SG", f"YTc{ib}"], ["SSMf"])
            TT("pool", SQb[:, oc, :], SSMf, SSMf, ALU.mult, ["SSMf"], [f"SQb{oc}"])
            TS("dve", SSMo[ib][:, oc, :], SSMf, SOG[:, oc:oc + 1], None, ALU.mult, None, ["SSMf", "SOG"], [f"SSMo{ib}_{oc}"])
        for tt_ in range(4):
            for oc in range(8):
                MM(ps[2][:, tt_:tt_ + 1], SQb[:, oc, tt_ * 128:(tt_ + 1) * 128], ONEC[:, 0:1], oc == 0, oc == 7,
                   [f"SQb{o2}" for o2 in range(8)] + ["ONEC"], ["ps2"])
        CP("dve", SSS[:, tc * 4:(tc + 1) * 4], ps[2][:, 0:4], ["ps2"], ["SSS"])
        DMA("sp", SSMT_v[:, :, t0:t0 + 512], SSMo[ib], f"sSSMo{ib}", [f"SSMo{ib}_{oc}" for oc in range(8)], ["SSMT_d"])
    P.barrier()

    if upto < 8:
        P.emit(); es.close(); return nc

    X1_d = dscr("X1_d", [L, D], F32)
    A.reset()
    WO = A.t([128, 16, 2048], BF16)
    G1B = A.t([128, 2048], F32)
    AOG = A.t([128, 8], F32)
    XT2 = [A.t([128, 2048], F32) for _ in range(2)]
    MIXT = [A.t([128, 16, 128], BF16) for _ in range(2)]
    X1t = [A.t([128, 2048], F32) for _ in range(2)]
    TE1 = [A.t([128, 512], F32) for _ in range(2)]
    TE2 = [A.t([128, 512], F32) for _ in range(2)]
    ST4 = A.t([128, 8], F32)
    WST = [A.t([128, 16, 256], BF16) for _ in range(2)]
    DMA("sp", G1B, MOD_d[2 * D:3 * D].partition_broadcast(128), "lG1B", ["MOD_d"], ["G1B"])
    for k in range(16):
        DMA("pool", WO[:, k, :], w_o[k * 128:(k + 1) * 128, :], f"lWO{k}", [], [f"WO{k}"])
        TT("dve", WO[:, k, :], WO[:, k, :], G1B, ALU.mult, [f"WO{k}", "G1B"], [f"WO{k}"])
    SSMT_v2 = SSMT_d.rearrange("(b f) t -> f b t", f=128)
    ATTT_v2 = ATTT_d.rearrange("(b f) t -> f b t", f=128)
    ne = 0
    def load4a(t_):
        rr = t_ * 128
        tb_ = t_ % 2
        DMA("sp", XT2[tb_], x[rr:rr + 128, :], f"lXT2{tb_}", [], [f"XT2{tb_}"])
        DMA("sp", MIXT[tb_][:, 8:16, :], SSMT_v2[:, :, rr:rr + 128], f"lMIXs{tb_}", [], [f"MIXs{tb_}"])
        DMA("sp", MIXT[tb_][:, 0:8, :], ATTT_v2[:, :, rr:rr + 128], f"lMIXa{tb_}", [], [f"MIXa{tb_}"])
    load4a(0)
    for t in range(NT):
        r0 = t * 128
        tb = t % 2
        if t + 1 < NT:
            load4a(t + 1)
        RSTD(ST4[:, tb * 4 + 1:tb * 4 + 2], SSA[:, t:t + 1], 1024, f"ar{tb}", "SSA")
        RSTD(ST4[:, tb * 4 + 2:tb * 4 + 3], SSS[:, t:t + 1], 1024, f"sr{tb}", "SSS")
        for nb in range(4):
            pa = ps[2 * nb]
            pss = ps[2 * nb + 1]
            na, ns = f"ps{2 * nb}", f"ps{2 * nb + 1}"
            cs = slice(nb * 512, (nb + 1) * 512)
            eb = ne % 2
            ne += 1
            for k in range(8):
                MM(pa[:, :], MIXT[tb][:, k, :], WO[:, k, cs], k == 0, k == 7, [f"MIXa{tb}", f"WO{k}"], [na])
            for k in range(8, 16):
                MM(pss[:, :], MIXT[tb][:, k, :], WO[:, k, cs], k == 8, k == 15, [f"MIXs{tb}", f"WO{k}"], [ns])
            ACT(TE1[eb], pa[:, :], AF.Copy, [na, f"ar{tb}"], [f"TE1{eb}"], scale=ST4[:, tb * 4 + 1:tb * 4 + 2])
            STT("dve", TE2[eb], pss[:, :], ST4[:, tb * 4 + 2:tb * 4 + 3], TE1[eb], ALU.mult, ALU.add, [ns, f"sr{tb}", f"TE1{eb}"], [f"TE2{eb}"])
            TT("pool", X1t[tb][:, cs], TE2[eb], XT2[tb][:, cs], ALU.add, [f"TE2{eb}", f"XT2{tb}"], [f"X1t{tb}_{nb}"])
        DMA("pool", X1_d[r0:r0 + 128, :], X1t[tb], f"sX1{tb}", [f"X1t{tb}_{nb}" for nb in range(4)], ["X1_d"])
    P.barrier()
    if upto < 9:
        P.emit(); es.close(); return nc

    A.reset()
    X1s = [A.t([128, 4, 2048], F32) for _ in range(2)]
    XB2 = [A.t([128, 2048], BF16) for _ in range(2)]
    JNK2 = A.t([128, 2048], BF16)
    H2T = A.t([128, 16, 512], BF16)
    W1t = [A.t([128, 16, 256], BF16) for _ in range(2)]
    W3t = [A.t([128, 16, 256], BF16) for _ in range(2)]
    GT = A.t([128, 44, 512], BF16)
    W2t = [A.t([128, 4, 1024], BF16) for _ in range(2)]
    OUTs = [A.t([128, 512], F32) for _ in range(4)]
    SA = [A.t([128, 512], F32) for _ in range(2)]
    ST5 = A.t([128, 8], F32)
    nw = 0
    no = 0
    NST = L // 512
    DMA("sp", X1s[0], X1_d[0:512, :].rearrange("(a p) n -> p a n", p=128), "lX1s0", [], ["X1s0"])
    for st_ in range(NST):
        r0 = st_ * 512
        xb = st_ % 2
        if st_ + 1 < NST:
            DMA("sp", X1s[1 - xb], X1_d[r0 + 512:r0 + 1024, :].rearrange("(a p) n -> p a n", p=128), f"lX1s{1 - xb}", [],
                [f"X1s{1 - xb}"])
        MSET("pool", ST5, 0.0, [f"f_ss{a}" for a in range(4)] + [f"fr{a}" for a in range(4)])
        for a in range(4):
            ab = a % 2
            ACT(JNK2, X1s[xb][:, a, :], AF.Square, [f"X1s{xb}"], ["JNK2", f"f_ss{a}"], accum=ST5[:, 2 * a:2 * a + 1])
            RSTD(ST5[:, 2 * a + 1:2 * a + 2], ST5[:, 2 * a:2 * a + 1], D, f"fr{a}", f"f_ss{a}")
            ACT(XB2[ab], X1s[xb][:, a, :], AF.Copy, [f"X1s{xb}", f"fr{a}"], [f"XB2{ab}"], scale=ST5[:, 2 * a + 1:2 * a + 2])
            for k in range(16):
                pi = 4 * ab + k // 4
                MM(ps[pi][:, (k % 4) * 128:(k % 4 + 1) * 128], XB2[ab][:, k * 128:(k + 1) * 128], IDB, True, True,
                   [f"XB2{ab}", "IDB"], [f"ps{pi}"])
            for k in range(16):
                pi = 4 * ab + k // 4
                TS("dve", H2T[:, k, a * 128:(a + 1) * 128], ps[pi][:, (k % 4) * 128:(k % 4 + 1) * 128],
                   GS[:, 32 + k:33 + k], GS[:, 48 + k:49 + k], ALU.mult, ALU.add, [f"ps{pi}", "GS"], [f"H2T{k}"])
        for pr in range(22):
            wb = nw % 2
            nw += 1
            DMA("sp", W1t[wb].rearrange("p a b -> p (a b)"), W1s[pr], f"lW1t{wb}", [], [f"W1t{wb}"])
            DMA("sp", W3t[wb].rearrange("p a b -> p (a b)"), W3s[pr], f"lW3t{wb}", [], [f"W3t{wb}"])
            for c2 in range(2):
                ffc = pr * 2 + c2
                sb_ = ffc % 2
                pa, pb_ = ps[sb_ * 2], ps[sb_ * 2 + 1]
                na, nb_ = f"ps{sb_ * 2}", f"ps{sb_ * 2 + 1}"
                for k in range(16):
                    MM(pa[:, :], W1t[wb][:, k, c2 * 128:(c2 + 1) * 128], H2T[:, k, :], k == 0, k == 15,
                       [f"W1t{wb}", f"H2T{k}"], [na])
                for k in range(16):
                    MM(pb_[:, :], W3t[wb][:, k, c2 * 128:(c2 + 1) * 128], H2T[:, k, :], k == 0, k == 15,
                       [f"W3t{wb}", f"H2T{k}"], [nb_])
                ACT(SA[sb_], pa[:, :], AF.Silu, [na], [f"SA{sb_}"])
                TT("dve", GT[:, ffc, :], SA[sb_], pb_[:, :], ALU.mult, [f"SA{sb_}", nb_], [f"GT{ffc}"])
        for h in range(2):
            for g4 in range(11):
                wb = nw % 2
                nw += 1
                DMA("sp", W2t[wb].rearrange("p a b -> p (a b)"), W2s[h, g4], f"lW2t{wb}", [], [f"W2t{wb}"])
                for j in range(4):
                    ffc = g4 * 4 + j
                    for a in range(4):
                        for nb in range(2):
                            MM(ps[a * 2 + nb][:, :], GT[:, ffc, a * 128:(a + 1) * 128], W2t[wb][:, j, nb * 512:(nb + 1) * 512],
                               ffc == 0, ffc == 43, [f"GT{ffc}", f"W2t{wb}"], [f"ps{a * 2 + nb}"])
            for a in range(4):
                for nb in range(2):
                    ob = no % 4
                    no += 1
                    cs = slice(h * 1024 + nb * 512, h * 1024 + (nb + 1) * 512)
                    TT("dve", OUTs[ob], ps[a * 2 + nb][:, :], X1s[xb][:, a, cs], ALU.add, [f"ps{a * 2 + nb}", f"X1s{xb}"],
                       [f"OUTs{ob}"])
                    DMA("pool", y_out[r0 + a * 128:r0 + (a + 1) * 128, cs], OUTs[ob], f"sOUT{ob}", [f"OUTs{ob}"], [])
    P.barrier()

    P.emit()
    es.close()
    return nc


def rope_tables_host(L):
    pos = np.arange(L, dtype=np.float32)
    inv_freq = (np.float32(10000.0) ** (-np.arange(0, 64, 2, dtype=np.float32) / np.float32(64))).astype(np.float32)
    ang = (pos[:, None] * inv_freq[None, :]).astype(np.float32)
    return np.cos(ang).astype(np.float32), np.sin(ang).astype(np.float32)


_S5C = {}


def s5_constants():
    if _S5C:
        return _S5C
    sel = np.zeros((128, 64, 128), np.float32)
    selt = np.zeros((128, 64, 128), np.float32)
    for s in range(8):
        for gl in range(8):
            for c in range(16):
                sel[gl * 16 + c, s * 8 + gl, s * 16 + c] = 1.0
                selt[s * 16 + c, s * 8 + gl, gl * 16 + c] = 1.0
    si = np.arange(128)[:, None] // 16
    ti = np.arange(128)[None, :] // 16
    _S5C.update(sel=sel, selt=selt, maskf=(si <= ti).astype(np.float32), maskb=(si >= ti).astype(np.float32))
    return _S5C


def make_core_inputs(inp, seq_x, seq_c, L):
    cos, sin = rope_tables_host(L)
    m = {"x": np.ascontiguousarray(seq_x[:L]), "c": np.ascontiguousarray(seq_c),
         "ident": np.eye(128, dtype=np.float32), "ropec": cos, "ropes": sin}
    m.update(s5_constants())
    for k, v in inp.items():
        if k in ("x_prompt", "x_sample", "c_prompt", "c_sample"):
            continue
        m[k] = np.ascontiguousarray(v[0])
    return m


def kernel(**inputs):
    L = 8192
    nc = build(L)
    xs = [inputs["x_prompt"][i] for i in range(4)] + [inputs["x_sample"][0]]
    cs = [inputs["c_prompt"][i] for i in range(4)] + [inputs["c_sample"][0]]
    in_maps = []
    for core in range(8):
        i = core if core < 5 else core - 5
        in_maps.append(make_core_inputs(inputs, xs[i], cs[i], L))
    res = run_bass_kernel_spmd(nc, in_maps, core_ids=list(range(8)))
    ys = [np.asarray(res.results[i]["y"], dtype=np.float32) for i in range(5)]
    y_prompt = np.stack(ys[:4], axis=0)
    y_sample = ys[4][None]
    return (y_prompt, y_sample)
```

```python
import contextlib
import numpy as np
import concourse.bass as bass
import concourse.mybir as mybir
from concourse.bass_utils import run_bass_kernel_spmd

F32 = mybir.dt.float32
BF16 = mybir.dt.bfloat16
AF = mybir.ActivationFunctionType
ALU = mybir.AluOpType
AX = mybir.AxisListType

D = 2048
NH = 8
DFF = 5632
INC = 3136
EPS = 1e-6
MEMF = 52000


class Prog:
    def __init__(self, nc):
        self.nc = nc
        self.ops = {e: [] for e in ("pe", "act", "dve", "pool", "sp")}
        self.count = {}
        self.waited = {e: {} for e in self.ops}
        self.res = {}
        self.semkeys = []
        self.chslot = {}

    def _tok(self, semkey, inc):
        if semkey not in self.count:
            self.count[semkey] = 0
            self.semkeys.append(semkey)
        self.count[semkey] += inc
        return (semkey, self.count[semkey])

    def op(self, eng, fn, reads=(), writes=(), ch=None):
        deps = {}

        def add(toks, raw):
            for k, v in toks.items():
                if ch is None and k == eng and (eng == "pe" or not raw):
                    continue
                if deps.get(k, 0) < v:
                    deps[k] = v
        for r in reads:
            st = self.res.get(r)
            if st is not None:
                add(st[0], True)
        for w in writes:
            st = self.res.get(w)
            if st is not None:
                add(st[0], True)
                add(st[1], False)
        if ch is not None:
            if ch not in self.chslot:
                self.chslot[ch] = len(self.chslot)
            chkey = "dma:%d" % self.chslot[ch]
            if self.count.get(chkey, 0) > 0:
                deps[chkey] = self.count[chkey]
        waits = []
        wd = self.waited[eng]
        for k, v in deps.items():
            if wd.get(k, 0) < v:
                wd[k] = v
                waits.append((k, v))
        if ch is None:
            tok = self._tok(eng, 1)
            inc = 1
        else:
            tok = self._tok(chkey, 16)
            inc = 16
        self.ops[eng].append((waits, fn, tok[0], inc))
        for r in reads:
            st = self.res.setdefault(r, [{}, {}])
            if st[1].get(tok[0], 0) < tok[1]:
                st[1][tok[0]] = tok[1]
        for w in writes:
            self.res[w] = [{tok[0]: tok[1]}, {}]
        return tok

    def barrier(self):
        for eng in self.ops:
            waits = []
            wd = self.waited[eng]
            for k, v in self.count.items():
                if wd.get(k, 0) < v:
                    wd[k] = v
                    waits.append((k, v))
            if waits:
                self.ops[eng].append((waits, None, None, 0))
        self.res = {}
        self.chslot = {}

    def emit(self):
        nc = self.nc
        with contextlib.ExitStack() as es:
            sems = {}
            for k in self.semkeys:
                sems[k] = es.enter_context(nc.semaphore("s_" + k.replace(":", "_")))
            block = es.enter_context(nc.Block())

            def run(engname):
                def body(e):
                    for waits, fn, semkey, inc in self.ops[engname]:
                        for k, v in waits:
                            e.wait_ge(sems[k], v)
                        if fn is not None:
                            ins = fn(e)
                            ins.then_inc(sems[semkey], inc)
                return body
            block.tensor(run("pe"))
            block.scalar(run("act"))
            block.vector(run("dve"))
            block.gpsimd(run("pool"))
            block.sync(run("sp"))


def build(L, dbg=(), upto=99):
    NT = L // 128
    nc = bass.Bass("TRN2", target_bir_lowering=False)
    P = Prog(nc)

    def din(name, shape):
        return nc.dram_tensor(name, list(shape), F32, kind="ExternalInput").ap()

    def dscr(name, shape, dt):
        kind = "ExternalOutput" if name in dbg else "Internal"
        return nc.dram_tensor(name, list(shape), dt, kind=kind).ap()

    x = din("x", [L, D]); cvec = din("c", [D])
    w_ada = din("w_ada", [D, 6 * D]); b_ada = din("b_ada", [6 * D])
    norm_mix_g = din("norm_mix_g", [D]); w_in = din("w_in", [D, INC])
    kv_norm_g = din("kv_norm_g", [512]); w_ukv = din("w_ukv", [512, 2048])
    q_norm_g = din("q_norm_g", [192]); k_norm_g = din("k_norm_g", [192])
    lam_re = din("lam_re", [2, 64, 64]); lam_im = din("lam_im", [2, 64, 64]); log_dt = din("log_dt", [2, 64])
    b_re = din("b_re", [64, 64, 16]); b_im = din("b_im", [64, 64, 16])
    c_re = din("c_re", [2, 64, 16, 64]); c_im = din("c_im", [2, 64, 16, 64])
    d_skip = din("d_skip", [1024]); w_glu = din("w_glu", [1024, 1024]); b_glu = din("b_glu", [1024])
    att_out_g = din("att_out_g", [1024]); ssm_out_g = din("ssm_out_g", [1024])
    w_o = din("w_o", [D, D]); norm_ffn_g = din("norm_ffn_g", [D])
    w1 = din("w1", [D, DFF]); w3 = din("w3", [D, DFF]); w2 = din("w2", [DFF, D])
    ident_d = din("ident", [128, 128]); ropec = din("ropec", [L, 32]); ropes = din("ropes", [L, 32])
    maskf_d = din("maskf", [128, 128]); maskb_d = din("maskb", [128, 128])
    sel_d = din("sel", [128, 64, 128]); selt_d = din("selt", [128, 64, 128])
    y_out = nc.dram_tensor("y", [L, D], F32, kind="ExternalOutput").ap()

    MOD_d = dscr("MOD_d", [6 * D], F32)
    QT_d = dscr("QT_d", [NH, 128, L], BF16); QR_d = dscr("QR_d", [NH, 64, L], BF16)
    KT_d = dscr("KT_d", [NH, 128, L], BF16); KR_d = dscr("KR_d", [NH, 64, L], BF16)
    V_d = dscr("V_d", [NH, L, 128], BF16)
    UT_d = dscr("UT_d", [1024, L], BF16)
    W1s = dscr("W1s", [22, 128, 16 * 256], BF16)
    W3s = dscr("W3s", [22, 128, 16 * 256], BF16)
    W2s = dscr("W2s", [2, 11, 128, 4 * 1024], BF16)
    ATTT_d = dscr("ATTT_d", [1024, L], BF16)

    es = contextlib.ExitStack()
    mem = es.enter_context(nc.sbuf_tensor("mem", [128, MEMF], F32))
    ps = [es.enter_context(nc.psum_tensor(f"ps{i}", [128, 512], F32)) for i in range(8)]

    class Alloc:
        def __init__(self, base=0):
            self.off = base
            self.base = base

        def reset(self):
            self.off = self.base

        def t(self, shape, dt):
            n = int(np.prod(shape[1:]))
            nb = n * (2 if dt == BF16 else 4)
            nb4 = (nb + 3) // 4
            assert self.off + nb4 <= MEMF, ("SBUF arena overflow", self.off, nb4)
            v = mem[0:shape[0], self.off:self.off + nb4]
            if dt != F32:
                v = v.bitcast(dt)
                if nb4 * 2 != n:
                    v = v[:, 0:n]
            if len(shape) == 3:
                v = v.rearrange("p (a b) -> p a b", b=shape[2])
            elif len(shape) == 4:
                v = v.rearrange("p (a b c) -> p a b c", b=shape[2], c=shape[3])
            self.off += nb4
            return v

    def MM(out, lhsT, rhs, st, sp, r, w):
        P.op("pe", lambda e: e.matmul(out, lhsT, rhs, start=st, stop=sp), reads=r, writes=w)

    def ACT(out, in_, func, r, w, bias=None, scale=None, accum=None):
        kw = {}
        if bias is not None:
            kw["bias"] = bias
        if scale is not None:
            kw["scale"] = scale
        if accum is not None:
            kw["accum_out"] = accum
        P.op("act", lambda e: e.activation(out, in_, func, **kw), reads=r, writes=w)

    def TT(eng, out, a, b, op, r, w):
        P.op(eng, lambda e: e.tensor_tensor(out, a, b, op), reads=r, writes=w)

    def TS(eng, out, a, s1, s2, op0, op1, r, w):
        if s2 is None:
            P.op(eng, lambda e: e.tensor_scalar(out, a, s1, None, op0), reads=r, writes=w)
        else:
            P.op(eng, lambda e: e.tensor_scalar(out, a, s1, s2, op0, op1), reads=r, writes=w)

    def STT(eng, out, a, s, b, op0, op1, r, w):
        P.op(eng, lambda e: e.scalar_tensor_tensor(out, a, s, b, op0, op1), reads=r, writes=w)

    def CP(eng, out, in_, r, w):
        if eng == "act":
            P.op(eng, lambda e: e.copy(out, in_), reads=r, writes=w)
        else:
            P.op(eng, lambda e: e.tensor_copy(out, in_), reads=r, writes=w)

    def RSUM(eng, out, in_, r, w):
        P.op(eng, lambda e: e.reduce_sum(out, in_, AX.X), reads=r, writes=w)

    def MSET(eng, out, val, w):
        P.op(eng, lambda e: e.memset(out, val), writes=w)

    def DMA(q, out, in_, ch, r, w, slow=False):
        if slow:
            P.op(q, lambda e: e.dma_start(out=out, in_=in_, allow_slow_non_contiguous=True), reads=r, writes=w, ch=ch)
        else:
            P.op(q, lambda e: e.dma_start(out=out, in_=in_), reads=r, writes=w, ch=ch)

    def RSTD(dst, src, n, name, srcname):
        TS("dve", dst, src, 1.0 / n, EPS, ALU.mult, ALU.add, r=[srcname], w=[name])
        P.op("act", lambda e: e.sqrt(dst, dst), reads=[name], writes=[name])
        P.op("dve", lambda e: e.reciprocal(dst, dst), reads=[name], writes=[name])

    def bc(ap, shape, axis):
        return ap.unsqueeze(axis).to_broadcast(shape)

    PA = Alloc(0)
    IDF = PA.t([128, 128], F32)
    IDB = PA.t([128, 128], BF16)
    GS = PA.t([128, 64], F32)
    A8T = PA.t([128, 2, 2, 64], F32)
    SSS = PA.t([128, 64], F32)
    SSA = PA.t([128, 64], F32)
    pers_end = PA.off
    A = Alloc(pers_end)

    DMA("sp", IDF, ident_d, "c0", [], ["IDF"])
    DMA("pool", IDB, ident_d, "c1", [], ["IDB"])

    SC = A.t([128, 16], F32)
    SCB = A.t([128, 16, 128], F32)
    WA = [A.t([128, 16, 512], F32) for _ in range(2)]
    BAb = [A.t([128, 512], F32) for _ in range(2)]
    ONES = A.t([1, 128], F32)
    MODB = A.t([128, 6 * D], F32)
    TMP0 = A.t([128, 96, 128], F32)
    COLS = A.t([128, 96], F32)
    NG = A.t([128, 32], F32)

    DMA("sp", SC, cvec.rearrange("(k p) -> p k", p=128), "c2", [], ["SC"], slow=True)
    DMA("sp", NG[:, 0:16], norm_mix_g.rearrange("(k p) -> p k", p=128), "c3", [], ["NG0"], slow=True)
    DMA("sp", NG[:, 16:32], norm_ffn_g.rearrange("(k p) -> p k", p=128), "c4", [], ["NG1"], slow=True)
    ACT(SC, SC, AF.Silu, ["SC"], ["SC"])
    CP("dve", SCB, bc(SC, [128, 16, 128], 2), ["SC"], ["SCB"])
    MSET("pool", ONES, 1.0, ["ONES"])
    w_ada_v = w_ada.rearrange("(k p) n -> p k n", p=128)
    b_ada_v = b_ada.rearrange("(o n) -> o n", o=1)
    for nb in range(24):
        b = nb % 2
        DMA("sp", WA[b], w_ada_v[:, :, nb * 512:(nb + 1) * 512], f"WA{b}", [], [f"WA{b}"])
        DMA("sp", BAb[b], b_ada[nb * 512:(nb + 1) * 512].partition_broadcast(128), f"BA{b}", [], [f"BA{b}"])
        pt = ps[b]
        for k in range(16):
            MM(pt[:, :], SCB[:, k, :], WA[b][:, k, :], k == 0, k == 15, ["SCB", f"WA{b}"], [f"ps{b}"])
        TT("dve", MODB[:, nb * 512:(nb + 1) * 512], pt[:, :], BAb[b], ALU.add, [f"ps{b}", f"BA{b}"], [f"MODB{nb}"])
    allmod = [f"MODB{nb}" for nb in range(24)]
    DMA("sp", MOD_d.rearrange("(o n) -> o n", o=1), MODB[0:1, :], "c5", allmod, ["MOD_d"])
    MODB3 = MODB.rearrange("p (a b) -> p a b", b=128)
    TT("dve", TMP0, MODB3, bc(IDF, [128, 96, 128], 1), ALU.mult, allmod + ["IDF"], ["TMP0"])
    RSUM("dve", COLS, TMP0, ["TMP0"], ["COLS"])
    STT("dve", GS[:, 0:16], COLS[:, 16:32], 1.0, NG[:, 0:16], ALU.add, ALU.mult, ["COLS", "NG0"], ["GS"])
    CP("dve", GS[:, 16:32], COLS[:, 0:16], ["COLS"], ["GS"])
    STT("dve", GS[:, 32:48], COLS[:, 64:80], 1.0, NG[:, 16:32], ALU.add, ALU.mult, ["COLS", "NG1"], ["GS"])
    CP("dve", GS[:, 48:64], COLS[:, 48:64], ["COLS"], ["GS"])
    P.barrier()
    if upto < 1:
        P.emit(); es.close(); return nc

    A.reset()
    WIN = A.t([128, 16, INC], BF16)
    WUKV = A.t([128, 4, 2048], BF16)
    XTd = [A.t([128, D], F32) for _ in range(2)]
    XBd = [A.t([128, D], BF16) for _ in range(2)]
    STX = A.t([128, 4], F32)
    JNKX = A.t([128, D], BF16)
    HT = A.t([128, 16, 128], BF16)
    QF = A.t([128, 1536], F32)
    CKV = A.t([128, 576], F32)
    UF = A.t([128, 1024], BF16)
    KVF = A.t([128, 2048], F32)
    SCR = A.t([128, 2048], F32)
    QN = A.t([128, 8, 192], BF16)
    KN = A.t([128, 8, 192], BF16)
    QTs = A.t([128, 8, 128], BF16)
    QRs = A.t([64, 8, 128], BF16)
    KTs = A.t([128, 8, 128], BF16)
    KRs = A.t([64, 8, 128], BF16)
    UTs = A.t([128, 8, 128], BF16)
    VS = A.t([128, 8, 128], BF16)
    CKN = A.t([128, 512], BF16)
    CKT = A.t([128, 4, 128], BF16)
    RC = A.t([128, 32], F32)
    RS = A.t([128, 32], F32)
    GQ = A.t([128, 192], F32)
    GK = A.t([128, 192], F32)
    KVG = A.t([128, 4], F32)
    ST = A.t([128, 32], F32)
    RT = A.t([128, 8, 32], F32)
    RT2 = A.t([128, 8, 32], F32)
    RT3 = A.t([128, 8, 32], F32)
    RT4 = A.t([128, 8, 32], F32)
    KRG = A.t([128, 64], F32)
    KRR = A.t([128, 64], F32)

    for k in range(16):
        DMA("pool", WIN[:, k, :], w_in[k * 128:(k + 1) * 128, :], f"WIN{k}", [], [f"WIN{k}"])
    for k in range(4):
        DMA("pool", WUKV[:, k, :], w_ukv[k * 128:(k + 1) * 128, :], f"WUKV{k}", [], [f"WUKV{k}"])
    DMA("sp", GQ, q_norm_g.partition_broadcast(128), "c6", [], ["GQ"])
    DMA("sp", GK, k_norm_g.partition_broadcast(128), "c7", [], ["GK"])
    DMA("sp", KVG, kv_norm_g.rearrange("(k p) -> p k", p=128), "c8", [], ["KVG"], slow=True)

    blocks = [(0, 512), (512, 512), (1024, 512), (1536, 512), (2048, 64), (2112, 512), (2624, 512)]
    import os
    P1STOP = int(os.environ.get("P1STOP", "-1"))

    class _Stop(Exception):
        pass

    def CK(n):
        if P1STOP == n:
            raise _Stop()
    def front_load(t_):
        pb_ = t_ % 2
        rr = t_ * 128
        DMA("sp", XTd[pb_], x[rr:rr + 128, :], f"XT{pb_}", [], [f"XT{pb_}"])

    def front_a(t_):
        pb_ = t_ % 2
        MSET("pool", STX[:, pb_ * 2:pb_ * 2 + 2], 0.0, [f"x_ss{pb_}", f"xr{pb_}"])
        ACT(JNKX, XTd[pb_], AF.Square, [f"XT{pb_}"], ["JNKX", f"x_ss{pb_}"], accum=STX[:, pb_ * 2:pb_ * 2 + 1])
        RSTD(STX[:, pb_ * 2 + 1:pb_ * 2 + 2], STX[:, pb_ * 2:pb_ * 2 + 1], D, f"xr{pb_}", f"x_ss{pb_}")
        ACT(XBd[pb_], XTd[pb_], AF.Copy, [f"XT{pb_}", f"xr{pb_}"], [f"XB{pb_}"], scale=STX[:, pb_ * 2 + 1:pb_ * 2 + 2])

    def p1_head(t):
        r0 = t * 128
        XB = XBd[t % 2]
        DMA("sp", RC, ropec[r0:r0 + 128, :], "RC", [], ["RC"])
        DMA("sp", RS, ropes[r0:r0 + 128, :], "RS", [], ["RS"])
        MSET("pool", ST, 0.0, ["c_ss", "qr_ss", "k_ssn", "kr_ss", "kr_ss2", "cr", "qr", "kr"])


    def p1_xT(t):
        r0 = t * 128
        XB = XBd[t % 2]
        allht = [f"HT{k}" for k in range(16)]
        for k in range(16):
            MM(ps[k // 4][:, (k % 4) * 128:(k % 4 + 1) * 128], XB[:, k * 128:(k + 1) * 128], IDB, True, True,
               [f"XB{t % 2}", "IDB"], [f"ps{k // 4}"])
        for k in range(16):
            src = ps[k // 4][:, (k % 4) * 128:(k % 4 + 1) * 128]
            if False:
                ACT(HT[:, k, :], src, AF.Identity, [f"ps{k // 4}", "GS"], [f"HT{k}"],
                    bias=GS[:, 16 + k:17 + k], scale=GS[:, k:k + 1])
            else:
                TS("dve", HT[:, k, :], src, GS[:, k:k + 1], GS[:, 16 + k:17 + k], ALU.mult, ALU.add,
                   [f"ps{k // 4}", "GS"], [f"HT{k}"])

    def p1_proj(t):
        allht = [f"HT{k}" for k in range(16)]
        for bi, (c0, w) in enumerate(blocks):
            pi = 4 + bi % 4
            pt = ps[pi]
            for k in range(16):
                MM(pt[:, 0:w], HT[:, k, :], WIN[:, k, c0:c0 + w], k == 0, k == 15, allht + [f"WIN{k}"], [f"ps{pi}"])
            if bi < 3:
                CP("act", QF[:, c0:c0 + w], pt[:, 0:w], [f"ps{pi}"], [f"QF{bi}"])
            elif bi == 3:
                CP("dve", CKV[:, 0:512], pt[:, 0:w], [f"ps{pi}"], ["CKVa"])
            elif bi == 4:
                CP("dve", CKV[:, 512:576], pt[:, 0:w], [f"ps{pi}"], ["CKVb"])
            else:
                CP("act", UF[:, c0 - 2112:c0 - 2112 + w], pt[:, 0:w], [f"ps{pi}"], [f"UF{bi}"])


    def p1_ckv(t):
        r0 = t * 128
        allkv = [f"KVF{nb}" for nb in range(4)]
        ACT(JNKX[:, 0:512], CKV[:, 0:512], AF.Square, ["CKVa"], ["JNKX", "c_ss"], accum=ST[:, 2:3])
        RSTD(ST[:, 3:4], ST[:, 2:3], 512, "cr", "c_ss")
        ACT(CKN, CKV[:, 0:512], AF.Copy, ["CKVa", "cr"], ["CKN"], scale=ST[:, 3:4])
        for k in range(4):
            MM(ps[0][:, k * 128:(k + 1) * 128], CKN[:, k * 128:(k + 1) * 128], IDB, True, True, ["CKN", "IDB"], ["ps0"])
        for k in range(4):
            TS("dve", CKT[:, k, :], ps[0][:, k * 128:(k + 1) * 128], KVG[:, k:k + 1], None, ALU.mult, None,
               ["ps0", "KVG"], ["CKT"])
        for nb in range(4):
            pi = 4 + nb
            for k in range(4):
                MM(ps[pi][:, :], CKT[:, k, :], WUKV[:, k, nb * 512:(nb + 1) * 512], k == 0, k == 3,
                   ["CKT", f"WUKV{k}"], [f"ps{pi}"])
            CP("act", KVF[:, nb * 512:(nb + 1) * 512], ps[pi][:, :], [f"ps{pi}"], [f"KVF{nb}"])
        allkv = [f"KVF{nb}" for nb in range(4)]


    def p1_qchain(t):
        r0 = t * 128
        allq = ["QF0", "QF1", "QF2"]
        allq = ["QF0", "QF1", "QF2"]
        QF3 = QF.rearrange("p (h d) -> p h d", d=192)
        SCR3 = SCR[:, 0:1536].rearrange("p (h d) -> p h d", d=192)
        TT("pool", SCR[:, 0:1536], QF, QF, ALU.mult, allq, ["SCR"])
        RSUM("dve", ST[:, 8:16], SCR3, ["SCR"], ["qr_ss"])
        RSTD(ST[:, 8:16], ST[:, 8:16], 192, "qr", "qr_ss")
        TT("pool", QF3, QF3, bc(ST[:, 8:16], [128, 8, 192], 2), ALU.mult, allq + ["qr"], ["QFn"])
        TT("pool", QF3, QF3, bc(GQ, [128, 8, 192], 1), ALU.mult, ["QFn", "GQ"], ["QFn"])
        cosb = bc(RC, [128, 8, 32], 1)
        sinb = bc(RS, [128, 8, 32], 1)
        TT("pool", RT, QF3[:, :, 128:160], cosb, ALU.mult, ["QFn", "RC"], ["RT"])
        TT("pool", RT2, QF3[:, :, 160:192], sinb, ALU.mult, ["QFn", "RS"], ["RT2"])
        TT("pool", RT3, QF3[:, :, 160:192], cosb, ALU.mult, ["QFn", "RC"], ["RT3"])
        TT("pool", RT4, QF3[:, :, 128:160], sinb, ALU.mult, ["QFn", "RS"], ["RT4"])
        TT("dve", QN[:, :, 128:160], RT, RT2, ALU.subtract, ["RT", "RT2"], ["QNa"])
        TT("dve", QN[:, :, 160:192], RT3, RT4, ALU.add, ["RT3", "RT4"], ["QNb"])
        CP("pool", QN[:, :, 0:128], QF3[:, :, 0:128], ["QFn"], ["QNc"])
        allqn = ["QNa", "QNb", "QNc"]


    def p1_qT(t):
        r0 = t * 128
        allqn = ["QNa", "QNb", "QNc"]
        for h in range(8):
            MM(ps[h // 4][:, (h % 4) * 128:(h % 4 + 1) * 128], QN[:, h, 0:128], IDB, True, True,
               allqn + ["IDB"], [f"ps{h // 4}"])
            MM(ps[2 + h // 4][0:64, (h % 4) * 128:(h % 4 + 1) * 128], QN[:, h, 128:192], IDB, True, True,
               allqn + ["IDB"], [f"ps{2 + h // 4}"])
        for hb in range(2):
            CP("dve", QTs[:, hb * 4:(hb + 1) * 4, :], ps[hb][:, :].rearrange("p (a b) -> p a b", b=128),
               [f"ps{hb}"], [f"QTs{hb}"])
            CP("act", QRs[:, hb * 4:(hb + 1) * 4, :], ps[2 + hb][0:64, :].rearrange("p (a b) -> p a b", b=128),
               [f"ps{2 + hb}"], [f"QRs{hb}"])
        DMA("sp", QT_d[:, :, r0:r0 + 128].rearrange("h d t -> d h t"), QTs, "sQT", ["QTs0", "QTs1"], [])
        DMA("sp", QR_d[:, :, r0:r0 + 128].rearrange("h d t -> d h t"), QRs, "sQR", ["QRs0", "QRs1"], [])


    def p1_kchain(t):
        r0 = t * 128
        allkv = [f"KVF{nb}" for nb in range(4)]
        allq = ["QF0", "QF1", "QF2"]
        KV3 = KVF.rearrange("p (h d) -> p h d", d=256)
        SCRk = SCR[:, 0:1024].rearrange("p (h d) -> p h d", d=128)
        TT("pool", SCRk, KV3[:, :, 0:128], KV3[:, :, 0:128], ALU.mult, allkv, ["SCR"])
        RSUM("dve", ST[:, 16:24], SCRk, ["SCR"], ["k_ssn"])
        TT("pool", KRG, CKV[:, 512:576], CKV[:, 512:576], ALU.mult, ["CKVb"], ["KRG"])
        RSUM("dve", ST[:, 4:5], KRG, ["KRG"], ["kr_ss"])
        TS("dve", ST[:, 16:24], ST[:, 16:24], ST[:, 4:5], None, ALU.add, None, ["k_ssn", "kr_ss"], ["kr_ss2"])
        TS("dve", ST[:, 16:24], ST[:, 16:24], 1.0 / 192, EPS, ALU.mult, ALU.add, ["kr_ss2"], ["kr"])
        P.op("act", lambda e: e.sqrt(ST[:, 16:24], ST[:, 16:24]), reads=["kr"], writes=["kr"])
        P.op("dve", lambda e: e.reciprocal(ST[:, 16:24], ST[:, 16:24]), reads=["kr"], writes=["kr"])
        TT("pool", SCRk, KV3[:, :, 0:128], bc(ST[:, 16:24], [128, 8, 128], 2), ALU.mult, allkv + ["kr"], ["SCR"])
        TT("pool", KN[:, :, 0:128], SCRk, bc(GK[:, 0:128], [128, 8, 128], 1), ALU.mult, ["SCR", "GK"], ["KNc"])
        TT("pool", KRG, CKV[:, 512:576], GK[:, 128:192], ALU.mult, ["CKVb", "GK", "kr_ss"], ["KRG"])
        TT("pool", RT[:, 0, :], KRG[:, 0:32], RC, ALU.mult, ["KRG", "RC", "QNa"], ["RT"])
        TT("pool", RT2[:, 0, :], KRG[:, 32:64], RS, ALU.mult, ["KRG", "RS", "QNa"], ["RT2"])
        TT("pool", RT3[:, 0, :], KRG[:, 32:64], RC, ALU.mult, ["KRG", "RC", "QNb"], ["RT3"])
        TT("pool", RT4[:, 0, :], KRG[:, 0:32], RS, ALU.mult, ["KRG", "RS", "QNb"], ["RT4"])
        TT("dve", KRR[:, 0:32], RT[:, 0, :], RT2[:, 0, :], ALU.subtract, ["RT", "RT2"], ["KRRa"])
        TT("dve", KRR[:, 32:64], RT3[:, 0, :], RT4[:, 0, :], ALU.add, ["RT3", "RT4"], ["KRRb"])
        TT("pool", KN[:, :, 128:192], bc(KRR, [128, 8, 64], 1), bc(ST[:, 16:24], [128, 8, 64], 2), ALU.mult,
           ["KRRa", "KRRb", "kr"], ["KNr"])
        CP("dve", VS, KV3[:, :, 128:256], allkv, ["VS"])
        DMA("sp", V_d[:, r0:r0 + 128, :].rearrange("h t d -> t h d"), VS, "sV", ["VS"], [])


    def p1_kT(t):
        r0 = t * 128
        allkn = ["KNc", "KNr"]
        for h in range(8):
            MM(ps[h // 4][:, (h % 4) * 128:(h % 4 + 1) * 128], KN[:, h, 0:128], IDB, True, True,
               allkn + ["IDB"], [f"ps{h // 4}"])
            MM(ps[2 + h // 4][0:64, (h % 4) * 128:(h % 4 + 1) * 128], KN[:, h, 128:192], IDB, True, True,
               allkn + ["IDB"], [f"ps{2 + h // 4}"])
        for hb in range(2):
            CP("dve", KTs[:, hb * 4:(hb + 1) * 4, :], ps[hb][:, :].rearrange("p (a b) -> p a b", b=128),
               [f"ps{hb}"], [f"KTs{hb}"])
            CP("act", KRs[:, hb * 4:(hb + 1) * 4, :], ps[2 + hb][0:64, :].rearrange("p (a b) -> p a b", b=128),
               [f"ps{2 + hb}"], [f"KRs{hb}"])
        DMA("sp", KT_d[:, :, r0:r0 + 128].rearrange("h d t -> d h t"), KTs, "sKT", ["KTs0", "KTs1"], [])
        DMA("sp", KR_d[:, :, r0:r0 + 128].rearrange("h d t -> d h t"), KRs, "sKR", ["KRs0", "KRs1"], [])


    def p1_u(t):
        r0 = t * 128
        for k in range(8):
            MM(ps[4 + k // 4][:, (k % 4) * 128:(k % 4 + 1) * 128], UF[:, k * 128:(k + 1) * 128], IDB, True, True,
               ["UF5", "UF6", "IDB"], [f"ps{4 + k // 4}"])
        for hb in range(2):
            CP("act" if hb == 0 else "dve", UTs[:, hb * 4:(hb + 1) * 4, :],
               ps[4 + hb][:, :].rearrange("p (a b) -> p a b", b=128), [f"ps{4 + hb}"], [f"UTs{hb}"])
        DMA("sp", UT_d.rearrange("(b f) t -> f b t", f=128)[:, :, r0:r0 + 128], UTs, "sUT", ["UTs0", "UTs1"], [])

    front_load(0)
    front_a(0)
    p1_xT(0)
    for t in range(NT):
        p1_head(t)
        if t + 1 < NT:
            front_load(t + 1)
        p1_proj(t)
        if t + 1 < NT:
            front_a(t + 1)
        if t > 0:
            p1_qT(t - 1)
            p1_kT(t - 1)
        if t + 1 < NT:
            p1_xT(t + 1)
        p1_ckv(t)
        p1_qchain(t)
        p1_u(t)
        p1_kchain(t)
    p1_qT(NT - 1)
    p1_kT(NT - 1)
    P.barrier()
    if upto < 2:
        P.emit(); es.close(); return nc

    if upto < 3:
        P.emit(); es.close(); return nc

    NJ = L // 8
    NJC = L // 1024
    WINC_d = dscr("WINC_d", [128, 256 * 128], BF16)
    WOUT_d = dscr("WOUT_d", [128, 256 * 128], BF16)
    M_d = dscr("M_d", [128, 64 * 128], BF16)
    INC_d = [dscr(f"INC{d}_d", [128, NJC, 128 * 64], F32) for d in range(2)]
    S_d = [dscr(f"S{d}_d", [128, NJC, 64, 128], BF16) for d in range(2)]
    YT_d = dscr("YT_d", [1024, L], BF16)
    SSMT_d = dscr("SSMT_d", [1024, L], BF16)

    A.reset()
    LR = A.t([128, 64], F32); LI = A.t([128, 64], F32); DT = A.t([128, 64], F32)
    XX = A.t([128, 64], F32); TH = A.t([128, 64], F32)
    CC = A.t([128, 64], F32); SS = A.t([128, 64], F32)
    T1 = A.t([128, 64], F32); T2 = A.t([128, 64], F32)
    HPI = A.t([128, 1], F32)
    UR = A.t([128, 9, 64], F32); UI = A.t([128, 9, 64], F32)
    ER = A.t([128, 9, 64], F32); EI = A.t([128, 9, 64], F32)
    EIR = A.t([128, 9, 64], F32); EII = A.t([128, 9, 64], F32)
    MG = A.t([128, 64], F32); IMG = A.t([128, 64], F32)
    QRE = A.t([128, 64], F32); QIM = A.t([128, 64], F32)
    BRE = A.t([128, 32, 16], F32); BIM = A.t([128, 32, 16], F32)
    BBR = A.t([128, 2, 32, 16], F32); BBI = A.t([128, 2, 32, 16], F32)
    CRE = A.t([128, 2, 32, 16], F32); CIM = A.t([128, 2, 32, 16], F32)
    CN2 = [A.t([128, 128], F32) for _ in range(2)]
    TA = A.t([128, 32, 16], F32); TB = A.t([128, 32, 16], F32)
    TA2 = A.t([128, 16, 16], F32); TB2 = A.t([128, 16, 16], F32)
    TA2 = A.t([128, 16, 16], F32); TB2 = A.t([128, 16, 16], F32)
    XR = A.t([128, 16, 8, 16], F32); XI = A.t([128, 16, 8, 16], F32)
    WR = A.t([128, 16, 8, 16], F32); WI = A.t([128, 16, 8, 16], F32)
    WTR = A.t([128, 16, 8, 16], F32); WTI = A.t([128, 16, 8, 16], F32)
    TD = A.t([128, 16, 8, 16], F32)
    MACC = A.t([128, 64, 128], F32)
    MB = A.t([128, 64, 128], BF16)
    MKF = A.t([128, 128], F32); MKB = A.t([128, 128], F32)
    IH = [A.t([128, 128], F32) for _ in range(2)]
    DCOL = A.t([128, 64], F32)
    WINs = [A.t([128, 4, 128], BF16) for _ in range(2)]
    WOUs = [A.t([128, 4, 128], BF16) for _ in range(2)]
    TMPM = A.t([128, 128], F32)

    def rawap(ap, off, dims):
        return bass.AP(ap.tensor, off, dims)
    for q4 in range(4):
        DMA("sp", LR[:, q4 * 16:(q4 + 1) * 16], rawap(lam_re, (q4 // 2) * 4096 + (q4 % 2) * 16 * 128, [[1, 128], [128, 16]]),
            f"g{q4}", [], ["LR"], slow=True)
        DMA("sp", LI[:, q4 * 16:(q4 + 1) * 16], rawap(lam_im, (q4 // 2) * 4096 + (q4 % 2) * 16 * 128, [[1, 128], [128, 16]]),
            f"g{4 + q4}", [], ["LI"], slow=True)
    for gpar in range(2):
        DMA("sp", DT[gpar * 64:(gpar + 1) * 64, :].rearrange("p (d g) -> p d g", d=2),
            rawap(log_dt, gpar, [[0, 64], [64, 2], [2, 32]]), f"g{8 + gpar}", [], ["DT"], slow=True)
    for q4 in range(4):
        DMA("sp", BRE[:, q4 * 8:(q4 + 1) * 8, :], rawap(b_re, q4 * 8 * 2048, [[16, 128], [2048, 8], [1, 16]]),
            f"g{10 + q4}", [], ["BRE"], slow=True)
        DMA("sp", BIM[:, q4 * 8:(q4 + 1) * 8, :], rawap(b_im, q4 * 8 * 2048, [[16, 128], [2048, 8], [1, 16]]),
            f"g{14 + q4}", [], ["BIM"], slow=True)
    for s in range(8):
        DMA("sp", DCOL[s * 16:(s + 1) * 16, :], rawap(d_skip, 0, [[1, 16], [16, 64]]), f"g{18 + s}", [], ["DCOL"], slow=True)
    DMA("sp", MKF, maskf_d, "g26", [], ["MKF"])
    DMA("sp", MKB, maskb_d, "g27", [], ["MKB"])
    MSET("pool", HPI, float(np.pi / 2), ["HPI"])
    MSET("pool", WOUs[0], 0.0, ["WOUs0"])
    MSET("pool", WOUs[1], 0.0, ["WOUs1"])
    for hh in range(2):
        CP("pool", IH[hh], IDF, ["IDF"], [f"IH{hh}"])
        MSET("pool", IH[hh][(1 - hh) * 64:(2 - hh) * 64, :], 0.0, [f"IH{hh}"])
    it = 0
    for ri, csrc, cdst in ((0, c_re, CRE), (1, c_im, CIM)):
        for d in range(2):
            cv = csrc[d].rearrange("g c p -> (g c) p")
            for gb in range(8):
                b = it % 2
                it += 1
                DMA("sp", CN2[b][:, 0:64], cv[gb * 128:(gb + 1) * 128, :], f"CNa{b}", [], [f"CN2a{b}"])
                DMA("sp", CN2[b][:, 64:128], cv[gb * 128:(gb + 1) * 128, :], f"CNb{b}", [], [f"CN2b{b}"])
                MM(ps[b][:, 0:128], CN2[b], IDF, True, True, [f"CN2a{b}", f"CN2b{b}", "IDF"], [f"ps{b}"])
                pv = ps[b][:, 0:128].rearrange("p (g c) -> p g c", c=16)
                CP("dve", cdst[0:64, d, gb * 4:(gb + 1) * 4, :], pv[0:64, 0:8:2, :], [f"ps{b}"], [f"C{ri}"])
                CP("act", cdst[64:128, d, gb * 4:(gb + 1) * 4, :], pv[64:128, 1:8:2, :], [f"ps{b}"], [f"C{ri}"])

    def E(out, a, b_, op, r, w, eng="dve"):
        TT(eng, out, a, b_, op, r, w)
    ACT(DT, DT, AF.Exp, ["DT"], ["DT"])
    E(XX, LR, DT, ALU.mult, ["LR", "DT"], ["XX"])
    E(TH, LI, DT, ALU.mult, ["LI", "DT"], ["TH"])
    ACT(SS, TH, AF.Sin, ["TH"], ["SS"], scale=1.0 / 16)
    ACT(CC, TH, AF.Sin, ["TH", "HPI"], ["CC"], scale=1.0 / 16, bias=HPI[:, 0:1])
    for i in range(4):
        E(T1, CC, CC, ALU.mult, ["CC"], ["T1"])
        E(T2, SS, SS, ALU.mult, ["SS"], ["T2"])
        STT("dve", SS, CC, 2.0, SS, ALU.mult, ALU.mult, ["CC", "SS"], ["SS"])
        E(CC, T1, T2, ALU.subtract, ["T1", "T2"], ["CC"])
    CP("dve", UR[:, 1, :], CC, ["CC"], ["U"])
    CP("dve", UI[:, 1, :], SS, ["SS"], ["U"])
    for k in range(2, 9):
        E(T1, UR[:, k - 1, :], CC, ALU.mult, ["U", "CC"], ["T1"])
        E(T2, UI[:, k - 1, :], SS, ALU.mult, ["U", "SS"], ["T2"])
        E(UR[:, k, :], T1, T2, ALU.subtract, ["T1", "T2"], ["U"])
        E(T1, UR[:, k - 1, :], SS, ALU.mult, ["U", "SS"], ["T1"])
        E(T2, UI[:, k - 1, :], CC, ALU.mult, ["U", "CC"], ["T2"])
        E(UI[:, k, :], T1, T2, ALU.add, ["T1", "T2"], ["U"])
    for k in range(1, 9):
        ACT(MG, XX, AF.Exp, ["XX"], ["MG"], scale=float(k))
        ACT(IMG, XX, AF.Exp, ["XX"], ["IMG"], scale=float(-k))
        E(ER[:, k, :], MG, UR[:, k, :], ALU.mult, ["MG", "U"], ["ET"])
        E(EI[:, k, :], MG, UI[:, k, :], ALU.mult, ["MG", "U"], ["ET"])
        E(EIR[:, k, :], IMG, UR[:, k, :], ALU.mult, ["IMG", "U"], ["ET"])
        STT("dve", EII[:, k, :], UI[:, k, :], -1.0, IMG, ALU.mult, ALU.mult, ["IMG", "U"], ["ET"])
    for d in range(2):
        dsl = slice(d * 32, (d + 1) * 32)
        CP("dve", A8T[:, d, 0, 0:32], ER[:, 8, dsl], ["ET"], ["A8T"])
        CP("dve", A8T[:, d, 0, 32:64], ER[:, 8, dsl], ["ET"], ["A8T"])
        TS("dve", A8T[:, d, 1, 0:32], EI[:, 8, dsl], -1.0, None, ALU.mult, None, ["ET"], ["A8T"])
        CP("dve", A8T[:, d, 1, 32:64], EI[:, 8, dsl], ["ET"], ["A8T"])
    TS("dve", T1, ER[:, 1, :], -1.0, None, ALU.add, None, ["ET"], ["T1"])
    E(QRE, T1, LR, ALU.mult, ["T1", "LR"], ["QRE"])
    E(T2, EI[:, 1, :], LI, ALU.mult, ["ET", "LI"], ["T2"])
    E(QRE, QRE, T2, ALU.add, ["QRE", "T2"], ["QRE"])
    E(QIM, EI[:, 1, :], LR, ALU.mult, ["ET", "LR"], ["QIM"])
    E(T2, T1, LI, ALU.mult, ["T1", "LI"], ["T2"])
    E(QIM, QIM, T2, ALU.subtract, ["QIM", "T2"], ["QIM"])
    E(T1, LR, LR, ALU.mult, ["LR"], ["T1"])
    E(T2, LI, LI, ALU.mult, ["LI"], ["T2"])
    E(T1, T1, T2, ALU.add, ["T1", "T2"], ["T1"])
    P.op("dve", lambda e: e.reciprocal(T1, T1), reads=["T1"], writes=["T1"])
    E(QRE, QRE, T1, ALU.mult, ["QRE", "T1"], ["QRE"])
    E(QIM, QIM, T1, ALU.mult, ["QIM", "T1"], ["QIM"])
    for d in range(2):
        dsl = slice(d * 32, (d + 1) * 32)
        qr = bc(QRE[:, dsl], [128, 32, 16], 2)
        qi = bc(QIM[:, dsl], [128, 32, 16], 2)
        E(TA, qr, BRE, ALU.mult, ["QRE", "BRE"], ["TA"])
        E(TB, qi, BIM, ALU.mult, ["QIM", "BIM"], ["TB"])
        E(BBR[:, d], TA, TB, ALU.subtract, ["TA", "TB"], ["BBR"])
        E(TA, qr, BIM, ALU.mult, ["QRE", "BIM"], ["TA"])
        E(TB, qi, BRE, ALU.mult, ["QIM", "BRE"], ["TB"])
        E(BBI[:, d], TA, TB, ALU.add, ["TA", "TB"], ["BBI"])

    for d in range(2):
      for gph in range(2):
        dsl = slice(d * 32 + gph * 16, d * 32 + (gph + 1) * 16)
        gsl = slice(gph * 16, (gph + 1) * 16)
        TAh = TA[:, 0:16, :]
        TBh = TB[:, 0:16, :]
        for s in range(8):
            k = s + 1 if d == 0 else 8 - s
            er = bc(EIR[:, k, dsl], [128, 16, 16], 2)
            ei = bc(EII[:, k, dsl], [128, 16, 16], 2)
            E(TAh, er, BBR[:, d, gsl], ALU.mult, ["ET", "BBR"], ["TA"])
            E(TBh, ei, BBI[:, d, gsl], ALU.mult, ["ET", "BBI"], ["TB"])
            E(XR[:, :, s, :], TAh, TBh, ALU.subtract, ["TA", "TB"], ["XR"])
            E(TAh, er, BBI[:, d, gsl], ALU.mult, ["ET", "BBI"], ["TA"])
            E(TBh, ei, BBR[:, d, gsl], ALU.mult, ["ET", "BBR"], ["TB"])
            E(XI[:, :, s, :], TAh, TBh, ALU.add, ["TA", "TB"], ["XI"])
        for t in range(8):
            k = t + 1 if d == 0 else 8 - t
            er = bc(ER[:, k, dsl], [128, 16, 16], 2)
            ei = bc(EI[:, k, dsl], [128, 16, 16], 2)
            E(TA2, CRE[:, d, gsl], er, ALU.mult, ["ET", "C0"], ["TA2"], eng="pool")
            E(TB2, CIM[:, d, gsl], ei, ALU.mult, ["ET", "C1"], ["TB2"], eng="pool")
            E(WR[:, :, t, :], TA2, TB2, ALU.subtract, ["TA2", "TB2"], ["WR"], eng="pool")
            E(TA2, CRE[:, d, gsl], ei, ALU.mult, ["ET", "C0"], ["TA2"], eng="pool")
            E(TB2, CIM[:, d, gsl], er, ALU.mult, ["ET", "C1"], ["TB2"], eng="pool")
            E(TA2, TA2, TB2, ALU.add, ["TA2", "TB2"], ["TA2"], eng="pool")
            TS("pool", WI[:, :, t, :], TA2, -1.0, 0.0, ALU.mult, ALU.add, ["TA2"], ["WI"])
        X3 = lambda tt_: tt_.rearrange("p g s c -> p g (s c)")
        e8r = bc(ER[:, 8, dsl], [128, 16, 128], 2)
        e8i = bc(EI[:, 8, dsl], [128, 16, 128], 2)
        E(X3(WTR), e8r, X3(XR), ALU.mult, ["ET", "XR"], ["WTR"])
        E(X3(TD), e8i, X3(XI), ALU.mult, ["ET", "XI"], ["TD"])
        E(X3(WTR), X3(WTR), X3(TD), ALU.subtract, ["WTR", "TD"], ["WTR"])
        E(X3(WTI), e8r, X3(XI), ALU.mult, ["ET", "XI"], ["WTI"])
        E(X3(TD), e8i, X3(XR), ALU.mult, ["ET", "XR"], ["TD"])
        E(X3(WTI), X3(WTI), X3(TD), ALU.add, ["WTI", "TD"], ["WTI"])
        for gpl in range(16):
            gp = gph * 16 + gpl
            sb_ = gp % 2
            for gpar in range(2):
                g = 2 * gp + gpar
                hs = slice(gpar * 64, (gpar + 1) * 64)
                pm = ps[gpar]
                MM(pm[:, 0:128], X3(XR)[hs, gpl, :], X3(WR)[hs, gpl, :], True, False, ["XR", "WR"], [f"ps{gpar}"])
                MM(pm[:, 0:128], X3(XI)[hs, gpl, :], X3(WI)[hs, gpl, :], False, True, ["XI", "WI"], [f"ps{gpar}"])
                if d == 0:
                    TT("dve", MACC[:, g, :], pm[:, 0:128], MKF, ALU.mult, [f"ps{gpar}", "MKF"], [f"MACC{g}"])
                else:
                    TT("dve", TMPM, pm[:, 0:128], MKB, ALU.mult, [f"ps{gpar}", "MKB"], ["TMPM"])
                    TT("dve", MACC[:, g, :], MACC[:, g, :], TMPM, ALU.add, [f"MACC{g}", "TMPM"], [f"MACC{g}"])
                for ri in range(2):
                    src = X3(WTR if ri == 0 else WTI)
                    pw = ps[2 + gpar * 2 + ri]
                    MM(pw[:, 0:128], src[:, gpl, :], IH[gpar], True, True, ["WTR", "WTI", f"IH{gpar}"],
                       [f"ps{2 + gpar * 2 + ri}"])
                    CP("act", WINs[sb_][:, gpar * 2 + ri, :], pw[:, 0:128], [f"ps{2 + gpar * 2 + ri}"],
                       [f"WINs{sb_}"])
                    CP("pool", WOUs[sb_][hs, gpar * 2 + ri, :], X3(WR if ri == 0 else WI)[hs, gpl, :],
                       ["WR", "WI"], [f"WOUs{sb_}"])
            base = (d * 128 + 4 * gp) * 128
            DMA("sp", WINC_d[:, base:base + 512], WINs[sb_].rearrange("p a b -> p (a b)"), f"sWIN{sb_}",
                [f"WINs{sb_}"], ["WINC_d"])
            DMA("sp", WOUT_d[:, base:base + 512], WOUs[sb_].rearrange("p a b -> p (a b)"), f"sWOU{sb_}",
                [f"WOUs{sb_}"], ["WOUT_d"])
    for g in range(64):
        STT("dve", MACC[:, g, :], IDF, DCOL[:, g:g + 1], MACC[:, g, :], ALU.mult, ALU.add,
            [f"MACC{g}", "IDF", "DCOL"], [f"MACC{g}"])
    CP("act", MB, MACC, [f"MACC{g}" for g in range(64)], ["MB"])
    DMA("sp", M_d, MB.rearrange("p a b -> p (a b)"), "sMB", ["MB"], ["M_d"])
    P.barrier()
    if upto < 4:
        P.emit(); es.close(); return nc

    A.reset()
    WINC = A.t([128, 256, 128], BF16)
    SEL = A.t([128, 64, 128], BF16)
    UTc = A.t([128, 8, 1024], BF16)
    UGall = A.t([128, 64, 128], BF16)
    INCc = [A.t([128, 128, 64], F32) for _ in range(2)]
    for i4 in range(4):
        DMA("sp", WINC[:, i4 * 64:(i4 + 1) * 64, :].rearrange("p a b -> p (a b)"),
            WINC_d[:, i4 * 64 * 128:(i4 + 1) * 64 * 128], f"lWINC{i4}", [], [f"WINC{i4}"])
    allwinc = [f"WINC{i4}" for i4 in range(4)]
    DMA("pool", SEL, sel_d, "lSEL", [], ["SEL"])
    UT_v = UT_d.rearrange("(b f) t -> f b t", f=128)
    for jc in range(NJC):
        t0 = jc * 1024
        DMA("sp", UTc, UT_v[:, :, t0:t0 + 1024], "lUTc", [], ["UTc"])
        for g4 in range(16):
            pb = g4 % 2
            for gi in range(4):
                g = g4 * 4 + gi
                blk, gl = g // 8, g % 8
                for s in range(8):
                    MM(ps[pb][:, gi * 128:(gi + 1) * 128], SEL[:, s * 8 + gl, :], UTc[:, blk, s:1024:8],
                       s == 0, s == 7, ["SEL", "UTc"], [f"ps{pb}"])
            CP("act" if pb == 0 else "dve", UGall[:, g4 * 4:(g4 + 1) * 4, :],
               ps[pb][:, :].rearrange("p (a b) -> p a b", b=128), [f"ps{pb}"], [f"UG{g4}"])
        allug = [f"UG{g4}" for g4 in range(16)]
        n = 0
        for d in range(2):
            for gp in range(32):
                pb = 2 + n % 4
                n += 1
                for ri in range(2):
                    for gpar in range(2):
                        g = 2 * gp + gpar
                        MM(ps[pb][:, ri * 128:(ri + 1) * 128], WINC[:, (d * 64 + g) * 2 + ri, :], UGall[:, g, :],
                           gpar == 0, gpar == 1, allwinc + allug, [f"ps{pb}"])
                dst = INCc[d].rearrange("p j (r g) -> p r j g", r=2)[:, :, :, gp]
                CP("act" if n % 2 == 0 else "dve", dst, ps[pb][:, 0:256].rearrange("p (r j) -> p r j", r=2),
                   [f"ps{pb}"], [f"INCc{d}"])
        for d in range(2):
            DMA("sp", INC_d[d][:, jc, :], INCc[d].rearrange("p a b -> p (a b)"), f"sINC{d}", [f"INCc{d}"], [f"INC_d{d}"])
    P.barrier()
    if upto < 5:
        P.emit(); es.close(); return nc

    A.reset()
    KTh = [A.t([128, L], BF16) for _ in range(1)]
    KRh = [A.t([128, L], BF16) for _ in range(1)]
    Vh = [A.t([128, NT, 128], BF16) for _ in range(1)]
    QTb = [A.t([128, 512], BF16) for _ in range(2)]
    QRb = [A.t([128, 512], BF16) for _ in range(2)]
    PT = [A.t([128, 512], BF16) for _ in range(4)]
    RSb = A.t([128, 512], F32)
    ATn = A.t([128, 512], F32)
    SQa = A.t([128, 512], BF16)
    ATo = [A.t([128, 512], BF16) for _ in range(2)]
    ONESB = A.t([128, 128], BF16)
    ONEC2 = A.t([128, 2], BF16)
    AOGc = A.t([128, 8], F32)
    HJ = 64
    INs = [A.t([128, HJ, 64], F32) for _ in range(2)]
    SO = [A.t([128, HJ + 1, 64], F32) for _ in range(2)]
    SB16 = [A.t([128, 64, 128], BF16) for _ in range(2)]
    PQ = [A.t([128, 2, 64], F32) for _ in range(2)]
    G2Bt = A.t([128, 2048], F32)
    WSTG = [A.t([128, 4, 1024], BF16) for _ in range(2)]
    DMA("sp", G2Bt, MOD_d[5 * D:6 * D].partition_broadcast(128), "lG2Bt", [], ["G2Bt"])
    MSET("pool", ONESB, 1.0, ["ONESB"])
    MSET("pool", ONEC2, 1.0, ["ONEC2"])
    MSET("pool", KRh[0][64:128, :], 0.0, ["KRh0"])
    for b_ in range(2):
        MSET("pool", QRb[b_][64:128, :], 0.0, [f"QRb{b_}"])
    DMA("sp", AOGc, att_out_g.rearrange("(k p) -> p k", p=128), "lAOGc", [], ["AOGc"], slow=True)

    def scan_gen(d, eng):
        if d == 0:
            MSET(eng, SO[0][:, 0, :], 0.0, ["SO0"])
        else:
            MSET(eng, SO[1][:, HJ, :], 0.0, ["SO1"])
        AA = A8T[:, d, 0, :]
        AIMS = A8T[:, d, 1, :]
        nh = 128 // HJ
        for step in range(NJC * nh):
            hc = step if d == 0 else NJC * nh - 1 - step
            jc, hh = hc // nh, hc % nh
            DMA("pool", INs[d].rearrange("p a b -> p (a b)"), INC_d[d][:, jc, hh * HJ * 64:(hh + 1) * HJ * 64],
                f"lIN{d}", [f"INC_d{d}"], [f"INs{d}"])
            for ii in range(HJ):
                i = ii if d == 0 else HJ - 1 - ii
                src = SO[d][:, i, :] if d == 0 else SO[d][:, i + 1, :]
                dst = SO[d][:, i + 1, :] if d == 0 else SO[d][:, i, :]
                swp = src.rearrange("p (r g) -> p r g", r=2)[:, ::-1, :]
                TT(eng, PQ[d][:, 0, :], AA, src, ALU.mult, [f"SO{d}", "A8T"], [f"P{d}"])
                TT(eng, PQ[d][:, 1, :].rearrange("p (r g) -> p r g", r=2), AIMS.rearrange("p (r g) -> p r g", r=2), swp,
                   ALU.mult, [f"SO{d}", "A8T"], [f"Q{d}"])
                TT(eng, PQ[d][:, 0, :], PQ[d][:, 0, :], PQ[d][:, 1, :], ALU.add, [f"P{d}", f"Q{d}"], [f"P{d}"])
                TT(eng, dst, PQ[d][:, 0, :], INs[d][:, i, :], ALU.add, [f"P{d}", f"INs{d}"], [f"SO{d}"])
                yield
            lo = 0 if d == 0 else 1
            CP(eng, SB16[d][:, :, hh * HJ:(hh + 1) * HJ], SO[d][:, lo:lo + HJ, :].rearrange("p j c -> p c j"),
               [f"SO{d}"], [f"SB16{d}"])
            last_half = (hh == nh - 1) if d == 0 else (hh == 0)
            if last_half:
                DMA("pool", S_d[d][:, jc], SB16[d], f"sS{d}", [f"SB16{d}"], [f"S_d{d}"])
            if d == 0:
                CP(eng, SO[d][:, 0, :], SO[d][:, HJ, :], [f"SO{d}"], [f"SO{d}"])
            else:
                CP(eng, SO[d][:, HJ, :], SO[d][:, 0, :], [f"SO{d}"], [f"SO{d}"])
        while True:
            yield

    def att_tail(h, qb, ib, po, pz):
        P.op("dve", (lambda pz_: lambda e: e.reciprocal(RSb, ps[pz_][:, :]))(pz), reads=[f"ps{pz}"], writes=["RSb"])
        TT("dve", ATn, ps[po][:, :], RSb, ALU.mult, [f"ps{po}", "RSb"], ["ATn"])
        TT("pool", SQa, ATn, ATn, ALU.mult, ["ATn"], ["SQa"])
        for qs in range(4):
            MM(ps[6][:, qs:qs + 1], SQa[:, qs * 128:(qs + 1) * 128], ONEC2[:, 0:1], True, True, ["SQa", "ONEC2"], ["ps6"])
        if h == 0:
            CP("dve", SSA[:, qb * 4:(qb + 1) * 4], ps[6][:, 0:4], ["ps6"], [f"SSA{qb}"])
        else:
            TT("dve", SSA[:, qb * 4:(qb + 1) * 4], SSA[:, qb * 4:(qb + 1) * 4], ps[6][:, 0:4], ALU.add,
               ["ps6", f"SSA{qb}"], [f"SSA{qb}"])
        TS("dve", ATo[ib], ATn, AOGc[:, h:h + 1], None, ALU.mult, None, ["ATn", "AOGc"], [f"ATo{ib}"])
        DMA("sp", ATTT_d[h * 128:(h + 1) * 128, qb * 512:(qb + 1) * 512], ATo[ib], f"sATo{ib}", [f"ATo{ib}"], [])

    def cast_gen():
        n = 0
        for wsrc, wdst in ((w1, W1s), (w3, W3s)):
            wv = wsrc.rearrange("(k p) n -> p k n", p=128)
            for pr in range(22):
                DMA("pool", wdst[pr].rearrange("p (k c) -> p k c", k=16), wv[:, :, pr * 256:(pr + 1) * 256], f"cast{n % 4}", [], ["Wsd"])
                n += 1
                yield
        w2v = w2.rearrange("(g j p) n -> g p j n", j=4, p=128)
        for hh in range(2):
            for g4 in range(11):
                b_ = n % 2
                n += 1
                DMA("pool", WSTG[b_], w2v[g4][:, :, hh * 1024:(hh + 1) * 1024], f"castl{b_}", [], [f"WSTG{b_}"])
                yield
                TT("dve", WSTG[b_], WSTG[b_], bc(G2Bt[:, hh * 1024:(hh + 1) * 1024], [128, 4, 1024], 1), ALU.mult,
                   [f"WSTG{b_}", "G2Bt"], [f"WSTG{b_}"])
                DMA("pool", W2s[hh, g4].rearrange("p (j c) -> p j c", j=4), WSTG[b_], f"casts{b_}", [f"WSTG{b_}"], ["Wsd"])
                yield
        cast_done.append(1)
        while True:
            yield

    cast_done = []
    castg = cast_gen()
    pending_tail = None
    scans = [scan_gen(0, "dve"), scan_gen(1, "pool")]
    steps_per_it = -(-(NJ) // (NH * (L // 512)))
    sc = 1.0 / float(np.sqrt(192.0))
    NQB = L // 512
    it = 0
    for h in range(NH):
        hb = 0
        DMA("sp", KTh[hb], KT_d[h], f"KTh{hb}", [], [f"KTh{hb}"])
        DMA("sp", KRh[hb][0:64, :], KR_d[h], f"KRh{hb}", [], [f"KRh{hb}"])
        DMA("sp", Vh[hb], V_d[h].rearrange("(n p) d -> p n d", p=128), f"Vh{hb}", [], [f"Vh{hb}"])
        for qb in range(NQB):
            ib = it % 2
            it += 1
            po = 2 + ib
            pz = 4 + ib
            DMA("sp", QTb[ib], QT_d[h][:, qb * 512:(qb + 1) * 512], f"QTb{ib}", [], [f"QTb{ib}"])
            DMA("sp", QRb[ib][0:64, :], QR_d[h][:, qb * 512:(qb + 1) * 512], f"QRb{ib}", [], [f"QRb{ib}"])

            SBK = [0, 1, 7]

            def S(kt):
                sb_ = SBK[kt % 3]
                MM(ps[sb_][:, :], KTh[hb][:, kt * 128:(kt + 1) * 128], QTb[ib][:, :], True, False,
                   [f"KTh{hb}", f"QTb{ib}"], [f"ps{sb_}"])
                MM(ps[sb_][:, :], KRh[hb][:, kt * 128:(kt + 1) * 128], QRb[ib][:, :], False, True,
                   [f"KRh{hb}", f"QRb{ib}"], [f"ps{sb_}"])
            S(0)
            if NT > 1:
                S(1)
            for kt in range(NT):
                sb_ = SBK[kt % 3]
                pb3 = kt % 4
                if kt + 2 < NT:
                    S(kt + 2)
                if kt == min(3, NT - 1) and pending_tail is not None:
                    att_tail(*pending_tail)
                    pending_tail = None
                ACT(PT[pb3], ps[sb_][:, :], AF.Exp, [f"ps{sb_}"], [f"PT{pb3}"], scale=sc)
                MM(ps[po][:, :], Vh[hb][:, kt, :], PT[pb3], kt == 0, kt == NT - 1, [f"PT{pb3}", f"Vh{hb}"], [f"ps{po}"])
                MM(ps[pz][:, :], ONESB, PT[pb3], kt == 0, kt == NT - 1, [f"PT{pb3}", "ONESB"], [f"ps{pz}"])
            pending_tail = (h, qb, ib, po, pz)
            for _ in range(steps_per_it):
                next(scans[0])
                next(scans[1])
            next(castg)
    att_tail(*pending_tail)
    while not cast_done:
        next(castg)
    for _ in range(4 * HJ):
        next(scans[0])
        next(scans[1])
    P.barrier()
    A.reset()
    JG = min(4, NJC)
    NW = 128 * JG
    NTK = 1024 * JG
    SELT = A.t([128, 64, 128], BF16)
    SEL = A.t([128, 64, 128], BF16)
    WOb = A.t([128, 2, 16, 128], BF16)
    Mb = A.t([128, 8, 128], BF16)
    UTb = [A.t([128, NTK], BF16) for _ in range(2)]
    Sb = [[A.t([128, 2, 4, NW], BF16) for _ in range(2)] for _ in range(2)]
    UG8 = A.t([128, 8, NW], BF16)
    YG = A.t([128, 8, NW], BF16)
    YTb = A.t([128, NTK], F32)
    GT1 = A.t([128, NTK], F32)
    GT2 = A.t([128, NTK], F32)
    GOUT = [A.t([128, NTK], BF16) for _ in range(2)]
    DMA("pool", SEL, sel_d, "lSEL", [], ["SEL"])
    DMA("pool", SELT, selt_d, "lSELT", [], ["SELT"])
    GC = float(2.0 * np.sqrt(2.0 / np.pi))

    def load53(it_):
        blk_, jg_ = it_ // (NJC // JG), it_ % (NJC // JG)
        ib_ = it_ % 2
        tt0 = jg_ * NTK
        DMA("sp", UTb[ib_], UT_d[blk_ * 128:(blk_ + 1) * 128, tt0:tt0 + NTK], f"lUTb{ib_}", [], [f"UTb{ib_}"])
        for d in range(2):
            for ri in range(2):
                for gpl in range(4):
                    DMA("sp", Sb[ib_][d][:, ri, gpl, :].rearrange("p (c j) -> p c j", c=JG),
                        S_d[d][:, jg_ * JG:(jg_ + 1) * JG, ri * 32 + blk_ * 4 + gpl, :],
                        f"lSb{ib_}{d}{ri}{gpl}", [], [f"Sb{ib_}{d}{ri}{gpl}"])
    it = 0
    ne = 0
    for blk in range(8):
        for d in range(2):
            base = ((d * 64 + blk * 8) * 2) * 128
            DMA("sp", WOb[:, d].rearrange("p a b -> p (a b)"), WOUT_d[:, base:base + 16 * 128], f"lWOb{d}", [], [f"WOb{d}"])
        DMA("sp", Mb.rearrange("p a b -> p (a b)"), M_d[:, blk * 8 * 128:(blk + 1) * 8 * 128], "lMb", [], ["Mb"])
        for jg in range(NJC // JG):
            ib = it % 2
            t0 = jg * NTK
            if it == 0:
                load53(0)
            if it + 1 < 8 * (NJC // JG):
                load53(it + 1)
            it += 1
            for gl in range(8):
                pb = gl % 4
                for s in range(8):
                    MM(ps[pb][:, 0:NW], SEL[:, s * 8 + gl, :], UTb[ib][:, s:NTK:8], s == 0, s == 7,
                       ["SEL", f"UTb{ib}"], [f"ps{pb}"])
                ne += 1
                CP("act" if ne % 2 == 0 else "dve", UG8[:, gl, :], ps[pb][:, 0:NW], [f"ps{pb}"], [f"UG8{gl}"])
            for gl in range(8):
                pb = 4 + gl % 4
                o = ps[pb][:, 0:NW]
                MM(o, Mb[:, gl, :], UG8[:, gl, :], True, False, ["Mb", f"UG8{gl}"], [f"ps{pb}"])
                for d in range(2):
                    for ri in range(2):
                        MM(o, WOb[:, d, gl * 2 + ri, :], Sb[ib][d][:, ri, gl // 2, :], False, (d == 1 and ri == 1),
                           [f"WOb{d}", f"Sb{ib}{d}{ri}{gl // 2}"], [f"ps{pb}"])
                ne += 1
                CP("act" if ne % 2 == 0 else "dve", YG[:, gl, :], o, [f"ps{pb}"], [f"YG{gl}"])
            allyg = [f"YG{gl}" for gl in range(8)]
            YTv = YTb.rearrange("p (j t) -> p t j", t=8)
            for t in range(8):
                pb = t % 4
                for gl in range(8):
                    MM(ps[pb][:, 0:NW], SELT[:, t * 8 + gl, :], YG[:, gl, :], gl == 0, gl == 7, ["SELT"] + allyg, [f"ps{pb}"])
                ne += 1
                CP("act" if ne % 2 == 0 else "dve", YTv[:, t, :], ps[pb][:, 0:NW], [f"ps{pb}"], [f"YTb{t}"])
            ally = [f"YTb{t}" for t in range(8)]
            TT("pool", GT1, YTb, YTb, ALU.mult, ally, ["GT1"])
            TS("pool", GT1, GT1, 0.044715, 1.0, ALU.mult, ALU.add, ["GT1"], ["GT1"])
            TT("pool", GT1, GT1, YTb, ALU.mult, ["GT1"] + ally, ["GT1"])
            ACT(GT2, GT1, AF.Sigmoid, ["GT1"], ["GT2"], scale=GC)
            TT("dve", GOUT[ib], GT2, YTb, ALU.mult, ["GT2"] + ally, [f"GOUT{ib}"])
            DMA("sp", YT_d[blk * 128:(blk + 1) * 128, t0:t0 + NTK], GOUT[ib], f"sGOUT{ib}", [f"GOUT{ib}"], ["YT_d"])
    P.barrier()
    if upto < 7:
        P.emit(); es.close(); return nc

    A.reset()
    WG = A.t([128, 8, 1024], BF16)
    BG = A.t([128, 8], F32)
    SOG = A.t([128, 8], F32)
    ONEC = A.t([128, 2], BF16)
    YTc = [A.t([128, 8, 512], BF16) for _ in range(2)]
    SG = A.t([128, 512], F32)
    SSMf = A.t([128, 512], F32)
    SQb = A.t([128, 8, 512], BF16)
    SSMo = [A.t([128, 8, 512], BF16) for _ in range(2)]
    for k in range(8):
        DMA("pool", WG[:, k, :], w_glu[k * 128:(k + 1) * 128, :], f"lWG{k}", [], [f"WG{k}"])
    allwg = [f"WG{k}" for k in range(8)]
    DMA("sp", BG, b_glu.rearrange("(k p) -> p k", p=128), "g0", [], ["BG"], slow=True)
    DMA("sp", SOG, ssm_out_g.rearrange("(k p) -> p k", p=128), "g1", [], ["SOG"], slow=True)
    MSET("pool", ONEC, 1.0, ["ONEC"])
    YT_v = YT_d.rearrange("(b f) t -> f b t", f=128)
    SSMT_v = SSMT_d.rearrange("(b f) t -> f b t", f=128)
    DMA("sp", YTc[0], YT_v[:, :, 0:512], "lYTc0", ["YT_d"], ["YTc0"])
    for tc in range(L // 512):
        ib = tc % 2
        t0 = tc * 512
        if tc + 1 < L // 512:
            DMA("sp", YTc[1 - ib], YT_v[:, :, t0 + 512:t0 + 1024], f"lYTc{1 - ib}", ["YT_d"], [f"YTc{1 - ib}"])
        for oc in range(8):
            pb = oc % 2
            for k in range(8):
                MM(ps[pb][:, :], WG[:, k, oc * 128:(oc + 1) * 128], YTc[ib][:, k, :], k == 0, k == 7,
                   allwg + [f"YTc{ib}"], [f"ps{pb}"])
            ACT(SG, ps[pb][:, :], AF.Sigmoid, [f"ps{pb}", "BG"], ["SG"], bias=BG[:, oc:oc + 1])
            TT("dve", SSMf, SG, YTc[ib][:, oc, :], ALU.mult, ["SG", f"YTc{ib}"], ["SSMf"])
            TT("pool", SQb[:, oc, :], SSMf, SSMf, ALU.mult, ["SSMf"], [f"SQb{oc}"])
            TS("dve", SSMo[ib][:, oc, :], SSMf, SOG[:, oc:oc + 1], None, ALU.mult, None, ["SSMf", "SOG"], [f"SSMo{ib}_{oc}"])
        for tt_ in range(4):
            for oc in range(8):
                MM(ps[2][:, tt_:tt_ + 1], SQb[:, oc, tt_ * 128:(tt_ + 1) * 128], ONEC[:, 0:1], oc == 0, oc == 7,
                   [f"SQb{o2}" for o2 in range(8)] + ["ONEC"], ["ps2"])
        CP("dve", SSS[:, tc * 4:(tc + 1) * 4], ps[2][:, 0:4], ["ps2"], ["SSS"])
        DMA("sp", SSMT_v[:, :, t0:t0 + 512], SSMo[ib], f"sSSMo{ib}", [f"SSMo{ib}_{oc}" for oc in range(8)], ["SSMT_d"])
    P.barrier()

    if upto < 8:
        P.emit(); es.close(); return nc

    X1_d = dscr("X1_d", [L, D], F32)
    A.reset()
    WO = A.t([128, 16, 2048], BF16)
    G1B = A.t([128, 2048], F32)
    AOG = A.t([128, 8], F32)
    XT2 = [A.t([128, 2048], F32) for _ in range(2)]
    MIXT = [A.t([128, 16, 128], BF16) for _ in range(2)]
    X1t = [A.t([128, 2048], F32) for _ in range(2)]
    TE1 = [A.t([128, 512], F32) for _ in range(2)]
    TE2 = [A.t([128, 512], F32) for _ in range(2)]
    ST4 = A.t([128, 8], F32)
    WST = [A.t([128, 16, 256], BF16) for _ in range(2)]
    DMA("sp", G1B, MOD_d[2 * D:3 * D].partition_broadcast(128), "lG1B", ["MOD_d"], ["G1B"])
    for k in range(16):
        DMA("pool", WO[:, k, :], w_o[k * 128:(k + 1) * 128, :], f"lWO{k}", [], [f"WO{k}"])
        TT("dve", WO[:, k, :], WO[:, k, :], G1B, ALU.mult, [f"WO{k}", "G1B"], [f"WO{k}"])
    SSMT_v2 = SSMT_d.rearrange("(b f) t -> f b t", f=128)
    ATTT_v2 = ATTT_d.rearrange("(b f) t -> f b t", f=128)
    ne = 0
    def load4a(t_):
        rr = t_ * 128
        tb_ = t_ % 2
        DMA("sp", XT2[tb_], x[rr:rr + 128, :], f"lXT2{tb_}", [], [f"XT2{tb_}"])
        DMA("sp", MIXT[tb_][:, 8:16, :], SSMT_v2[:, :, rr:rr + 128], f"lMIXs{tb_}", [], [f"MIXs{tb_}"])
        DMA("sp", MIXT[tb_][:, 0:8, :], ATTT_v2[:, :, rr:rr + 128], f"lMIXa{tb_}", [], [f"MIXa{tb_}"])
    load4a(0)
    for t in range(NT):
        r0 = t * 128
        tb = t % 2
        if t + 1 < NT:
            load4a(t + 1)
        RSTD(ST4[:, tb * 4 + 1:tb * 4 + 2], SSA[:, t:t + 1], 1024, f"ar{tb}", "SSA")
        RSTD(ST4[:, tb * 4 + 2:tb * 4 + 3], SSS[:, t:t + 1], 1024, f"sr{tb}", "SSS")
        for nb in range(4):
            pa = ps[2 * nb]
            pss = ps[2 * nb + 1]
            na, ns = f"ps{2 * nb}", f"ps{2 * nb + 1}"
            cs = slice(nb * 512, (nb + 1) * 512)
            eb = ne % 2
            ne += 1
            for k in range(8):
                MM(pa[:, :], MIXT[tb][:, k, :], WO[:, k, cs], k == 0, k == 7, [f"MIXa{tb}", f"WO{k}"], [na])
            for k in range(8, 16):
                MM(pss[:, :], MIXT[tb][:, k, :], WO[:, k, cs], k == 8, k == 15, [f"MIXs{tb}", f"WO{k}"], [ns])
            ACT(TE1[eb], pa[:, :], AF.Copy, [na, f"ar{tb}"], [f"TE1{eb}"], scale=ST4[:, tb * 4 + 1:tb * 4 + 2])
            STT("dve", TE2[eb], pss[:, :], ST4[:, tb * 4 + 2:tb * 4 + 3], TE1[eb], ALU.mult, ALU.add, [ns, f"sr{tb}", f"TE1{eb}"], [f"TE2{eb}"])
            TT("pool", X1t[tb][:, cs], TE2[eb], XT2[tb][:, cs], ALU.add, [f"TE2{eb}", f"XT2{tb}"], [f"X1t{tb}_{nb}"])
        DMA("pool", X1_d[r0:r0 + 128, :], X1t[tb], f"sX1{tb}", [f"X1t{tb}_{nb}" for nb in range(4)], ["X1_d"])
    P.barrier()
    if upto < 9:
        P.emit(); es.close(); return nc

    A.reset()
    X1s = [A.t([128, 4, 2048], F32) for _ in range(2)]
    XB2 = [A.t([128, 2048], BF16) for _ in range(2)]
    JNK2 = A.t([128, 2048], BF16)
    H2T = A.t([128, 16, 512], BF16)
    W1t = [A.t([128, 16, 256], BF16) for _ in range(2)]
    W3t = [A.t([128, 16, 256], BF16) for _ in range(2)]
    GT = A.t([128, 44, 512], BF16)
    W2t = [A.t([128, 4, 1024], BF16) for _ in range(2)]
    OUTs = [A.t([128, 512], F32) for _ in range(4)]
    SA = [A.t([128, 512], F32) for _ in range(2)]
    ST5 = A.t([128, 8], F32)
    nw = 0
    no = 0
    NST = L // 512
    DMA("sp", X1s[0], X1_d[0:512, :].rearrange("(a p) n -> p a n", p=128), "lX1s0", [], ["X1s0"])
    for st_ in range(NST):
        r0 = st_ * 512
        xb = st_ % 2
        if st_ + 1 < NST:
            DMA("sp", X1s[1 - xb], X1_d[r0 + 512:r0 + 1024, :].rearrange("(a p) n -> p a n", p=128), f"lX1s{1 - xb}", [],
                [f"X1s{1 - xb}"])
        MSET("pool", ST5, 0.0, [f"f_ss{a}" for a in range(4)] + [f"fr{a}" for a in range(4)])
        for a in range(4):
            ab = a % 2
            ACT(JNK2, X1s[xb][:, a, :], AF.Square, [f"X1s{xb}"], ["JNK2", f"f_ss{a}"], accum=ST5[:, 2 * a:2 * a + 1])
            RSTD(ST5[:, 2 * a + 1:2 * a + 2], ST5[:, 2 * a:2 * a + 1], D, f"fr{a}", f"f_ss{a}")
            ACT(XB2[ab], X1s[xb][:, a, :], AF.Copy, [f"X1s{xb}", f"fr{a}"], [f"XB2{ab}"], scale=ST5[:, 2 * a + 1:2 * a + 2])
            for k in range(16):
                pi = 4 * ab + k // 4
                MM(ps[pi][:, (k % 4) * 128:(k % 4 + 1) * 128], XB2[ab][:, k * 128:(k + 1) * 128], IDB, True, True,
                   [f"XB2{ab}", "IDB"], [f"ps{pi}"])
            for k in range(16):
                pi = 4 * ab + k // 4
                TS("dve", H2T[:, k, a * 128:(a + 1) * 128], ps[pi][:, (k % 4) * 128:(k % 4 + 1) * 128],
                   GS[:, 32 + k:33 + k], GS[:, 48 + k:49 + k], ALU.mult, ALU.add, [f"ps{pi}", "GS"], [f"H2T{k}"])
        for pr in range(22):
            wb = nw % 2
            nw += 1
            DMA("sp", W1t[wb].rearrange("p a b -> p (a b)"), W1s[pr], f"lW1t{wb}", [], [f"W1t{wb}"])
            DMA("sp", W3t[wb].rearrange("p a b -> p (a b)"), W3s[pr], f"lW3t{wb}", [], [f"W3t{wb}"])
            for c2 in range(2):
                ffc = pr * 2 + c2
                sb_ = ffc % 2
                pa, pb_ = ps[sb_ * 2], ps[sb_ * 2 + 1]
                na, nb_ = f"ps{sb_ * 2}", f"ps{sb_ * 2 + 1}"
                for k in range(16):
                    MM(pa[:, :], W1t[wb][:, k, c2 * 128:(c2 + 1) * 128], H2T[:, k, :], k == 0, k == 15,
                       [f"W1t{wb}", f"H2T{k}"], [na])
                for k in range(16):
                    MM(pb_[:, :], W3t[wb][:, k, c2 * 128:(c2 + 1) * 128], H2T[:, k, :], k == 0, k == 15,
                       [f"W3t{wb}", f"H2T{k}"], [nb_])
                ACT(SA[sb_], pa[:, :], AF.Silu, [na], [f"SA{sb_}"])
                TT("dve", GT[:, ffc, :], SA[sb_], pb_[:, :], ALU.mult, [f"SA{sb_}", nb_], [f"GT{ffc}"])
        for h in range(2):
            for g4 in range(11):
                wb = nw % 2
                nw += 1
                DMA("sp", W2t[wb].rearrange("p a b -> p (a b)"), W2s[h, g4], f"lW2t{wb}", [], [f"W2t{wb}"])
                for j in range(4):
                    ffc = g4 * 4 + j
                    for a in range(4):
                        for nb in range(2):
                            MM(ps[a * 2 + nb][:, :], GT[:, ffc, a * 128:(a + 1) * 128], W2t[wb][:, j, nb * 512:(nb + 1) * 512],
                               ffc == 0, ffc == 43, [f"GT{ffc}", f"W2t{wb}"], [f"ps{a * 2 + nb}"])
            for a in range(4):
                for nb in range(2):
                    ob = no % 4
                    no += 1
                    cs = slice(h * 1024 + nb * 512, h * 1024 + (nb + 1) * 512)
                    TT("dve", OUTs[ob], ps[a * 2 + nb][:, :], X1s[xb][:, a, cs], ALU.add, [f"ps{a * 2 + nb}", f"X1s{xb}"],
                       [f"OUTs{ob}"])
                    DMA("pool", y_out[r0 + a * 128:r0 + (a + 1) * 128, cs], OUTs[ob], f"sOUT{ob}", [f"OUTs{ob}"], [])
    P.barrier()

    P.emit()
    es.close()
    return nc


def rope_tables_host(L):
    pos = np.arange(L, dtype=np.float32)
    inv_freq = (np.float32(10000.0) ** (-np.arange(0, 64, 2, dtype=np.float32) / np.float32(64))).astype(np.float32)
    ang = (pos[:, None] * inv_freq[None, :]).astype(np.float32)
    return np.cos(ang).astype(np.float32), np.sin(ang).astype(np.float32)


_S5C = {}


def s5_constants():
    if _S5C:
        return _S5C
    sel = np.zeros((128, 64, 128), np.float32)
    selt = np.zeros((128, 64, 128), np.float32)
    for s in range(8):
        for gl in range(8):
            for c in range(16):
                sel[gl * 16 + c, s * 8 + gl, s * 16 + c] = 1.0
                selt[s * 16 + c, s * 8 + gl, gl * 16 + c] = 1.0
    si = np.arange(128)[:, None] // 16
    ti = np.arange(128)[None, :] // 16
    _S5C.update(sel=sel, selt=selt, maskf=(si <= ti).astype(np.float32), maskb=(si >= ti).astype(np.float32))
    return _S5C


def make_core_inputs(inp, seq_x, seq_c, L):
    cos, sin = rope_tables_host(L)
    m = {"x": np.ascontiguousarray(seq_x[:L]), "c": np.ascontiguousarray(seq_c),
         "ident": np.eye(128, dtype=np.float32), "ropec": cos, "ropes": sin}
    m.update(s5_constants())
    for k, v in inp.items():
        if k in ("x_prompt", "x_sample", "c_prompt", "c_sample"):
            continue
        m[k] = np.ascontiguousarray(v[0])
    return m


def kernel(**inputs):
    L = 8192
    nc = build(L)
    xs = [inputs["x_prompt"][i] for i in range(4)] + [inputs["x_sample"][0]]
    cs = [inputs["c_prompt"][i] for i in range(4)] + [inputs["c_sample"][0]]
    in_maps = []
    for core in range(8):
        i = core if core < 5 else core - 5
        in_maps.append(make_core_inputs(inputs, xs[i], cs[i], L))
    res = run_bass_kernel_spmd(nc, in_maps, core_ids=list(range(8)))
    ys = [np.asarray(res.results[i]["y"], dtype=np.float32) for i in range(5)]
    y_prompt = np.stack(ys[:4], axis=0)
    y_sample = ys[4][None]
    return (y_prompt, y_sample)
```

```python
import contextlib
import numpy as np
import concourse.bass as bass
import concourse.mybir as mybir
from concourse.bass_utils import run_bass_kernel_spmd

F32 = mybir.dt.float32
BF16 = mybir.dt.bfloat16
AF = mybir.ActivationFunctionType
ALU = mybir.AluOpType
AX = mybir.AxisListType

D = 2048
NH = 8
DFF = 5632
INC = 3136
EPS = 1e-6
MEMF = 52900


class Prog:
    def __init__(self, nc):
        self.nc = nc
        self.ops = {e: [] for e in ("pe", "act", "dve", "pool", "sp")}
        self.count = {}
        self.waited = {e: {} for e in self.ops}
        self.res = {}
        self.semkeys = []
        self.chslot = {}

    def _tok(self, semkey, inc):
        if semkey not in self.count:
            self.count[semkey] = 0
            self.semkeys.append(semkey)
        self.count[semkey] += inc
        return (semkey, self.count[semkey])

    def op(self, eng, fn, reads=(), writes=(), ch=None):
        deps = {}

        def add(toks, raw):
            for k, v in toks.items():
                if ch is None and k == eng and (eng == "pe" or not raw):
                    continue
                if deps.get(k, 0) < v:
                    deps[k] = v
        for r in reads:
            st = self.res.get(r)
            if st is not None:
                add(st[0], True)
        for w in writes:
            st = self.res.get(w)
            if st is not None:
                add(st[0], True)
                add(st[1], False)
        if ch is not None:
            if ch not in self.chslot:
                self.chslot[ch] = len(self.chslot)
            chkey = "dma:%d" % self.chslot[ch]
            if self.count.get(chkey, 0) > 0:
                deps[chkey] = self.count[chkey]
        waits = []
        wd = self.waited[eng]
        for k, v in deps.items():
            if wd.get(k, 0) < v:
                wd[k] = v
                waits.append((k, v))
        if ch is None:
            tok = self._tok(eng, 1)
            inc = 1
        else:
            tok = self._tok(chkey, 16)
            inc = 16
        self.ops[eng].append((waits, fn, tok[0], inc))
        for r in reads:
            st = self.res.setdefault(r, [{}, {}])
            if st[1].get(tok[0], 0) < tok[1]:
                st[1][tok[0]] = tok[1]
        for w in writes:
            self.res[w] = [{tok[0]: tok[1]}, {}]
        return tok

    def barrier(self):
        for eng in self.ops:
            waits = []
            wd = self.waited[eng]
            for k, v in self.count.items():
                if wd.get(k, 0) < v:
                    wd[k] = v
                    waits.append((k, v))
            if waits:
                self.ops[eng].append((waits, None, None, 0))
        self.res = {}
        self.chslot = {}

    def emit(self):
        nc = self.nc
        with contextlib.ExitStack() as es:
            sems = {}
            for k in self.semkeys:
                sems[k] = es.enter_context(nc.semaphore("s_" + k.replace(":", "_")))
            block = es.enter_context(nc.Block())

            def run(engname):
                def body(e):
                    for waits, fn, semkey, inc in self.ops[engname]:
                        for k, v in waits:
                            e.wait_ge(sems[k], v)
                        if fn is not None:
                            ins = fn(e)
                            ins.then_inc(sems[semkey], inc)
                return body
            block.tensor(run("pe"))
            block.scalar(run("act"))
            block.vector(run("dve"))
            block.gpsimd(run("pool"))
            block.sync(run("sp"))


def build(L, dbg=(), upto=99):
    NT = L // 128
    nc = bass.Bass("TRN2", target_bir_lowering=False)
    P = Prog(nc)

    def din(name, shape):
        return nc.dram_tensor(name, list(shape), F32, kind="ExternalInput").ap()

    def dscr(name, shape, dt):
        kind = "ExternalOutput" if name in dbg else "Internal"
        return nc.dram_tensor(name, list(shape), dt, kind=kind).ap()

    x = din("x", [L, D]); cvec = din("c", [D])
    w_ada = din("w_ada", [D, 6 * D]); b_ada = din("b_ada", [6 * D])
    norm_mix_g = din("norm_mix_g", [D]); w_in = din("w_in", [D, INC])
    kv_norm_g = din("kv_norm_g", [512]); w_ukv = din("w_ukv", [512, 2048])
    q_norm_g = din("q_norm_g", [192]); k_norm_g = din("k_norm_g", [192])
    lam_re = din("lam_re", [2, 64, 64]); lam_im = din("lam_im", [2, 64, 64]); log_dt = din("log_dt", [2, 64])
    b_re = din("b_re", [64, 64, 16]); b_im = din("b_im", [64, 64, 16])
    c_re = din("c_re", [2, 64, 16, 64]); c_im = din("c_im", [2, 64, 16, 64])
    d_skip = din("d_skip", [1024]); w_glu = din("w_glu", [1024, 1024]); b_glu = din("b_glu", [1024])
    att_out_g = din("att_out_g", [1024]); ssm_out_g = din("ssm_out_g", [1024])
    w_o = din("w_o", [D, D]); norm_ffn_g = din("norm_ffn_g", [D])
    w1 = din("w1", [D, DFF]); w3 = din("w3", [D, DFF]); w2 = din("w2", [DFF, D])
    ident_d = din("ident", [128, 128]); ropec = din("ropec", [L, 32]); ropes = din("ropes", [L, 32])
    maskf_d = din("maskf", [128, 128]); maskb_d = din("maskb", [128, 128])
    sel_d = din("sel", [128, 64, 128]); selt_d = din("selt", [128, 64, 128])
    y_out = nc.dram_tensor("y", [L, D], F32, kind="ExternalOutput").ap()

    MOD_d = dscr("MOD_d", [6 * D], F32)
    QT_d = dscr("QT_d", [NH, 128, L], BF16); QR_d = dscr("QR_d", [NH, 64, L], BF16)
    KT_d = dscr("KT_d", [NH, 128, L], BF16); KR_d = dscr("KR_d", [NH, 64, L], BF16)
    V_d = dscr("V_d", [NH, L, 128], BF16)
    UT_d = dscr("UT_d", [1024, L], BF16)
    W1s = dscr("W1s", [22, 128, 16 * 256], BF16)
    W3s = dscr("W3s", [22, 128, 16 * 256], BF16)
    W2s = dscr("W2s", [2, 11, 128, 4 * 1024], BF16)
    ATTT_d = dscr("ATTT_d", [1024, L], BF16)

    es = contextlib.ExitStack()
    mem = es.enter_context(nc.sbuf_tensor("mem", [128, MEMF], F32))
    ps = [es.enter_context(nc.psum_tensor(f"ps{i}", [128, 512], F32)) for i in range(8)]

    class Alloc:
        def __init__(self, base=0):
            self.off = base
            self.base = base

        def reset(self):
            self.off = self.base

        def t(self, shape, dt):
            n = int(np.prod(shape[1:]))
            nb = n * (2 if dt == BF16 else 4)
            nb4 = (nb + 3) // 4
            assert self.off + nb4 <= MEMF, ("SBUF arena overflow", self.off, nb4)
            v = mem[0:shape[0], self.off:self.off + nb4]
            if dt != F32:
                v = v.bitcast(dt)
                if nb4 * 2 != n:
                    v = v[:, 0:n]
            if len(shape) == 3:
                v = v.rearrange("p (a b) -> p a b", b=shape[2])
            elif len(shape) == 4:
                v = v.rearrange("p (a b c) -> p a b c", b=shape[2], c=shape[3])
            self.off += nb4
            return v

    def MM(out, lhsT, rhs, st, sp, r, w):
        P.op("pe", lambda e: e.matmul(out, lhsT, rhs, start=st, stop=sp), reads=r, writes=w)

    def ACT(out, in_, func, r, w, bias=None, scale=None, accum=None):
        kw = {}
        if bias is not None:
            kw["bias"] = bias
        if scale is not None:
            kw["scale"] = scale
        if accum is not None:
            kw["accum_out"] = accum
        P.op("act", lambda e: e.activation(out, in_, func, **kw), reads=r, writes=w)

    def TT(eng, out, a, b, op, r, w):
        P.op(eng, lambda e: e.tensor_tensor(out, a, b, op), reads=r, writes=w)

    def TS(eng, out, a, s1, s2, op0, op1, r, w):
        if s2 is None:
            P.op(eng, lambda e: e.tensor_scalar(out, a, s1, None, op0), reads=r, writes=w)
        else:
            P.op(eng, lambda e: e.tensor_scalar(out, a, s1, s2, op0, op1), reads=r, writes=w)

    def STT(eng, out, a, s, b, op0, op1, r, w):
        P.op(eng, lambda e: e.scalar_tensor_tensor(out, a, s, b, op0, op1), reads=r, writes=w)

    def CP(eng, out, in_, r, w):
        if eng == "act":
            P.op(eng, lambda e: e.copy(out, in_), reads=r, writes=w)
        else:
            P.op(eng, lambda e: e.tensor_copy(out, in_), reads=r, writes=w)

    def RSUM(eng, out, in_, r, w):
        P.op(eng, lambda e: e.reduce_sum(out, in_, AX.X), reads=r, writes=w)

    def MSET(eng, out, val, w):
        P.op(eng, lambda e: e.memset(out, val), writes=w)

    def DMA(q, out, in_, ch, r, w, slow=False):
        if slow:
            P.op(q, lambda e: e.dma_start(out=out, in_=in_, allow_slow_non_contiguous=True), reads=r, writes=w, ch=ch)
        else:
            P.op(q, lambda e: e.dma_start(out=out, in_=in_), reads=r, writes=w, ch=ch)

    def RSTD(dst, src, n, name, srcname):
        TS("dve", dst, src, 1.0 / n, EPS, ALU.mult, ALU.add, r=[srcname], w=[name])
        P.op("act", lambda e: e.sqrt(dst, dst), reads=[name], writes=[name])
        P.op("dve", lambda e: e.reciprocal(dst, dst), reads=[name], writes=[name])

    def bc(ap, shape, axis):
        return ap.unsqueeze(axis).to_broadcast(shape)

    PA = Alloc(0)
    IDF = PA.t([128, 128], F32)
    IDB = PA.t([128, 128], BF16)
    GS = PA.t([128, 64], F32)
    A8T = PA.t([128, 2, 2, 64], F32)
    SSS = PA.t([128, 64], F32)
    SSA = PA.t([128, 64], F32)
    pers_end = PA.off
    A = Alloc(pers_end)

    DMA("sp", IDF, ident_d, "c0", [], ["IDF"])
    DMA("pool", IDB, ident_d, "c1", [], ["IDB"])

    SC = A.t([128, 16], F32)
    SCB = A.t([128, 16, 128], F32)
    WA = [A.t([128, 16, 512], F32) for _ in range(2)]
    BAb = [A.t([128, 512], F32) for _ in range(2)]
    ONES = A.t([1, 128], F32)
    MODB = A.t([128, 6 * D], F32)
    TMP0 = A.t([128, 96, 128], F32)
    COLS = A.t([128, 96], F32)
    NG = A.t([128, 32], F32)

    DMA("sp", SC, cvec.rearrange("(k p) -> p k", p=128), "c2", [], ["SC"], slow=True)
    DMA("sp", NG[:, 0:16], norm_mix_g.rearrange("(k p) -> p k", p=128), "c3", [], ["NG0"], slow=True)
    DMA("sp", NG[:, 16:32], norm_ffn_g.rearrange("(k p) -> p k", p=128), "c4", [], ["NG1"], slow=True)
    ACT(SC, SC, AF.Silu, ["SC"], ["SC"])
    CP("dve", SCB, bc(SC, [128, 16, 128], 2), ["SC"], ["SCB"])
    MSET("pool", ONES, 1.0, ["ONES"])
    w_ada_v = w_ada.rearrange("(k p) n -> p k n", p=128)
    b_ada_v = b_ada.rearrange("(o n) -> o n", o=1)
    for nb in range(24):
        b = nb % 2
        DMA("sp", WA[b], w_ada_v[:, :, nb * 512:(nb + 1) * 512], f"WA{b}", [], [f"WA{b}"])
        DMA("sp", BAb[b], b_ada[nb * 512:(nb + 1) * 512].partition_broadcast(128), f"BA{b}", [], [f"BA{b}"])
        pt = ps[b]
        for k in range(16):
            MM(pt[:, :], SCB[:, k, :], WA[b][:, k, :], k == 0, k == 15, ["SCB", f"WA{b}"], [f"ps{b}"])
        TT("dve", MODB[:, nb * 512:(nb + 1) * 512], pt[:, :], BAb[b], ALU.add, [f"ps{b}", f"BA{b}"], [f"MODB{nb}"])
    allmod = [f"MODB{nb}" for nb in range(24)]
    DMA("sp", MOD_d.rearrange("(o n) -> o n", o=1), MODB[0:1, :], "c5", allmod, ["MOD_d"])
    MODB3 = MODB.rearrange("p (a b) -> p a b", b=128)
    TT("dve", TMP0, MODB3, bc(IDF, [128, 96, 128], 1), ALU.mult, allmod + ["IDF"], ["TMP0"])
    RSUM("dve", COLS, TMP0, ["TMP0"], ["COLS"])
    STT("dve", GS[:, 0:16], COLS[:, 16:32], 1.0, NG[:, 0:16], ALU.add, ALU.mult, ["COLS", "NG0"], ["GS"])
    CP("dve", GS[:, 16:32], COLS[:, 0:16], ["COLS"], ["GS"])
    STT("dve", GS[:, 32:48], COLS[:, 64:80], 1.0, NG[:, 16:32], ALU.add, ALU.mult, ["COLS", "NG1"], ["GS"])
    CP("dve", GS[:, 48:64], COLS[:, 48:64], ["COLS"], ["GS"])
    P.barrier()
    if upto < 1:
        P.emit(); es.close(); return nc

    A.reset()
    WIN = A.t([128, 16, INC], BF16)
    WUKV = A.t([128, 4, 2048], BF16)
    XTd = [A.t([128, D], F32) for _ in range(2)]
    XBd = [A.t([128, D], BF16) for _ in range(2)]
    STX = A.t([128, 4], F32)
    JNKX = A.t([128, D], BF16)
    HT = A.t([128, 16, 128], BF16)
    QF = A.t([128, 1536], F32)
    CKV = A.t([128, 576], F32)
    UF = A.t([128, 1024], BF16)
    KVF = A.t([128, 2048], F32)
    SCR = A.t([128, 2048], F32)
    QN = A.t([128, 8, 192], BF16)
    KN = A.t([128, 8, 192], BF16)
    QTs = A.t([128, 8, 128], BF16)
    QRs = A.t([64, 8, 128], BF16)
    KTs = A.t([128, 8, 128], BF16)
    KRs = A.t([64, 8, 128], BF16)
    UTs = A.t([128, 8, 128], BF16)
    VS = A.t([128, 8, 128], BF16)
    CKN = A.t([128, 512], BF16)
    CKT = A.t([128, 4, 128], BF16)
    RC = A.t([128, 32], F32)
    RS = A.t([128, 32], F32)
    GQ = A.t([128, 192], F32)
    GK = A.t([128, 192], F32)
    KVG = A.t([128, 4], F32)
    ST = A.t([128, 32], F32)
    RT = A.t([128, 8, 32], F32)
    RT2 = A.t([128, 8, 32], F32)
    RT3 = A.t([128, 8, 32], F32)
    RT4 = A.t([128, 8, 32], F32)
    KRG = A.t([128, 64], F32)
    KRR = A.t([128, 64], F32)

    for k in range(16):
        DMA("pool", WIN[:, k, :], w_in[k * 128:(k + 1) * 128, :], f"WIN{k}", [], [f"WIN{k}"])
    for k in range(4):
        DMA("pool", WUKV[:, k, :], w_ukv[k * 128:(k + 1) * 128, :], f"WUKV{k}", [], [f"WUKV{k}"])
    DMA("sp", GQ, q_norm_g.partition_broadcast(128), "c6", [], ["GQ"])
    DMA("sp", GK, k_norm_g.partition_broadcast(128), "c7", [], ["GK"])
    DMA("sp", KVG, kv_norm_g.rearrange("(k p) -> p k", p=128), "c8", [], ["KVG"], slow=True)

    blocks = [(0, 512), (512, 512), (1024, 512), (1536, 512), (2048, 64), (2112, 512), (2624, 512)]
    import os
    P1STOP = int(os.environ.get("P1STOP", "-1"))

    class _Stop(Exception):
        pass

    def CK(n):
        if P1STOP == n:
            raise _Stop()
    def front_load(t_):
        pb_ = t_ % 2
        rr = t_ * 128
        DMA("sp", XTd[pb_], x[rr:rr + 128, :], f"XT{pb_}", [], [f"XT{pb_}"])

    def front_a(t_):
        pb_ = t_ % 2
        MSET("pool", STX[:, pb_ * 2:pb_ * 2 + 2], 0.0, [f"x_ss{pb_}", f"xr{pb_}"])
        ACT(JNKX, XTd[pb_], AF.Square, [f"XT{pb_}"], ["JNKX", f"x_ss{pb_}"], accum=STX[:, pb_ * 2:pb_ * 2 + 1])
        RSTD(STX[:, pb_ * 2 + 1:pb_ * 2 + 2], STX[:, pb_ * 2:pb_ * 2 + 1], D, f"xr{pb_}", f"x_ss{pb_}")
        ACT(XBd[pb_], XTd[pb_], AF.Copy, [f"XT{pb_}", f"xr{pb_}"], [f"XB{pb_}"], scale=STX[:, pb_ * 2 + 1:pb_ * 2 + 2])

    def p1_head(t):
        r0 = t * 128
        XB = XBd[t % 2]
        DMA("sp", RC, ropec[r0:r0 + 128, :], "RC", [], ["RC"])
        DMA("sp", RS, ropes[r0:r0 + 128, :], "RS", [], ["RS"])
        MSET("pool", ST, 0.0, ["c_ss", "qr_ss", "k_ssn", "kr_ss", "kr_ss2", "cr", "qr", "kr"])


    def p1_xT(t):
        r0 = t * 128
        XB = XBd[t % 2]
        allht = [f"HT{k}" for k in range(16)]
        for k in range(16):
            MM(ps[k // 4][:, (k % 4) * 128:(k % 4 + 1) * 128], XB[:, k * 128:(k + 1) * 128], IDB, True, True,
               [f"XB{t % 2}", "IDB"], [f"ps{k // 4}"])
        for k in range(16):
            src = ps[k // 4][:, (k % 4) * 128:(k % 4 + 1) * 128]
            if False:
                ACT(HT[:, k, :], src, AF.Identity, [f"ps{k // 4}", "GS"], [f"HT{k}"],
                    bias=GS[:, 16 + k:17 + k], scale=GS[:, k:k + 1])
            else:
                TS("dve", HT[:, k, :], src, GS[:, k:k + 1], GS[:, 16 + k:17 + k], ALU.mult, ALU.add,
                   [f"ps{k // 4}", "GS"], [f"HT{k}"])

    def p1_proj(t):
        allht = [f"HT{k}" for k in range(16)]
        for bi, (c0, w) in enumerate(blocks):
            pi = 4 + bi % 4
            pt = ps[pi]
            for k in range(16):
                MM(pt[:, 0:w], HT[:, k, :], WIN[:, k, c0:c0 + w], k == 0, k == 15, allht + [f"WIN{k}"], [f"ps{pi}"])
            if bi < 3:
                CP("act", QF[:, c0:c0 + w], pt[:, 0:w], [f"ps{pi}"], [f"QF{bi}"])
            elif bi == 3:
                CP("dve", CKV[:, 0:512], pt[:, 0:w], [f"ps{pi}"], ["CKVa"])
            elif bi == 4:
                CP("dve", CKV[:, 512:576], pt[:, 0:w], [f"ps{pi}"], ["CKVb"])
            else:
                CP("act", UF[:, c0 - 2112:c0 - 2112 + w], pt[:, 0:w], [f"ps{pi}"], [f"UF{bi}"])


    def p1_ckv(t):
        r0 = t * 128
        allkv = [f"KVF{nb}" for nb in range(4)]
        ACT(JNKX[:, 0:512], CKV[:, 0:512], AF.Square, ["CKVa"], ["JNKX", "c_ss"], accum=ST[:, 2:3])
        RSTD(ST[:, 3:4], ST[:, 2:3], 512, "cr", "c_ss")
        ACT(CKN, CKV[:, 0:512], AF.Copy, ["CKVa", "cr"], ["CKN"], scale=ST[:, 3:4])
        for k in range(4):
            MM(ps[0][:, k * 128:(k + 1) * 128], CKN[:, k * 128:(k + 1) * 128], IDB, True, True, ["CKN", "IDB"], ["ps0"])
        for k in range(4):
            TS("dve", CKT[:, k, :], ps[0][:, k * 128:(k + 1) * 128], KVG[:, k:k + 1], None, ALU.mult, None,
               ["ps0", "KVG"], ["CKT"])
        for nb in range(4):
            pi = 4 + nb
            for k in range(4):
                MM(ps[pi][:, :], CKT[:, k, :], WUKV[:, k, nb * 512:(nb + 1) * 512], k == 0, k == 3,
                   ["CKT", f"WUKV{k}"], [f"ps{pi}"])
            CP("act", KVF[:, nb * 512:(nb + 1) * 512], ps[pi][:, :], [f"ps{pi}"], [f"KVF{nb}"])
        allkv = [f"KVF{nb}" for nb in range(4)]


    def p1_qchain(t):
        r0 = t * 128
        allq = ["QF0", "QF1", "QF2"]
        allq = ["QF0", "QF1", "QF2"]
        QF3 = QF.rearrange("p (h d) -> p h d", d=192)
        SCR3 = SCR[:, 0:1536].rearrange("p (h d) -> p h d", d=192)
        TT("pool", SCR[:, 0:1536], QF, QF, ALU.mult, allq, ["SCR"])
        RSUM("dve", ST[:, 8:16], SCR3, ["SCR"], ["qr_ss"])
        RSTD(ST[:, 8:16], ST[:, 8:16], 192, "qr", "qr_ss")
        TT("pool", QF3, QF3, bc(ST[:, 8:16], [128, 8, 192], 2), ALU.mult, allq + ["qr"], ["QFn"])
        TT("pool", QF3, QF3, bc(GQ, [128, 8, 192], 1), ALU.mult, ["QFn", "GQ"], ["QFn"])
        cosb = bc(RC, [128, 8, 32], 1)
        sinb = bc(RS, [128, 8, 32], 1)
        TT("pool", RT, QF3[:, :, 128:160], cosb, ALU.mult, ["QFn", "RC"], ["RT"])
        TT("pool", RT2, QF3[:, :, 160:192], sinb, ALU.mult, ["QFn", "RS"], ["RT2"])
        TT("pool", RT3, QF3[:, :, 160:192], cosb, ALU.mult, ["QFn", "RC"], ["RT3"])
        TT("pool", RT4, QF3[:, :, 128:160], sinb, ALU.mult, ["QFn", "RS"], ["RT4"])
        TT("dve", QN[:, :, 128:160], RT, RT2, ALU.subtract, ["RT", "RT2"], ["QNa"])
        TT("dve", QN[:, :, 160:192], RT3, RT4, ALU.add, ["RT3", "RT4"], ["QNb"])
        CP("pool", QN[:, :, 0:128], QF3[:, :, 0:128], ["QFn"], ["QNc"])
        allqn = ["QNa", "QNb", "QNc"]


    def p1_qT(t):
        r0 = t * 128
        allqn = ["QNa", "QNb", "QNc"]
        for h in range(8):
            MM(ps[h // 4][:, (h % 4) * 128:(h % 4 + 1) * 128], QN[:, h, 0:128], IDB, True, True,
               allqn + ["IDB"], [f"ps{h // 4}"])
            MM(ps[2 + h // 4][0:64, (h % 4) * 128:(h % 4 + 1) * 128], QN[:, h, 128:192], IDB, True, True,
               allqn + ["IDB"], [f"ps{2 + h // 4}"])
        for hb in range(2):
            CP("dve", QTs[:, hb * 4:(hb + 1) * 4, :], ps[hb][:, :].rearrange("p (a b) -> p a b", b=128),
               [f"ps{hb}"], [f"QTs{hb}"])
            CP("act", QRs[:, hb * 4:(hb + 1) * 4, :], ps[2 + hb][0:64, :].rearrange("p (a b) -> p a b", b=128),
               [f"ps{2 + hb}"], [f"QRs{hb}"])
        DMA("sp", QT_d[:, :, r0:r0 + 128].rearrange("h d t -> d h t"), QTs, "sQT", ["QTs0", "QTs1"], [])
        DMA("sp", QR_d[:, :, r0:r0 + 128].rearrange("h d t -> d h t"), QRs, "sQR", ["QRs0", "QRs1"], [])


    def p1_kchain(t):
        r0 = t * 128
        allkv = [f"KVF{nb}" for nb in range(4)]
        allq = ["QF0", "QF1", "QF2"]
        KV3 = KVF.rearrange("p (h d) -> p h d", d=256)
        SCRk = SCR[:, 0:1024].rearrange("p (h d) -> p h d", d=128)
        TT("pool", SCRk, KV3[:, :, 0:128], KV3[:, :, 0:128], ALU.mult, allkv, ["SCR"])
        RSUM("dve", ST[:, 16:24], SCRk, ["SCR"], ["k_ssn"])
        TT("pool", KRG, CKV[:, 512:576], CKV[:, 512:576], ALU.mult, ["CKVb"], ["KRG"])
        RSUM("dve", ST[:, 4:5], KRG, ["KRG"], ["kr_ss"])
        TS("dve", ST[:, 16:24], ST[:, 16:24], ST[:, 4:5], None, ALU.add, None, ["k_ssn", "kr_ss"], ["kr_ss2"])
        TS("dve", ST[:, 16:24], ST[:, 16:24], 1.0 / 192, EPS, ALU.mult, ALU.add, ["kr_ss2"], ["kr"])
        P.op("act", lambda e: e.sqrt(ST[:, 16:24], ST[:, 16:24]), reads=["kr"], writes=["kr"])
        P.op("dve", lambda e: e.reciprocal(ST[:, 16:24], ST[:, 16:24]), reads=["kr"], writes=["kr"])
        TT("pool", SCRk, KV3[:, :, 0:128], bc(ST[:, 16:24], [128, 8, 128], 2), ALU.mult, allkv + ["kr"], ["SCR"])
        TT("pool", KN[:, :, 0:128], SCRk, bc(GK[:, 0:128], [128, 8, 128], 1), ALU.mult, ["SCR", "GK"], ["KNc"])
        TT("pool", KRG, CKV[:, 512:576], GK[:, 128:192], ALU.mult, ["CKVb", "GK", "kr_ss"], ["KRG"])
        TT("pool", RT[:, 0, :], KRG[:, 0:32], RC, ALU.mult, ["KRG", "RC", "QNa"], ["RT"])
        TT("pool", RT2[:, 0, :], KRG[:, 32:64], RS, ALU.mult, ["KRG", "RS", "QNa"], ["RT2"])
        TT("pool", RT3[:, 0, :], KRG[:, 32:64], RC, ALU.mult, ["KRG", "RC", "QNb"], ["RT3"])
        TT("pool", RT4[:, 0, :], KRG[:, 0:32], RS, ALU.mult, ["KRG", "RS", "QNb"], ["RT4"])
        TT("dve", KRR[:, 0:32], RT[:, 0, :], RT2[:, 0, :], ALU.subtract, ["RT", "RT2"], ["KRRa"])
        TT("dve", KRR[:, 32:64], RT3[:, 0, :], RT4[:, 0, :], ALU.add, ["RT3", "RT4"], ["KRRb"])
        TT("pool", KN[:, :, 128:192], bc(KRR, [128, 8, 64], 1), bc(ST[:, 16:24], [128, 8, 64], 2), ALU.mult,
           ["KRRa", "KRRb", "kr"], ["KNr"])
        CP("dve", VS, KV3[:, :, 128:256], allkv, ["VS"])
        DMA("sp", V_d[:, r0:r0 + 128, :].rearrange("h t d -> t h d"), VS, "sV", ["VS"], [])


    def p1_kT(t):
        r0 = t * 128
        allkn = ["KNc", "KNr"]
        for h in range(8):
            MM(ps[h // 4][:, (h % 4) * 128:(h % 4 + 1) * 128], KN[:, h, 0:128], IDB, True, True,
               allkn + ["IDB"], [f"ps{h // 4}"])
            MM(ps[2 + h // 4][0:64, (h % 4) * 128:(h % 4 + 1) * 128], KN[:, h, 128:192], IDB, True, True,
               allkn + ["IDB"], [f"ps{2 + h // 4}"])
        for hb in range(2):
            CP("dve", KTs[:, hb * 4:(hb + 1) * 4, :], ps[hb][:, :].rearrange("p (a b) -> p a b", b=128),
               [f"ps{hb}"], [f"KTs{hb}"])
            CP("act", KRs[:, hb * 4:(hb + 1) * 4, :], ps[2 + hb][0:64, :].rearrange("p (a b) -> p a b", b=128),
               [f"ps{2 + hb}"], [f"KRs{hb}"])
        DMA("sp", KT_d[:, :, r0:r0 + 128].rearrange("h d t -> d h t"), KTs, "sKT", ["KTs0", "KTs1"], [])
        DMA("sp", KR_d[:, :, r0:r0 + 128].rearrange("h d t -> d h t"), KRs, "sKR", ["KRs0", "KRs1"], [])


    def p1_u(t):
        r0 = t * 128
        for k in range(8):
            MM(ps[4 + k // 4][:, (k % 4) * 128:(k % 4 + 1) * 128], UF[:, k * 128:(k + 1) * 128], IDB, True, True,
               ["UF5", "UF6", "IDB"], [f"ps{4 + k // 4}"])
        for hb in range(2):
            CP("act" if hb == 0 else "dve", UTs[:, hb * 4:(hb + 1) * 4, :],
               ps[4 + hb][:, :].rearrange("p (a b) -> p a b", b=128), [f"ps{4 + hb}"], [f"UTs{hb}"])
        DMA("sp", UT_d.rearrange("(b f) t -> f b t", f=128)[:, :, r0:r0 + 128], UTs, "sUT", ["UTs0", "UTs1"], [])

    front_load(0)
    front_a(0)
    p1_xT(0)
    for t in range(NT):
        p1_head(t)
        if t + 1 < NT:
            front_load(t + 1)
        p1_proj(t)
        if t + 1 < NT:
            front_a(t + 1)
        if t > 0:
            p1_qT(t - 1)
            p1_kT(t - 1)
        if t + 1 < NT:
            p1_xT(t + 1)
        p1_ckv(t)
        p1_qchain(t)
        p1_u(t)
        p1_kchain(t)
    p1_qT(NT - 1)
    p1_kT(NT - 1)
    P.barrier()
    if upto < 2:
        P.emit(); es.close(); return nc

    if upto < 3:
        P.emit(); es.close(); return nc

    NJ = L // 8
    NJC = L // 1024
    WINC_d = dscr("WINC_d", [128, 256 * 128], BF16)
    WOUT_d = dscr("WOUT_d", [128, 256 * 128], BF16)
    M_d = dscr("M_d", [128, 64 * 128], BF16)
    INC_d = [dscr(f"INC{d}_d", [128, NJC, 128 * 64], F32) for d in range(2)]
    S_d = [dscr(f"S{d}_d", [128, NJC, 64, 128], BF16) for d in range(2)]
    YT_d = dscr("YT_d", [1024, L], BF16)
    SSMT_d = dscr("SSMT_d", [1024, L], BF16)

    A.reset()
    LR = A.t([128, 64], F32); LI = A.t([128, 64], F32); DT = A.t([128, 64], F32)
    XX = A.t([128, 64], F32); TH = A.t([128, 64], F32)
    CC = A.t([128, 64], F32); SS = A.t([128, 64], F32)
    T1 = A.t([128, 64], F32); T2 = A.t([128, 64], F32)
    HPI = A.t([128, 1], F32)
    UR = A.t([128, 9, 64], F32); UI = A.t([128, 9, 64], F32)
    ER = A.t([128, 9, 64], F32); EI = A.t([128, 9, 64], F32)
    EIR = A.t([128, 9, 64], F32); EII = A.t([128, 9, 64], F32)
    MG = A.t([128, 64], F32); IMG = A.t([128, 64], F32)
    QRE = A.t([128, 64], F32); QIM = A.t([128, 64], F32)
    BRE = A.t([128, 32, 16], F32); BIM = A.t([128, 32, 16], F32)
    BBR = A.t([128, 2, 32, 16], F32); BBI = A.t([128, 2, 32, 16], F32)
    CRE = A.t([128, 2, 32, 16], F32); CIM = A.t([128, 2, 32, 16], F32)
    CN2 = [A.t([128, 128], F32) for _ in range(2)]
    TA = A.t([128, 32, 16], F32); TB = A.t([128, 32, 16], F32)
    TA2 = A.t([128, 16, 16], F32); TB2 = A.t([128, 16, 16], F32)
    TA2 = A.t([128, 16, 16], F32); TB2 = A.t([128, 16, 16], F32)
    XR = A.t([128, 16, 8, 16], F32); XI = A.t([128, 16, 8, 16], F32)
    WR = A.t([128, 16, 8, 16], F32); WI = A.t([128, 16, 8, 16], F32)
    WTR = A.t([128, 16, 8, 16], F32); WTI = A.t([128, 16, 8, 16], F32)
    TD = A.t([128, 16, 8, 16], F32)
    MACC = A.t([128, 64, 128], F32)
    MB = A.t([128, 64, 128], BF16)
    MKF = A.t([128, 128], F32); MKB = A.t([128, 128], F32)
    IH = [A.t([128, 128], F32) for _ in range(2)]
    DCOL = A.t([128, 64], F32)
    WINs = [A.t([128, 4, 128], BF16) for _ in range(2)]
    WOUs = [A.t([128, 4, 128], BF16) for _ in range(2)]
    TMPM = A.t([128, 128], F32)

    def rawap(ap, off, dims):
        return bass.AP(ap.tensor, off, dims)
    for q4 in range(4):
        DMA("sp", LR[:, q4 * 16:(q4 + 1) * 16], rawap(lam_re, (q4 // 2) * 4096 + (q4 % 2) * 16 * 128, [[1, 128], [128, 16]]),
            f"g{q4}", [], ["LR"], slow=True)
        DMA("sp", LI[:, q4 * 16:(q4 + 1) * 16], rawap(lam_im, (q4 // 2) * 4096 + (q4 % 2) * 16 * 128, [[1, 128], [128, 16]]),
            f"g{4 + q4}", [], ["LI"], slow=True)
    for gpar in range(2):
        DMA("sp", DT[gpar * 64:(gpar + 1) * 64, :].rearrange("p (d g) -> p d g", d=2),
            rawap(log_dt, gpar, [[0, 64], [64, 2], [2, 32]]), f"g{8 + gpar}", [], ["DT"], slow=True)
    for q4 in range(4):
        DMA("sp", BRE[:, q4 * 8:(q4 + 1) * 8, :], rawap(b_re, q4 * 8 * 2048, [[16, 128], [2048, 8], [1, 16]]),
            f"g{10 + q4}", [], ["BRE"], slow=True)
        DMA("sp", BIM[:, q4 * 8:(q4 + 1) * 8, :], rawap(b_im, q4 * 8 * 2048, [[16, 128], [2048, 8], [1, 16]]),
            f"g{14 + q4}", [], ["BIM"], slow=True)
    for s in range(8):
        DMA("sp", DCOL[s * 16:(s + 1) * 16, :], rawap(d_skip, 0, [[1, 16], [16, 64]]), f"g{18 + s}", [], ["DCOL"], slow=True)
    DMA("sp", MKF, maskf_d, "g26", [], ["MKF"])
    DMA("sp", MKB, maskb_d, "g27", [], ["MKB"])
    MSET("pool", HPI, float(np.pi / 2), ["HPI"])
    MSET("pool", WOUs[0], 0.0, ["WOUs0"])
    MSET("pool", WOUs[1], 0.0, ["WOUs1"])
    for hh in range(2):
        CP("pool", IH[hh], IDF, ["IDF"], [f"IH{hh}"])
        MSET("pool", IH[hh][(1 - hh) * 64:(2 - hh) * 64, :], 0.0, [f"IH{hh}"])
    it = 0
    for ri, csrc, cdst in ((0, c_re, CRE), (1, c_im, CIM)):
        for d in range(2):
            cv = csrc[d].rearrange("g c p -> (g c) p")
            for gb in range(8):
                b = it % 2
                it += 1
                DMA("sp", CN2[b][:, 0:64], cv[gb * 128:(gb + 1) * 128, :], f"CNa{b}", [], [f"CN2a{b}"])
                DMA("sp", CN2[b][:, 64:128], cv[gb * 128:(gb + 1) * 128, :], f"CNb{b}", [], [f"CN2b{b}"])
                MM(ps[b][:, 0:128], CN2[b], IDF, True, True, [f"CN2a{b}", f"CN2b{b}", "IDF"], [f"ps{b}"])
                pv = ps[b][:, 0:128].rearrange("p (g c) -> p g c", c=16)
                CP("dve", cdst[0:64, d, gb * 4:(gb + 1) * 4, :], pv[0:64, 0:8:2, :], [f"ps{b}"], [f"C{ri}"])
                CP("act", cdst[64:128, d, gb * 4:(gb + 1) * 4, :], pv[64:128, 1:8:2, :], [f"ps{b}"], [f"C{ri}"])

    def E(out, a, b_, op, r, w, eng="dve"):
        TT(eng, out, a, b_, op, r, w)
    ACT(DT, DT, AF.Exp, ["DT"], ["DT"])
    E(XX, LR, DT, ALU.mult, ["LR", "DT"], ["XX"])
    E(TH, LI, DT, ALU.mult, ["LI", "DT"], ["TH"])
    ACT(SS, TH, AF.Sin, ["TH"], ["SS"], scale=1.0 / 16)
    ACT(CC, TH, AF.Sin, ["TH", "HPI"], ["CC"], scale=1.0 / 16, bias=HPI[:, 0:1])
    for i in range(4):
        E(T1, CC, CC, ALU.mult, ["CC"], ["T1"])
        E(T2, SS, SS, ALU.mult, ["SS"], ["T2"])
        STT("dve", SS, CC, 2.0, SS, ALU.mult, ALU.mult, ["CC", "SS"], ["SS"])
        E(CC, T1, T2, ALU.subtract, ["T1", "T2"], ["CC"])
    CP("dve", UR[:, 1, :], CC, ["CC"], ["U"])
    CP("dve", UI[:, 1, :], SS, ["SS"], ["U"])
    for k in range(2, 9):
        E(T1, UR[:, k - 1, :], CC, ALU.mult, ["U", "CC"], ["T1"])
        E(T2, UI[:, k - 1, :], SS, ALU.mult, ["U", "SS"], ["T2"])
        E(UR[:, k, :], T1, T2, ALU.subtract, ["T1", "T2"], ["U"])
        E(T1, UR[:, k - 1, :], SS, ALU.mult, ["U", "SS"], ["T1"])
        E(T2, UI[:, k - 1, :], CC, ALU.mult, ["U", "CC"], ["T2"])
        E(UI[:, k, :], T1, T2, ALU.add, ["T1", "T2"], ["U"])
    for k in range(1, 9):
        ACT(MG, XX, AF.Exp, ["XX"], ["MG"], scale=float(k))
        ACT(IMG, XX, AF.Exp, ["XX"], ["IMG"], scale=float(-k))
        E(ER[:, k, :], MG, UR[:, k, :], ALU.mult, ["MG", "U"], ["ET"])
        E(EI[:, k, :], MG, UI[:, k, :], ALU.mult, ["MG", "U"], ["ET"])
        E(EIR[:, k, :], IMG, UR[:, k, :], ALU.mult, ["IMG", "U"], ["ET"])
        STT("dve", EII[:, k, :], UI[:, k, :], -1.0, IMG, ALU.mult, ALU.mult, ["IMG", "U"], ["ET"])
    for d in range(2):
        dsl = slice(d * 32, (d + 1) * 32)
        CP("dve", A8T[:, d, 0, 0:32], ER[:, 8, dsl], ["ET"], ["A8T"])
        CP("dve", A8T[:, d, 0, 32:64], ER[:, 8, dsl], ["ET"], ["A8T"])
        TS("dve", A8T[:, d, 1, 0:32], EI[:, 8, dsl], -1.0, None, ALU.mult, None, ["ET"], ["A8T"])
        CP("dve", A8T[:, d, 1, 32:64], EI[:, 8, dsl], ["ET"], ["A8T"])
    TS("dve", T1, ER[:, 1, :], -1.0, None, ALU.add, None, ["ET"], ["T1"])
    E(QRE, T1, LR, ALU.mult, ["T1", "LR"], ["QRE"])
    E(T2, EI[:, 1, :], LI, ALU.mult, ["ET", "LI"], ["T2"])
    E(QRE, QRE, T2, ALU.add, ["QRE", "T2"], ["QRE"])
    E(QIM, EI[:, 1, :], LR, ALU.mult, ["ET", "LR"], ["QIM"])
    E(T2, T1, LI, ALU.mult, ["T1", "LI"], ["T2"])
    E(QIM, QIM, T2, ALU.subtract, ["QIM", "T2"], ["QIM"])
    E(T1, LR, LR, ALU.mult, ["LR"], ["T1"])
    E(T2, LI, LI, ALU.mult, ["LI"], ["T2"])
    E(T1, T1, T2, ALU.add, ["T1", "T2"], ["T1"])
    P.op("dve", lambda e: e.reciprocal(T1, T1), reads=["T1"], writes=["T1"])
    E(QRE, QRE, T1, ALU.mult, ["QRE", "T1"], ["QRE"])
    E(QIM, QIM, T1, ALU.mult, ["QIM", "T1"], ["QIM"])
    for d in range(2):
        dsl = slice(d * 32, (d + 1) * 32)
        qr = bc(QRE[:, dsl], [128, 32, 16], 2)
        qi = bc(QIM[:, dsl], [128, 32, 16], 2)
        E(TA, qr, BRE, ALU.mult, ["QRE", "BRE"], ["TA"])
        E(TB, qi, BIM, ALU.mult, ["QIM", "BIM"], ["TB"])
        E(BBR[:, d], TA, TB, ALU.subtract, ["TA", "TB"], ["BBR"])
        E(TA, qr, BIM, ALU.mult, ["QRE", "BIM"], ["TA"])
        E(TB, qi, BRE, ALU.mult, ["QIM", "BRE"], ["TB"])
        E(BBI[:, d], TA, TB, ALU.add, ["TA", "TB"], ["BBI"])

    for d in range(2):
      for gph in range(2):
        dsl = slice(d * 32 + gph * 16, d * 32 + (gph + 1) * 16)
        gsl = slice(gph * 16, (gph + 1) * 16)
        TAh = TA[:, 0:16, :]
        TBh = TB[:, 0:16, :]
        for s in range(8):
            k = s + 1 if d == 0 else 8 - s
            er = bc(EIR[:, k, dsl], [128, 16, 16], 2)
            ei = bc(EII[:, k, dsl], [128, 16, 16], 2)
            E(TAh, er, BBR[:, d, gsl], ALU.mult, ["ET", "BBR"], ["TA"])
            E(TBh, ei, BBI[:, d, gsl], ALU.mult, ["ET", "BBI"], ["TB"])
            E(XR[:, :, s, :], TAh, TBh, ALU.subtract, ["TA", "TB"], ["XR"])
            E(TAh, er, BBI[:, d, gsl], ALU.mult, ["ET", "BBI"], ["TA"])
            E(TBh, ei, BBR[:, d, gsl], ALU.mult, ["ET", "BBR"], ["TB"])
            E(XI[:, :, s, :], TAh, TBh, ALU.add, ["TA", "TB"], ["XI"])
        for t in range(8):
            k = t + 1 if d == 0 else 8 - t
            er = bc(ER[:, k, dsl], [128, 16, 16], 2)
            ei = bc(EI[:, k, dsl], [128, 16, 16], 2)
            E(TA2, CRE[:, d, gsl], er, ALU.mult, ["ET", "C0"], ["TA2"], eng="pool")
            E(TB2, CIM[:, d, gsl], ei, ALU.mult, ["ET", "C1"], ["TB2"], eng="pool")
            E(WR[:, :, t, :], TA2, TB2, ALU.subtract, ["TA2", "TB2"], ["WR"], eng="pool")
            E(TA2, CRE[:, d, gsl], ei, ALU.mult, ["ET", "C0"], ["TA2"], eng="pool")
            E(TB2, CIM[:, d, gsl], er, ALU.mult, ["ET", "C1"], ["TB2"], eng="pool")
            E(TA2, TA2, TB2, ALU.add, ["TA2", "TB2"], ["TA2"], eng="pool")
            TS("pool", WI[:, :, t, :], TA2, -1.0, 0.0, ALU.mult, ALU.add, ["TA2"], ["WI"])
        X3 = lambda tt_: tt_.rearrange("p g s c -> p g (s c)")
        e8r = bc(ER[:, 8, dsl], [128, 16, 128], 2)
        e8i = bc(EI[:, 8, dsl], [128, 16, 128], 2)
        E(X3(WTR), e8r, X3(XR), ALU.mult, ["ET", "XR"], ["WTR"])
        E(X3(TD), e8i, X3(XI), ALU.mult, ["ET", "XI"], ["TD"])
        E(X3(WTR), X3(WTR), X3(TD), ALU.subtract, ["WTR", "TD"], ["WTR"])
        E(X3(WTI), e8r, X3(XI), ALU.mult, ["ET", "XI"], ["WTI"])
        E(X3(TD), e8i, X3(XR), ALU.mult, ["ET", "XR"], ["TD"])
        E(X3(WTI), X3(WTI), X3(TD), ALU.add, ["WTI", "TD"], ["WTI"])
        for gpl in range(16):
            gp = gph * 16 + gpl
            sb_ = gp % 2
            for gpar in range(2):
                g = 2 * gp + gpar
                hs = slice(gpar * 64, (gpar + 1) * 64)
                pm = ps[gpar]
                MM(pm[:, 0:128], X3(XR)[hs, gpl, :], X3(WR)[hs, gpl, :], True, False, ["XR", "WR"], [f"ps{gpar}"])
                MM(pm[:, 0:128], X3(XI)[hs, gpl, :], X3(WI)[hs, gpl, :], False, True, ["XI", "WI"], [f"ps{gpar}"])
                if d == 0:
                    TT("dve", MACC[:, g, :], pm[:, 0:128], MKF, ALU.mult, [f"ps{gpar}", "MKF"], [f"MACC{g}"])
                else:
                    TT("dve", TMPM, pm[:, 0:128], MKB, ALU.mult, [f"ps{gpar}", "MKB"], ["TMPM"])
                    TT("dve", MACC[:, g, :], MACC[:, g, :], TMPM, ALU.add, [f"MACC{g}", "TMPM"], [f"MACC{g}"])
                for ri in range(2):
                    src = X3(WTR if ri == 0 else WTI)
                    pw = ps[2 + gpar * 2 + ri]
                    MM(pw[:, 0:128], src[:, gpl, :], IH[gpar], True, True, ["WTR", "WTI", f"IH{gpar}"],
                       [f"ps{2 + gpar * 2 + ri}"])
                    CP("act", WINs[sb_][:, gpar * 2 + ri, :], pw[:, 0:128], [f"ps{2 + gpar * 2 + ri}"],
                       [f"WINs{sb_}"])
                    CP("pool", WOUs[sb_][hs, gpar * 2 + ri, :], X3(WR if ri == 0 else WI)[hs, gpl, :],
                       ["WR", "WI"], [f"WOUs{sb_}"])
            base = (d * 128 + 4 * gp) * 128
            DMA("sp", WINC_d[:, base:base + 512], WINs[sb_].rearrange("p a b -> p (a b)"), f"sWIN{sb_}",
                [f"WINs{sb_}"], ["WINC_d"])
            DMA("sp", WOUT_d[:, base:base + 512], WOUs[sb_].rearrange("p a b -> p (a b)"), f"sWOU{sb_}",
                [f"WOUs{sb_}"], ["WOUT_d"])
    for g in range(64):
        STT("dve", MACC[:, g, :], IDF, DCOL[:, g:g + 1], MACC[:, g, :], ALU.mult, ALU.add,
            [f"MACC{g}", "IDF", "DCOL"], [f"MACC{g}"])
    CP("act", MB, MACC, [f"MACC{g}" for g in range(64)], ["MB"])
    DMA("sp", M_d, MB.rearrange("p a b -> p (a b)"), "sMB", ["MB"], ["M_d"])
    P.barrier()
    if upto < 4:
        P.emit(); es.close(); return nc

    A.reset()
    WINC = A.t([128, 256, 128], BF16)
    SEL = A.t([128, 64, 128], BF16)
    UTc = A.t([128, 8, 1024], BF16)
    UGall = A.t([128, 64, 128], BF16)
    INCc = [A.t([128, 128, 64], F32) for _ in range(2)]
    for i4 in range(4):
        DMA("sp", WINC[:, i4 * 64:(i4 + 1) * 64, :].rearrange("p a b -> p (a b)"),
            WINC_d[:, i4 * 64 * 128:(i4 + 1) * 64 * 128], f"lWINC{i4}", [], [f"WINC{i4}"])
    allwinc = [f"WINC{i4}" for i4 in range(4)]
    DMA("pool", SEL, sel_d, "lSEL", [], ["SEL"])
    UT_v = UT_d.rearrange("(b f) t -> f b t", f=128)
    for jc in range(NJC):
        t0 = jc * 1024
        DMA("sp", UTc, UT_v[:, :, t0:t0 + 1024], "lUTc", [], ["UTc"])
        for g4 in range(16):
            pb = g4 % 2
            for gi in range(4):
                g = g4 * 4 + gi
                blk, gl = g // 8, g % 8
                for s in range(8):
                    MM(ps[pb][:, gi * 128:(gi + 1) * 128], SEL[:, s * 8 + gl, :], UTc[:, blk, s:1024:8],
                       s == 0, s == 7, ["SEL", "UTc"], [f"ps{pb}"])
            CP("act" if pb == 0 else "dve", UGall[:, g4 * 4:(g4 + 1) * 4, :],
               ps[pb][:, :].rearrange("p (a b) -> p a b", b=128), [f"ps{pb}"], [f"UG{g4}"])
        allug = [f"UG{g4}" for g4 in range(16)]
        n = 0
        for d in range(2):
            for gp in range(32):
                pb = 2 + n % 4
                n += 1
                for ri in range(2):
                    for gpar in range(2):
                        g = 2 * gp + gpar
                        MM(ps[pb][:, ri * 128:(ri + 1) * 128], WINC[:, (d * 64 + g) * 2 + ri, :], UGall[:, g, :],
                           gpar == 0, gpar == 1, allwinc + allug, [f"ps{pb}"])
                dst = INCc[d].rearrange("p j (r g) -> p r j g", r=2)[:, :, :, gp]
                CP("act" if n % 2 == 0 else "dve", dst, ps[pb][:, 0:256].rearrange("p (r j) -> p r j", r=2),
                   [f"ps{pb}"], [f"INCc{d}"])
        for d in range(2):
            DMA("sp", INC_d[d][:, jc, :], INCc[d].rearrange("p a b -> p (a b)"), f"sINC{d}", [f"INCc{d}"], [f"INC_d{d}"])
    P.barrier()
    if upto < 5:
        P.emit(); es.close(); return nc

    A.reset()
    KTh = [A.t([128, L], BF16) for _ in range(1)]
    KRh = [A.t([128, L], BF16) for _ in range(1)]
    Vh = [A.t([128, NT, 128], BF16) for _ in range(1)]
    QTb = [A.t([128, 512], BF16) for _ in range(2)]
    QRb = [A.t([128, 512], BF16) for _ in range(2)]
    PT = [A.t([128, 512], BF16) for _ in range(4)]
    RSb = A.t([128, 512], F32)
    ATn = A.t([128, 512], F32)
    SQa = A.t([128, 512], BF16)
    ATo = [A.t([128, 512], BF16) for _ in range(2)]
    ONESB = A.t([128, 128], BF16)
    ONEC2 = A.t([128, 2], BF16)
    AOGc = A.t([128, 8], F32)
    HJ = 64
    INs = [A.t([128, HJ, 64], F32) for _ in range(2)]
    SO = [A.t([128, HJ + 1, 64], F32) for _ in range(2)]
    SB16 = [A.t([128, 64, 128], BF16) for _ in range(2)]
    PQ = [A.t([128, 2, 64], F32) for _ in range(2)]
    G2Bt = A.t([128, 2048], F32)
    WSTG = [A.t([128, 4, 1024], BF16) for _ in range(2)]
    DMA("sp", G2Bt, MOD_d[5 * D:6 * D].partition_broadcast(128), "lG2Bt", [], ["G2Bt"])
    MSET("pool", ONESB, 1.0, ["ONESB"])
    MSET("pool", ONEC2, 1.0, ["ONEC2"])
    MSET("pool", KRh[0][64:128, :], 0.0, ["KRh0"])
    for b_ in range(2):
        MSET("pool", QRb[b_][64:128, :], 0.0, [f"QRb{b_}"])
    DMA("sp", AOGc, att_out_g.rearrange("(k p) -> p k", p=128), "lAOGc", [], ["AOGc"], slow=True)

    def scan_gen(d, eng):
        if d == 0:
            MSET(eng, SO[0][:, 0, :], 0.0, ["SO0"])
        else:
            MSET(eng, SO[1][:, HJ, :], 0.0, ["SO1"])
        AA = A8T[:, d, 0, :]
        AIMS = A8T[:, d, 1, :]
        nh = 128 // HJ
        for step in range(NJC * nh):
            hc = step if d == 0 else NJC * nh - 1 - step
            jc, hh = hc // nh, hc % nh
            DMA("pool", INs[d].rearrange("p a b -> p (a b)"), INC_d[d][:, jc, hh * HJ * 64:(hh + 1) * HJ * 64],
                f"lIN{d}", [f"INC_d{d}"], [f"INs{d}"])
            for ii in range(HJ):
                i = ii if d == 0 else HJ - 1 - ii
                src = SO[d][:, i, :] if d == 0 else SO[d][:, i + 1, :]
                dst = SO[d][:, i + 1, :] if d == 0 else SO[d][:, i, :]
                swp = src.rearrange("p (r g) -> p r g", r=2)[:, ::-1, :]
                TT(eng, PQ[d][:, 0, :], AA, src, ALU.mult, [f"SO{d}", "A8T"], [f"P{d}"])
                TT(eng, PQ[d][:, 1, :].rearrange("p (r g) -> p r g", r=2), AIMS.rearrange("p (r g) -> p r g", r=2), swp,
                   ALU.mult, [f"SO{d}", "A8T"], [f"Q{d}"])
                TT(eng, PQ[d][:, 0, :], PQ[d][:, 0, :], PQ[d][:, 1, :], ALU.add, [f"P{d}", f"Q{d}"], [f"P{d}"])
                TT(eng, dst, PQ[d][:, 0, :], INs[d][:, i, :], ALU.add, [f"P{d}", f"INs{d}"], [f"SO{d}"])
                yield
            lo = 0 if d == 0 else 1
            CP(eng, SB16[d][:, :, hh * HJ:(hh + 1) * HJ], SO[d][:, lo:lo + HJ, :].rearrange("p j c -> p c j"),
               [f"SO{d}"], [f"SB16{d}"])
            last_half = (hh == nh - 1) if d == 0 else (hh == 0)
            if last_half:
                DMA("pool", S_d[d][:, jc], SB16[d], f"sS{d}", [f"SB16{d}"], [f"S_d{d}"])
            if d == 0:
                CP(eng, SO[d][:, 0, :], SO[d][:, HJ, :], [f"SO{d}"], [f"SO{d}"])
            else:
                CP(eng, SO[d][:, HJ, :], SO[d][:, 0, :], [f"SO{d}"], [f"SO{d}"])
        while True:
            yield

    def att_tail(h, qb, ib, po, pz):
        P.op("dve", (lambda pz_: lambda e: e.reciprocal(RSb, ps[pz_][:, :]))(pz), reads=[f"ps{pz}"], writes=["RSb"])
        TT("dve", ATn, ps[po][:, :], RSb, ALU.mult, [f"ps{po}", "RSb"], ["ATn"])
        TT("pool", SQa, ATn, ATn, ALU.mult, ["ATn"], ["SQa"])
        for qs in range(4):
            MM(ps[6][:, qs:qs + 1], SQa[:, qs * 128:(qs + 1) * 128], ONEC2[:, 0:1], True, True, ["SQa", "ONEC2"], ["ps6"])
        if h == 0:
            CP("dve", SSA[:, qb * 4:(qb + 1) * 4], ps[6][:, 0:4], ["ps6"], [f"SSA{qb}"])
        else:
            TT("dve", SSA[:, qb * 4:(qb + 1) * 4], SSA[:, qb * 4:(qb + 1) * 4], ps[6][:, 0:4], ALU.add,
               ["ps6", f"SSA{qb}"], [f"SSA{qb}"])
        TS("dve", ATo[ib], ATn, AOGc[:, h:h + 1], None, ALU.mult, None, ["ATn", "AOGc"], [f"ATo{ib}"])
        DMA("sp", ATTT_d[h * 128:(h + 1) * 128, qb * 512:(qb + 1) * 512], ATo[ib], f"sATo{ib}", [f"ATo{ib}"], [])

    def cast_gen():
        n = 0
        for wsrc, wdst in ((w1, W1s), (w3, W3s)):
            wv = wsrc.rearrange("(k p) n -> p k n", p=128)
            for pr in range(22):
                DMA("pool", wdst[pr].rearrange("p (k c) -> p k c", k=16), wv[:, :, pr * 256:(pr + 1) * 256], f"cast{n % 4}", [], ["Wsd"])
                n += 1
                yield
        w2v = w2.rearrange("(g j p) n -> g p j n", j=4, p=128)
        for hh in range(2):
            for g4 in range(11):
                b_ = n % 2
                n += 1
                DMA("pool", WSTG[b_], w2v[g4][:, :, hh * 1024:(hh + 1) * 1024], f"castl{b_}", [], [f"WSTG{b_}"])
                yield
                TT("dve", WSTG[b_], WSTG[b_], bc(G2Bt[:, hh * 1024:(hh + 1) * 1024], [128, 4, 1024], 1), ALU.mult,
                   [f"WSTG{b_}", "G2Bt"], [f"WSTG{b_}"])
                DMA("pool", W2s[hh, g4].rearrange("p (j c) -> p j c", j=4), WSTG[b_], f"casts{b_}", [f"WSTG{b_}"], ["Wsd"])
                yield
        cast_done.append(1)
        while True:
            yield

    cast_done = []
    castg = cast_gen()
    pending_tail = None
    scans = [scan_gen(0, "dve"), scan_gen(1, "pool")]
    steps_per_it = -(-(NJ) // (NH * (L // 512)))
    sc = 1.0 / float(np.sqrt(192.0))
    NQB = L // 512
    it = 0
    for h in range(NH):
        hb = 0
        DMA("sp", KTh[hb], KT_d[h], f"KTh{hb}", [], [f"KTh{hb}"])
        DMA("sp", KRh[hb][0:64, :], KR_d[h], f"KRh{hb}", [], [f"KRh{hb}"])
        DMA("sp", Vh[hb], V_d[h].rearrange("(n p) d -> p n d", p=128), f"Vh{hb}", [], [f"Vh{hb}"])
        for qb in range(NQB):
            ib = it % 2
            it += 1
            po = 2 + ib
            pz = 4 + ib
            DMA("sp", QTb[ib], QT_d[h][:, qb * 512:(qb + 1) * 512], f"QTb{ib}", [], [f"QTb{ib}"])
            DMA("sp", QRb[ib][0:64, :], QR_d[h][:, qb * 512:(qb + 1) * 512], f"QRb{ib}", [], [f"QRb{ib}"])

            SBK = [0, 1, 7]

            def S(kt):
                sb_ = SBK[kt % 3]
                MM(ps[sb_][:, :], KTh[hb][:, kt * 128:(kt + 1) * 128], QTb[ib][:, :], True, False,
                   [f"KTh{hb}", f"QTb{ib}"], [f"ps{sb_}"])
                MM(ps[sb_][:, :], KRh[hb][:, kt * 128:(kt + 1) * 128], QRb[ib][:, :], False, True,
                   [f"KRh{hb}", f"QRb{ib}"], [f"ps{sb_}"])
            S(0)
            if NT > 1:
                S(1)
            for kt in range(NT):
                sb_ = SBK[kt % 3]
                pb3 = kt % 4
                if kt + 2 < NT:
                    S(kt + 2)
                if kt == min(3, NT - 1) and pending_tail is not None:
                    att_tail(*pending_tail)
                    pending_tail = None
                ACT(PT[pb3], ps[sb_][:, :], AF.Exp, [f"ps{sb_}"], [f"PT{pb3}"], scale=sc)
                MM(ps[po][:, :], Vh[hb][:, kt, :], PT[pb3], kt == 0, kt == NT - 1, [f"PT{pb3}", f"Vh{hb}"], [f"ps{po}"])
                MM(ps[pz][:, :], ONESB, PT[pb3], kt == 0, kt == NT - 1, [f"PT{pb3}", "ONESB"], [f"ps{pz}"])
            pending_tail = (h, qb, ib, po, pz)
            for _ in range(steps_per_it):
                next(scans[0])
                next(scans[1])
            next(castg)
    att_tail(*pending_tail)
    while not cast_done:
        next(castg)
    for _ in range(4 * HJ):
        next(scans[0])
        next(scans[1])
    P.barrier()
    A.reset()
    JG = min(4, NJC)
    NW = 128 * JG
    NTK = 1024 * JG
    SELT = A.t([128, 64, 128], BF16)
    SEL = A.t([128, 64, 128], BF16)
    WOb = A.t([128, 2, 16, 128], BF16)
    Mb = A.t([128, 8, 128], BF16)
    UTb = [A.t([128, NTK], BF16) for _ in range(2)]
    Sb = [[A.t([128, 2, 4, NW], BF16) for _ in range(2)] for _ in range(2)]
    UG8 = A.t([128, 8, NW], BF16)
    YG = A.t([128, 8, NW], BF16)
    YTb = A.t([128, NTK], F32)
    GT1 = A.t([128, NTK], F32)
    GT2 = A.t([128, NTK], F32)
    GOUT = [A.t([128, NTK], BF16) for _ in range(2)]
    DMA("pool", SEL, sel_d, "lSEL", [], ["SEL"])
    DMA("pool", SELT, selt_d, "lSELT", [], ["SELT"])
    GC = float(2.0 * np.sqrt(2.0 / np.pi))

    def load53(it_):
        blk_, jg_ = it_ // (NJC // JG), it_ % (NJC // JG)
        ib_ = it_ % 2
        tt0 = jg_ * NTK
        DMA("sp", UTb[ib_], UT_d[blk_ * 128:(blk_ + 1) * 128, tt0:tt0 + NTK], f"lUTb{ib_}", [], [f"UTb{ib_}"])
        for d in range(2):
            for ri in range(2):
                for gpl in range(4):
                    DMA("sp", Sb[ib_][d][:, ri, gpl, :].rearrange("p (c j) -> p c j", c=JG),
                        S_d[d][:, jg_ * JG:(jg_ + 1) * JG, ri * 32 + blk_ * 4 + gpl, :],
                        f"lSb{ib_}{d}{ri}{gpl}", [], [f"Sb{ib_}{d}{ri}{gpl}"])
    it = 0
    ne = 0
    for blk in range(8):
        for d in range(2):
            base = ((d * 64 + blk * 8) * 2) * 128
            DMA("sp", WOb[:, d].rearrange("p a b -> p (a b)"), WOUT_d[:, base:base + 16 * 128], f"lWOb{d}", [], [f"WOb{d}"])
        DMA("sp", Mb.rearrange("p a b -> p (a b)"), M_d[:, blk * 8 * 128:(blk + 1) * 8 * 128], "lMb", [], ["Mb"])
        for jg in range(NJC // JG):
            ib = it % 2
            t0 = jg * NTK
            if it == 0:
                load53(0)
            if it + 1 < 8 * (NJC // JG):
                load53(it + 1)
            it += 1
            for gl in range(8):
                pb = gl % 4
                for s in range(8):
                    MM(ps[pb][:, 0:NW], SEL[:, s * 8 + gl, :], UTb[ib][:, s:NTK:8], s == 0, s == 7,
                       ["SEL", f"UTb{ib}"], [f"ps{pb}"])
                ne += 1
                CP("act" if ne % 2 == 0 else "dve", UG8[:, gl, :], ps[pb][:, 0:NW], [f"ps{pb}"], [f"UG8{gl}"])
            for gl in range(8):
                pb = 4 + gl % 4
                o = ps[pb][:, 0:NW]
                MM(o, Mb[:, gl, :], UG8[:, gl, :], True, False, ["Mb", f"UG8{gl}"], [f"ps{pb}"])
                for d in range(2):
                    for ri in range(2):
                        MM(o, WOb[:, d, gl * 2 + ri, :], Sb[ib][d][:, ri, gl // 2, :], False, (d == 1 and ri == 1),
                           [f"WOb{d}", f"Sb{ib}{d}{ri}{gl // 2}"], [f"ps{pb}"])
                ne += 1
                CP("act" if ne % 2 == 0 else "dve", YG[:, gl, :], o, [f"ps{pb}"], [f"YG{gl}"])
            allyg = [f"YG{gl}" for gl in range(8)]
            YTv = YTb.rearrange("p (j t) -> p t j", t=8)
            for t in range(8):
                pb = t % 4
                for gl in range(8):
                    MM(ps[pb][:, 0:NW], SELT[:, t * 8 + gl, :], YG[:, gl, :], gl == 0, gl == 7, ["SELT"] + allyg, [f"ps{pb}"])
                ne += 1
                CP("act" if ne % 2 == 0 else "dve", YTv[:, t, :], ps[pb][:, 0:NW], [f"ps{pb}"], [f"YTb{t}"])
            ally = [f"YTb{t}" for t in range(8)]
            TT("pool", GT1, YTb, YTb, ALU.mult, ally, ["GT1"])
            TS("pool", GT1, GT1, 0.044715, 1.0, ALU.mult, ALU.add, ["GT1"], ["GT1"])
            TT("pool", GT1, GT1, YTb, ALU.mult, ["GT1"] + ally, ["GT1"])
            ACT(GT2, GT1, AF.Sigmoid, ["GT1"], ["GT2"], scale=GC)
            TT("dve", GOUT[ib], GT2, YTb, ALU.mult, ["GT2"] + ally, [f"GOUT{ib}"])
            DMA("sp", YT_d[blk * 128:(blk + 1) * 128, t0:t0 + NTK], GOUT[ib], f"sGOUT{ib}", [f"GOUT{ib}"], ["YT_d"])
    P.barrier()
    if upto < 7:
        P.emit(); es.close(); return nc

    A.reset()
    WG = A.t([128, 8, 1024], BF16)
    BG = A.t([128, 8], F32)
    SOG = A.t([128, 8], F32)
    ONEC = A.t([128, 2], BF16)
    YTc = [A.t([128, 8, 512], BF16) for _ in range(2)]
    SG = A.t([128, 512], F32)
    SSMf = A.t([128, 512], F32)
    SQb = A.t([128, 8, 512], BF16)
    SSMo = [A.t([128, 8, 512], BF16) for _ in range(2)]
    for k in range(8):
        DMA("pool", WG[:, k, :], w_glu[k * 128:(k + 1) * 128, :], f"lWG{k}", [], [f"WG{k}"])
    allwg = [f"WG{k}" for k in range(8)]
    DMA("sp", BG, b_glu.rearrange("(k p) -> p k", p=128), "g0", [], ["BG"], slow=True)
    DMA("sp", SOG, ssm_out_g.rearrange("(k p) -> p k", p=128), "g1", [], ["SOG"], slow=True)
    MSET("pool", ONEC, 1.0, ["ONEC"])
    YT_v = YT_d.rearrange("(b f) t -> f b t", f=128)
    SSMT_v = SSMT_d.rearrange("(b f) t -> f b t", f=128)
    DMA("sp", YTc[0], YT_v[:, :, 0:512], "lYTc0", ["YT_d"], ["YTc0"])
    for tc in range(L // 512):
        ib = tc % 2
        t0 = tc * 512
        if tc + 1 < L // 512:
            DMA("sp", YTc[1 - ib], YT_v[:, :, t0 + 512:t0 + 1024], f"lYTc{1 - ib}", ["YT_d"], [f"YTc{1 - ib}"])
        for oc in range(8):
            pb = oc % 2
            for k in range(8):
                MM(ps[pb][:, :], WG[:, k, oc * 128:(oc + 1) * 128], YTc[ib][:, k, :], k == 0, k == 7,
                   allwg + [f"YTc{ib}"], [f"ps{pb}"])
            ACT(SG, ps[pb][:, :], AF.Sigmoid, [f"ps{pb}", "BG"], ["SG"], bias=BG[:, oc:oc + 1])
            TT("dve", SSMf, SG, YTc[ib][:, oc, :], ALU.mult, ["SG", f"YTc{ib}"], ["SSMf"])
            TT("pool", SQb[:, oc, :], SSMf, SSMf, ALU.mult, ["SSMf"], [f"SQb{oc}"])
            TS("dve", SSMo[ib][:, oc, :], SSMf, SOG[:, oc:oc + 1], None, ALU.mult, None, ["SSMf", "SOG"], [f"SSMo{ib}_{oc}"])
        for tt_ in range(4):
            for oc in range(8):
                MM(ps[2][:, tt_:tt_ + 1], SQb[:, oc, tt_ * 128:(tt_ + 1) * 128], ONEC[:, 0:1], oc == 0, oc == 7,
                   [f"SQb{o2}" for o2 in range(8)] + ["ONEC"], ["ps2"])
        CP("dve", SSS[:, tc * 4:(tc + 1) * 4], ps[2][:, 0:4], ["ps2"], ["SSS"])
        DMA("sp", SSMT_v[:, :, t0:t0 + 512], SSMo[ib], f"sSSMo{ib}", [f"SSMo{ib}_{oc}" for oc in range(8)], ["SSMT_d"])
    P.barrier()

    if upto < 8:
        P.emit(); es.close(); return nc

    X1_d = dscr("X1_d", [L, D], F32)
    A.reset()
    WO = A.t([128, 16, 2048], BF16)
    G1B = A.t([128, 2048], F32)
    AOG = A.t([128, 8], F32)
    XT2 = [A.t([128, 2048], F32) for _ in range(2)]
    MIXT = [A.t([128, 16, 128], BF16) for _ in range(2)]
    X1t = [A.t([128, 2048], F32) for _ in range(2)]
    TE1 = [A.t([128, 512], F32) for _ in range(2)]
    TE2 = [A.t([128, 512], F32) for _ in range(2)]
    ST4 = A.t([128, 8], F32)
    WST = [A.t([128, 16, 256], BF16) for _ in range(2)]
    DMA("sp", G1B, MOD_d[2 * D:3 * D].partition_broadcast(128), "lG1B", ["MOD_d"], ["G1B"])
    for k in range(16):
        DMA("pool", WO[:, k, :], w_o[k * 128:(k + 1) * 128, :], f"lWO{k}", [], [f"WO{k}"])
        TT("dve", WO[:, k, :], WO[:, k, :], G1B, ALU.mult, [f"WO{k}", "G1B"], [f"WO{k}"])
    SSMT_v2 = SSMT_d.rearrange("(b f) t -> f b t", f=128)
    ATTT_v2 = ATTT_d.rearrange("(b f) t -> f b t", f=128)
    ne = 0
    def load4a(t_):
        rr = t_ * 128
        tb_ = t_ % 2
        DMA("sp", XT2[tb_], x[rr:rr + 128, :], f"lXT2{tb_}", [], [f"XT2{tb_}"])
        DMA("sp", MIXT[tb_][:, 8:16, :], SSMT_v2[:, :, rr:rr + 128], f"lMIXs{tb_}", [], [f"MIXs{tb_}"])
        DMA("sp", MIXT[tb_][:, 0:8, :], ATTT_v2[:, :, rr:rr + 128], f"lMIXa{tb_}", [], [f"MIXa{tb_}"])
    load4a(0)
    for t in range(NT):
        r0 = t * 128
        tb = t % 2
        if t + 1 < NT:
            load4a(t + 1)
        RSTD(ST4[:, tb * 4 + 1:tb * 4 + 2], SSA[:, t:t + 1], 1024, f"ar{tb}", "SSA")
        RSTD(ST4[:, tb * 4 + 2:tb * 4 + 3], SSS[:, t:t + 1], 1024, f"sr{tb}", "SSS")
        for nb in range(4):
            pa = ps[2 * nb]
            pss = ps[2 * nb + 1]
            na, ns = f"ps{2 * nb}", f"ps{2 * nb + 1}"
            cs = slice(nb * 512, (nb + 1) * 512)
            eb = ne % 2
            ne += 1
            for k in range(8):
                MM(pa[:, :], MIXT[tb][:, k, :], WO[:, k, cs], k == 0, k == 7, [f"MIXa{tb}", f"WO{k}"], [na])
            for k in range(8, 16):
                MM(pss[:, :], MIXT[tb][:, k, :], WO[:, k, cs], k == 8, k == 15, [f"MIXs{tb}", f"WO{k}"], [ns])
            ACT(TE1[eb], pa[:, :], AF.Copy, [na, f"ar{tb}"], [f"TE1{eb}"], scale=ST4[:, tb * 4 + 1:tb * 4 + 2])
            STT("dve", TE2[eb], pss[:, :], ST4[:, tb * 4 + 2:tb * 4 + 3], TE1[eb], ALU.mult, ALU.add, [ns, f"sr{tb}", f"TE1{eb}"], [f"TE2{eb}"])
            TT("pool", X1t[tb][:, cs], TE2[eb], XT2[tb][:, cs], ALU.add, [f"TE2{eb}", f"XT2{tb}"], [f"X1t{tb}_{nb}"])
        DMA("pool", X1_d[r0:r0 + 128, :], X1t[tb], f"sX1{tb}", [f"X1t{tb}_{nb}" for nb in range(4)], ["X1_d"])
    P.barrier()
    if upto < 9:
        P.emit(); es.close(); return nc

    A.reset()
    X1s = [A.t([128, 4, 2048], F32) for _ in range(2)]
    XB2 = [A.t([128, 2048], BF16) for _ in range(4)]
    JNK2 = A.t([128, 2048], BF16)
    H2T = A.t([128, 16, 512], BF16)
    W1t = [A.t([128, 16, 256], BF16) for _ in range(2)]
    W3t = [A.t([128, 16, 256], BF16) for _ in range(2)]
    GT = A.t([128, 44, 512], BF16)
    W2t = [A.t([128, 4, 1024], BF16) for _ in range(2)]
    OUTs = [A.t([128, 512], F32) for _ in range(3)]
    SA = [A.t([128, 512], F32) for _ in range(2)]
    ST5 = A.t([128, 16], F32)
    nw = 0
    no = 0
    NST = L // 512

    def pre_front(st2):
        xb2 = st2 % 2
        o5 = xb2 * 8
        MSET("pool", ST5[:, o5:o5 + 8], 0.0, [f"f_ss{xb2}{a}" for a in range(4)] + [f"fr{xb2}{a}" for a in range(4)])
        for a in range(4):
            ACT(JNK2, X1s[xb2][:, a, :], AF.Square, [f"X1s{xb2}"], ["JNK2", f"f_ss{xb2}{a}"], accum=ST5[:, o5 + 2 * a:o5 + 2 * a + 1])
            RSTD(ST5[:, o5 + 2 * a + 1:o5 + 2 * a + 2], ST5[:, o5 + 2 * a:o5 + 2 * a + 1], D, f"fr{xb2}{a}", f"f_ss{xb2}{a}")
            ACT(XB2[a], X1s[xb2][:, a, :], AF.Copy, [f"X1s{xb2}", f"fr{xb2}{a}"], [f"XB2{a}"],
                scale=ST5[:, o5 + 2 * a + 1:o5 + 2 * a + 2])

    DMA("sp", X1s[0], X1_d[0:512, :].rearrange("(a p) n -> p a n", p=128), "lX1s0", [], ["X1s0"])
    pre_front(0)
    for st_ in range(NST):
        r0 = st_ * 512
        xb = st_ % 2
        if st_ + 1 < NST:
            DMA("sp", X1s[1 - xb], X1_d[r0 + 512:r0 + 1024, :].rearrange("(a p) n -> p a n", p=128), f"lX1s{1 - xb}", [],
                [f"X1s{1 - xb}"])
        for a in range(4):
            ab = a % 2
            for k in range(16):
                pi = 4 * ab + k // 4
                MM(ps[pi][:, (k % 4) * 128:(k % 4 + 1) * 128], XB2[a][:, k * 128:(k + 1) * 128], IDB, True, True,
                   [f"XB2{a}", "IDB"], [f"ps{pi}"])
            for k in range(16):
                pi = 4 * ab + k // 4
                TS("dve", H2T[:, k, a * 128:(a + 1) * 128], ps[pi][:, (k % 4) * 128:(k % 4 + 1) * 128],
                   GS[:, 32 + k:33 + k], GS[:, 48 + k:49 + k], ALU.mult, ALU.add, [f"ps{pi}", "GS"], [f"H2T{k}"])
        for pr in range(22):
            wb = nw % 2
            nw += 1
            DMA("sp", W1t[wb].rearrange("p a b -> p (a b)"), W1s[pr], f"lW1t{wb}", [], [f"W1t{wb}"])
            DMA("sp", W3t[wb].rearrange("p a b -> p (a b)"), W3s[pr], f"lW3t{wb}", [], [f"W3t{wb}"])
            for c2 in range(2):
                ffc = pr * 2 + c2
                sb_ = ffc % 2
                pa, pb_ = ps[sb_ * 2], ps[sb_ * 2 + 1]
                na, nb_ = f"ps{sb_ * 2}", f"ps{sb_ * 2 + 1}"
                for k in range(16):
                    MM(pa[:, :], W1t[wb][:, k, c2 * 128:(c2 + 1) * 128], H2T[:, k, :], k == 0, k == 15,
                       [f"W1t{wb}", f"H2T{k}"], [na])
                for k in range(16):
                    MM(pb_[:, :], W3t[wb][:, k, c2 * 128:(c2 + 1) * 128], H2T[:, k, :], k == 0, k == 15,
                       [f"W3t{wb}", f"H2T{k}"], [nb_])
                ACT(SA[sb_], pa[:, :], AF.Silu, [na], [f"SA{sb_}"])
                TT("dve", GT[:, ffc, :], SA[sb_], pb_[:, :], ALU.mult, [f"SA{sb_}", nb_], [f"GT{ffc}"])
        if st_ + 1 < NST:
            pre_front(st_ + 1)
        for h in range(2):
            for g4 in range(11):
                wb = nw % 2
                nw += 1
                DMA("sp", W2t[wb].rearrange("p a b -> p (a b)"), W2s[h, g4], f"lW2t{wb}", [], [f"W2t{wb}"])
                for j in range(4):
                    ffc = g4 * 4 + j
                    for a in range(4):
                        for nb in range(2):
                            MM(ps[a * 2 + nb][:, :], GT[:, ffc, a * 128:(a + 1) * 128], W2t[wb][:, j, nb * 512:(nb + 1) * 512],
                               ffc == 0, ffc == 43, [f"GT{ffc}", f"W2t{wb}"], [f"ps{a * 2 + nb}"])
            for a in range(4):
                for nb in range(2):
                    ob = no % 3
                    no += 1
                    cs = slice(h * 1024 + nb * 512, h * 1024 + (nb + 1) * 512)
                    TT("dve", OUTs[ob], ps[a * 2 + nb][:, :], X1s[xb][:, a, cs], ALU.add, [f"ps{a * 2 + nb}", f"X1s{xb}"],
                       [f"OUTs{ob}"])
                    DMA("pool", y_out[r0 + a * 128:r0 + (a + 1) * 128, cs], OUTs[ob], f"sOUT{ob}", [f"OUTs{ob}"], [])
    P.barrier()

    P.emit()
    es.close()
    return nc


def rope_tables_host(L):
    pos = np.arange(L, dtype=np.float32)
    inv_freq = (np.float32(10000.0) ** (-np.arange(0, 64, 2, dtype=np.float32) / np.float32(64))).astype(np.float32)
    ang = (pos[:, None] * inv_freq[None, :]).astype(np.float32)
    return np.cos(ang).astype(np.float32), np.sin(ang).astype(np.float32)


_S5C = {}


def s5_constants():
    if _S5C:
        return _S5C
    sel = np.zeros((128, 64, 128), np.float32)
    selt = np.zeros((128, 64, 128), np.float32)
    for s in range(8):
        for gl in range(8):
            for c in range(16):
                sel[gl * 16 + c, s * 8 + gl, s * 16 + c] = 1.0
                selt[s * 16 + c, s * 8 + gl, gl * 16 + c] = 1.0
    si = np.arange(128)[:, None] // 16
    ti = np.arange(128)[None, :] // 16
    _S5C.update(sel=sel, selt=selt, maskf=(si <= ti).astype(np.float32), maskb=(si >= ti).astype(np.float32))
    return _S5C


def make_core_inputs(inp, seq_x, seq_c, L):
    cos, sin = rope_tables_host(L)
    m = {"x": np.ascontiguousarray(seq_x[:L]), "c": np.ascontiguousarray(seq_c),
         "ident": np.eye(128, dtype=np.float32), "ropec": cos, "ropes": sin}
    m.update(s5_constants())
    for k, v in inp.items():
        if k in ("x_prompt", "x_sample", "c_prompt", "c_sample"):
            continue
        m[k] = np.ascontiguousarray(v[0])
    return m


def kernel(**inputs):
    L = 8192
    nc = build(L)
    xs = [inputs["x_prompt"][i] for i in range(4)] + [inputs["x_sample"][0]]
    cs = [inputs["c_prompt"][i] for i in range(4)] + [inputs["c_sample"][0]]
    in_maps = []
    for core in range(8):
        i = core if core < 5 else core - 5
        in_maps.append(make_core_inputs(inputs, xs[i], cs[i], L))
    res = run_bass_kernel_spmd(nc, in_maps, core_ids=list(range(8)))
    ys = [np.asarray(res.results[i]["y"], dtype=np.float32) for i in range(5)]
    y_prompt = np.stack(ys[:4], axis=0)
    y_sample = ys[4][None]
    return (y_prompt, y_sample)
```
